# Optimizing a Trainium2 kernel written in Bass

```python
import jax, jax.numpy as jnp
from jax import lax
import numpy as np

D_MODEL = 1024
BATCH = 8
SEQ = 4096
DEPTH = 2

N_MEM = 256
RWKV_HEAD = 64
RWKV_WIDTH = D_MODEL
RWKV_HEADS = RWKV_WIDTH // RWKV_HEAD
DECAY_LORA = 64
AAA_LORA = 64
VRES_LORA = 32
RWKV_GN_EPS = 64e-5
CONV_WIDTH = D_MODEL
CONV_KERNEL = 31
LN_EPS = 1e-5
MEM_HEADS = 4
MEM_WIDTH = D_MODEL
MEM_HEAD_DIM = MEM_WIDTH // MEM_HEADS
N_BRANCH = 3
RMS_EPS = 1e-6

SHIFT_SIZES = (RWKV_WIDTH, RWKV_WIDTH, RWKV_WIDTH, DECAY_LORA, AAA_LORA)
REST_SIZES = (RWKV_WIDTH, 2 * CONV_WIDTH, CONV_WIDTH, MEM_WIDTH, MEM_WIDTH, N_BRANCH * D_MODEL)
RWKV_SHIFT = sum(SHIFT_SIZES)
N_IN = RWKV_SHIFT + sum(REST_SIZES)

kernel_name = "hybrid_rwkv7_conformer_memxattn_gated"


def _split(a, sizes):
    pts = [int(v) for v in np.cumsum(sizes)[:-1]]
    return jnp.split(a, pts, axis=-1)


def rms_norm(x, g):
    xf = x.astype(jnp.float32)
    y = xf * lax.rsqrt(jnp.mean(xf * xf, axis=-1, keepdims=True) + RMS_EPS)
    return (y * g.astype(jnp.float32)).astype(x.dtype)


def token_shift_mix(p, mu):
    prev = jnp.pad(p, ((0, 0), (1, 0), (0, 0)))[:, :-1]
    return p + (prev - p) * mu


def wkv7_scan(r, decay, k, v, a_vec, b_vec):
    B, S, H, N = r.shape
    seqs = tuple(jnp.moveaxis(t.astype(jnp.float32), 1, 0) for t in (r, decay, k, v, a_vec, b_vec))

    def step(state, inp):
        r_t, w_t, k_t, v_t, a_t, b_t = inp
        sa = jnp.einsum('bhvk,bhk->bhv', state, a_t)
        state = (state * w_t[:, :, None, :]
                 + sa[..., :, None] * b_t[:, :, None, :]
                 + v_t[..., :, None] * k_t[:, :, None, :])
        y = jnp.einsum('bhvk,bhk->bhv', state, r_t)
        return state, y

    s0 = jnp.zeros((B, H, N, N), jnp.float32)
    _, ys = lax.scan(step, s0, seqs)
    return jnp.moveaxis(ys, 0, 1)


def hybrid_layer(x, mem, v_first, vres, g_norm, w_in, mu_shift, w0, w_decay_up, a0, w_aaa_up,
                 k_k, k_a, r_k, gn_g, gn_b, w_proj_rwkv, b_glu, w_dw, b_dw, ln_g, ln_b,
                 w_proj_conv, b_proj_conv, g_mem_norm, w_mem_kv, w_proj_mem, w_out):
    B, S, _ = x.shape
    H, N = RWKV_HEADS, RWKV_HEAD
    h = rms_norm(x, g_norm)
    w_all = w_in if vres is None else jnp.concatenate([w_in, vres[0]], axis=1)
    proj = h @ w_all

    shifted = token_shift_mix(proj[..., :RWKV_SHIFT], mu_shift)
    r, k, v, w_lo, a_lo = _split(shifted, SHIFT_SIZES)
    rwkv_gate, glu_in, conv_gate, q, mem_gate, merge_logits = _split(proj[..., RWKV_SHIFT:N_IN], REST_SIZES)

    decay_log = -jax.nn.softplus(-(w0 + jnp.tanh(w_lo) @ w_decay_up)) - 0.5
    decay = jnp.exp(-jnp.exp(decay_log.astype(jnp.float32)))
    a = jax.nn.sigmoid(a0 + a_lo @ w_aaa_up)
    if vres is None:
        v_first = v
    else:
        _, mu_vres, v0, w_vres_up = vres
        v_lo = token_shift_mix(proj[..., N_IN:], mu_vres)
        v = v + (v_first - v) * jax.nn.sigmoid(v0 + v_lo @ w_vres_up)
    kk = (k * k_k).reshape(B, S, H, N).astype(jnp.float32)
    kk = kk / jnp.maximum(jnp.sqrt(jnp.sum(kk * kk, axis=-1, keepdims=True)), 1e-12)
    k = k * (1.0 + (a - 1.0) * k_a)
    rh = r.reshape(B, S, H, N)
    kh = k.reshape(B, S, H, N)
    vh = v.reshape(B, S, H, N)
    ah = a.reshape(B, S, H, N).astype(jnp.float32)
    wkv = wkv7_scan(rh, decay.reshape(B, S, H, N), kh, vh, -kk, kk * ah)
    mu = jnp.mean(wkv, axis=-1, keepdims=True)
    var = jnp.mean(jnp.square(wkv - mu), axis=-1, keepdims=True)
    wkv = ((wkv - mu) * lax.rsqrt(var + RWKV_GN_EPS)).reshape(B, S, RWKV_WIDTH)
    wkv = (wkv * gn_g.astype(jnp.float32) + gn_b.astype(jnp.float32)).astype(x.dtype)
    bonus = jnp.sum(rh * kh * r_k, axis=-1, keepdims=True) * vh
    o_rwkv = wkv + bonus.reshape(B, S, RWKV_WIDTH)
    y_rwkv = (o_rwkv * jax.nn.silu(rwkv_gate)) @ w_proj_rwkv

    glu_in = glu_in + b_glu
    u = glu_in[..., :CONV_WIDTH] * jax.nn.sigmoid(glu_in[..., CONV_WIDTH:])
    u = lax.conv_general_dilated(u, w_dw[:, None, :], window_strides=(1,),
                                 padding=[(CONV_KERNEL - 1, 0)],
                                 dimension_numbers=('NWC', 'WIO', 'NWC'),
                                 feature_group_count=CONV_WIDTH) + b_dw
    uf = u.astype(jnp.float32)
    um = jnp.mean(uf, axis=-1, keepdims=True)
    uv = jnp.mean(jnp.square(uf - um), axis=-1, keepdims=True)
    u = ((uf - um) * lax.rsqrt(uv + LN_EPS) * ln_g.astype(jnp.float32) + ln_b.astype(jnp.float32)).astype(x.dtype)
    u = jax.nn.silu(u) * jax.nn.silu(conv_gate)
    y_conv = u @ w_proj_conv + b_proj_conv

    m = rms_norm(mem, g_mem_norm)
    km, vm = _split(m @ w_mem_kv, (MEM_WIDTH, MEM_WIDTH))
    qh = q.reshape(B, S, MEM_HEADS, MEM_HEAD_DIM)
    kmh = km.reshape(B, -1, MEM_HEADS, MEM_HEAD_DIM)
    vmh = vm.reshape(B, -1, MEM_HEADS, MEM_HEAD_DIM)
    scores = jnp.einsum('bshd,bmhd->bhsm', qh, kmh).astype(jnp.float32) * (MEM_HEAD_DIM ** -0.5)
    probs = jax.nn.softmax(scores, axis=-1).astype(x.dtype)
    att = jnp.einsum('bhsm,bmhd->bshd', probs, vmh).reshape(B, S, MEM_WIDTH)
    y_mem = (att * jax.nn.silu(mem_gate)) @ w_proj_mem

    gates = jax.nn.sigmoid(merge_logits).reshape(B, S, N_BRANCH, D_MODEL)
    y = gates[:, :, 0] * y_rwkv + gates[:, :, 1] * y_conv + gates[:, :, 2] * y_mem
    return x + y @ w_out, v_first


def setup_inputs(seed: int = 0) -> dict:
    key = jax.random.key(seed)
    ks = jax.random.split(key, 32)
    f = jnp.float32
    D, W, C, L = D_MODEL, RWKV_WIDTH, CONV_WIDTH, DEPTH
    nrm = lambda k, shape, s: jax.random.normal(k, shape, f) * s
    return {
        "x": jax.random.normal(ks[0], (BATCH, SEQ, D), f),
        "mem": jax.random.normal(ks[1], (BATCH, N_MEM, D), f),
        "g_norm": 1.0 + nrm(ks[2], (L, D), 0.02),
        "w_in": nrm(ks[3], (L, D, N_IN), D ** -0.5),
        "mu_shift": jax.random.uniform(ks[4], (L, RWKV_SHIFT), f),
        "w0": jax.random.uniform(ks[5], (L, W), f, -5.0, 0.0),
        "w_decay_up": nrm(ks[6], (L, DECAY_LORA, W), 0.5 * DECAY_LORA ** -0.5),
        "a0": nrm(ks[7], (L, W), 0.1),
        "w_aaa_up": nrm(ks[8], (L, AAA_LORA, W), 0.5 * AAA_LORA ** -0.5),
        "k_k": 0.85 + nrm(ks[9], (L, W), 0.02),
        "k_a": 1.0 + nrm(ks[10], (L, W), 0.02),
        "r_k": nrm(ks[11], (L, RWKV_HEADS, RWKV_HEAD), 0.1),
        "gn_g": 1.0 + nrm(ks[12], (L, W), 0.02),
        "gn_b": nrm(ks[13], (L, W), 0.01),
        "w_proj_rwkv": nrm(ks[14], (L, W, D), W ** -0.5),
        "w_vres_down": nrm(ks[15], (L - 1, D, VRES_LORA), D ** -0.5),
        "mu_vres": jax.random.uniform(ks[16], (L - 1, VRES_LORA), f),
        "v0": nrm(ks[17], (L - 1, W), 0.1),
        "w_vres_up": nrm(ks[18], (L - 1, VRES_LORA, W), 0.5 * VRES_LORA ** -0.5),
        "b_glu": nrm(ks[19], (L, 2 * C), 0.01),
        "w_dw": nrm(ks[20], (L, CONV_KERNEL, C), CONV_KERNEL ** -0.5),
        "b_dw": nrm(ks[21], (L, C), 0.01),
        "ln_g": 1.0 + nrm(ks[22], (L, C), 0.02),
        "ln_b": nrm(ks[23], (L, C), 0.01),
        "w_proj_conv": nrm(ks[24], (L, C, D), C ** -0.5),
        "b_proj_conv": nrm(ks[25], (L, D), 0.01),
        "g_mem_norm": 1.0 + nrm(ks[26], (L, D), 0.02),
        "w_mem_kv": nrm(ks[27], (L, D, 2 * MEM_WIDTH), D ** -0.5),
        "w_proj_mem": nrm(ks[28], (L, MEM_WIDTH, D), MEM_WIDTH ** -0.5),
        "w_out": nrm(ks[29], (L, D, D), D ** -0.5),
        "g_final": 1.0 + nrm(ks[30], (D,), 0.02),
    }


def reference(x, mem, g_norm, w_in, mu_shift, w0, w_decay_up, a0, w_aaa_up, k_k, k_a, r_k,
              gn_g, gn_b, w_proj_rwkv, w_vres_down, mu_vres, v0, w_vres_up, b_glu, w_dw, b_dw,
              ln_g, ln_b, w_proj_conv, b_proj_conv, g_mem_norm, w_mem_kv, w_proj_mem, w_out,
              g_final):
    v_first = None
    for i in range(DEPTH):
        vres = None if i == 0 else (w_vres_down[i - 1], mu_vres[i - 1], v0[i - 1], w_vres_up[i - 1])
        x, v_first = hybrid_layer(
            x, mem, v_first, vres, g_norm[i], w_in[i], mu_shift[i], w0[i], w_decay_up[i], a0[i],
            w_aaa_up[i], k_k[i], k_a[i], r_k[i], gn_g[i], gn_b[i], w_proj_rwkv[i], b_glu[i],
            w_dw[i], b_dw[i], ln_g[i], ln_b[i], w_proj_conv[i], b_proj_conv[i], g_mem_norm[i],
            w_mem_kv[i], w_proj_mem[i], w_out[i])
    return rms_norm(x, g_final)
```

```python
import numpy as np
import concourse.bass as bass
import concourse.mybir as mybir
from concourse.bass_utils import run_bass_kernel_spmd

F32 = mybir.dt.float32
BF16 = mybir.dt.bfloat16
AF = mybir.ActivationFunctionType
ALU = mybir.AluOpType
AX = mybir.AxisListType
NDS = 24

D = 1024
SEQ = 4096
T = 512
NMEM = 256
KC = 8
C0 = float(np.exp(-0.5))


class Buf:
    __slots__ = ("ap", "w", "r")

    def __init__(self, ap):
        self.ap = ap
        self.w = None
        self.r = {}

    def __getitem__(self, k):
        return self.ap[k]


class Prog:
    def __init__(self, nc):
        self.nc = nc
        self.eng = dict(pe=nc.tensor, dve=nc.vector, act=nc.scalar, pool=nc.gpsimd, sp=nc.sync)
        self.esem = {k: nc.alloc_semaphore("es_" + k) for k in self.eng}
        self.ecnt = {k: 0 for k in self.eng}
        self.seen = {k: {} for k in self.eng}
        self.dsem, self.dtgt, self.dnext = {}, {}, {}
        self.ninst = 0

    def _wait(self, e, ev):
        sem, key, val = ev
        if key == ("e", e) and e == "pe":
            return
        if self.seen[e].get(key, 0) >= val:
            return
        self.eng[e].wait_ge(sem, val)
        self.seen[e][key] = val

    def _deps(self, e, reads, writes):
        for b in reads:
            if b.w is not None:
                self._wait(e, b.w)
        for b in writes:
            if b.w is not None:
                self._wait(e, b.w)
            for ev in b.r.values():
                self._wait(e, ev)

    def _record(self, ev, reads, writes):
        for b in reads:
            b.r[ev[1]] = ev
        for b in writes:
            b.w = ev
            b.r = {}

    def op(self, e, fn, reads=(), writes=()):
        self._deps(e, reads, writes)
        inst = fn(self.eng[e])
        self.ecnt[e] += 1
        inst.then_inc(self.esem[e], 1)
        self._record((self.esem[e], ("e", e), self.ecnt[e]), reads, writes)
        self.ninst += 1

    def dma(self, q, out_ap, in_ap, reads=(), writes=(), **kw):
        if q not in self.dsem:
            self.dsem[q] = [self.nc.alloc_semaphore(f"ds_{q}{i}") for i in range(NDS)]
            self.dtgt[q] = [0] * NDS
            self.dnext[q] = 0
        j = self.dnext[q]
        self.dnext[q] = (j + 1) % NDS
        key = ("d", q, j)
        if self.dtgt[q][j] > 0:
            self._wait(q, (self.dsem[q][j], key, self.dtgt[q][j]))
        self._deps(q, reads, writes)
        inst = self.eng[q].dma_start(out=out_ap, in_=in_ap, **kw)
        self.dtgt[q][j] += 16
        inst.then_inc(self.dsem[q][j], 16)
        self._record((self.dsem[q][j], key, self.dtgt[q][j]), reads, writes)
        self.ninst += 1

    def finish(self, e="sp"):
        for q in self.dsem:
            for j in range(NDS):
                if self.dtgt[q][j] > 0:
                    self._wait(e, (self.dsem[q][j], ("d", q, j), self.dtgt[q][j]))
        for k in self.eng:
            if k != e and self.ecnt[k] > 0:
                self._wait(e, (self.esem[k], ("e", k), self.ecnt[k]))


PV = {}
_o = 0
for _n, _w in [("g_norm", 8), ("mu", 26), ("w0", 8), ("a0", 8), ("k_k", 8), ("k_a", 8), ("r_k", 8),
               ("gn_g", 8), ("gn_b", 8), ("v0", 8), ("b_glu", 16), ("w_dw", 248), ("b_dw", 8),
               ("ln_g", 8), ("ln_b", 8), ("b_pc", 8), ("g_mem", 8), ("g_final", 8), ("omu", 26), ("omka", 8)]:
    PV[_n] = _o
    _o += _w
NPV = _o

WCOL = {}
_o = 0
for _n, _w in [("lora", 256)] + [(f"rkvg{c}", 512) for c in range(8)] + \
        [(f"pr{j}", 512) for j in range(4)] + [(f"glu{j}", 512) for j in range(4)] + \
        [(f"cg{j}", 512) for j in range(2)] + [(f"pc{j}", 512) for j in range(4)] + \
        [(f"q{j}", 512) for j in range(2)] + [(f"mg{j}", 512) for j in range(2)] + \
        [(f"pm{j}", 512) for j in range(4)] + [(f"wo{j}", 512) for j in range(2)] + \
        [(f"kv{j}", 512) for j in range(4)]:
    WCOL[_n] = (_o, _w)
    _o += _w
TOTC = _o


def _fm(v):
    return np.ascontiguousarray(v.reshape(8, 128).T)


def host_prep(inp):
    f = np.float32
    L = 2
    pv = np.zeros((L, 128, NPV), f)
    wbig = np.zeros((L, 128, KC, TOTC), f)
    lora = np.zeros((L, 128, 2, 1024), f)
    for l in range(L):
        def put(name, arr):
            pv[l][:, PV[name]:PV[name] + arr.shape[1]] = arr
        put("g_norm", _fm(inp["g_norm"][l]))
        mu = inp["mu_shift"][l]
        mucols = np.zeros((128, 26), f)
        mucols[:, 0:25] = mu.reshape(25, 128).T
        if l >= 1:
            mucols[0:32, 25] = inp["mu_vres"][l - 1]
        put("mu", mucols)
        for n in ["w0", "a0", "k_k", "k_a", "gn_g", "gn_b", "b_dw", "ln_g", "ln_b", "g_mem"]:
            src = {"g_mem": "g_mem_norm"}.get(n, n)
            put(n, _fm(inp[src][l]))
        put("r_k", _fm(inp["r_k"][l].reshape(-1)))
        put("b_pc", _fm(inp["b_proj_conv"][l]))
        if l >= 1:
            put("v0", _fm(inp["v0"][l - 1]))
        put("b_glu", np.ascontiguousarray(inp["b_glu"][l].reshape(16, 128).T))
        wd = inp["w_dw"][l]
        put("w_dw", np.ascontiguousarray(wd.reshape(31, 8, 128).transpose(2, 1, 0).reshape(128, 248)))
        put("g_final", _fm(inp["g_final"]))
        w_in = inp["w_in"][l]
        Wc = np.zeros((D, TOTC), f)

        def setc(name, off, arr):
            o, w = WCOL[name]
            Wc[:, o + off:o + off + arr.shape[1]] = arr
        setc("lora", 0, w_in[:, 3072:3200])
        if l >= 1:
            setc("lora", 128, inp["w_vres_down"][l - 1])
        for c in range(8):
            setc(f"rkvg{c}", 0, w_in[:, c * 128:(c + 1) * 128])
            setc(f"rkvg{c}", 128, w_in[:, 1024 + c * 128:1024 + (c + 1) * 128])
            setc(f"rkvg{c}", 256, w_in[:, 2048 + c * 128:2048 + (c + 1) * 128])
            setc(f"rkvg{c}", 384, w_in[:, 3200 + c * 128:3200 + (c + 1) * 128])
        for br, (pn, wp) in enumerate([("pr", inp["w_proj_rwkv"][l]), ("pc", inp["w_proj_conv"][l]),
                                       ("pm", inp["w_proj_mem"][l])]):
            for j in range(4):
                setc(f"{pn}{j}", 0, wp[:, j * 256:(j + 1) * 256])
                mo = 9344 + br * 1024 + j * 256
                setc(f"{pn}{j}", 256, w_in[:, mo:mo + 256])
        for j in range(4):
            for i in range(2):
                c = 2 * j + i
                setc(f"glu{j}", i * 256, w_in[:, 4224 + c * 128:4224 + (c + 1) * 128])
                setc(f"glu{j}", i * 256 + 128, w_in[:, 5248 + c * 128:5248 + (c + 1) * 128])
        for j in range(2):
            setc(f"cg{j}", 0, w_in[:, 6272 + j * 512:6272 + (j + 1) * 512])
            setc(f"q{j}", 0, w_in[:, 7296 + j * 512:7296 + (j + 1) * 512])
            setc(f"mg{j}", 0, w_in[:, 8320 + j * 512:8320 + (j + 1) * 512])
            setc(f"wo{j}", 0, inp["w_out"][l][:, j * 512:(j + 1) * 512])
        for j in range(4):
            setc(f"kv{j}", 0, inp["w_mem_kv"][l][:, j * 512:(j + 1) * 512])
        wbig[l] = Wc.reshape(KC, 128, TOTC).transpose(1, 0, 2)
        lora[l][0:64, 0] = inp["w_decay_up"][l]
        lora[l][64:128, 0] = inp["w_aaa_up"][l]
        if l >= 1:
            lora[l][0:32, 1] = inp["w_vres_up"][l - 1]
    cst = np.zeros((128, 8, 128), f)
    i = np.arange(128)
    cst[:, 0] = np.eye(128)
    cst[:, 1] = 1.0
    cst[:, 2] = (i[:, None] // 64 == i[None, :] // 64)
    cst[:, 3] = (i[:, None] < i[None, :])
    cst[:, 4] = (i[:, None] <= i[None, :])
    cst[:, 5] = (i[:, None] > i[None, :])
    rm = np.ones((128, 512), f)
    rm[:, 0::128] = 0.0
    return pv, wbig, lora, cst, rm


class _Stop(Exception):
    pass


def build(nc, NT=SEQ // T, dbg_names=(), stop_after=None):
    P = Prog(nc)
    try:
        _build(nc, P, NT, dbg_names, stop_after)
    except _Stop:
        pass
    P.finish()
    return P


def _build(nc, P, NT, dbg_names, stop_after):
    def chk(tag):
        if tag == stop_after:
            raise _Stop()
    dt = nc.dram_tensor
    xT_d = dt("xT", [D, SEQ], F32, kind="ExternalInput").ap()
    memT_d = dt("memT", [D, NMEM], F32, kind="ExternalInput").ap()
    pv_d = dt("pv", [2, 128, NPV], F32, kind="ExternalInput").ap()
    wbig_d = dt("wbig", [2, 128, KC, TOTC], F32, kind="ExternalInput").ap()
    lora_d = dt("lora", [2, 128, 2, 1024], F32, kind="ExternalInput").ap()
    cst_d = dt("cst", [128, 8, 128], F32, kind="ExternalInput").ap()
    rm_d = dt("rm", [128, 512], F32, kind="ExternalInput").ap()
    outT_d = dt("outT", [D, SEQ], F32, kind="ExternalOutput").ap()
    dbg_d = None
    if dbg_names:
        dbg_d = dt("dbg", [len(dbg_names), 128, 512], F32, kind="ExternalOutput").ap()
    Bdram_in = Buf(None)
    Bout = Buf(None)
    cnt = [0]

    def sb(shape, dtype, name=None):
        cnt[0] += 1
        return nc.alloc_sbuf_tensor(name or f"t{cnt[0]}", list(shape), dtype)

    def sbuf(shape, dtype):
        return Buf(sb(shape, dtype).ap())

    big8 = [sbuf([128, T], F32) for _ in range(8)]
    cst_f = Buf(big8[0].ap.rearrange("p (a b) -> p a b", a=4))
    cst_f2 = Buf(big8[1].ap.rearrange("p (a b) -> p a b", a=4))
    P.dma("sp", cst_f.ap, cst_d[:, 0:4, :], reads=[Bdram_in], writes=[cst_f, big8[0]])
    P.dma("sp", cst_f2.ap, cst_d[:, 4:8, :], reads=[Bdram_in], writes=[cst_f2, big8[1]])
    cst_b = sbuf([128, 8, 128], BF16)
    P.op("dve", lambda e: e.tensor_copy(out=cst_b[:, 0:4, :], in_=cst_f.ap), [cst_f, big8[0]], [cst_b])
    P.op("dve", lambda e: e.tensor_copy(out=cst_b[:, 4:8, :], in_=cst_f2.ap), [cst_f2, big8[1]], [cst_b])
    ident, ones, bones = cst_b[:, 0, :], cst_b[:, 1, :], cst_b[:, 2, :]
    identf = sbuf([128, 128], F32)
    P.op("dve", lambda e: e.tensor_copy(out=identf.ap, in_=cst_f[:, 0, :]), [cst_f, big8[0]], [identf])
    stage = sbuf([128, 512], F32)
    m12 = Buf(cst_b[:, 3:5, :])
    m12.w = None
    mSL2 = sbuf([128, 2, 128], BF16)
    id2 = sbuf([128, 2, 128], BF16)
    for h in range(2):
        P.op("dve", lambda e: e.tensor_copy(out=mSL2[:, h, :], in_=cst_b[:, 5, :]), [cst_b], [mSL2])
        P.op("dve", lambda e: e.tensor_copy(out=id2[:, h, :], in_=cst_b[:, 0, :]), [cst_b], [id2])
    rmf = Buf(big8[2].ap)
    P.dma("sp", rmf.ap, rm_d, reads=[Bdram_in], writes=[big8[2]])
    rmask = sbuf([128, 512], BF16)
    P.op("dve", lambda e: e.tensor_copy(out=rmask.ap, in_=big8[2].ap), [big8[2]], [rmask])
    pvs = sbuf([128, 2, NPV], F32)
    for l in range(2):
        P.dma("sp", pvs[:, l, :], pv_d[l], reads=[Bdram_in], writes=[pvs])
    for l in range(2):
        for (src, dst, w) in [("mu", "omu", 26), ("k_a", "omka", 8)]:
            P.op("dve", lambda e: e.tensor_scalar(out=pvs[:, l, PV[dst]:PV[dst] + w], in0=pvs[:, l, PV[src]:PV[src] + w],
                                                  scalar1=-1.0, scalar2=1.0, op0=ALU.mult, op1=ALU.add), [pvs], [pvs])
    epsc = sbuf([128, 4], F32)
    for i, v in enumerate([1e-6, 64e-5, 1e-5, 0.0]):
        P.op("dve", lambda e: e.memset(epsc[:, i:i + 1], v), [], [epsc])

    def pcol(l, name, c=0):
        o = PV[name] + c
        return pvs[:, l, o:o + 1]

    lor = [sbuf([128, 2, 1024], BF16) for _ in range(2)]
    for l in range(2):
        P.dma("pool", lor[l].ap, lora_d[l], reads=[Bdram_in], writes=[lor[l]])

    banks = [Buf(nc.alloc_psum_tensor(f"ps{i}", [128, 512], F32).ap()) for i in range(7)]
    ring = banks[:5]
    pin = banks[5:]
    pstT = nc.alloc_psum_tensor("psT", [128, 1024], BF16).ap()
    pst_halves = [Buf(pstT[:, 0:512]), Buf(pstT[:, 512:1024])]
    pst_i = [0]

    def pst_next():
        b = pst_halves[pst_i[0] % 2]
        pst_i[0] += 1
        return b
    rp = [0]

    def ps_next():
        b = ring[rp[0] % len(ring)]
        rp[0] += 1
        return b

    def bfv(b):
        return b.ap.bitcast(BF16)

    class Ring:
        def __init__(self, n, shape, dtype):
            self.b = [sbuf(shape, dtype) for _ in range(n)]
            self.i = 0

        def get(self):
            b = self.b[self.i % len(self.b)]
            self.i += 1
            return b

    slots = [sbuf([128, 512], F32) for _ in range(12)]
    tf = Ring(0, [128, 512], F32)
    tf.b = slots[0:10]
    tb = Ring(6, [128, 512], BF16)
    lob_, vlo_ = sbuf([128, 512], BF16), sbuf([128, 512], BF16)
    wring = Ring(2, [128, KC, 512], BF16)

    xT = [sbuf([128, T], F32) for _ in range(8)]
    vf = [sbuf([128, T], F32) for _ in range(8)]
    hT = [sbuf([128, T], BF16) for _ in range(8)]
    OG = [sbuf([128, T], BF16) for _ in range(8)]
    yacc = [sbuf([128, T], F32) for _ in range(8)]
    carry = [sbuf([128, 26], F32) for _ in range(2)]
    Sf = [[sbuf([128, 64], F32) for _ in range(8)] for _ in range(2)]
    Sb = [[sbuf([128, 2, 64], BF16) for _ in range(8)] for _ in range(2)]
    halo = [sbuf([128, 8, 30], BF16) for _ in range(2)]
    ubr = Ring(2, [128, 30 + T], BF16)
    dg_keep = sbuf([128, 31, 128], BF16)
    prT_keep = [[sbuf([128, T], BF16) for _ in range(2)] for _ in range(2)]
    small_keep = sbuf([128, 8], F32)
    kmT = [[sbuf([128, NMEM], BF16) for _ in range(8)] for _ in range(2)]
    vmt = [[sbuf([128, D], BF16) for _ in range(2)] for _ in range(2)]
    for l in range(2):
        P.op("pool", lambda e: e.memset(carry[l].ap, 0.0), [], [carry[l]])
        P.op("pool", lambda e: e.memset(halo[l].ap, 0.0), [], [halo[l]])
        for c in range(8):
            P.op("pool", lambda e: e.memset(Sf[l][c].ap, 0.0), [], [Sf[l][c]])
            P.op("pool", lambda e: e.memset(Sb[l][c].ap, 0.0), [], [Sb[l][c]])

    dbg_list = list(dbg_names)
    dbg_buf = sbuf([128, 512], F32) if dbg_names else None

    def dump(name, b, ap=None):
        if name in dbg_list:
            i = dbg_list.index(name)
            a = b.ap if ap is None else ap
            t = dbg_buf
            P.op("dve", lambda e: e.tensor_copy(out=t[:, 0:a.shape[-1]], in_=a), [b], [t])
            P.dma("sp", dbg_d[i][0:a.shape[0], 0:a.shape[-1]], t[0:a.shape[0], 0:a.shape[-1]], reads=[t], writes=[Bout])
            dbg_list[i] = None

    def act(out_b, out_ap, in_b, in_ap, func, extra_r=(), **kw):
        P.op("act", lambda e: e.activation(out=out_ap, in_=in_ap, func=func, **kw), [in_b] + list(extra_r), [out_b])

    def tt(eng, out_b, out_ap, a_b, a_ap, b_b, b_ap, op):
        P.op(eng, lambda e: e.tensor_tensor(out=out_ap, in0=a_ap, in1=b_ap, op=op), [a_b, b_b], [out_b])

    def stt(out_b, out_ap, a_b, a_ap, scalar, b_b, b_ap, op0, op1, extra_r=()):
        P.op("dve", lambda e: e.scalar_tensor_tensor(out=out_ap, in0=a_ap, scalar=scalar, in1=b_ap, op0=op0, op1=op1),
             [a_b, b_b] + list(extra_r), [out_b])

    def ts(eng, out_b, out_ap, a_b, a_ap, s1, s2, op0, op1=None, extra_r=()):
        if op1 is None:
            P.op(eng, lambda e: e.tensor_scalar(out=out_ap, in0=a_ap, scalar1=s1, scalar2=None, op0=op0),
                 [a_b] + list(extra_r), [out_b])
        else:
            P.op(eng, lambda e: e.tensor_scalar(out=out_ap, in0=a_ap, scalar1=s1, scalar2=s2, op0=op0, op1=op1),
                 [a_b] + list(extra_r), [out_b])

    def cp(eng, out_b, out_ap, in_b, in_ap):
        if eng == "act":
            P.op("act", lambda e: e.copy(out=out_ap, in_=in_ap), [in_b], [out_b])
        elif eng == "dve":
            P.op(eng, lambda e: e.tensor_scalar(out=out_ap, in0=in_ap, scalar1=1.0, scalar2=None, op0=ALU.mult), [in_b], [out_b])
        else:
            P.op(eng, lambda e: e.tensor_copy(out=out_ap, in_=in_ap), [in_b], [out_b])

    def mm(out_b, out_ap, l_b, l_ap, r_b, r_ap, start=True, stop=True):
        P.op("pe", lambda e: e.matmul(out_ap, lhsT=l_ap, rhs=r_ap, start=start, stop=stop), [l_b, r_b], [out_b])

    def tr(out_b, out_ap, in_b, in_ap):
        P.op("pe", lambda e: e.transpose(out_ap, in_ap, identf.ap), [in_b, identf], [out_b])

    def wload(l, name):
        o, w = WCOL[name]
        wb = wring.get()
        P.dma("pool", wb[:, :, 0:w], wbig_d[l][:, :, o:o + w], reads=[Bdram_in], writes=[wb])
        return wb

    def proj(l, name, rhs_for_chunk, consume):
        o, w = WCOL[name]
        wb = wload(l, name)
        for j in range(w // 128):
            rhs = rhs_for_chunk(j)
            if rhs is None:
                continue
            ps = ps_next()
            for kc in range(KC):
                mm(ps, ps.ap, wb, wb[:, kc, j * 128:(j + 1) * 128], rhs[kc], rhs[kc].ap, start=(kc == 0), stop=(kc == KC - 1))
            consume(j, ps)

    def bcast_stat(src_list, src_aps, scale, epscol):
        raise NotImplementedError

    def rms_to(l, gname, src, dst, n):
        ps = pin[0]
        for c in range(8):
            sq = tb.get()
            act(sq, sq[:, 0:n], src[c], src[c][:, 0:n], AF.Square)
            mm(ps, ps[:, 0:n], cst_b, ones, sq, sq[:, 0:n], start=(c == 0), stop=(c == 7))
        sd = tf.get()
        act(sd, sd[:, 0:n], ps, ps[:, 0:n], AF.Sqrt, extra_r=[epsc], scale=1.0 / D, bias=epsc[:, 0:1])
        rs = tf.get()
        P.op("dve", lambda e: e.reciprocal(out=rs[:, 0:n], in_=sd[:, 0:n]), [sd], [rs])
        for c in range(8):
            stt(dst[c], dst[c][:, 0:n], src[c], src[c][:, 0:n], pcol(l, gname, c), rs, rs[:, 0:n], ALU.mult, ALU.mult, extra_r=[pvs])
        return rs

    mraw = [Buf(big8[c][:, 0:NMEM]) for c in range(8)]
    for c in range(8):
        P.dma("sp", mraw[c].ap, memT_d[c * 128:(c + 1) * 128, :], reads=[Bdram_in], writes=[big8[c]])
        mraw[c] = big8[c]
    for l in range(2):
        mT = OG
        rms_to(l, "g_mem", mraw, mT, NMEM)
        for j in range(2):
            def cons(jj, ps, j=j):
                cp("act", kmT[l][j * 4 + jj], kmT[l][j * 4 + jj].ap, ps, ps[:, 0:NMEM])
            o, w = WCOL[f"kv{j}"]
            wb = wload(l, f"kv{j}")
            for jj in range(4):
                ps = ps_next()
                for kc in range(KC):
                    mm(ps, ps[:, 0:NMEM], wb, wb[:, kc, jj * 128:(jj + 1) * 128], mT[kc], mT[kc][:, 0:NMEM], start=(kc == 0), stop=(kc == KC - 1))
                cons(jj, ps)
        for j in range(2):
            wb = wload(l, f"kv{2 + j}")
            for mb in range(2):
                ps = ps_next()
                for kc in range(KC):
                    mm(ps, ps.ap, mT[kc], mT[kc][:, mb * 128:(mb + 1) * 128], wb, wb[:, kc, :], start=(kc == 0), stop=(kc == KC - 1))
                cp("act", vmt[l][mb], vmt[l][mb][:, j * 512:(j + 1) * 512], ps, ps.ap)

    chk("memkv")
    AR = sbuf([128, 4, 2, 128], BF16)
    Bbd = sbuf([128, 4, 2, 128], BF16)
    Kbd = sbuf([128, 4, 2, 128], BF16)
    P.op("pool", lambda e: e.memset(Bbd.ap, 0.0), [], [Bbd])
    P.op("pool", lambda e: e.memset(Kbd.ap, 0.0), [], [Kbd])
    NXr = Ring(3, [128, 2, 2, 128], BF16)
    Ar = Ring(3, [128, 2, 128], BF16)
    Arb = Ring(2, [128, 2, 128], BF16)
    Aak = Ring(2, [128, 2, 128], BF16)
    Ark = Ring(2, [128, 2, 128], BF16)
    Atok = [sbuf([128, 2, 128], BF16) for _ in range(2)]
    Vtok = [sbuf([128, 2, 128], BF16) for _ in range(2)]
    Utok = [sbuf([128, 2, 128], BF16) for _ in range(2)]
    for b in Atok + Vtok + Utok:
        P.op("pool", lambda e: e.memset(b.ap, 0.0), [], [b])
    BKtok = Ring(2, [128, 2, 128], BF16)
    ApT = Ring(2, [128, 128], BF16)
    Wp = Ring(2, [128, 128], BF16)
    Up = Ring(2, [128, 128], F32)
    sc_i = [0]

    def scan_chunk(l, c, q, rT, E1, bpT, kpT, vb, yT):
        k_ = sc_i[0] % 2
        sc_i[0] += 1
        qs = slice(q * 128, (q + 1) * 128)
        pst = ps_next()
        pv_ = pst.ap
        cp("pool", stage, stage[:, 0:128], AR, AR[:, q, 0, :])
        cp("pool", stage, stage[:, 128:256], bpT, bpT[:, qs])
        cp("pool", stage, stage[:, 256:384], kpT, kpT[:, qs])
        cp("pool", stage, stage[:, 384:512], vb, vb[:, qs])
        chk("sc_a0")
        for i4 in range(4):
            tr(pst, pv_[:, i4 * 128:(i4 + 1) * 128], stage, stage[:, i4 * 128:(i4 + 1) * 128])
        chk("sc_a1")
        at, vt, ut = Atok[k_], Vtok[k_], Utok[k_]
        for h in range(2):
            cp("act", at, at[:, h, h * 64:(h + 1) * 64], pst, pv_[:, h * 64:(h + 1) * 64])
            chk(f"sc_x{h}")
            cp("act", vt, vt[:, h, h * 64:(h + 1) * 64], pst, pv_[:, 384 + h * 64:384 + (h + 1) * 64])
            chk(f"sc_y{h}")
        chk("sc_a2")
        bk = BKtok.get()
        cp("act", bk, bk.ap, pst, pv_[:, 128:384].rearrange("p (a b) -> p a b", a=2))
        chk("sc_a")
        ps1, ps2, ps3 = ps_next(), ps_next(), ps_next()
        arq = AR[:, q, :, :]
        for h in range(2):
            mm(ps1, ps1[:, h * 256:(h + 1) * 256], Bbd, Bbd[:, q, h, :], AR, arq)
            mm(ps2, ps2[:, h * 256:(h + 1) * 256], Kbd, Kbd[:, q, h, :], AR, arq)
        mm(ps3, ps3[:, 0:256], AR, AR[:, q, 0, :], Bbd, Bbd[:, q, :, :])
        nx = NXr.get()
        arb, aak, ark, a0 = Arb.get(), Aak.get(), Ark.get(), Ar.get()
        p1v = ps1.ap.rearrange("p (h a t) -> p h a t", h=2, a=2)
        p2v = ps2.ap.rearrange("p (h a t) -> p h a t", h=2, a=2)
        for h in range(2):
            tt("dve", nx, nx[:, h, 0, :], ps1, p1v[:, h, 0, :], cst_b, cst_b[:, 3, :], ALU.mult)
            tt("dve", arb, arb[:, h, :], ps1, p1v[:, h, 1, :], cst_b, cst_b[:, 4, :], ALU.mult)
            tt("dve", aak, aak[:, h, :], ps2, p2v[:, h, 0, :], cst_b, cst_b[:, 3, :], ALU.mult)
            tt("dve", ark, ark[:, h, :], ps2, p2v[:, h, 1, :], cst_b, cst_b[:, 4, :], ALU.mult)
        tt("dve", a0, a0.ap, ps3, ps3[:, 0:256].rearrange("p (h t) -> p h t", h=2), mSL2, mSL2.ap, ALU.mult)
        tt("pool", nx, nx[:, :, 1, :], nx, nx[:, :, 0, :], id2, id2.ap, ALU.add)
        chk("sc_b")
        A_i = a0
        for lev in range(7):
            psn = ps_next()
            last = (lev == 6)
            for h in range(2):
                if lev == 0:
                    mm(psn, psn[:, h * 256:h * 256 + 128], A_i, A_i[:, h, :], nx, nx[:, h, 0, :])
                elif not last:
                    mm(psn, psn[:, h * 256:(h + 1) * 256], A_i, A_i[:, h, :], nx, nx[:, h, :, :])
                else:
                    mm(psn, psn[:, h * 256 + 128:(h + 1) * 256], A_i, A_i[:, h, :], nx, nx[:, h, 1, :])
            if not last:
                psa = ps_next()
                for h in range(2):
                    mm(psa, psa[:, h * 128:(h + 1) * 128], nx, nx[:, h, 0, :], A_i, A_i[:, h, :])
            pnv = psn.ap.rearrange("p (h a t) -> p h a t", h=2, a=2)
            nx2 = NXr.get()
            if lev == 0:
                cp("act", nx2, nx2[:, :, 0, :], psn, pnv[:, :, 0, :])
                cp("pool", nx2, nx2[:, :, 1, :], nx, nx[:, :, 1, :])
            else:
                if not last:
                    cp("act", nx2, nx2[:, :, 0, :], psn, pnv[:, :, 0, :])
                tt("dve", nx2, nx2[:, :, 1, :], psn, pnv[:, :, 1, :], nx, nx[:, :, 1, :], ALU.add)
            if not last:
                a2 = Ar.get()
                cp("act", a2, a2.ap, psa, psa[:, 0:256].rearrange("p (h t) -> p h t", h=2))
                A_i = a2
            nx = nx2
        chk('sc_c')
        psw = ps_next()
        for h in range(2):
            mm(psw, psw[:, 0:128], at, at[:, h, :], nx, nx[:, h, 1, :], start=(h == 0), stop=(h == 1))
        for h in range(2):
            mm(psw, psw[:, 128 + h * 64:128 + (h + 1) * 64], aak, aak[:, h, :], vt, vt[:, h, h * 64:(h + 1) * 64])
        apT, wp = ApT.get(), Wp.get()
        cp("act", apT, apT.ap, psw, psw[:, 0:128])
        cp("act", wp, wp.ap, psw, psw[:, 128:256])
        psu = ps_next()
        for h in range(2):
            mm(psu, psu[:, h * 64:(h + 1) * 64], nx, nx[:, h, 1, :], wp, wp[:, h * 64:(h + 1) * 64])
        up = Up.get()
        cp("act", up, up.ap, psu, psu[:, 0:128])
        chk('sc_d')
        sbd, sfd = Sb[l][c], Sf[l][c]
        ps_u = ps_next()
        mm(ps_u, ps_u[:, 0:128], apT, apT.ap, sbd, sbd.ap.rearrange("p h v -> p (h v)"))
        for h in range(2):
            tt("dve", ut, ut[:, h, h * 64:(h + 1) * 64], ps_u, ps_u[:, h * 64:(h + 1) * 64], up, up[:, h * 64:(h + 1) * 64], ALU.add)
        ps_y = ps_next()
        mm(ps_y, ps_y[:, 0:128], sbd, sbd.ap.rearrange("p h v -> p (h v)"), AR, AR[:, q, 1, :], start=True, stop=False)
        for h in range(2):
            mm(ps_y, ps_y[:, 0:128], ut, ut[:, h, :], arb, arb[:, h, :], start=False, stop=False)
            mm(ps_y, ps_y[:, 0:128], vt, vt[:, h, :], ark, ark[:, h, :], start=False, stop=(h == 1))
        cp("act", yT, yT[:, qs], ps_y, ps_y[:, 0:128])
        chk('sc_e')
        ps_s = ps_next()
        mm(ps_s, ps_s[:, 0:256], bk, bk[:, 0, :], ut, ut.ap.rearrange("p h v -> p (h v)"), start=True, stop=False)
        mm(ps_s, ps_s[:, 0:256], bk, bk[:, 1, :], vt, vt.ap.rearrange("p h v -> p (h v)"), start=False, stop=True)
        pc = E1[:, q * 128 + 127:q * 128 + 128]
        for h in range(2):
            hp = slice(h * 64, (h + 1) * 64)
            stt(sfd, sfd[hp, :], sfd, sfd[hp, :], pc[hp, :], ps_s, ps_s[hp, h * 128 + h * 64:h * 128 + (h + 1) * 64],
                ALU.mult, ALU.add, extra_r=[E1])
        for h in range(2):
            hp = slice(h * 64, (h + 1) * 64)
            cp("act", sbd, sbd[hp, h, :], sfd, sfd[hp, :])

    for it in range(NT):
        tsl = slice(it * T, (it + 1) * T)
        for c in range(8):
            P.dma("sp", xT[c].ap, xT_d[c * 128:(c + 1) * 128, tsl], reads=[Bdram_in], writes=[xT[c]])
        for l in range(2):
            rms_to(l, "g_norm", xT, hT, T)
            chk(f"rms{l}")
            hrhs = lambda j: hT
            car = carry[l]

            def shiftmix(ps, mi, npart=128, A=None):
                A = A or slots[11]
                pp = slice(0, npart)
                mu_c, omu_c = pcol(l, "mu", mi), pcol(l, "omu", mi)
                act(A, A[pp, :], ps, ps[pp, :], AF.Identity, extra_r=[pvs], scale=omu_c[pp, :])
                stt(A, A[pp, 1:T], ps, ps[pp, 0:T - 1], mu_c[pp, :], A, A[pp, 1:T], ALU.mult, ALU.add, extra_r=[pvs])
                stt(A, A[pp, 0:1], car, car[pp, mi:mi + 1], mu_c[pp, :], A, A[pp, 0:1], ALU.mult, ALU.add, extra_r=[pvs])
                cp("act", car, car[pp, mi:mi + 1], ps, ps[pp, T - 1:T])
                return A

            lob, vlo = lob_, vlo_

            def cons_lora(j, ps):
                if j == 0:
                    lo = shiftmix(ps, 24)
                    act(lob, lob[0:64, :], lo, lo[0:64, :], AF.Tanh)
                    cp("dve", lob, lob[64:128, :], lo, lo[64:128, :])
                elif l == 1:
                    v_ = shiftmix(ps, 25, 32)
                    cp("dve", vlo, vlo[0:32, :], v_, v_[0:32, :])
            proj(l, "lora", lambda j: hT if (j == 0 or l == 1) else None, cons_lora)

            chk(f"lora{l}")
            for c in range(8):
                got = {}

                def cons_rkvg(j, ps):
                    if j < 3:
                        got[j] = shiftmix(ps, j * 8 + c, A=slots[j])
                    else:
                        g = slots[3]
                        act(g, g.ap, ps, ps.ap, AF.Silu)
                        got[3] = g
                proj(l, f"rkvg{c}", hrhs, cons_rkvg)
                r_, k_, v_, gs = got[0], got[1], got[2], got[3]
                cs = slice(c * 128, (c + 1) * 128)
                psd, psa = ps_next(), ps_next()
                mm(psd, psd.ap, lor[l], lor[l][0:64, 0, cs], lob, lob[0:64, :])
                mm(psa, psa.ap, lor[l], lor[l][64:128, 0, cs], lob, lob[64:128, :])
                sgd, a_ = slots[4], slots[5]
                act(sgd, sgd.ap, psd, psd.ap, AF.Sigmoid, extra_r=[pvs], bias=pcol(l, "w0", c))
                act(a_, a_.ap, psa, psa.ap, AF.Sigmoid, extra_r=[pvs], bias=pcol(l, "a0", c))
                if l == 1:
                    psv = ps_next()
                    mm(psv, psv.ap, lor[l], lor[l][0:32, 1, cs], vlo, vlo[0:32, :])
                    gv = slots[7]
                    act(gv, gv.ap, psv, psv.ap, AF.Sigmoid, extra_r=[pvs], bias=pcol(l, "v0", c))
                    dd = slots[8]
                    tt("dve", dd, dd.ap, vf[c], vf[c].ap, v_, v_.ap, ALU.subtract)
                    tt("pool", dd, dd.ap, dd, dd.ap, gv, gv.ap, ALU.mult)
                    tt("pool", v_, v_.ap, v_, v_.ap, dd, dd.ap, ALU.add)
                else:
                    cp("pool", vf[c], vf[c].ap, v_, v_.ap)
                dump(f"r{l}_{c}", r_)
                dump(f"k{l}_{c}", k_)
                dump(f"v{l}_{c}", v_)
                dump(f"a{l}_{c}", a_)
                kkr = slots[6]
                ts("dve", kkr, kkr.ap, k_, k_.ap, pcol(l, "k_k", c), None, ALU.mult, extra_r=[pvs])
                sq = tb.get()
                act(sq, sq.ap, kkr, kkr.ap, AF.Square)
                psn = ps_next()
                mm(psn, psn.ap, cst_b, bones, sq, sq.ap)
                nrm = slots[7]
                act(nrm, nrm.ap, psn, psn.ap, AF.Sqrt)
                ts("dve", nrm, nrm.ap, nrm, nrm.ap, 1e-12, None, ALU.max)
                P.op("dve", lambda e: e.reciprocal(out=nrm.ap, in_=nrm.ap), [nrm], [nrm])
                tt("dve", kkr, kkr.ap, kkr, kkr.ap, nrm, nrm.ap, ALU.mult)
                kk = kkr
                f_ = slots[7]
                ts("dve", f_, f_.ap, a_, a_.ap, pcol(l, "k_a", c), pcol(l, "omka", c), ALU.mult, ALU.add, extra_r=[pvs])
                tt("pool", k_, k_.ap, k_, k_.ap, f_, f_.ap, ALU.mult)
                k2 = k_
                rk = tb.get()
                stt(rk, rk.ap, r_, r_.ap, pcol(l, "r_k", c), k2, k2.ap, ALU.mult, ALU.mult, extra_r=[pvs])
                psb = ps_next()
                mm(psb, psb.ap, cst_b, bones, rk, rk.ap)
                bon = slots[8]
                tt("dve", bon, bon.ap, psb, psb.ap, v_, v_.ap, ALU.mult)
                cum = slots[7]
                P.op("dve", lambda e: e.tensor_tensor_scan(out=cum.ap, data0=rmask.ap, data1=sgd.ap, initial=0.0,
                                                           op0=ALU.mult, op1=ALU.add), [rmask, sgd], [cum])
                E1, E2, E3 = slots[9], slots[10], slots[11]
                act(E1, E1.ap, cum, cum.ap, AF.Exp, scale=-C0)
                act(E2, E2.ap, cum, cum.ap, AF.Exp, scale=C0)
                tt("pool", sgd, sgd.ap, cum, cum.ap, sgd, sgd.ap, ALU.subtract)
                act(E3, E3.ap, sgd, sgd.ap, AF.Exp, scale=-C0)
                dump(f"E1{l}_{c}", E1)
                tt("dve", AR, AR[:, :, 1, :], r_, r_.ap.rearrange("p (q t) -> p q t", q=4), E1, E1.ap.rearrange("p (q t) -> p q t", q=4), ALU.mult)
                stt(AR, AR[:, :, 0, :], kk, kk.ap.rearrange("p (q t) -> p q t", q=4), -1.0, E3, E3.ap.rearrange("p (q t) -> p q t", q=4), ALU.mult, ALU.mult)
                bt, kt = slots[0], slots[11]
                tt("pool", bt, bt.ap, kk, kk.ap, a_, a_.ap, ALU.mult)
                tt("dve", bt, bt.ap, bt, bt.ap, E2, E2.ap, ALU.mult)
                tt("dve", kt, kt.ap, k2, k2.ap, E2, E2.ap, ALU.mult)
                for h in range(2):
                    hp = slice(h * 64, (h + 1) * 64)
                    cp("act", Bbd, Bbd[hp, :, h, :], bt, bt[hp, :].rearrange("p (q t) -> p q t", q=4))
                    cp("pool", Kbd, Kbd[hp, :, h, :], kt, kt[hp, :].rearrange("p (q t) -> p q t", q=4))
                bpT, kpT, vb = tb.get(), tb.get(), tb.get()
                for q in range(4):
                    qs = slice(q * 128, (q + 1) * 128)
                    pc = E1[:, q * 128 + 127:q * 128 + 128]
                    act(bpT, bpT[:, qs], bt, bt[:, qs], AF.Identity, extra_r=[E1], scale=pc)
                    act(kpT, kpT[:, qs], kt, kt[:, qs], AF.Identity, extra_r=[E1], scale=pc)
                cp("pool", vb, vb.ap, v_, v_.ap)
                chk(f"prep{l}_{c}")
                yT = slots[1]
                for q in range(4):
                    scan_chunk(l, c, q, r_, E1, bpT, kpT, vb, yT)
                    chk(f"scan{l}_{c}_{q}")
                dump(f"y{l}_{c}", yT)
                yb_, ysq = tb.get(), tb.get()
                cp("pool", yb_, yb_.ap, yT, yT.ap)
                act(ysq, ysq.ap, yT, yT.ap, AF.Square)
                p1, p2 = ps_next(), ps_next()
                mm(p1, p1.ap, cst_b, bones, yb_, yb_.ap)
                mm(p2, p2.ap, cst_b, bones, ysq, ysq.ap)
                mean, var = slots[5], slots[6]
                act(mean, mean.ap, p1, p1.ap, AF.Identity, scale=1.0 / 64)
                tt("pool", var, var.ap, mean, mean.ap, mean, mean.ap, ALU.mult)
                stt(var, var.ap, p2, p2.ap, 1.0 / 64, var, var.ap, ALU.mult, ALU.subtract)
                act(var, var.ap, var, var.ap, AF.Sqrt, extra_r=[epsc], bias=epsc[:, 1:2])
                P.op("dve", lambda e: e.reciprocal(out=var.ap, in_=var.ap), [var], [var])
                tt("dve", yT, yT.ap, yT, yT.ap, mean, mean.ap, ALU.subtract)
                tt("dve", yT, yT.ap, yT, yT.ap, var, var.ap, ALU.mult)
                act(yT, yT.ap, yT, yT.ap, AF.Identity, extra_r=[pvs], scale=pcol(l, "gn_g", c), bias=pcol(l, "gn_b", c))
                tt("pool", yT, yT.ap, yT, yT.ap, bon, bon.ap, ALU.add)
                tt("dve", OG[c], OG[c].ap, yT, yT.ap, gs, gs.ap, ALU.mult)
                dump(f"og{l}_{c}", OG[c])

            def branch_out(pn, first, bias_name=None):
                for j in range(4):
                    tmpy = {}

                    def cons(jj, ps, j=j):
                        if jj < 2:
                            t_ = tf.get()
                            if bias_name is None:
                                cp("act", t_, t_.ap, ps, ps.ap)
                            else:
                                act(t_, t_.ap, ps, ps.ap, AF.Identity, extra_r=[pvs], bias=pcol(l, bias_name, 2 * j + jj))
                            tmpy[jj] = t_
                        else:
                            cidx = 2 * j + (jj - 2)
                            sg = tf.get()
                            act(sg, sg.ap, ps, ps.ap, AF.Sigmoid)
                            yb = tmpy[jj - 2]
                            if first:
                                tt("dve", yacc[cidx], yacc[cidx].ap, sg, sg.ap, yb, yb.ap, ALU.mult)
                            else:
                                tt("pool", sg, sg.ap, sg, sg.ap, yb, yb.ap, ALU.mult)
                                tt("dve", yacc[cidx], yacc[cidx].ap, yacc[cidx], yacc[cidx].ap, sg, sg.ap, ALU.add)
                    proj(l, f"{pn}{j}", lambda jj: OG if jj < 2 else hT, cons)

            chk(f"rwkv{l}")
            branch_out("pr", True)
            chk(f"pr{l}")
            dump(f"yacc0_{l}", yacc[0])

            uc = big8
            s1, s2 = pin[0], pin[1]
            dg = dg_keep
            for j in range(4):
                def cons_glu(jj, ps, j=j):
                    c = 2 * j + jj // 2
                    if jj % 2 == 0:
                        cons_glu.pa = ps
                        return
                    gb = tf.get()
                    act(gb, gb.ap, ps, ps.ap, AF.Sigmoid, extra_r=[pvs], bias=pcol(l, "b_glu", 8 + c))
                    pa = cons_glu.pa
                    u = ubr.get()
                    cp("pool", u, u[:, 0:30], halo[l], halo[l][:, c, :])
                    stt(u, u[:, 30:30 + T], pa, pa.ap, pcol(l, "b_glu", c), gb, gb.ap, ALU.add, ALU.mult, extra_r=[pvs])
                    for tp in range(31):
                        ts("pool", dg, dg[:, tp, :], cst_b, ident, pcol(l, "w_dw", c * 31 + tp), None, ALU.mult, extra_r=[pvs])
                    pc_ = ps_next()
                    for tp in range(31):
                        mm(pc_, pc_.ap, dg, dg[:, tp, :], u, u[:, tp:tp + T], start=(tp == 0), stop=(tp == 30))
                    act(uc[c], uc[c].ap, pc_, pc_.ap, AF.Identity, extra_r=[pvs], bias=pcol(l, "b_dw", c))
                    cp("act", halo[l], halo[l][:, c, :], u, u[:, T:T + 30])
                    ucb, ucs = tb.get(), tb.get()
                    cp("pool", ucb, ucb.ap, uc[c], uc[c].ap)
                    act(ucs, ucs.ap, uc[c], uc[c].ap, AF.Square)
                    mm(s1, s1.ap, cst_b, ones, ucb, ucb.ap, start=(c == 0), stop=(c == 7))
                    mm(s2, s2.ap, cst_b, ones, ucs, ucs.ap, start=(c == 0), stop=(c == 7))
                proj(l, f"glu{j}", hrhs, cons_glu)
            mean, var = slots[10], slots[11]
            act(mean, mean.ap, s1, s1.ap, AF.Identity, scale=1.0 / D)
            tt("pool", var, var.ap, mean, mean.ap, mean, mean.ap, ALU.mult)
            stt(var, var.ap, s2, s2.ap, 1.0 / D, var, var.ap, ALU.mult, ALU.subtract)
            act(var, var.ap, var, var.ap, AF.Sqrt, extra_r=[epsc], bias=epsc[:, 2:3])
            P.op("dve", lambda e: e.reciprocal(out=var.ap, in_=var.ap), [var], [var])
            dump(f"uc{l}_0", uc[0])
            for j in range(2):
                def cons_cg(jj, ps, j=j):
                    c = 4 * j + jj
                    cg = tf.get()
                    act(cg, cg.ap, ps, ps.ap, AF.Silu)
                    t_ = uc[c]
                    tt("dve", t_, t_.ap, t_, t_.ap, mean, mean.ap, ALU.subtract)
                    tt("dve", t_, t_.ap, t_, t_.ap, var, var.ap, ALU.mult)
                    act(t_, t_.ap, t_, t_.ap, AF.Identity, extra_r=[pvs], scale=pcol(l, "ln_g", c), bias=pcol(l, "ln_b", c))
                    act(t_, t_.ap, t_, t_.ap, AF.Silu)
                    tt("dve", OG[c], OG[c].ap, t_, t_.ap, cg, cg.ap, ALU.mult)
                proj(l, f"cg{j}", hrhs, cons_cg)
            dump(f"ug{l}_0", OG[0])
            chk(f"conv{l}")
            branch_out("pc", False, "b_pc")
            chk(f"pc{l}")
            dump(f"yacc1_{l}", yacc[0])

            qT = OG
            for j in range(2):
                def cons_q(jj, ps, j=j):
                    c = 4 * j + jj
                    act(qT[c], qT[c].ap, ps, ps.ap, AF.Identity, scale=1.0 / 16.0)
                proj(l, f"q{j}", hrhs, cons_q)
            att = big8
            prT = prT_keep
            small = small_keep
            for hm in range(4):
                pt = prT[hm % 2]
                for sbk in range(4):
                    ss = slice(sbk * 128, (sbk + 1) * 128)
                    psc = ps_next()
                    for dc in range(2):
                        mm(psc, psc[:, 0:NMEM], qT[2 * hm + dc], qT[2 * hm + dc][:, ss], kmT[l][2 * hm + dc], kmT[l][2 * hm + dc].ap,
                           start=(dc == 0), stop=(dc == 1))
                    P.op("dve", lambda e: e.tensor_reduce(out=small[:, 0:1], in_=psc[:, 0:NMEM], axis=AX.X, op=ALU.max), [psc], [small])
                    ts("dve", small, small[:, 1:2], small, small[:, 0:1], -1.0, None, ALU.mult)
                    ex = tf.get()
                    P.op("act", lambda e: e.activation(out=ex[:, 0:NMEM], in_=psc[:, 0:NMEM], func=AF.Exp, bias=small[:, 1:2],
                                                       accum_out=small[:, 2:3]), [psc, small], [ex, small])
                    P.op("dve", lambda e: e.reciprocal(out=small[:, 3:4], in_=small[:, 2:3]), [small], [small])
                    pb = stage
                    ts("dve", pb, pb[:, 0:NMEM], ex, ex[:, 0:NMEM], small[:, 3:4], None, ALU.mult, extra_r=[small])
                    ptp = ps_next()
                    pv_ = ptp.ap
                    for mb in range(2):
                        tr(ptp, pv_[:, mb * 128:(mb + 1) * 128], pb, pb[:, mb * 128:(mb + 1) * 128])
                    for mb in range(2):
                        cp("act", pt[mb], pt[mb][:, ss], ptp, pv_[:, mb * 128:(mb + 1) * 128])
                for dc in range(2):
                    c = 2 * hm + dc
                    pa_ = ps_next()
                    for mb in range(2):
                        mm(pa_, pa_.ap, vmt[l][mb], vmt[l][mb][:, c * 128:(c + 1) * 128], pt[mb], pt[mb].ap, start=(mb == 0), stop=(mb == 1))
                    cp("act", att[c], att[c].ap, pa_, pa_.ap)
            dump(f"att{l}_0", att[0])
            for j in range(2):
                def cons_mg(jj, ps, j=j):
                    c = 4 * j + jj
                    mg = tf.get()
                    act(mg, mg.ap, ps, ps.ap, AF.Silu)
                    tt("dve", OG[c], OG[c].ap, att[c], att[c].ap, mg, mg.ap, ALU.mult)
                proj(l, f"mg{j}", hrhs, cons_mg)
            chk(f"mem{l}")
            branch_out("pm", False)
            chk(f"pm{l}")
            dump(f"yacc2_{l}", yacc[0])

            for c in range(8):
                cp("pool", OG[c], OG[c].ap, yacc[c], yacc[c].ap)
            for j in range(2):
                def cons_o(jj, ps, j=j):
                    c = 4 * j + jj
                    tt("dve", xT[c], xT[c].ap, xT[c], xT[c].ap, ps, ps.ap, ALU.add)
                proj(l, f"wo{j}", lambda jj: OG, cons_o)
            dump(f"x{l}_0", xT[0])

        ps = pin[0]
        for c in range(8):
            sq = tb.get()
            act(sq, sq.ap, xT[c], xT[c].ap, AF.Square)
            mm(ps, ps.ap, cst_b, ones, sq, sq.ap, start=(c == 0), stop=(c == 7))
        sd, rs = tf.get(), tf.get()
        act(sd, sd.ap, ps, ps.ap, AF.Sqrt, extra_r=[epsc], scale=1.0 / D, bias=epsc[:, 0:1])
        P.op("dve", lambda e: e.reciprocal(out=rs.ap, in_=sd.ap), [sd], [rs])
        for c in range(8):
            o_ = tf.get()
            stt(o_, o_.ap, xT[c], xT[c].ap, pcol(0, "g_final", c), rs, rs.ap, ALU.mult, ALU.mult, extra_r=[pvs])
            P.dma("sp", outT_d[c * 128:(c + 1) * 128, tsl], o_.ap, reads=[o_], writes=[Bout])


_CACHE = {}


def kernel(**inp):
    inp = {k: np.asarray(v) for k, v in inp.items()}
    pv, wbig, lora, cst, rm = host_prep(inp)
    x, mem = inp["x"], inp["mem"]
    B = x.shape[0]
    nc = bass.Bass("TRN2", target_bir_lowering=False)
    build(nc)
    in_maps = []
    for b in range(B):
        in_maps.append({"xT": np.ascontiguousarray(x[b].T), "memT": np.ascontiguousarray(mem[b].T),
                        "pv": pv, "wbig": wbig, "lora": lora, "cst": cst, "rm": rm})
    res = run_bass_kernel_spmd(nc, in_maps, core_ids=list(range(B)))
    out = np.stack([np.ascontiguousarray(r["outT"].T) for r in res.results], axis=0)
    return out.astype(np.float32)
```

```python
import numpy as np
import concourse.bass as bass
import concourse.mybir as mybir
from concourse.bass_utils import run_bass_kernel_spmd

F32 = mybir.dt.float32
BF16 = mybir.dt.bfloat16
AF = mybir.ActivationFunctionType
ALU = mybir.AluOpType
AX = mybir.AxisListType
NDS = 24

D = 1024
SEQ = 4096
T = 512
NMEM = 256
KC = 8
C0 = float(np.exp(-0.5))


class Buf:
    __slots__ = ("ap", "w", "r")

    def __init__(self, ap):
        self.ap = ap
        self.w = None
        self.r = {}

    def __getitem__(self, k):
        return self.ap[k]


class Prog:
    def __init__(self, nc):
        self.nc = nc
        self.eng = dict(pe=nc.tensor, dve=nc.vector, act=nc.scalar, pool=nc.gpsimd, sp=nc.sync)
        self.esem = {k: nc.alloc_semaphore("es_" + k) for k in self.eng}
        self.ecnt = {k: 0 for k in self.eng}
        self.seen = {k: {} for k in self.eng}
        self.dsem, self.dtgt, self.dnext = {}, {}, {}
        self.ninst = 0

    def _wait(self, e, ev):
        sem, key, val = ev
        if key == ("e", e) and e == "pe":
            return
        if self.seen[e].get(key, 0) >= val:
            return
        self.eng[e].wait_ge(sem, val)
        self.seen[e][key] = val

    def _deps(self, e, reads, writes):
        for b in reads:
            if b.w is not None:
                self._wait(e, b.w)
        for b in writes:
            if b.w is not None:
                self._wait(e, b.w)
            for ev in b.r.values():
                self._wait(e, ev)

    def _record(self, ev, reads, writes):
        for b in reads:
            b.r[ev[1]] = ev
        for b in writes:
            b.w = ev
            b.r = {}

    def op(self, e, fn, reads=(), writes=()):
        self._deps(e, reads, writes)
        inst = fn(self.eng[e])
        self.ecnt[e] += 1
        inst.then_inc(self.esem[e], 1)
        self._record((self.esem[e], ("e", e), self.ecnt[e]), reads, writes)
        self.ninst += 1

    def dma(self, q, out_ap, in_ap, reads=(), writes=(), **kw):
        if q not in self.dsem:
            self.dsem[q] = [self.nc.alloc_semaphore(f"ds_{q}{i}") for i in range(NDS)]
            self.dtgt[q] = [0] * NDS
            self.dnext[q] = 0
        j = self.dnext[q]
        self.dnext[q] = (j + 1) % NDS
        key = ("d", q, j)
        if self.dtgt[q][j] > 0:
            self._wait(q, (self.dsem[q][j], key, self.dtgt[q][j]))
        self._deps(q, reads, writes)
        inst = self.eng[q].dma_start(out=out_ap, in_=in_ap, **kw)
        self.dtgt[q][j] += 16
        inst.then_inc(self.dsem[q][j], 16)
        self._record((self.dsem[q][j], key, self.dtgt[q][j]), reads, writes)
        self.ninst += 1

    def finish(self, e="sp"):
        for q in self.dsem:
            for j in range(NDS):
                if self.dtgt[q][j] > 0:
                    self._wait(e, (self.dsem[q][j], ("d", q, j), self.dtgt[q][j]))
        for k in self.eng:
            if k != e and self.ecnt[k] > 0:
                self._wait(e, (self.esem[k], ("e", k), self.ecnt[k]))


PV = {}
_o = 0
for _n, _w in [("g_norm", 8), ("mu", 26), ("w0", 8), ("a0", 8), ("k_k", 8), ("k_a", 8), ("r_k", 8),
               ("gn_g", 8), ("gn_b", 8), ("v0", 8), ("b_glu", 16), ("w_dw", 248), ("b_dw", 8),
               ("ln_g", 8), ("ln_b", 8), ("b_pc", 8), ("g_mem", 8), ("g_final", 8), ("omu", 26), ("omka", 8)]:
    PV[_n] = _o
    _o += _w
NPV = _o

WCOL = {}
_o = 0
for _n, _w in [("lora", 256)] + [(f"rkvg{c}", 512) for c in range(8)] + \
        [(f"pr{j}", 512) for j in range(4)] + [(f"glu{j}", 512) for j in range(4)] + \
        [(f"cg{j}", 512) for j in range(2)] + [(f"pc{j}", 512) for j in range(4)] + \
        [(f"q{j}", 512) for j in range(2)] + [(f"mg{j}", 512) for j in range(2)] + \
        [(f"pm{j}", 512) for j in range(4)] + [(f"wo{j}", 512) for j in range(2)] + \
        [(f"kv{j}", 512) for j in range(4)]:
    WCOL[_n] = (_o, _w)
    _o += _w
TOTC = _o


def _fm(v):
    return np.ascontiguousarray(v.reshape(8, 128).T)


def host_prep(inp):
    f = np.float32
    L = 2
    pv = np.zeros((L, 128, NPV), f)
    wbig = np.zeros((L, 128, KC, TOTC), f)
    lora = np.zeros((L, 128, 2, 1024), f)
    for l in range(L):
        def put(name, arr):
            pv[l][:, PV[name]:PV[name] + arr.shape[1]] = arr
        put("g_norm", _fm(inp["g_norm"][l]))
        mu = inp["mu_shift"][l]
        mucols = np.zeros((128, 26), f)
        mucols[:, 0:25] = mu.reshape(25, 128).T
        if l >= 1:
            mucols[0:32, 25] = inp["mu_vres"][l - 1]
        put("mu", mucols)
        for n in ["w0", "a0", "k_k", "k_a", "gn_g", "gn_b", "b_dw", "ln_g", "ln_b", "g_mem"]:
            src = {"g_mem": "g_mem_norm"}.get(n, n)
            put(n, _fm(inp[src][l]))
        put("r_k", _fm(inp["r_k"][l].reshape(-1)))
        put("b_pc", _fm(inp["b_proj_conv"][l]))
        if l >= 1:
            put("v0", _fm(inp["v0"][l - 1]))
        put("b_glu", np.ascontiguousarray(inp["b_glu"][l].reshape(16, 128).T))
        wd = inp["w_dw"][l]
        put("w_dw", np.ascontiguousarray(wd.reshape(31, 8, 128).transpose(2, 1, 0).reshape(128, 248)))
        put("g_final", _fm(inp["g_final"]))
        w_in = inp["w_in"][l]
        Wc = np.zeros((D, TOTC), f)

        def setc(name, off, arr):
            o, w = WCOL[name]
            Wc[:, o + off:o + off + arr.shape[1]] = arr
        setc("lora", 0, w_in[:, 3072:3200])
        if l >= 1:
            setc("lora", 128, inp["w_vres_down"][l - 1])
        for c in range(8):
            setc(f"rkvg{c}", 0, w_in[:, c * 128:(c + 1) * 128])
            setc(f"rkvg{c}", 128, w_in[:, 1024 + c * 128:1024 + (c + 1) * 128])
            setc(f"rkvg{c}", 256, w_in[:, 2048 + c * 128:2048 + (c + 1) * 128])
            setc(f"rkvg{c}", 384, w_in[:, 3200 + c * 128:3200 + (c + 1) * 128])
        for br, (pn, wp) in enumerate([("pr", inp["w_proj_rwkv"][l]), ("pc", inp["w_proj_conv"][l]),
                                       ("pm", inp["w_proj_mem"][l])]):
            for j in range(4):
                setc(f"{pn}{j}", 0, wp[:, j * 256:(j + 1) * 256])
                mo = 9344 + br * 1024 + j * 256
                setc(f"{pn}{j}", 256, w_in[:, mo:mo + 256])
        for j in range(4):
            for i in range(2):
                c = 2 * j + i
                setc(f"glu{j}", i * 256, w_in[:, 4224 + c * 128:4224 + (c + 1) * 128])
                setc(f"glu{j}", i * 256 + 128, w_in[:, 5248 + c * 128:5248 + (c + 1) * 128])
        for j in range(2):
            setc(f"cg{j}", 0, w_in[:, 6272 + j * 512:6272 + (j + 1) * 512])
            setc(f"q{j}", 0, w_in[:, 7296 + j * 512:7296 + (j + 1) * 512])
            setc(f"mg{j}", 0, w_in[:, 8320 + j * 512:8320 + (j + 1) * 512])
            setc(f"wo{j}", 0, inp["w_out"][l][:, j * 512:(j + 1) * 512])
        for j in range(4):
            setc(f"kv{j}", 0, inp["w_mem_kv"][l][:, j * 512:(j + 1) * 512])
        wbig[l] = Wc.reshape(KC, 128, TOTC).transpose(1, 0, 2)
        lora[l][0:64, 0] = inp["w_decay_up"][l]
        lora[l][64:128, 0] = inp["w_aaa_up"][l]
        if l >= 1:
            lora[l][0:32, 1] = inp["w_vres_up"][l - 1]
    cst = np.zeros((128, 8, 128), f)
    i = np.arange(128)
    cst[:, 0] = np.eye(128)
    cst[:, 1] = 1.0
    cst[:, 2] = (i[:, None] // 64 == i[None, :] // 64)
    cst[:, 3] = (i[:, None] < i[None, :])
    cst[:, 4] = (i[:, None] <= i[None, :])
    cst[:, 5] = (i[:, None] > i[None, :])
    rm = np.ones((128, 512), f)
    rm[:, 0::128] = 0.0
    return pv, wbig, lora, cst, rm


class _Stop(Exception):
    pass


def build(nc, NT=SEQ // T, dbg_names=(), stop_after=None):
    P = Prog(nc)
    try:
        _build(nc, P, NT, dbg_names, stop_after)
    except _Stop:
        pass
    P.finish()
    return P


def _build(nc, P, NT, dbg_names, stop_after):
    def chk(tag):
        if tag == stop_after:
            raise _Stop()
    dt = nc.dram_tensor
    xT_d = dt("xT", [D, SEQ], F32, kind="ExternalInput").ap()
    memT_d = dt("memT", [D, NMEM], F32, kind="ExternalInput").ap()
    pv_d = dt("pv", [2, 128, NPV], F32, kind="ExternalInput").ap()
    wbig_d = dt("wbig", [2, 128, KC, TOTC], F32, kind="ExternalInput").ap()
    lora_d = dt("lora", [2, 128, 2, 1024], F32, kind="ExternalInput").ap()
    cst_d = dt("cst", [128, 8, 128], F32, kind="ExternalInput").ap()
    rm_d = dt("rm", [128, 512], F32, kind="ExternalInput").ap()
    outT_d = dt("outT", [D, SEQ], F32, kind="ExternalOutput").ap()
    dbg_d = None
    if dbg_names:
        dbg_d = dt("dbg", [len(dbg_names), 128, 512], F32, kind="ExternalOutput").ap()
    Bdram_in = Buf(None)
    Bout = Buf(None)
    cnt = [0]

    def sb(shape, dtype, name=None):
        cnt[0] += 1
        return nc.alloc_sbuf_tensor(name or f"t{cnt[0]}", list(shape), dtype)

    def sbuf(shape, dtype):
        return Buf(sb(shape, dtype).ap())

    big8 = [sbuf([128, T], F32) for _ in range(8)]
    cst_f = Buf(big8[0].ap.rearrange("p (a b) -> p a b", a=4))
    cst_f2 = Buf(big8[1].ap.rearrange("p (a b) -> p a b", a=4))
    P.dma("sp", cst_f.ap, cst_d[:, 0:4, :], reads=[Bdram_in], writes=[cst_f, big8[0]])
    P.dma("sp", cst_f2.ap, cst_d[:, 4:8, :], reads=[Bdram_in], writes=[cst_f2, big8[1]])
    cst_b = sbuf([128, 8, 128], BF16)
    P.op("dve", lambda e: e.tensor_copy(out=cst_b[:, 0:4, :], in_=cst_f.ap), [cst_f, big8[0]], [cst_b])
    P.op("dve", lambda e: e.tensor_copy(out=cst_b[:, 4:8, :], in_=cst_f2.ap), [cst_f2, big8[1]], [cst_b])
    ident, ones, bones = cst_b[:, 0, :], cst_b[:, 1, :], cst_b[:, 2, :]
    identf = sbuf([128, 128], F32)
    P.op("dve", lambda e: e.tensor_copy(out=identf.ap, in_=cst_f[:, 0, :]), [cst_f, big8[0]], [identf])
    m12 = Buf(cst_b[:, 3:5, :])
    m12.w = None
    mSL2 = sbuf([128, 2, 128], BF16)
    id2 = sbuf([128, 2, 128], BF16)
    for h in range(2):
        P.op("dve", lambda e: e.tensor_copy(out=mSL2[:, h, :], in_=cst_b[:, 5, :]), [cst_b], [mSL2])
        P.op("dve", lambda e: e.tensor_copy(out=id2[:, h, :], in_=cst_b[:, 0, :]), [cst_b], [id2])
    rmf = Buf(big8[2].ap)
    P.dma("sp", rmf.ap, rm_d, reads=[Bdram_in], writes=[big8[2]])
    rmask = sbuf([128, 512], BF16)
    P.op("dve", lambda e: e.tensor_copy(out=rmask.ap, in_=big8[2].ap), [big8[2]], [rmask])
    pvs = sbuf([128, 2, NPV], F32)
    for l in range(2):
        P.dma("sp", pvs[:, l, :], pv_d[l], reads=[Bdram_in], writes=[pvs])
    for l in range(2):
        for (src, dst, w) in [("mu", "omu", 26), ("k_a", "omka", 8)]:
            P.op("dve", lambda e: e.tensor_scalar(out=pvs[:, l, PV[dst]:PV[dst] + w], in0=pvs[:, l, PV[src]:PV[src] + w],
                                                  scalar1=-1.0, scalar2=1.0, op0=ALU.mult, op1=ALU.add), [pvs], [pvs])
    epsc = sbuf([128, 4], F32)
    for i, v in enumerate([1e-6, 64e-5, 1e-5, 0.0]):
        P.op("dve", lambda e: e.memset(epsc[:, i:i + 1], v), [], [epsc])

    def pcol(l, name, c=0):
        o = PV[name] + c
        return pvs[:, l, o:o + 1]

    lor = [sbuf([128, 2, 1024], BF16) for _ in range(2)]
    for l in range(2):
        P.dma("pool", lor[l].ap, lora_d[l], reads=[Bdram_in], writes=[lor[l]])

    banks = [Buf(nc.alloc_psum_tensor(f"ps{i}", [128, 512], F32).ap()) for i in range(8)]
    ring = banks[:6]
    pin = banks[6:]
    rp = [0]

    def ps_next():
        b = ring[rp[0] % len(ring)]
        rp[0] += 1
        return b

    def bfv(b):
        return b.ap.bitcast(BF16)

    class Ring:
        def __init__(self, n, shape, dtype):
            self.b = [sbuf(shape, dtype) for _ in range(n)]
            self.i = 0

        def get(self):
            b = self.b[self.i % len(self.b)]
            self.i += 1
            return b

    slots = [sbuf([128, 512], F32) for _ in range(12)]
    tf = Ring(0, [128, 512], F32)
    tf.b = slots[0:10]
    tb = Ring(6, [128, 512], BF16)
    lob_, vlo_ = sbuf([128, 512], BF16), sbuf([128, 512], BF16)
    wring = Ring(2, [128, KC, 512], BF16)

    xT = [sbuf([128, T], F32) for _ in range(8)]
    vf = [sbuf([128, T], F32) for _ in range(8)]
    hT = [sbuf([128, T], BF16) for _ in range(8)]
    OG = [sbuf([128, T], BF16) for _ in range(8)]
    yacc = [sbuf([128, T], F32) for _ in range(8)]
    carry = [sbuf([128, 26], F32) for _ in range(2)]
    Sf = [[sbuf([128, 64], F32) for _ in range(8)] for _ in range(2)]
    Sb = [[sbuf([128, 2, 64], BF16) for _ in range(8)] for _ in range(2)]
    halo = [sbuf([128, 8, 30], BF16) for _ in range(2)]
    ubr = Ring(2, [128, 30 + T], BF16)
    dg_keep = sbuf([128, 31, 128], BF16)
    prT_keep = [[sbuf([128, T], BF16) for _ in range(2)] for _ in range(2)]
    small_keep = sbuf([128, 8], F32)
    kmT = [[sbuf([128, NMEM], BF16) for _ in range(8)] for _ in range(2)]
    vmt = [[sbuf([128, D], BF16) for _ in range(2)] for _ in range(2)]
    for l in range(2):
        P.op("pool", lambda e: e.memset(carry[l].ap, 0.0), [], [carry[l]])
        P.op("pool", lambda e: e.memset(halo[l].ap, 0.0), [], [halo[l]])
        for c in range(8):
            P.op("pool", lambda e: e.memset(Sf[l][c].ap, 0.0), [], [Sf[l][c]])
            P.op("pool", lambda e: e.memset(Sb[l][c].ap, 0.0), [], [Sb[l][c]])

    dbg_list = list(dbg_names)
    dbg_buf = sbuf([128, 512], F32) if dbg_names else None

    def dump(name, b, ap=None):
        if name in dbg_list:
            i = dbg_list.index(name)
            a = b.ap if ap is None else ap
            t = dbg_buf
            P.op("dve", lambda e: e.tensor_copy(out=t[:, 0:a.shape[-1]], in_=a), [b], [t])
            P.dma("sp", dbg_d[i][0:a.shape[0], 0:a.shape[-1]], t[0:a.shape[0], 0:a.shape[-1]], reads=[t], writes=[Bout])
            dbg_list[i] = None

    def act(out_b, out_ap, in_b, in_ap, func, extra_r=(), **kw):
        P.op("act", lambda e: e.activation(out=out_ap, in_=in_ap, func=func, **kw), [in_b] + list(extra_r), [out_b])

    def tt(eng, out_b, out_ap, a_b, a_ap, b_b, b_ap, op):
        P.op(eng, lambda e: e.tensor_tensor(out=out_ap, in0=a_ap, in1=b_ap, op=op), [a_b, b_b], [out_b])

    def stt(out_b, out_ap, a_b, a_ap, scalar, b_b, b_ap, op0, op1, extra_r=()):
        P.op("dve", lambda e: e.scalar_tensor_tensor(out=out_ap, in0=a_ap, scalar=scalar, in1=b_ap, op0=op0, op1=op1),
             [a_b, b_b] + list(extra_r), [out_b])

    def ts(eng, out_b, out_ap, a_b, a_ap, s1, s2, op0, op1=None, extra_r=()):
        if op1 is None:
            P.op(eng, lambda e: e.tensor_scalar(out=out_ap, in0=a_ap, scalar1=s1, scalar2=None, op0=op0),
                 [a_b] + list(extra_r), [out_b])
        else:
            P.op(eng, lambda e: e.tensor_scalar(out=out_ap, in0=a_ap, scalar1=s1, scalar2=s2, op0=op0, op1=op1),
                 [a_b] + list(extra_r), [out_b])

    def cp(eng, out_b, out_ap, in_b, in_ap):
        if eng == "act":
            P.op("act", lambda e: e.copy(out=out_ap, in_=in_ap), [in_b], [out_b])
        elif eng == "dve":
            P.op(eng, lambda e: e.tensor_scalar(out=out_ap, in0=in_ap, scalar1=1.0, scalar2=None, op0=ALU.mult), [in_b], [out_b])
        else:
            P.op(eng, lambda e: e.tensor_copy(out=out_ap, in_=in_ap), [in_b], [out_b])

    def mm(out_b, out_ap, l_b, l_ap, r_b, r_ap, start=True, stop=True):
        P.op("pe", lambda e: e.matmul(out_ap, lhsT=l_ap, rhs=r_ap, start=start, stop=stop), [l_b, r_b], [out_b])

    def tr(out_b, out_ap, in_b, in_ap):
        P.op("pe", lambda e: e.transpose(out_ap, in_ap, identf.ap), [in_b, identf], [out_b])

    def wload(l, name):
        o, w = WCOL[name]
        wb = wring.get()
        P.dma("pool", wb[:, :, 0:w], wbig_d[l][:, :, o:o + w], reads=[Bdram_in], writes=[wb])
        return wb

    def proj(l, name, rhs_for_chunk, consume):
        o, w = WCOL[name]
        wb = wload(l, name)
        for j in range(w // 128):
            rhs = rhs_for_chunk(j)
            if rhs is None:
                continue
            ps = ps_next()
            for kc in range(KC):
                mm(ps, ps.ap, wb, wb[:, kc, j * 128:(j + 1) * 128], rhs[kc], rhs[kc].ap, start=(kc == 0), stop=(kc == KC - 1))
            consume(j, ps)

    def bcast_stat(src_list, src_aps, scale, epscol):
        raise NotImplementedError

    def rms_to(l, gname, src, dst, n):
        ps = pin[0]
        for c in range(8):
            sq = tb.get()
            act(sq, sq[:, 0:n], src[c], src[c][:, 0:n], AF.Square)
            mm(ps, ps[:, 0:n], cst_b, ones, sq, sq[:, 0:n], start=(c == 0), stop=(c == 7))
        sd = tf.get()
        act(sd, sd[:, 0:n], ps, ps[:, 0:n], AF.Sqrt, extra_r=[epsc], scale=1.0 / D, bias=epsc[:, 0:1])
        rs = tf.get()
        P.op("dve", lambda e: e.reciprocal(out=rs[:, 0:n], in_=sd[:, 0:n]), [sd], [rs])
        for c in range(8):
            stt(dst[c], dst[c][:, 0:n], src[c], src[c][:, 0:n], pcol(l, gname, c), rs, rs[:, 0:n], ALU.mult, ALU.mult, extra_r=[pvs])
        return rs

    mraw = [Buf(big8[c][:, 0:NMEM]) for c in range(8)]
    for c in range(8):
        P.dma("sp", mraw[c].ap, memT_d[c * 128:(c + 1) * 128, :], reads=[Bdram_in], writes=[big8[c]])
        mraw[c] = big8[c]
    for l in range(2):
        mT = OG
        rms_to(l, "g_mem", mraw, mT, NMEM)
        for j in range(2):
            def cons(jj, ps, j=j):
                cp("act", kmT[l][j * 4 + jj], kmT[l][j * 4 + jj].ap, ps, ps[:, 0:NMEM])
            o, w = WCOL[f"kv{j}"]
            wb = wload(l, f"kv{j}")
            for jj in range(4):
                ps = ps_next()
                for kc in range(KC):
                    mm(ps, ps[:, 0:NMEM], wb, wb[:, kc, jj * 128:(jj + 1) * 128], mT[kc], mT[kc][:, 0:NMEM], start=(kc == 0), stop=(kc == KC - 1))
                cons(jj, ps)
        for j in range(2):
            wb = wload(l, f"kv{2 + j}")
            for mb in range(2):
                ps = ps_next()
                for kc in range(KC):
                    mm(ps, ps.ap, mT[kc], mT[kc][:, mb * 128:(mb + 1) * 128], wb, wb[:, kc, :], start=(kc == 0), stop=(kc == KC - 1))
                cp("act", vmt[l][mb], vmt[l][mb][:, j * 512:(j + 1) * 512], ps, ps.ap)

    chk("memkv")
    AR = sbuf([128, 4, 2, 128], BF16)
    Bbd = sbuf([128, 4, 2, 128], BF16)
    Kbd = sbuf([128, 4, 2, 128], BF16)
    P.op("pool", lambda e: e.memset(Bbd.ap, 0.0), [], [Bbd])
    P.op("pool", lambda e: e.memset(Kbd.ap, 0.0), [], [Kbd])
    NXr = Ring(4, [128, 2, 2, 128], BF16)
    Ar = Ring(4, [128, 2, 128], BF16)
    Arb = Ring(4, [128, 2, 128], BF16)
    Aak = Ring(2, [128, 2, 128], BF16)
    Ark = Ring(4, [128, 2, 128], BF16)
    Atok = [sbuf([128, 2, 128], BF16) for _ in range(2)]
    Vtok = [sbuf([128, 2, 128], BF16) for _ in range(4)]
    Utok = [sbuf([128, 2, 128], BF16) for _ in range(2)]
    for b_ in Atok + Vtok + Utok:
        P.op("pool", lambda e: e.memset(b_.ap, 0.0), [], [b_])
    BKtok = Ring(4, [128, 2, 128], BF16)
    ApT = Ring(4, [128, 128], BF16)
    Wp = Ring(2, [128, 128], BF16)
    stage = slots[0]
    Up4 = slots[11]
    cntr = dict(at=0, vt=0, ut=0, up=0)

    def scan_pre(l, c, qs2, ctx, E1, bpT, kpT, vb):
        for q in qs2:
            d = ctx[q] = {}
            d["at"] = Atok[cntr["at"] % 2]; cntr["at"] += 1
            d["vt"] = Vtok[cntr["vt"] % 4]; cntr["vt"] += 1
            d["upc"] = cntr["up"] % 4; cntr["up"] += 1
            d["bk"], d["arb"], d["aak"], d["ark"] = BKtok.get(), Arb.get(), Aak.get(), Ark.get()
            d["apT"], d["wp"] = ApT.get(), Wp.get()
        for q in qs2:
            d = ctx[q]
            qs = slice(q * 128, (q + 1) * 128)
            pst = ps_next()
            pv_ = pst.ap
            cp("pool", stage, stage[:, 0:128], AR, AR[:, q, 0, :])
            cp("pool", stage, stage[:, 128:256], bpT, bpT[:, qs])
            cp("pool", stage, stage[:, 256:384], kpT, kpT[:, qs])
            cp("pool", stage, stage[:, 384:512], vb, vb[:, qs])
            for i4 in range(4):
                tr(pst, pv_[:, i4 * 128:(i4 + 1) * 128], stage, stage[:, i4 * 128:(i4 + 1) * 128])
            at, vt, bk = d["at"], d["vt"], d["bk"]
            for h in range(2):
                cp("act", at, at[:, h, h * 64:(h + 1) * 64], pst, pv_[:, h * 64:(h + 1) * 64])
                cp("act", vt, vt[:, h, h * 64:(h + 1) * 64], pst, pv_[:, 384 + h * 64:384 + (h + 1) * 64])
            cp("act", bk, bk.ap, pst, pv_[:, 128:384].rearrange("p (a b) -> p a b", a=2))
            yield
        for q in qs2:
            d = ctx[q]
            ps1, ps2, ps3 = ps_next(), ps_next(), ps_next()
            arq = AR[:, q, :, :].rearrange("p a t -> p (a t)")
            for h in range(2):
                mm(ps1, ps1[:, h * 256:(h + 1) * 256], Bbd, Bbd[:, q, h, :], AR, arq)
                mm(ps2, ps2[:, h * 256:(h + 1) * 256], Kbd, Kbd[:, q, h, :], AR, arq)
            mm(ps3, ps3[:, 0:256], AR, AR[:, q, 0, :], Bbd, Bbd[:, q, :, :].rearrange("p h j -> p (h j)"))
            nx = NXr.get()
            arb, aak, ark, a0 = d["arb"], d["aak"], d["ark"], Ar.get()
            p1v = ps1.ap.rearrange("p (h a t) -> p h a t", h=2, a=2)
            p2v = ps2.ap.rearrange("p (h a t) -> p h a t", h=2, a=2)
            for h in range(2):
                tt("dve", nx, nx[:, h, 0, :], ps1, p1v[:, h, 0, :], cst_b, cst_b[:, 3, :], ALU.mult)
                tt("dve", arb, arb[:, h, :], ps1, p1v[:, h, 1, :], cst_b, cst_b[:, 4, :], ALU.mult)
                tt("dve", aak, aak[:, h, :], ps2, p2v[:, h, 0, :], cst_b, cst_b[:, 3, :], ALU.mult)
                tt("dve", ark, ark[:, h, :], ps2, p2v[:, h, 1, :], cst_b, cst_b[:, 4, :], ALU.mult)
            tt("dve", a0, a0.ap, ps3, ps3[:, 0:256].rearrange("p (h t) -> p h t", h=2), mSL2, mSL2.ap, ALU.mult)
            tt("pool", nx, nx[:, :, 1, :], nx, nx[:, :, 0, :], id2, id2.ap, ALU.add)
            d["nx"], d["A"] = nx, a0
            yield
        for lev in range(7):
            last = (lev == 6)
            pss = {}
            for q in qs2:
                d = ctx[q]
                nx, A_i = d["nx"], d["A"]
                psn = ps_next()
                for h in range(2):
                    if lev == 0:
                        mm(psn, psn[:, h * 256:h * 256 + 128], A_i, A_i[:, h, :], nx, nx[:, h, 0, :])
                    elif not last:
                        mm(psn, psn[:, h * 256:(h + 1) * 256], A_i, A_i[:, h, :], nx, nx[:, h, :, :].rearrange("p a t -> p (a t)"))
                    else:
                        mm(psn, psn[:, h * 256 + 128:(h + 1) * 256], A_i, A_i[:, h, :], nx, nx[:, h, 1, :])
                psa = None
                if not last:
                    psa = ps_next()
                    for h in range(2):
                        mm(psa, psa[:, h * 128:(h + 1) * 128], nx, nx[:, h, 0, :], A_i, A_i[:, h, :])
                pss[q] = (psn, psa)
            for q in qs2:
                d = ctx[q]
                nx = d["nx"]
                psn, psa = pss[q]
                pnv = psn.ap.rearrange("p (h a t) -> p h a t", h=2, a=2)
                nx2 = NXr.get()
                if lev == 0:
                    cp("act", nx2, nx2[:, :, 0, :], psn, pnv[:, :, 0, :])
                    cp("pool", nx2, nx2[:, :, 1, :], nx, nx[:, :, 1, :])
                else:
                    if not last:
                        cp("act", nx2, nx2[:, :, 0, :], psn, pnv[:, :, 0, :])
                    tt("dve", nx2, nx2[:, :, 1, :], psn, pnv[:, :, 1, :], nx, nx[:, :, 1, :], ALU.add)
                if not last:
                    a2 = Ar.get()
                    cp("act", a2, a2.ap, psa, psa[:, 0:256].rearrange("p (h t) -> p h t", h=2))
                    d["A"] = a2
                d["nx"] = nx2
            yield
        for q in qs2:
            d = ctx[q]
            nx, at, vt, aak, apT, wp = d["nx"], d["at"], d["vt"], d["aak"], d["apT"], d["wp"]
            psw = ps_next()
            for h in range(2):
                mm(psw, psw[:, 0:128], at, at[:, h, :], nx, nx[:, h, 1, :], start=(h == 0), stop=(h == 1))
            for h in range(2):
                mm(psw, psw[:, 128 + h * 64:128 + (h + 1) * 64], aak, aak[:, h, :], vt, vt[:, h, h * 64:(h + 1) * 64])
            cp("act", apT, apT.ap, psw, psw[:, 0:128])
            cp("act", wp, wp.ap, psw, psw[:, 128:256])
        yield
        for q in qs2:
            d = ctx[q]
            nx, wp = d["nx"], d["wp"]
            psu = ps_next()
            for h in range(2):
                mm(psu, psu[:, h * 64:(h + 1) * 64], nx, nx[:, h, 1, :], wp, wp[:, h * 64:(h + 1) * 64])
            uc_ = d["upc"]
            cp("act", Up4, Up4[:, uc_ * 128:(uc_ + 1) * 128], psu, psu[:, 0:128])
        yield

    def scan_seq(l, c, qs2, ctx, E1, yT):
        sbd, sfd = Sb[l][c], Sf[l][c]
        sbd2 = sbd.ap.rearrange("p h v -> p (h v)")
        for q in qs2:
            d = ctx[q]
            qs = slice(q * 128, (q + 1) * 128)
            vt, bk, arb, ark, apT, uc_ = d["vt"], d["bk"], d["arb"], d["ark"], d["apT"], d["upc"]
            ut = Utok[cntr["ut"] % 2]; cntr["ut"] += 1
            ps_u = ps_next()
            mm(ps_u, ps_u[:, 0:128], apT, apT.ap, sbd, sbd2)
            for h in range(2):
                tt("dve", ut, ut[:, h, h * 64:(h + 1) * 64], ps_u, ps_u[:, h * 64:(h + 1) * 64],
                   Up4, Up4[:, uc_ * 128 + h * 64:uc_ * 128 + (h + 1) * 64], ALU.add)
            yield
            ps_y = ps_next()
            mm(ps_y, ps_y[:, 0:128], sbd, sbd2, AR, AR[:, q, 1, :], start=True, stop=False)
            for h in range(2):
                mm(ps_y, ps_y[:, 0:128], ut, ut[:, h, :], arb, arb[:, h, :], start=False, stop=False)
                mm(ps_y, ps_y[:, 0:128], vt, vt[:, h, :], ark, ark[:, h, :], start=False, stop=(h == 1))
            ps_s = ps_next()
            mm(ps_s, ps_s[:, 0:256], bk, bk[:, 0, :], ut, ut.ap.rearrange("p h v -> p (h v)"), start=True, stop=False)
            mm(ps_s, ps_s[:, 0:256], bk, bk[:, 1, :], vt, vt.ap.rearrange("p h v -> p (h v)"), start=False, stop=True)
            pc = E1[:, q * 128 + 127:q * 128 + 128]
            for h in range(2):
                hp = slice(h * 64, (h + 1) * 64)
                stt(sfd, sfd[hp, :], sfd, sfd[hp, :], pc[hp, :], ps_s, ps_s[hp, h * 128 + h * 64:h * 128 + (h + 1) * 64],
                    ALU.mult, ALU.add, extra_r=[E1])
            for h in range(2):
                hp = slice(h * 64, (h + 1) * 64)
                cp("act", sbd, sbd[hp, h, :], sfd, sfd[hp, :])
            cp("act", yT, yT[:, qs], ps_y, ps_y[:, 0:128])
            yield

    def scan_all(l, c, E1, bpT, kpT, vb, yT):
        ctxA, ctxB = {}, {}
        for _ in scan_pre(l, c, (0, 1), ctxA, E1, bpT, kpT, vb):
            pass
        gB = scan_pre(l, c, (2, 3), ctxB, E1, bpT, kpT, vb)
        gA = scan_seq(l, c, (0, 1), ctxA, E1, yT)
        aliveA = aliveB = True
        while aliveA or aliveB:
            if aliveB:
                try:
                    next(gB)
                except StopIteration:
                    aliveB = False
            if aliveA:
                try:
                    next(gA)
                except StopIteration:
                    aliveA = False
        for _ in scan_seq(l, c, (2, 3), ctxB, E1, yT):
            pass

    for it in range(NT):
        tsl = slice(it * T, (it + 1) * T)
        for c in range(8):
            P.dma("sp", xT[c].ap, xT_d[c * 128:(c + 1) * 128, tsl], reads=[Bdram_in], writes=[xT[c]])
        for l in range(2):
            rms_to(l, "g_norm", xT, hT, T)
            chk(f"rms{l}")
            hrhs = lambda j: hT
            car = carry[l]

            def shiftmix(ps, mi, npart=128, A=None):
                A = A or slots[11]
                pp = slice(0, npart)
                mu_c, omu_c = pcol(l, "mu", mi), pcol(l, "omu", mi)
                act(A, A[pp, :], ps, ps[pp, :], AF.Identity, extra_r=[pvs], scale=omu_c[pp, :])
                stt(A, A[pp, 1:T], ps, ps[pp, 0:T - 1], mu_c[pp, :], A, A[pp, 1:T], ALU.mult, ALU.add, extra_r=[pvs])
                stt(A, A[pp, 0:1], car, car[pp, mi:mi + 1], mu_c[pp, :], A, A[pp, 0:1], ALU.mult, ALU.add, extra_r=[pvs])
                cp("act", car, car[pp, mi:mi + 1], ps, ps[pp, T - 1:T])
                return A

            lob, vlo = lob_, vlo_

            def cons_lora(j, ps):
                if j == 0:
                    lo = shiftmix(ps, 24)
                    act(lob, lob[0:64, :], lo, lo[0:64, :], AF.Tanh)
                    cp("dve", lob, lob[64:128, :], lo, lo[64:128, :])
                elif l == 1:
                    v_ = shiftmix(ps, 25, 32)
                    cp("dve", vlo, vlo[0:32, :], v_, v_[0:32, :])
            proj(l, "lora", lambda j: hT if (j == 0 or l == 1) else None, cons_lora)

            chk(f"lora{l}")
            for c in range(8):
                got = {}

                def cons_rkvg(j, ps):
                    if j < 3:
                        got[j] = shiftmix(ps, j * 8 + c, A=slots[j])
                    else:
                        g = slots[3]
                        act(g, g.ap, ps, ps.ap, AF.Silu)
                        got[3] = g
                proj(l, f"rkvg{c}", hrhs, cons_rkvg)
                r_, k_, v_, gs = got[0], got[1], got[2], got[3]
                cs = slice(c * 128, (c + 1) * 128)
                psd, psa = ps_next(), ps_next()
                mm(psd, psd.ap, lor[l], lor[l][0:64, 0, cs], lob, lob[0:64, :])
                mm(psa, psa.ap, lor[l], lor[l][64:128, 0, cs], lob, lob[64:128, :])
                sgd, a_ = slots[4], slots[5]
                act(sgd, sgd.ap, psd, psd.ap, AF.Sigmoid, extra_r=[pvs], bias=pcol(l, "w0", c))
                act(a_, a_.ap, psa, psa.ap, AF.Sigmoid, extra_r=[pvs], bias=pcol(l, "a0", c))
                if l == 1:
                    psv = ps_next()
                    mm(psv, psv.ap, lor[l], lor[l][0:32, 1, cs], vlo, vlo[0:32, :])
                    gv = slots[7]
                    act(gv, gv.ap, psv, psv.ap, AF.Sigmoid, extra_r=[pvs], bias=pcol(l, "v0", c))
                    dd = slots[8]
                    tt("dve", dd, dd.ap, vf[c], vf[c].ap, v_, v_.ap, ALU.subtract)
                    tt("pool", dd, dd.ap, dd, dd.ap, gv, gv.ap, ALU.mult)
                    tt("pool", v_, v_.ap, v_, v_.ap, dd, dd.ap, ALU.add)
                else:
                    cp("pool", vf[c], vf[c].ap, v_, v_.ap)
                dump(f"r{l}_{c}", r_)
                dump(f"k{l}_{c}", k_)
                dump(f"v{l}_{c}", v_)
                dump(f"a{l}_{c}", a_)
                kkr = slots[6]
                ts("dve", kkr, kkr.ap, k_, k_.ap, pcol(l, "k_k", c), None, ALU.mult, extra_r=[pvs])
                sq = tb.get()
                act(sq, sq.ap, kkr, kkr.ap, AF.Square)
                psn = ps_next()
                mm(psn, psn.ap, cst_b, bones, sq, sq.ap)
                nrm = slots[7]
                act(nrm, nrm.ap, psn, psn.ap, AF.Sqrt)
                ts("dve", nrm, nrm.ap, nrm, nrm.ap, 1e-12, None, ALU.max)
                P.op("dve", lambda e: e.reciprocal(out=nrm.ap, in_=nrm.ap), [nrm], [nrm])
                tt("dve", kkr, kkr.ap, kkr, kkr.ap, nrm, nrm.ap, ALU.mult)
                kk = kkr
                f_ = slots[7]
                ts("dve", f_, f_.ap, a_, a_.ap, pcol(l, "k_a", c), pcol(l, "omka", c), ALU.mult, ALU.add, extra_r=[pvs])
                tt("pool", k_, k_.ap, k_, k_.ap, f_, f_.ap, ALU.mult)
                k2 = k_
                rk = tb.get()
                stt(rk, rk.ap, r_, r_.ap, pcol(l, "r_k", c), k2, k2.ap, ALU.mult, ALU.mult, extra_r=[pvs])
                psb = ps_next()
                mm(psb, psb.ap, cst_b, bones, rk, rk.ap)
                bon = slots[8]
                tt("dve", bon, bon.ap, psb, psb.ap, v_, v_.ap, ALU.mult)
                cum = slots[7]
                P.op("dve", lambda e: e.tensor_tensor_scan(out=cum.ap, data0=rmask.ap, data1=sgd.ap, initial=0.0,
                                                           op0=ALU.mult, op1=ALU.add), [rmask, sgd], [cum])
                E1, E2, E3 = slots[9], slots[10], slots[11]
                act(E1, E1.ap, cum, cum.ap, AF.Exp, scale=-C0)
                act(E2, E2.ap, cum, cum.ap, AF.Exp, scale=C0)
                tt("pool", sgd, sgd.ap, cum, cum.ap, sgd, sgd.ap, ALU.subtract)
                act(E3, E3.ap, sgd, sgd.ap, AF.Exp, scale=-C0)
                dump(f"E1{l}_{c}", E1)
                tt("dve", AR, AR[:, :, 1, :], r_, r_.ap.rearrange("p (q t) -> p q t", q=4), E1, E1.ap.rearrange("p (q t) -> p q t", q=4), ALU.mult)
                stt(AR, AR[:, :, 0, :], kk, kk.ap.rearrange("p (q t) -> p q t", q=4), -1.0, E3, E3.ap.rearrange("p (q t) -> p q t", q=4), ALU.mult, ALU.mult)
                bt, kt = slots[0], slots[11]
                tt("pool", bt, bt.ap, kk, kk.ap, a_, a_.ap, ALU.mult)
                tt("dve", bt, bt.ap, bt, bt.ap, E2, E2.ap, ALU.mult)
                tt("dve", kt, kt.ap, k2, k2.ap, E2, E2.ap, ALU.mult)
                for h in range(2):
                    hp = slice(h * 64, (h + 1) * 64)
                    cp("act", Bbd, Bbd[hp, :, h, :], bt, bt[hp, :].rearrange("p (q t) -> p q t", q=4))
                    cp("pool", Kbd, Kbd[hp, :, h, :], kt, kt[hp, :].rearrange("p (q t) -> p q t", q=4))
                bpT, kpT, vb = tb.get(), tb.get(), tb.get()
                for q in range(4):
                    qs = slice(q * 128, (q + 1) * 128)
                    pc = E1[:, q * 128 + 127:q * 128 + 128]
                    act(bpT, bpT[:, qs], bt, bt[:, qs], AF.Identity, extra_r=[E1], scale=pc)
                    act(kpT, kpT[:, qs], kt, kt[:, qs], AF.Identity, extra_r=[E1], scale=pc)
                cp("pool", vb, vb.ap, v_, v_.ap)
                chk(f"prep{l}_{c}")
                yT = slots[1]
                scan_all(l, c, E1, bpT, kpT, vb, yT)
                dump(f"y{l}_{c}", yT)
                yb_, ysq = tb.get(), tb.get()
                cp("pool", yb_, yb_.ap, yT, yT.ap)
                act(ysq, ysq.ap, yT, yT.ap, AF.Square)
                p1, p2 = ps_next(), ps_next()
                mm(p1, p1.ap, cst_b, bones, yb_, yb_.ap)
                mm(p2, p2.ap, cst_b, bones, ysq, ysq.ap)
                mean, var = slots[5], slots[6]
                act(mean, mean.ap, p1, p1.ap, AF.Identity, scale=1.0 / 64)
                tt("pool", var, var.ap, mean, mean.ap, mean, mean.ap, ALU.mult)
                stt(var, var.ap, p2, p2.ap, 1.0 / 64, var, var.ap, ALU.mult, ALU.subtract)
                act(var, var.ap, var, var.ap, AF.Sqrt, extra_r=[epsc], bias=epsc[:, 1:2])
                P.op("dve", lambda e: e.reciprocal(out=var.ap, in_=var.ap), [var], [var])
                tt("dve", yT, yT.ap, yT, yT.ap, mean, mean.ap, ALU.subtract)
                tt("dve", yT, yT.ap, yT, yT.ap, var, var.ap, ALU.mult)
                act(yT, yT.ap, yT, yT.ap, AF.Identity, extra_r=[pvs], scale=pcol(l, "gn_g", c), bias=pcol(l, "gn_b", c))
                tt("pool", yT, yT.ap, yT, yT.ap, bon, bon.ap, ALU.add)
                tt("dve", OG[c], OG[c].ap, yT, yT.ap, gs, gs.ap, ALU.mult)
                dump(f"og{l}_{c}", OG[c])

            def branch_out(pn, first, bias_name=None):
                for j in range(4):
                    tmpy = {}

                    def cons(jj, ps, j=j):
                        if jj < 2:
                            t_ = tf.get()
                            if bias_name is None:
                                cp("act", t_, t_.ap, ps, ps.ap)
                            else:
                                act(t_, t_.ap, ps, ps.ap, AF.Identity, extra_r=[pvs], bias=pcol(l, bias_name, 2 * j + jj))
                            tmpy[jj] = t_
                        else:
                            cidx = 2 * j + (jj - 2)
                            sg = tf.get()
                            act(sg, sg.ap, ps, ps.ap, AF.Sigmoid)
                            yb = tmpy[jj - 2]
                            if first:
                                tt("dve", yacc[cidx], yacc[cidx].ap, sg, sg.ap, yb, yb.ap, ALU.mult)
                            else:
                                tt("pool", sg, sg.ap, sg, sg.ap, yb, yb.ap, ALU.mult)
                                tt("dve", yacc[cidx], yacc[cidx].ap, yacc[cidx], yacc[cidx].ap, sg, sg.ap, ALU.add)
                    proj(l, f"{pn}{j}", lambda jj: OG if jj < 2 else hT, cons)

            chk(f"rwkv{l}")
            branch_out("pr", True)
            chk(f"pr{l}")
            dump(f"yacc0_{l}", yacc[0])

            uc = big8
            s1, s2 = pin[0], pin[1]
            dg = dg_keep
            for j in range(4):
                def cons_glu(jj, ps, j=j):
                    c = 2 * j + jj // 2
                    if jj % 2 == 0:
                        cons_glu.pa = ps
                        return
                    gb = tf.get()
                    act(gb, gb.ap, ps, ps.ap, AF.Sigmoid, extra_r=[pvs], bias=pcol(l, "b_glu", 8 + c))
                    pa = cons_glu.pa
                    u = ubr.get()
                    cp("pool", u, u[:, 0:30], halo[l], halo[l][:, c, :])
                    stt(u, u[:, 30:30 + T], pa, pa.ap, pcol(l, "b_glu", c), gb, gb.ap, ALU.add, ALU.mult, extra_r=[pvs])
                    for tp in range(31):
                        ts("pool", dg, dg[:, tp, :], cst_b, ident, pcol(l, "w_dw", c * 31 + tp), None, ALU.mult, extra_r=[pvs])
                    pc_ = ps_next()
                    for tp in range(31):
                        mm(pc_, pc_.ap, dg, dg[:, tp, :], u, u[:, tp:tp + T], start=(tp == 0), stop=(tp == 30))
                    act(uc[c], uc[c].ap, pc_, pc_.ap, AF.Identity, extra_r=[pvs], bias=pcol(l, "b_dw", c))
                    cp("act", halo[l], halo[l][:, c, :], u, u[:, T:T + 30])
                    ucb, ucs = tb.get(), tb.get()
                    cp("pool", ucb, ucb.ap, uc[c], uc[c].ap)
                    act(ucs, ucs.ap, uc[c], uc[c].ap, AF.Square)
                    mm(s1, s1.ap, cst_b, ones, ucb, ucb.ap, start=(c == 0), stop=(c == 7))
                    mm(s2, s2.ap, cst_b, ones, ucs, ucs.ap, start=(c == 0), stop=(c == 7))
                proj(l, f"glu{j}", hrhs, cons_glu)
            mean, var = slots[10], slots[11]
            act(mean, mean.ap, s1, s1.ap, AF.Identity, scale=1.0 / D)
            tt("pool", var, var.ap, mean, mean.ap, mean, mean.ap, ALU.mult)
            stt(var, var.ap, s2, s2.ap, 1.0 / D, var, var.ap, ALU.mult, ALU.subtract)
            act(var, var.ap, var, var.ap, AF.Sqrt, extra_r=[epsc], bias=epsc[:, 2:3])
            P.op("dve", lambda e: e.reciprocal(out=var.ap, in_=var.ap), [var], [var])
            dump(f"uc{l}_0", uc[0])
            for j in range(2):
                def cons_cg(jj, ps, j=j):
                    c = 4 * j + jj
                    cg = tf.get()
                    act(cg, cg.ap, ps, ps.ap, AF.Silu)
                    t_ = uc[c]
                    tt("dve", t_, t_.ap, t_, t_.ap, mean, mean.ap, ALU.subtract)
                    tt("dve", t_, t_.ap, t_, t_.ap, var, var.ap, ALU.mult)
                    act(t_, t_.ap, t_, t_.ap, AF.Identity, extra_r=[pvs], scale=pcol(l, "ln_g", c), bias=pcol(l, "ln_b", c))
                    act(t_, t_.ap, t_, t_.ap, AF.Silu)
                    tt("dve", OG[c], OG[c].ap, t_, t_.ap, cg, cg.ap, ALU.mult)
                proj(l, f"cg{j}", hrhs, cons_cg)
            dump(f"ug{l}_0", OG[0])
            chk(f"conv{l}")
            branch_out("pc", False, "b_pc")
            chk(f"pc{l}")
            dump(f"yacc1_{l}", yacc[0])

            qT = OG
            for j in range(2):
                def cons_q(jj, ps, j=j):
                    c = 4 * j + jj
                    act(qT[c], qT[c].ap, ps, ps.ap, AF.Identity, scale=1.0 / 16.0)
                proj(l, f"q{j}", hrhs, cons_q)
            att = big8
            prT = prT_keep
            small = small_keep
            for hm in range(4):
                pt = prT[hm % 2]
                for sbk in range(4):
                    ss = slice(sbk * 128, (sbk + 1) * 128)
                    psc = ps_next()
                    for dc in range(2):
                        mm(psc, psc[:, 0:NMEM], qT[2 * hm + dc], qT[2 * hm + dc][:, ss], kmT[l][2 * hm + dc], kmT[l][2 * hm + dc].ap,
                           start=(dc == 0), stop=(dc == 1))
                    P.op("dve", lambda e: e.tensor_reduce(out=small[:, 0:1], in_=psc[:, 0:NMEM], axis=AX.X, op=ALU.max), [psc], [small])
                    ts("dve", small, small[:, 1:2], small, small[:, 0:1], -1.0, None, ALU.mult)
                    ex = tf.get()
                    P.op("act", lambda e: e.activation(out=ex[:, 0:NMEM], in_=psc[:, 0:NMEM], func=AF.Exp, bias=small[:, 1:2],
                                                       accum_out=small[:, 2:3]), [psc, small], [ex, small])
                    P.op("dve", lambda e: e.reciprocal(out=small[:, 3:4], in_=small[:, 2:3]), [small], [small])
                    pb = tf.get()
                    ts("dve", pb, pb[:, 0:NMEM], ex, ex[:, 0:NMEM], small[:, 3:4], None, ALU.mult, extra_r=[small])
                    ptp = ps_next()
                    pv_ = ptp.ap
                    for mb in range(2):
                        tr(ptp, pv_[:, mb * 128:(mb + 1) * 128], pb, pb[:, mb * 128:(mb + 1) * 128])
                    for mb in range(2):
                        cp("act", pt[mb], pt[mb][:, ss], ptp, pv_[:, mb * 128:(mb + 1) * 128])
                for dc in range(2):
                    c = 2 * hm + dc
                    pa_ = ps_next()
                    for mb in range(2):
                        mm(pa_, pa_.ap, vmt[l][mb], vmt[l][mb][:, c * 128:(c + 1) * 128], pt[mb], pt[mb].ap, start=(mb == 0), stop=(mb == 1))
                    cp("act", att[c], att[c].ap, pa_, pa_.ap)
            dump(f"att{l}_0", att[0])
            for j in range(2):
                def cons_mg(jj, ps, j=j):
                    c = 4 * j + jj
                    mg = tf.get()
                    act(mg, mg.ap, ps, ps.ap, AF.Silu)
                    tt("dve", OG[c], OG[c].ap, att[c], att[c].ap, mg, mg.ap, ALU.mult)
                proj(l, f"mg{j}", hrhs, cons_mg)
            chk(f"mem{l}")
            branch_out("pm", False)
            chk(f"pm{l}")
            dump(f"yacc2_{l}", yacc[0])

            for c in range(8):
                cp("pool", OG[c], OG[c].ap, yacc[c], yacc[c].ap)
            for j in range(2):
                def cons_o(jj, ps, j=j):
                    c = 4 * j + jj
                    tt("dve", xT[c], xT[c].ap, xT[c], xT[c].ap, ps, ps.ap, ALU.add)
                proj(l, f"wo{j}", lambda jj: OG, cons_o)
            dump(f"x{l}_0", xT[0])

        ps = pin[0]
        for c in range(8):
            sq = tb.get()
            act(sq, sq.ap, xT[c], xT[c].ap, AF.Square)
            mm(ps, ps.ap, cst_b, ones, sq, sq.ap, start=(c == 0), stop=(c == 7))
        sd, rs = tf.get(), tf.get()
        act(sd, sd.ap, ps, ps.ap, AF.Sqrt, extra_r=[epsc], scale=1.0 / D, bias=epsc[:, 0:1])
        P.op("dve", lambda e: e.reciprocal(out=rs.ap, in_=sd.ap), [sd], [rs])
        for c in range(8):
            o_ = tf.get()
            stt(o_, o_.ap, xT[c], xT[c].ap, pcol(0, "g_final", c), rs, rs.ap, ALU.mult, ALU.mult, extra_r=[pvs])
            P.dma("sp", outT_d[c * 128:(c + 1) * 128, tsl], o_.ap, reads=[o_], writes=[Bout])


_CACHE = {}


def kernel(**inp):
    inp = {k: np.asarray(v) for k, v in inp.items()}
    pv, wbig, lora, cst, rm = host_prep(inp)
    x, mem = inp["x"], inp["mem"]
    B = x.shape[0]
    nc = bass.Bass("TRN2", target_bir_lowering=False)
    build(nc)
    in_maps = []
    for b in range(B):
        in_maps.append({"xT": np.ascontiguousarray(x[b].T), "memT": np.ascontiguousarray(mem[b].T),
                        "pv": pv, "wbig": wbig, "lora": lora, "cst": cst, "rm": rm})
    res = run_bass_kernel_spmd(nc, in_maps, core_ids=list(range(B)))
    out = np.stack([np.ascontiguousarray(r["outT"].T) for r in res.results], axis=0)
    return out.astype(np.float32)
```

```python
import numpy as np
import concourse.bass as bass
import concourse.mybir as mybir
from concourse.bass_utils import run_bass_kernel_spmd

F32 = mybir.dt.float32
BF16 = mybir.dt.bfloat16
AF = mybir.ActivationFunctionType
ALU = mybir.AluOpType
AX = mybir.AxisListType
NDS = 24

D = 1024
SEQ = 4096
T = 512
NMEM = 256
KC = 8
C0 = float(np.exp(-0.5))


class Buf:
    __slots__ = ("ap", "w", "r")

    def __init__(self, ap):
        self.ap = ap
        self.w = None
        self.r = {}

    def __getitem__(self, k):
        return self.ap[k]


class Prog:
    def __init__(self, nc):
        self.nc = nc
        self.eng = dict(pe=nc.tensor, dve=nc.vector, act=nc.scalar, pool=nc.gpsimd, sp=nc.sync)
        self.esem = {k: nc.alloc_semaphore("es_" + k) for k in self.eng}
        self.ecnt = {k: 0 for k in self.eng}
        self.seen = {k: {} for k in self.eng}
        self.dsem, self.dtgt, self.dnext = {}, {}, {}
        self.ninst = 0

    def _wait(self, e, ev):
        sem, key, val = ev
        if key == ("e", e) and e == "pe":
            return
        if self.seen[e].get(key, 0) >= val:
            return
        self.eng[e].wait_ge(sem, val)
        self.seen[e][key] = val

    def _deps(self, e, reads, writes):
        for b in reads:
            if b.w is not None:
                self._wait(e, b.w)
        for b in writes:
            if b.w is not None:
                self._wait(e, b.w)
            for ev in b.r.values():
                self._wait(e, ev)

    def _record(self, ev, reads, writes):
        for b in reads:
            b.r[ev[1]] = ev
        for b in writes:
            b.w = ev
            b.r = {}

    def op(self, e, fn, reads=(), writes=()):
        self._deps(e, reads, writes)
        inst = fn(self.eng[e])
        self.ecnt[e] += 1
        inst.then_inc(self.esem[e], 1)
        self._record((self.esem[e], ("e", e), self.ecnt[e]), reads, writes)
        self.ninst += 1

    def dma(self, q, out_ap, in_ap, reads=(), writes=(), **kw):
        if q not in self.dsem:
            self.dsem[q] = [self.nc.alloc_semaphore(f"ds_{q}{i}") for i in range(NDS)]
            self.dtgt[q] = [0] * NDS
            self.dnext[q] = 0
        j = self.dnext[q]
        self.dnext[q] = (j + 1) % NDS
        key = ("d", q, j)
        if self.dtgt[q][j] > 0:
            self._wait(q, (self.dsem[q][j], key, self.dtgt[q][j]))
        self._deps(q, reads, writes)
        inst = self.eng[q].dma_start(out=out_ap, in_=in_ap, **kw)
        self.dtgt[q][j] += 16
        inst.then_inc(self.dsem[q][j], 16)
        self._record((self.dsem[q][j], key, self.dtgt[q][j]), reads, writes)
        self.ninst += 1

    def finish(self, e="sp"):
        for q in self.dsem:
            for j in range(NDS):
                if self.dtgt[q][j] > 0:
                    self._wait(e, (self.dsem[q][j], ("d", q, j), self.dtgt[q][j]))
        for k in self.eng:
            if k != e and self.ecnt[k] > 0:
                self._wait(e, (self.esem[k], ("e", k), self.ecnt[k]))


PV = {}
_o = 0
for _n, _w in [("g_norm", 8), ("mu", 26), ("w0", 8), ("a0", 8), ("k_k", 8), ("k_a", 8), ("r_k", 8),
               ("gn_g", 8), ("gn_b", 8), ("v0", 8), ("b_glu", 16), ("w_dw", 248), ("b_dw", 8),
               ("ln_g", 8), ("ln_b", 8), ("b_pc", 8), ("g_mem", 8), ("g_final", 8), ("omu", 26), ("omka", 8)]:
    PV[_n] = _o
    _o += _w
NPV = _o

WCOL = {}
_o = 0
for _n, _w in [("lora", 256)] + [(f"rkvg{c}", 512) for c in range(8)] + \
        [(f"pr{j}", 512) for j in range(4)] + [(f"glu{j}", 512) for j in range(4)] + \
        [(f"cg{j}", 512) for j in range(2)] + [(f"pc{j}", 512) for j in range(4)] + \
        [(f"q{j}", 512) for j in range(2)] + [(f"mg{j}", 512) for j in range(2)] + \
        [(f"pm{j}", 512) for j in range(4)] + [(f"wo{j}", 512) for j in range(2)] + \
        [(f"kv{j}", 512) for j in range(4)]:
    WCOL[_n] = (_o, _w)
    _o += _w
TOTC = _o


def _fm(v):
    return np.ascontiguousarray(v.reshape(8, 128).T)


def host_prep(inp):
    f = np.float32
    L = 2
    pv = np.zeros((L, 128, NPV), f)
    wbig = np.zeros((L, 128, KC, TOTC), f)
    lora = np.zeros((L, 128, 2, 1024), f)
    for l in range(L):
        def put(name, arr):
            pv[l][:, PV[name]:PV[name] + arr.shape[1]] = arr
        put("g_norm", _fm(inp["g_norm"][l]))
        mu = inp["mu_shift"][l]
        mucols = np.zeros((128, 26), f)
        mucols[:, 0:25] = mu.reshape(25, 128).T
        if l >= 1:
            mucols[0:32, 25] = inp["mu_vres"][l - 1]
        put("mu", mucols)
        for n in ["w0", "a0", "k_k", "k_a", "gn_g", "gn_b", "b_dw", "ln_g", "ln_b", "g_mem"]:
            src = {"g_mem": "g_mem_norm"}.get(n, n)
            put(n, _fm(inp[src][l]))
        put("r_k", _fm(inp["r_k"][l].reshape(-1)))
        put("b_pc", _fm(inp["b_proj_conv"][l]))
        if l >= 1:
            put("v0", _fm(inp["v0"][l - 1]))
        put("b_glu", np.ascontiguousarray(inp["b_glu"][l].reshape(16, 128).T))
        wd = inp["w_dw"][l]
        put("w_dw", np.ascontiguousarray(wd.reshape(31, 8, 128).transpose(2, 1, 0).reshape(128, 248)))
        put("g_final", _fm(inp["g_final"]))
        w_in = inp["w_in"][l]
        Wc = np.zeros((D, TOTC), f)

        def setc(name, off, arr):
            o, w = WCOL[name]
            Wc[:, o + off:o + off + arr.shape[1]] = arr
        setc("lora", 0, w_in[:, 3072:3200])
        if l >= 1:
            setc("lora", 128, inp["w_vres_down"][l - 1])
        for c in range(8):
            setc(f"rkvg{c}", 0, w_in[:, c * 128:(c + 1) * 128])
            setc(f"rkvg{c}", 128, w_in[:, 1024 + c * 128:1024 + (c + 1) * 128])
            setc(f"rkvg{c}", 256, w_in[:, 2048 + c * 128:2048 + (c + 1) * 128])
            setc(f"rkvg{c}", 384, w_in[:, 3200 + c * 128:3200 + (c + 1) * 128])
        for br, (pn, wp) in enumerate([("pr", inp["w_proj_rwkv"][l]), ("pc", inp["w_proj_conv"][l]),
                                       ("pm", inp["w_proj_mem"][l])]):
            for j in range(4):
                setc(f"{pn}{j}", 0, wp[:, j * 256:(j + 1) * 256])
                mo = 9344 + br * 1024 + j * 256
                setc(f"{pn}{j}", 256, w_in[:, mo:mo + 256])
        for j in range(4):
            for i in range(2):
                c = 2 * j + i
                setc(f"glu{j}", i * 256, w_in[:, 4224 + c * 128:4224 + (c + 1) * 128])
                setc(f"glu{j}", i * 256 + 128, w_in[:, 5248 + c * 128:5248 + (c + 1) * 128])
        for j in range(2):
            setc(f"cg{j}", 0, w_in[:, 6272 + j * 512:6272 + (j + 1) * 512])
            setc(f"q{j}", 0, w_in[:, 7296 + j * 512:7296 + (j + 1) * 512])
            setc(f"mg{j}", 0, w_in[:, 8320 + j * 512:8320 + (j + 1) * 512])
            setc(f"wo{j}", 0, inp["w_out"][l][:, j * 512:(j + 1) * 512])
        for j in range(4):
            setc(f"kv{j}", 0, inp["w_mem_kv"][l][:, j * 512:(j + 1) * 512])
        wbig[l] = Wc.reshape(KC, 128, TOTC).transpose(1, 0, 2)
        lora[l][0:64, 0] = inp["w_decay_up"][l]
        lora[l][64:128, 0] = inp["w_aaa_up"][l]
        if l >= 1:
            lora[l][0:32, 1] = inp["w_vres_up"][l - 1]
    cst = np.zeros((128, 8, 128), f)
    i = np.arange(128)
    cst[:, 0] = np.eye(128)
    cst[:, 1] = 1.0
    cst[:, 2] = (i[:, None] // 64 == i[None, :] // 64)
    cst[:, 3] = (i[:, None] < i[None, :])
    cst[:, 4] = (i[:, None] <= i[None, :])
    cst[:, 5] = (i[:, None] > i[None, :])
    rm = np.ones((128, 512), f)
    rm[:, 0::128] = 0.0
    return pv, wbig, lora, cst, rm


class _Stop(Exception):
    pass


def build(nc, NT=SEQ // T, dbg_names=(), stop_after=None):
    P = Prog(nc)
    try:
        _build(nc, P, NT, dbg_names, stop_after)
    except _Stop:
        pass
    P.finish()
    return P


def _build(nc, P, NT, dbg_names, stop_after):
    def chk(tag):
        if tag == stop_after:
            raise _Stop()
    dt = nc.dram_tensor
    xT_d = dt("xT", [D, SEQ], F32, kind="ExternalInput").ap()
    memT_d = dt("memT", [D, NMEM], F32, kind="ExternalInput").ap()
    pv_d = dt("pv", [2, 128, NPV], F32, kind="ExternalInput").ap()
    wbig_d = dt("wbig", [2, 128, KC, TOTC], F32, kind="ExternalInput").ap()
    lora_d = dt("lora", [2, 128, 2, 1024], F32, kind="ExternalInput").ap()
    cst_d = dt("cst", [128, 8, 128], F32, kind="ExternalInput").ap()
    rm_d = dt("rm", [128, 512], F32, kind="ExternalInput").ap()
    outT_d = dt("outT", [D, SEQ], F32, kind="ExternalOutput").ap()
    dbg_d = None
    if dbg_names:
        dbg_d = dt("dbg", [len(dbg_names), 128, 512], F32, kind="ExternalOutput").ap()
    Bdram_in = Buf(None)
    Bout = Buf(None)
    cnt = [0]

    def sb(shape, dtype, name=None):
        cnt[0] += 1
        return nc.alloc_sbuf_tensor(name or f"t{cnt[0]}", list(shape), dtype)

    def sbuf(shape, dtype):
        return Buf(sb(shape, dtype).ap())

    big8 = [sbuf([128, T], F32) for _ in range(8)]
    cst_f = Buf(big8[0].ap.rearrange("p (a b) -> p a b", a=4))
    cst_f2 = Buf(big8[1].ap.rearrange("p (a b) -> p a b", a=4))
    P.dma("sp", cst_f.ap, cst_d[:, 0:4, :], reads=[Bdram_in], writes=[cst_f, big8[0]])
    P.dma("sp", cst_f2.ap, cst_d[:, 4:8, :], reads=[Bdram_in], writes=[cst_f2, big8[1]])
    cst_b = sbuf([128, 8, 128], BF16)
    P.op("dve", lambda e: e.tensor_copy(out=cst_b[:, 0:4, :], in_=cst_f.ap), [cst_f, big8[0]], [cst_b])
    P.op("dve", lambda e: e.tensor_copy(out=cst_b[:, 4:8, :], in_=cst_f2.ap), [cst_f2, big8[1]], [cst_b])
    ident, ones, bones = cst_b[:, 0, :], cst_b[:, 1, :], cst_b[:, 2, :]
    identf = sbuf([128, 128], F32)
    P.op("dve", lambda e: e.tensor_copy(out=identf.ap, in_=cst_f[:, 0, :]), [cst_f, big8[0]], [identf])
    m12 = Buf(cst_b[:, 3:5, :])
    m12.w = None
    mSL2 = sbuf([128, 2, 128], BF16)
    id2 = sbuf([128, 2, 128], BF16)
    for h in range(2):
        P.op("dve", lambda e: e.tensor_copy(out=mSL2[:, h, :], in_=cst_b[:, 5, :]), [cst_b], [mSL2])
        P.op("dve", lambda e: e.tensor_copy(out=id2[:, h, :], in_=cst_b[:, 0, :]), [cst_b], [id2])
    rmf = Buf(big8[2].ap)
    P.dma("sp", rmf.ap, rm_d, reads=[Bdram_in], writes=[big8[2]])
    rmask = sbuf([128, 512], BF16)
    P.op("dve", lambda e: e.tensor_copy(out=rmask.ap, in_=big8[2].ap), [big8[2]], [rmask])
    pvs = sbuf([128, 2, NPV], F32)
    for l in range(2):
        P.dma("sp", pvs[:, l, :], pv_d[l], reads=[Bdram_in], writes=[pvs])
    for l in range(2):
        for (src, dst, w) in [("mu", "omu", 26), ("k_a", "omka", 8)]:
            P.op("dve", lambda e: e.tensor_scalar(out=pvs[:, l, PV[dst]:PV[dst] + w], in0=pvs[:, l, PV[src]:PV[src] + w],
                                                  scalar1=-1.0, scalar2=1.0, op0=ALU.mult, op1=ALU.add), [pvs], [pvs])
    epsc = sbuf([128, 4], F32)
    for i, v in enumerate([1e-6, 64e-5, 1e-5, 0.0]):
        P.op("dve", lambda e: e.memset(epsc[:, i:i + 1], v), [], [epsc])

    def pcol(l, name, c=0):
        o = PV[name] + c
        return pvs[:, l, o:o + 1]

    lor = [sbuf([128, 2, 1024], BF16) for _ in range(2)]
    for l in range(2):
        P.dma("pool", lor[l].ap, lora_d[l], reads=[Bdram_in], writes=[lor[l]])

    banks = [Buf(nc.alloc_psum_tensor(f"ps{i}", [128, 512], F32).ap()) for i in range(8)]
    ring = banks[:6]
    pin = banks[6:]
    rp = [0]

    def ps_next():
        b = ring[rp[0] % len(ring)]
        rp[0] += 1
        return b

    def bfv(b):
        return b.ap.bitcast(BF16)

    class Ring:
        def __init__(self, n, shape, dtype):
            self.b = [sbuf(shape, dtype) for _ in range(n)]
            self.i = 0

        def get(self):
            b = self.b[self.i % len(self.b)]
            self.i += 1
            return b

    slots = [sbuf([128, 512], F32) for _ in range(12)]
    tf = Ring(0, [128, 512], F32)
    tf.b = slots[0:10]
    tb = Ring(6, [128, 512], BF16)
    lob_, vlo_ = sbuf([128, 512], BF16), sbuf([128, 512], BF16)
    wring = Ring(2, [128, KC, 512], BF16)

    xT = [sbuf([128, T], F32) for _ in range(8)]
    vf = [sbuf([128, T], F32) for _ in range(8)]
    hT = [sbuf([128, T], BF16) for _ in range(8)]
    OG = [sbuf([128, T], BF16) for _ in range(8)]
    yacc = [sbuf([128, T], F32) for _ in range(8)]
    carry = [sbuf([128, 26], F32) for _ in range(2)]
    Sf = [[sbuf([128, 64], F32) for _ in range(8)] for _ in range(2)]
    Sb = [[sbuf([128, 2, 64], BF16) for _ in range(8)] for _ in range(2)]
    halo = [sbuf([128, 8, 30], BF16) for _ in range(2)]
    ubr = Ring(2, [128, 30 + T], BF16)
    dg_keep = sbuf([128, 31, 128], BF16)
    prT_keep = [[sbuf([128, T], BF16) for _ in range(2)] for _ in range(2)]
    small_keep = sbuf([128, 8], F32)
    kmT = [[sbuf([128, NMEM], BF16) for _ in range(8)] for _ in range(2)]
    vmt = [[sbuf([128, D], BF16) for _ in range(2)] for _ in range(2)]
    for l in range(2):
        P.op("pool", lambda e: e.memset(carry[l].ap, 0.0), [], [carry[l]])
        P.op("pool", lambda e: e.memset(halo[l].ap, 0.0), [], [halo[l]])
        for c in range(8):
            P.op("pool", lambda e: e.memset(Sf[l][c].ap, 0.0), [], [Sf[l][c]])
            P.op("pool", lambda e: e.memset(Sb[l][c].ap, 0.0), [], [Sb[l][c]])

    dbg_list = list(dbg_names)
    dbg_buf = sbuf([128, 512], F32) if dbg_names else None

    def dump(name, b, ap=None):
        if name in dbg_list:
            i = dbg_list.index(name)
            a = b.ap if ap is None else ap
            t = dbg_buf
            P.op("dve", lambda e: e.tensor_copy(out=t[:, 0:a.shape[-1]], in_=a), [b], [t])
            P.dma("sp", dbg_d[i][0:a.shape[0], 0:a.shape[-1]], t[0:a.shape[0], 0:a.shape[-1]], reads=[t], writes=[Bout])
            dbg_list[i] = None

    def act(out_b, out_ap, in_b, in_ap, func, extra_r=(), **kw):
        P.op("act", lambda e: e.activation(out=out_ap, in_=in_ap, func=func, **kw), [in_b] + list(extra_r), [out_b])

    def tt(eng, out_b, out_ap, a_b, a_ap, b_b, b_ap, op):
        P.op(eng, lambda e: e.tensor_tensor(out=out_ap, in0=a_ap, in1=b_ap, op=op), [a_b, b_b], [out_b])

    def stt(out_b, out_ap, a_b, a_ap, scalar, b_b, b_ap, op0, op1, extra_r=()):
        P.op("dve", lambda e: e.scalar_tensor_tensor(out=out_ap, in0=a_ap, scalar=scalar, in1=b_ap, op0=op0, op1=op1),
             [a_b, b_b] + list(extra_r), [out_b])

    def ts(eng, out_b, out_ap, a_b, a_ap, s1, s2, op0, op1=None, extra_r=()):
        if op1 is None:
            P.op(eng, lambda e: e.tensor_scalar(out=out_ap, in0=a_ap, scalar1=s1, scalar2=None, op0=op0),
                 [a_b] + list(extra_r), [out_b])
        else:
            P.op(eng, lambda e: e.tensor_scalar(out=out_ap, in0=a_ap, scalar1=s1, scalar2=s2, op0=op0, op1=op1),
                 [a_b] + list(extra_r), [out_b])

    def cp(eng, out_b, out_ap, in_b, in_ap):
        if eng == "act":
            P.op("act", lambda e: e.copy(out=out_ap, in_=in_ap), [in_b], [out_b])
        elif eng == "dve":
            P.op(eng, lambda e: e.tensor_scalar(out=out_ap, in0=in_ap, scalar1=1.0, scalar2=None, op0=ALU.mult), [in_b], [out_b])
        else:
            P.op(eng, lambda e: e.tensor_copy(out=out_ap, in_=in_ap), [in_b], [out_b])

    def mm(out_b, out_ap, l_b, l_ap, r_b, r_ap, start=True, stop=True):
        P.op("pe", lambda e: e.matmul(out_ap, lhsT=l_ap, rhs=r_ap, start=start, stop=stop), [l_b, r_b], [out_b])

    def tr(out_b, out_ap, in_b, in_ap):
        P.op("pe", lambda e: e.transpose(out_ap, in_ap, identf.ap), [in_b, identf], [out_b])

    def wload(l, name):
        o, w = WCOL[name]
        wb = wring.get()
        P.dma("pool", wb[:, :, 0:w], wbig_d[l][:, :, o:o + w], reads=[Bdram_in], writes=[wb])
        return wb

    def proj(l, name, rhs_for_chunk, consume):
        o, w = WCOL[name]
        wb = wload(l, name)
        for j in range(w // 128):
            rhs = rhs_for_chunk(j)
            if rhs is None:
                continue
            ps = ps_next()
            for kc in range(KC):
                mm(ps, ps.ap, wb, wb[:, kc, j * 128:(j + 1) * 128], rhs[kc], rhs[kc].ap, start=(kc == 0), stop=(kc == KC - 1))
            consume(j, ps)

    def bcast_stat(src_list, src_aps, scale, epscol):
        raise NotImplementedError

    def rms_to(l, gname, src, dst, n):
        ps = pin[0]
        for c in range(8):
            sq = tb.get()
            act(sq, sq[:, 0:n], src[c], src[c][:, 0:n], AF.Square)
            mm(ps, ps[:, 0:n], cst_b, ones, sq, sq[:, 0:n], start=(c == 0), stop=(c == 7))
        sd = tf.get()
        act(sd, sd[:, 0:n], ps, ps[:, 0:n], AF.Sqrt, extra_r=[epsc], scale=1.0 / D, bias=epsc[:, 0:1])
        rs = tf.get()
        P.op("dve", lambda e: e.reciprocal(out=rs[:, 0:n], in_=sd[:, 0:n]), [sd], [rs])
        for c in range(8):
            stt(dst[c], dst[c][:, 0:n], src[c], src[c][:, 0:n], pcol(l, gname, c), rs, rs[:, 0:n], ALU.mult, ALU.mult, extra_r=[pvs])
        return rs

    mraw = [Buf(big8[c][:, 0:NMEM]) for c in range(8)]
    for c in range(8):
        P.dma("sp", mraw[c].ap, memT_d[c * 128:(c + 1) * 128, :], reads=[Bdram_in], writes=[big8[c]])
        mraw[c] = big8[c]
    for l in range(2):
        mT = OG
        rms_to(l, "g_mem", mraw, mT, NMEM)
        for j in range(2):
            def cons(jj, ps, j=j):
                cp("act", kmT[l][j * 4 + jj], kmT[l][j * 4 + jj].ap, ps, ps[:, 0:NMEM])
            o, w = WCOL[f"kv{j}"]
            wb = wload(l, f"kv{j}")
            for jj in range(4):
                ps = ps_next()
                for kc in range(KC):
                    mm(ps, ps[:, 0:NMEM], wb, wb[:, kc, jj * 128:(jj + 1) * 128], mT[kc], mT[kc][:, 0:NMEM], start=(kc == 0), stop=(kc == KC - 1))
                cons(jj, ps)
        for j in range(2):
            wb = wload(l, f"kv{2 + j}")
            for mb in range(2):
                ps = ps_next()
                for kc in range(KC):
                    mm(ps, ps.ap, mT[kc], mT[kc][:, mb * 128:(mb + 1) * 128], wb, wb[:, kc, :], start=(kc == 0), stop=(kc == KC - 1))
                cp("act", vmt[l][mb], vmt[l][mb][:, j * 512:(j + 1) * 512], ps, ps.ap)

    chk("memkv")
    AR = sbuf([128, 4, 2, 128], BF16)
    Bbd = sbuf([128, 4, 2, 128], BF16)
    Kbd = sbuf([128, 4, 2, 128], BF16)
    P.op("pool", lambda e: e.memset(Bbd.ap, 0.0), [], [Bbd])
    P.op("pool", lambda e: e.memset(Kbd.ap, 0.0), [], [Kbd])
    NXr = Ring(4, [128, 2, 2, 128], BF16)
    Ar = Ring(4, [128, 2, 128], BF16)
    Arb = Ring(4, [128, 2, 128], BF16)
    Aak = Ring(2, [128, 2, 128], BF16)
    Ark = Ring(4, [128, 2, 128], BF16)
    Atok = [sbuf([128, 2, 128], BF16) for _ in range(2)]
    Vtok = [sbuf([128, 2, 128], BF16) for _ in range(4)]
    Utok = [sbuf([128, 2, 128], BF16) for _ in range(2)]
    for b_ in Atok + Vtok + Utok:
        P.op("pool", lambda e: e.memset(b_.ap, 0.0), [], [b_])
    BKtok = Ring(4, [128, 2, 128], BF16)
    ApT = Ring(4, [128, 128], BF16)
    Wp = Ring(2, [128, 128], BF16)
    stage = slots[0]
    Up4 = slots[11]
    cntr = dict(at=0, vt=0, ut=0, up=0)

    def scan_pre(l, c, qs2, ctx, E1, bpT, kpT, vb):
        for q in qs2:
            d = ctx[q] = {}
            d["at"] = Atok[cntr["at"] % 2]; cntr["at"] += 1
            d["vt"] = Vtok[cntr["vt"] % 4]; cntr["vt"] += 1
            d["upc"] = cntr["up"] % 4; cntr["up"] += 1
            d["bk"], d["arb"], d["aak"], d["ark"] = BKtok.get(), Arb.get(), Aak.get(), Ark.get()
            d["apT"], d["wp"] = ApT.get(), Wp.get()
        for q in qs2:
            d = ctx[q]
            qs = slice(q * 128, (q + 1) * 128)
            pst = ps_next()
            pv_ = pst.ap
            cp("dve", stage, stage[:, 0:128], AR, AR[:, q, 0, :])
            cp("dve", stage, stage[:, 128:256], bpT, bpT[:, qs])
            cp("dve", stage, stage[:, 256:384], kpT, kpT[:, qs])
            cp("dve", stage, stage[:, 384:512], vb, vb[:, qs])
            for i4 in range(4):
                tr(pst, pv_[:, i4 * 128:(i4 + 1) * 128], stage, stage[:, i4 * 128:(i4 + 1) * 128])
            at, vt, bk = d["at"], d["vt"], d["bk"]
            for h in range(2):
                cp("act", at, at[:, h, h * 64:(h + 1) * 64], pst, pv_[:, h * 64:(h + 1) * 64])
                cp("act", vt, vt[:, h, h * 64:(h + 1) * 64], pst, pv_[:, 384 + h * 64:384 + (h + 1) * 64])
            cp("act", bk, bk.ap, pst, pv_[:, 128:384].rearrange("p (a b) -> p a b", a=2))
            yield
        for q in qs2:
            d = ctx[q]
            ps1, ps2, ps3 = ps_next(), ps_next(), ps_next()
            arq = AR[:, q, :, :].rearrange("p a t -> p (a t)")
            for h in range(2):
                mm(ps1, ps1[:, h * 256:(h + 1) * 256], Bbd, Bbd[:, q, h, :], AR, arq)
                mm(ps2, ps2[:, h * 256:(h + 1) * 256], Kbd, Kbd[:, q, h, :], AR, arq)
            mm(ps3, ps3[:, 0:256], AR, AR[:, q, 0, :], Bbd, Bbd[:, q, :, :].rearrange("p h j -> p (h j)"))
            nx = NXr.get()
            arb, aak, ark, a0 = d["arb"], d["aak"], d["ark"], Ar.get()
            p1v = ps1.ap.rearrange("p (h a t) -> p h a t", h=2, a=2)
            p2v = ps2.ap.rearrange("p (h a t) -> p h a t", h=2, a=2)
            for h in range(2):
                tt("dve", nx, nx[:, h, 0, :], ps1, p1v[:, h, 0, :], cst_b, cst_b[:, 3, :], ALU.mult)
                tt("dve", arb, arb[:, h, :], ps1, p1v[:, h, 1, :], cst_b, cst_b[:, 4, :], ALU.mult)
                tt("dve", aak, aak[:, h, :], ps2, p2v[:, h, 0, :], cst_b, cst_b[:, 3, :], ALU.mult)
                tt("dve", ark, ark[:, h, :], ps2, p2v[:, h, 1, :], cst_b, cst_b[:, 4, :], ALU.mult)
            tt("dve", a0, a0.ap, ps3, ps3[:, 0:256].rearrange("p (h t) -> p h t", h=2), mSL2, mSL2.ap, ALU.mult)
            tt("dve", nx, nx[:, :, 1, :], nx, nx[:, :, 0, :], id2, id2.ap, ALU.add)
            d["nx"], d["A"] = nx, a0
            yield
        for lev in range(7):
            last = (lev == 6)
            pss = {}
            for q in qs2:
                d = ctx[q]
                nx, A_i = d["nx"], d["A"]
                psn = ps_next()
                for h in range(2):
                    if lev == 0:
                        mm(psn, psn[:, h * 256:h * 256 + 128], A_i, A_i[:, h, :], nx, nx[:, h, 0, :])
                    elif not last:
                        mm(psn, psn[:, h * 256:(h + 1) * 256], A_i, A_i[:, h, :], nx, nx[:, h, :, :].rearrange("p a t -> p (a t)"))
                    else:
                        mm(psn, psn[:, h * 256 + 128:(h + 1) * 256], A_i, A_i[:, h, :], nx, nx[:, h, 1, :])
                psa = None
                if not last:
                    psa = ps_next()
                    for h in range(2):
                        mm(psa, psa[:, h * 128:(h + 1) * 128], nx, nx[:, h, 0, :], A_i, A_i[:, h, :])
                pss[q] = (psn, psa)
            for q in qs2:
                d = ctx[q]
                nx = d["nx"]
                psn, psa = pss[q]
                pnv = psn.ap.rearrange("p (h a t) -> p h a t", h=2, a=2)
                nx2 = NXr.get()
                if lev == 0:
                    cp("act", nx2, nx2[:, :, 0, :], psn, pnv[:, :, 0, :])
                    cp("dve", nx2, nx2[:, :, 1, :], nx, nx[:, :, 1, :])
                else:
                    if not last:
                        cp("act", nx2, nx2[:, :, 0, :], psn, pnv[:, :, 0, :])
                    tt("dve", nx2, nx2[:, :, 1, :], psn, pnv[:, :, 1, :], nx, nx[:, :, 1, :], ALU.add)
                if not last:
                    a2 = Ar.get()
                    cp("act", a2, a2.ap, psa, psa[:, 0:256].rearrange("p (h t) -> p h t", h=2))
                    d["A"] = a2
                d["nx"] = nx2
            yield
        for q in qs2:
            d = ctx[q]
            nx, at, vt, aak, apT, wp = d["nx"], d["at"], d["vt"], d["aak"], d["apT"], d["wp"]
            psw = ps_next()
            for h in range(2):
                mm(psw, psw[:, 0:128], at, at[:, h, :], nx, nx[:, h, 1, :], start=(h == 0), stop=(h == 1))
            for h in range(2):
                mm(psw, psw[:, 128 + h * 64:128 + (h + 1) * 64], aak, aak[:, h, :], vt, vt[:, h, h * 64:(h + 1) * 64])
            cp("act", apT, apT.ap, psw, psw[:, 0:128])
            cp("act", wp, wp.ap, psw, psw[:, 128:256])
        yield
        for q in qs2:
            d = ctx[q]
            nx, wp = d["nx"], d["wp"]
            psu = ps_next()
            for h in range(2):
                mm(psu, psu[:, h * 64:(h + 1) * 64], nx, nx[:, h, 1, :], wp, wp[:, h * 64:(h + 1) * 64])
            uc_ = d["upc"]
            cp("act", Up4, Up4[:, uc_ * 128:(uc_ + 1) * 128], psu, psu[:, 0:128])
        yield

    def scan_seq(l, c, qs2, ctx, E1, yT):
        sbd, sfd = Sb[l][c], Sf[l][c]
        sbd2 = sbd.ap.rearrange("p h v -> p (h v)")
        for q in qs2:
            d = ctx[q]
            qs = slice(q * 128, (q + 1) * 128)
            vt, bk, arb, ark, apT, uc_ = d["vt"], d["bk"], d["arb"], d["ark"], d["apT"], d["upc"]
            ut = Utok[cntr["ut"] % 2]; cntr["ut"] += 1
            ps_u = ps_next()
            mm(ps_u, ps_u[:, 0:128], apT, apT.ap, sbd, sbd2)
            for h in range(2):
                tt("dve", ut, ut[:, h, h * 64:(h + 1) * 64], ps_u, ps_u[:, h * 64:(h + 1) * 64],
                   Up4, Up4[:, uc_ * 128 + h * 64:uc_ * 128 + (h + 1) * 64], ALU.add)
            yield
            ps_y = ps_next()
            mm(ps_y, ps_y[:, 0:128], sbd, sbd2, AR, AR[:, q, 1, :], start=True, stop=False)
            for h in range(2):
                mm(ps_y, ps_y[:, 0:128], ut, ut[:, h, :], arb, arb[:, h, :], start=False, stop=False)
                mm(ps_y, ps_y[:, 0:128], vt, vt[:, h, :], ark, ark[:, h, :], start=False, stop=(h == 1))
            ps_s = ps_next()
            mm(ps_s, ps_s[:, 0:256], bk, bk[:, 0, :], ut, ut.ap.rearrange("p h v -> p (h v)"), start=True, stop=False)
            mm(ps_s, ps_s[:, 0:256], bk, bk[:, 1, :], vt, vt.ap.rearrange("p h v -> p (h v)"), start=False, stop=True)
            pc = E1[:, q * 128 + 127:q * 128 + 128]
            for h in range(2):
                hp = slice(h * 64, (h + 1) * 64)
                stt(sfd, sfd[hp, :], sfd, sfd[hp, :], pc[hp, :], ps_s, ps_s[hp, h * 128 + h * 64:h * 128 + (h + 1) * 64],
                    ALU.mult, ALU.add, extra_r=[E1])
            for h in range(2):
                hp = slice(h * 64, (h + 1) * 64)
                cp("act", sbd, sbd[hp, h, :], sfd, sfd[hp, :])
            cp("act", yT, yT[:, qs], ps_y, ps_y[:, 0:128])
            yield

    def scan_all(l, c, E1, bpT, kpT, vb, yT):
        ctxA, ctxB = {}, {}
        for _ in scan_pre(l, c, (0, 1), ctxA, E1, bpT, kpT, vb):
            pass
        gB = scan_pre(l, c, (2, 3), ctxB, E1, bpT, kpT, vb)
        gA = scan_seq(l, c, (0, 1), ctxA, E1, yT)
        aliveA = aliveB = True
        while aliveA or aliveB:
            if aliveB:
                try:
                    next(gB)
                except StopIteration:
                    aliveB = False
            if aliveA:
                try:
                    next(gA)
                except StopIteration:
                    aliveA = False
        for _ in scan_seq(l, c, (2, 3), ctxB, E1, yT):
            pass

    for it in range(NT):
        tsl = slice(it * T, (it + 1) * T)
        for c in range(8):
            P.dma("sp", xT[c].ap, xT_d[c * 128:(c + 1) * 128, tsl], reads=[Bdram_in], writes=[xT[c]])
        for l in range(2):
            rms_to(l, "g_norm", xT, hT, T)
            chk(f"rms{l}")
            hrhs = lambda j: hT
            car = carry[l]

            def shiftmix(ps, mi, npart=128, A=None):
                A = A or slots[11]
                pp = slice(0, npart)
                mu_c, omu_c = pcol(l, "mu", mi), pcol(l, "omu", mi)
                act(A, A[pp, :], ps, ps[pp, :], AF.Identity, extra_r=[pvs], scale=omu_c[pp, :])
                stt(A, A[pp, 1:T], ps, ps[pp, 0:T - 1], mu_c[pp, :], A, A[pp, 1:T], ALU.mult, ALU.add, extra_r=[pvs])
                stt(A, A[pp, 0:1], car, car[pp, mi:mi + 1], mu_c[pp, :], A, A[pp, 0:1], ALU.mult, ALU.add, extra_r=[pvs])
                cp("act", car, car[pp, mi:mi + 1], ps, ps[pp, T - 1:T])
                return A

            lob, vlo = lob_, vlo_

            def cons_lora(j, ps):
                if j == 0:
                    lo = shiftmix(ps, 24)
                    act(lob, lob[0:64, :], lo, lo[0:64, :], AF.Tanh)
                    cp("dve", lob, lob[64:128, :], lo, lo[64:128, :])
                elif l == 1:
                    v_ = shiftmix(ps, 25, 32)
                    cp("dve", vlo, vlo[0:32, :], v_, v_[0:32, :])
            proj(l, "lora", lambda j: hT if (j == 0 or l == 1) else None, cons_lora)

            chk(f"lora{l}")
            for c in range(8):
                got = {}

                def cons_rkvg(j, ps):
                    if j < 3:
                        got[j] = shiftmix(ps, j * 8 + c, A=slots[j])
                    else:
                        g = slots[3]
                        act(g, g.ap, ps, ps.ap, AF.Silu)
                        got[3] = g
                proj(l, f"rkvg{c}", hrhs, cons_rkvg)
                r_, k_, v_, gs = got[0], got[1], got[2], got[3]
                cs = slice(c * 128, (c + 1) * 128)
                psd, psa = ps_next(), ps_next()
                mm(psd, psd.ap, lor[l], lor[l][0:64, 0, cs], lob, lob[0:64, :])
                mm(psa, psa.ap, lor[l], lor[l][64:128, 0, cs], lob, lob[64:128, :])
                sgd, a_ = slots[4], slots[5]
                act(sgd, sgd.ap, psd, psd.ap, AF.Sigmoid, extra_r=[pvs], bias=pcol(l, "w0", c))
                act(a_, a_.ap, psa, psa.ap, AF.Sigmoid, extra_r=[pvs], bias=pcol(l, "a0", c))
                if l == 1:
                    psv = ps_next()
                    mm(psv, psv.ap, lor[l], lor[l][0:32, 1, cs], vlo, vlo[0:32, :])
                    gv = slots[7]
                    act(gv, gv.ap, psv, psv.ap, AF.Sigmoid, extra_r=[pvs], bias=pcol(l, "v0", c))
                    dd = slots[8]
                    tt("dve", dd, dd.ap, vf[c], vf[c].ap, v_, v_.ap, ALU.subtract)
                    tt("dve", dd, dd.ap, dd, dd.ap, gv, gv.ap, ALU.mult)
                    tt("dve", v_, v_.ap, v_, v_.ap, dd, dd.ap, ALU.add)
                else:
                    cp("act", vf[c], vf[c].ap, v_, v_.ap)
                dump(f"r{l}_{c}", r_)
                dump(f"k{l}_{c}", k_)
                dump(f"v{l}_{c}", v_)
                dump(f"a{l}_{c}", a_)
                kkr = slots[6]
                ts("dve", kkr, kkr.ap, k_, k_.ap, pcol(l, "k_k", c), None, ALU.mult, extra_r=[pvs])
                sq = tb.get()
                act(sq, sq.ap, kkr, kkr.ap, AF.Square)
                psn = ps_next()
                mm(psn, psn.ap, cst_b, bones, sq, sq.ap)
                nrm = slots[7]
                act(nrm, nrm.ap, psn, psn.ap, AF.Sqrt)
                ts("dve", nrm, nrm.ap, nrm, nrm.ap, 1e-12, None, ALU.max)
                P.op("dve", lambda e: e.reciprocal(out=nrm.ap, in_=nrm.ap), [nrm], [nrm])
                tt("dve", kkr, kkr.ap, kkr, kkr.ap, nrm, nrm.ap, ALU.mult)
                kk = kkr
                f_ = slots[7]
                ts("dve", f_, f_.ap, a_, a_.ap, pcol(l, "k_a", c), pcol(l, "omka", c), ALU.mult, ALU.add, extra_r=[pvs])
                tt("dve", k_, k_.ap, k_, k_.ap, f_, f_.ap, ALU.mult)
                k2 = k_
                rk = tb.get()
                stt(rk, rk.ap, r_, r_.ap, pcol(l, "r_k", c), k2, k2.ap, ALU.mult, ALU.mult, extra_r=[pvs])
                psb = ps_next()
                mm(psb, psb.ap, cst_b, bones, rk, rk.ap)
                bon = slots[8]
                tt("dve", bon, bon.ap, psb, psb.ap, v_, v_.ap, ALU.mult)
                cum = slots[7]
                P.op("dve", lambda e: e.tensor_tensor_scan(out=cum.ap, data0=rmask.ap, data1=sgd.ap, initial=0.0,
                                                           op0=ALU.mult, op1=ALU.add), [rmask, sgd], [cum])
                E1, E2, E3 = slots[9], slots[10], slots[11]
                act(E1, E1.ap, cum, cum.ap, AF.Exp, scale=-C0)
                act(E2, E2.ap, cum, cum.ap, AF.Exp, scale=C0)
                tt("dve", sgd, sgd.ap, cum, cum.ap, sgd, sgd.ap, ALU.subtract)
                act(E3, E3.ap, sgd, sgd.ap, AF.Exp, scale=-C0)
                dump(f"E1{l}_{c}", E1)
                tt("dve", AR, AR[:, :, 1, :], r_, r_.ap.rearrange("p (q t) -> p q t", q=4), E1, E1.ap.rearrange("p (q t) -> p q t", q=4), ALU.mult)
                stt(AR, AR[:, :, 0, :], kk, kk.ap.rearrange("p (q t) -> p q t", q=4), -1.0, E3, E3.ap.rearrange("p (q t) -> p q t", q=4), ALU.mult, ALU.mult)
                bt, kt = slots[0], slots[11]
                tt("dve", bt, bt.ap, kk, kk.ap, a_, a_.ap, ALU.mult)
                tt("dve", bt, bt.ap, bt, bt.ap, E2, E2.ap, ALU.mult)
                tt("dve", kt, kt.ap, k2, k2.ap, E2, E2.ap, ALU.mult)
                for h in range(2):
                    hp = slice(h * 64, (h + 1) * 64)
                    cp("act", Bbd, Bbd[hp, :, h, :], bt, bt[hp, :].rearrange("p (q t) -> p q t", q=4))
                    cp("dve", Kbd, Kbd[hp, :, h, :], kt, kt[hp, :].rearrange("p (q t) -> p q t", q=4))
                bpT, kpT, vb = tb.get(), tb.get(), tb.get()
                for q in range(4):
                    qs = slice(q * 128, (q + 1) * 128)
                    pc = E1[:, q * 128 + 127:q * 128 + 128]
                    act(bpT, bpT[:, qs], bt, bt[:, qs], AF.Identity, extra_r=[E1], scale=pc)
                    act(kpT, kpT[:, qs], kt, kt[:, qs], AF.Identity, extra_r=[E1], scale=pc)
                cp("act", vb, vb.ap, v_, v_.ap)
                chk(f"prep{l}_{c}")
                yT = slots[1]
                scan_all(l, c, E1, bpT, kpT, vb, yT)
                dump(f"y{l}_{c}", yT)
                yb_, ysq = tb.get(), tb.get()
                cp("dve", yb_, yb_.ap, yT, yT.ap)
                act(ysq, ysq.ap, yT, yT.ap, AF.Square)
                p1, p2 = ps_next(), ps_next()
                mm(p1, p1.ap, cst_b, bones, yb_, yb_.ap)
                mm(p2, p2.ap, cst_b, bones, ysq, ysq.ap)
                mean, var = slots[5], slots[6]
                act(mean, mean.ap, p1, p1.ap, AF.Identity, scale=1.0 / 64)
                tt("dve", var, var.ap, mean, mean.ap, mean, mean.ap, ALU.mult)
                stt(var, var.ap, p2, p2.ap, 1.0 / 64, var, var.ap, ALU.mult, ALU.subtract)
                act(var, var.ap, var, var.ap, AF.Sqrt, extra_r=[epsc], bias=epsc[:, 1:2])
                P.op("dve", lambda e: e.reciprocal(out=var.ap, in_=var.ap), [var], [var])
                tt("dve", yT, yT.ap, yT, yT.ap, mean, mean.ap, ALU.subtract)
                tt("dve", yT, yT.ap, yT, yT.ap, var, var.ap, ALU.mult)
                act(yT, yT.ap, yT, yT.ap, AF.Identity, extra_r=[pvs], scale=pcol(l, "gn_g", c), bias=pcol(l, "gn_b", c))
                tt("dve", yT, yT.ap, yT, yT.ap, bon, bon.ap, ALU.add)
                tt("dve", OG[c], OG[c].ap, yT, yT.ap, gs, gs.ap, ALU.mult)
                dump(f"og{l}_{c}", OG[c])

            def branch_out(pn, first, bias_name=None):
                for j in range(4):
                    tmpy = {}

                    def cons(jj, ps, j=j):
                        if jj < 2:
                            t_ = tf.get()
                            if bias_name is None:
                                cp("act", t_, t_.ap, ps, ps.ap)
                            else:
                                act(t_, t_.ap, ps, ps.ap, AF.Identity, extra_r=[pvs], bias=pcol(l, bias_name, 2 * j + jj))
                            tmpy[jj] = t_
                        else:
                            cidx = 2 * j + (jj - 2)
                            sg = tf.get()
                            act(sg, sg.ap, ps, ps.ap, AF.Sigmoid)
                            yb = tmpy[jj - 2]
                            if first:
                                tt("dve", yacc[cidx], yacc[cidx].ap, sg, sg.ap, yb, yb.ap, ALU.mult)
                            else:
                                tt("dve", sg, sg.ap, sg, sg.ap, yb, yb.ap, ALU.mult)
                                tt("dve", yacc[cidx], yacc[cidx].ap, yacc[cidx], yacc[cidx].ap, sg, sg.ap, ALU.add)
                    proj(l, f"{pn}{j}", lambda jj: OG if jj < 2 else hT, cons)

            chk(f"rwkv{l}")
            branch_out("pr", True)
            chk(f"pr{l}")
            dump(f"yacc0_{l}", yacc[0])

            uc = big8
            s1, s2 = pin[0], pin[1]
            dg = dg_keep
            for j in range(4):
                def cons_glu(jj, ps, j=j):
                    c = 2 * j + jj // 2
                    if jj % 2 == 0:
                        cons_glu.pa = ps
                        return
                    gb = tf.get()
                    act(gb, gb.ap, ps, ps.ap, AF.Sigmoid, extra_r=[pvs], bias=pcol(l, "b_glu", 8 + c))
                    pa = cons_glu.pa
                    u = ubr.get()
                    cp("act", u, u[:, 0:30], halo[l], halo[l][:, c, :])
                    stt(u, u[:, 30:30 + T], pa, pa.ap, pcol(l, "b_glu", c), gb, gb.ap, ALU.add, ALU.mult, extra_r=[pvs])
                    for tp in range(31):
                        ts("pool", dg, dg[:, tp, :], cst_b, ident, pcol(l, "w_dw", c * 31 + tp), None, ALU.mult, extra_r=[pvs])
                    pc_ = ps_next()
                    for tp in range(31):
                        mm(pc_, pc_.ap, dg, dg[:, tp, :], u, u[:, tp:tp + T], start=(tp == 0), stop=(tp == 30))
                    act(uc[c], uc[c].ap, pc_, pc_.ap, AF.Identity, extra_r=[pvs], bias=pcol(l, "b_dw", c))
                    cp("act", halo[l], halo[l][:, c, :], u, u[:, T:T + 30])
                    ucb, ucs = tb.get(), tb.get()
                    cp("dve", ucb, ucb.ap, uc[c], uc[c].ap)
                    act(ucs, ucs.ap, uc[c], uc[c].ap, AF.Square)
                    mm(s1, s1.ap, cst_b, ones, ucb, ucb.ap, start=(c == 0), stop=(c == 7))
                    mm(s2, s2.ap, cst_b, ones, ucs, ucs.ap, start=(c == 0), stop=(c == 7))
                proj(l, f"glu{j}", hrhs, cons_glu)
            mean, var = slots[10], slots[11]
            act(mean, mean.ap, s1, s1.ap, AF.Identity, scale=1.0 / D)
            tt("dve", var, var.ap, mean, mean.ap, mean, mean.ap, ALU.mult)
            stt(var, var.ap, s2, s2.ap, 1.0 / D, var, var.ap, ALU.mult, ALU.subtract)
            act(var, var.ap, var, var.ap, AF.Sqrt, extra_r=[epsc], bias=epsc[:, 2:3])
            P.op("dve", lambda e: e.reciprocal(out=var.ap, in_=var.ap), [var], [var])
            dump(f"uc{l}_0", uc[0])
            for j in range(2):
                def cons_cg(jj, ps, j=j):
                    c = 4 * j + jj
                    cg = tf.get()
                    act(cg, cg.ap, ps, ps.ap, AF.Silu)
                    t_ = uc[c]
                    tt("dve", t_, t_.ap, t_, t_.ap, mean, mean.ap, ALU.subtract)
                    tt("dve", t_, t_.ap, t_, t_.ap, var, var.ap, ALU.mult)
                    act(t_, t_.ap, t_, t_.ap, AF.Identity, extra_r=[pvs], scale=pcol(l, "ln_g", c), bias=pcol(l, "ln_b", c))
                    act(t_, t_.ap, t_, t_.ap, AF.Silu)
                    tt("dve", OG[c], OG[c].ap, t_, t_.ap, cg, cg.ap, ALU.mult)
                proj(l, f"cg{j}", hrhs, cons_cg)
            dump(f"ug{l}_0", OG[0])
            chk(f"conv{l}")
            branch_out("pc", False, "b_pc")
            chk(f"pc{l}")
            dump(f"yacc1_{l}", yacc[0])

            qT = OG
            for j in range(2):
                def cons_q(jj, ps, j=j):
                    c = 4 * j + jj
                    act(qT[c], qT[c].ap, ps, ps.ap, AF.Identity, scale=1.0 / 16.0)
                proj(l, f"q{j}", hrhs, cons_q)
            att = big8
            prT = prT_keep
            small = small_keep
            for hm in range(4):
                pt = prT[hm % 2]
                for sbk in range(4):
                    ss = slice(sbk * 128, (sbk + 1) * 128)
                    psc = ps_next()
                    for dc in range(2):
                        mm(psc, psc[:, 0:NMEM], qT[2 * hm + dc], qT[2 * hm + dc][:, ss], kmT[l][2 * hm + dc], kmT[l][2 * hm + dc].ap,
                           start=(dc == 0), stop=(dc == 1))
                    P.op("dve", lambda e: e.tensor_reduce(out=small[:, 0:1], in_=psc[:, 0:NMEM], axis=AX.X, op=ALU.max), [psc], [small])
                    ts("dve", small, small[:, 1:2], small, small[:, 0:1], -1.0, None, ALU.mult)
                    ex = tf.get()
                    P.op("act", lambda e: e.activation(out=ex[:, 0:NMEM], in_=psc[:, 0:NMEM], func=AF.Exp, bias=small[:, 1:2],
                                                       accum_out=small[:, 2:3]), [psc, small], [ex, small])
                    P.op("dve", lambda e: e.reciprocal(out=small[:, 3:4], in_=small[:, 2:3]), [small], [small])
                    pb = tf.get()
                    ts("dve", pb, pb[:, 0:NMEM], ex, ex[:, 0:NMEM], small[:, 3:4], None, ALU.mult, extra_r=[small])
                    ptp = ps_next()
                    pv_ = ptp.ap
                    for mb in range(2):
                        tr(ptp, pv_[:, mb * 128:(mb + 1) * 128], pb, pb[:, mb * 128:(mb + 1) * 128])
                    for mb in range(2):
                        cp("act", pt[mb], pt[mb][:, ss], ptp, pv_[:, mb * 128:(mb + 1) * 128])
                for dc in range(2):
                    c = 2 * hm + dc
                    pa_ = ps_next()
                    for mb in range(2):
                        mm(pa_, pa_.ap, vmt[l][mb], vmt[l][mb][:, c * 128:(c + 1) * 128], pt[mb], pt[mb].ap, start=(mb == 0), stop=(mb == 1))
                    cp("act", att[c], att[c].ap, pa_, pa_.ap)
            dump(f"att{l}_0", att[0])
            for j in range(2):
                def cons_mg(jj, ps, j=j):
                    c = 4 * j + jj
                    mg = tf.get()
                    act(mg, mg.ap, ps, ps.ap, AF.Silu)
                    tt("dve", OG[c], OG[c].ap, att[c], att[c].ap, mg, mg.ap, ALU.mult)
                proj(l, f"mg{j}", hrhs, cons_mg)
            chk(f"mem{l}")
            branch_out("pm", False)
            chk(f"pm{l}")
            dump(f"yacc2_{l}", yacc[0])

            for c in range(8):
                cp("act", OG[c], OG[c].ap, yacc[c], yacc[c].ap)
            for j in range(2):
                def cons_o(jj, ps, j=j):
                    c = 4 * j + jj
                    tt("dve", xT[c], xT[c].ap, xT[c], xT[c].ap, ps, ps.ap, ALU.add)
                proj(l, f"wo{j}", lambda jj: OG, cons_o)
            dump(f"x{l}_0", xT[0])

        ps = pin[0]
        for c in range(8):
            sq = tb.get()
            act(sq, sq.ap, xT[c], xT[c].ap, AF.Square)
            mm(ps, ps.ap, cst_b, ones, sq, sq.ap, start=(c == 0), stop=(c == 7))
        sd, rs = tf.get(), tf.get()
        act(sd, sd.ap, ps, ps.ap, AF.Sqrt, extra_r=[epsc], scale=1.0 / D, bias=epsc[:, 0:1])
        P.op("dve", lambda e: e.reciprocal(out=rs.ap, in_=sd.ap), [sd], [rs])
        for c in range(8):
            o_ = tf.get()
            stt(o_, o_.ap, xT[c], xT[c].ap, pcol(0, "g_final", c), rs, rs.ap, ALU.mult, ALU.mult, extra_r=[pvs])
            P.dma("sp", outT_d[c * 128:(c + 1) * 128, tsl], o_.ap, reads=[o_], writes=[Bout])


_CACHE = {}


def kernel(**inp):
    inp = {k: np.asarray(v) for k, v in inp.items()}
    pv, wbig, lora, cst, rm = host_prep(inp)
    x, mem = inp["x"], inp["mem"]
    B = x.shape[0]
    nc = bass.Bass("TRN2", target_bir_lowering=False)
    build(nc)
    in_maps = []
    for b in range(B):
        in_maps.append({"xT": np.ascontiguousarray(x[b].T), "memT": np.ascontiguousarray(mem[b].T),
                        "pv": pv, "wbig": wbig, "lora": lora, "cst": cst, "rm": rm})
    res = run_bass_kernel_spmd(nc, in_maps, core_ids=list(range(B)))
    out = np.stack([np.ascontiguousarray(r["outT"].T) for r in res.results], axis=0)
    return out.astype(np.float32)
```

```python
import numpy as np
import concourse.bass as bass
import concourse.mybir as mybir
from concourse.bass_utils import run_bass_kernel_spmd

F32 = mybir.dt.float32
BF16 = mybir.dt.bfloat16
AF = mybir.ActivationFunctionType
ALU = mybir.AluOpType
AX = mybir.AxisListType
NDS = 24

D = 1024
SEQ = 4096
T = 512
NMEM = 256
KC = 8
C0 = float(np.exp(-0.5))


class Buf:
    __slots__ = ("ap", "w", "r")

    def __init__(self, ap):
        self.ap = ap
        self.w = None
        self.r = {}

    def __getitem__(self, k):
        return self.ap[k]


class Prog:
    def __init__(self, nc):
        self.nc = nc
        self.eng = dict(pe=nc.tensor, dve=nc.vector, act=nc.scalar, pool=nc.gpsimd, sp=nc.sync)
        self.esem = {k: nc.alloc_semaphore("es_" + k) for k in self.eng}
        self.ecnt = {k: 0 for k in self.eng}
        self.seen = {k: {} for k in self.eng}
        self.dsem, self.dtgt, self.dnext = {}, {}, {}
        self.ninst = 0

    def _wait(self, e, ev):
        sem, key, val = ev
        if key == ("e", e) and e == "pe":
            return
        if self.seen[e].get(key, 0) >= val:
            return
        self.eng[e].wait_ge(sem, val)
        self.seen[e][key] = val

    def _deps(self, e, reads, writes):
        for b in reads:
            if b.w is not None:
                self._wait(e, b.w)
        for b in writes:
            if b.w is not None:
                self._wait(e, b.w)
            for ev in b.r.values():
                self._wait(e, ev)

    def _record(self, ev, reads, writes):
        for b in reads:
            b.r[ev[1]] = ev
        for b in writes:
            b.w = ev
            b.r = {}

    def op(self, e, fn, reads=(), writes=()):
        self._deps(e, reads, writes)
        inst = fn(self.eng[e])
        self.ecnt[e] += 1
        inst.then_inc(self.esem[e], 1)
        self._record((self.esem[e], ("e", e), self.ecnt[e]), reads, writes)
        self.ninst += 1

    def dma(self, q, out_ap, in_ap, reads=(), writes=(), **kw):
        if q not in self.dsem:
            self.dsem[q] = [self.nc.alloc_semaphore(f"ds_{q}{i}") for i in range(NDS)]
            self.dtgt[q] = [0] * NDS
            self.dnext[q] = 0
        j = self.dnext[q]
        self.dnext[q] = (j + 1) % NDS
        key = ("d", q, j)
        if self.dtgt[q][j] > 0:
            self._wait(q, (self.dsem[q][j], key, self.dtgt[q][j]))
        self._deps(q, reads, writes)
        inst = self.eng[q].dma_start(out=out_ap, in_=in_ap, **kw)
        self.dtgt[q][j] += 16
        inst.then_inc(self.dsem[q][j], 16)
        self._record((self.dsem[q][j], key, self.dtgt[q][j]), reads, writes)
        self.ninst += 1

    def finish(self, e="sp"):
        for q in self.dsem:
            for j in range(NDS):
                if self.dtgt[q][j] > 0:
                    self._wait(e, (self.dsem[q][j], ("d", q, j), self.dtgt[q][j]))
        for k in self.eng:
            if k != e and self.ecnt[k] > 0:
                self._wait(e, (self.esem[k], ("e", k), self.ecnt[k]))


PV = {}
_o = 0
for _n, _w in [("g_norm", 8), ("mu", 26), ("w0", 8), ("a0", 8), ("k_k", 8), ("k_a", 8), ("r_k", 8),
               ("gn_g", 8), ("gn_b", 8), ("v0", 8), ("b_glu", 16), ("w_dw", 248), ("b_dw", 8),
               ("ln_g", 8), ("ln_b", 8), ("b_pc", 8), ("g_mem", 8), ("g_final", 8), ("omu", 26), ("omka", 8)]:
    PV[_n] = _o
    _o += _w
NPV = _o

WCOL = {}
_o = 0
for _n, _w in [("lora", 256)] + [(f"rkvg{c}", 512) for c in range(8)] + \
        [(f"pr{j}", 512) for j in range(4)] + [(f"glu{j}", 512) for j in range(4)] + \
        [(f"cg{j}", 512) for j in range(2)] + [(f"pc{j}", 512) for j in range(4)] + \
        [(f"q{j}", 512) for j in range(2)] + [(f"mg{j}", 512) for j in range(2)] + \
        [(f"pm{j}", 512) for j in range(4)] + [(f"wo{j}", 512) for j in range(2)] + \
        [(f"kv{j}", 512) for j in range(4)]:
    WCOL[_n] = (_o, _w)
    _o += _w
TOTC = _o


def _fm(v):
    return np.ascontiguousarray(v.reshape(8, 128).T)


def host_prep(inp):
    f = np.float32
    L = 2
    pv = np.zeros((L, 128, NPV), f)
    wbig = np.zeros((L, 128, KC, TOTC), f)
    lora = np.zeros((L, 128, 2, 1024), f)
    for l in range(L):
        def put(name, arr):
            pv[l][:, PV[name]:PV[name] + arr.shape[1]] = arr
        put("g_norm", _fm(inp["g_norm"][l]))
        mu = inp["mu_shift"][l]
        mucols = np.zeros((128, 26), f)
        mucols[:, 0:25] = mu.reshape(25, 128).T
        if l >= 1:
            mucols[0:32, 25] = inp["mu_vres"][l - 1]
        put("mu", mucols)
        for n in ["w0", "a0", "k_k", "k_a", "gn_g", "gn_b", "b_dw", "ln_g", "ln_b", "g_mem"]:
            src = {"g_mem": "g_mem_norm"}.get(n, n)
            put(n, _fm(inp[src][l]))
        put("r_k", _fm(inp["r_k"][l].reshape(-1)))
        put("b_pc", _fm(inp["b_proj_conv"][l]))
        if l >= 1:
            put("v0", _fm(inp["v0"][l - 1]))
        put("b_glu", np.ascontiguousarray(inp["b_glu"][l].reshape(16, 128).T))
        wd = inp["w_dw"][l]
        put("w_dw", np.ascontiguousarray(wd.reshape(31, 8, 128).transpose(2, 1, 0).reshape(128, 248)))
        put("g_final", _fm(inp["g_final"]))
        w_in = inp["w_in"][l]
        Wc = np.zeros((D, TOTC), f)

        def setc(name, off, arr):
            o, w = WCOL[name]
            Wc[:, o + off:o + off + arr.shape[1]] = arr
        setc("lora", 0, w_in[:, 3072:3200])
        if l >= 1:
            setc("lora", 128, inp["w_vres_down"][l - 1])
        for c in range(8):
            setc(f"rkvg{c}", 0, w_in[:, c * 128:(c + 1) * 128])
            setc(f"rkvg{c}", 128, w_in[:, 1024 + c * 128:1024 + (c + 1) * 128])
            setc(f"rkvg{c}", 256, w_in[:, 2048 + c * 128:2048 + (c + 1) * 128])
            setc(f"rkvg{c}", 384, w_in[:, 3200 + c * 128:3200 + (c + 1) * 128])
        for br, (pn, wp) in enumerate([("pr", inp["w_proj_rwkv"][l]), ("pc", inp["w_proj_conv"][l]),
                                       ("pm", inp["w_proj_mem"][l])]):
            for j in range(4):
                setc(f"{pn}{j}", 0, wp[:, j * 256:(j + 1) * 256])
                mo = 9344 + br * 1024 + j * 256
                setc(f"{pn}{j}", 256, w_in[:, mo:mo + 256])
        for j in range(4):
            for i in range(2):
                c = 2 * j + i
                setc(f"glu{j}", i * 256, w_in[:, 4224 + c * 128:4224 + (c + 1) * 128])
                setc(f"glu{j}", i * 256 + 128, w_in[:, 5248 + c * 128:5248 + (c + 1) * 128])
        for j in range(2):
            setc(f"cg{j}", 0, w_in[:, 6272 + j * 512:6272 + (j + 1) * 512])
            setc(f"q{j}", 0, w_in[:, 7296 + j * 512:7296 + (j + 1) * 512])
            setc(f"mg{j}", 0, w_in[:, 8320 + j * 512:8320 + (j + 1) * 512])
            setc(f"wo{j}", 0, inp["w_out"][l][:, j * 512:(j + 1) * 512])
        for j in range(4):
            setc(f"kv{j}", 0, inp["w_mem_kv"][l][:, j * 512:(j + 1) * 512])
        wbig[l] = Wc.reshape(KC, 128, TOTC).transpose(1, 0, 2)
        lora[l][0:64, 0] = inp["w_decay_up"][l]
        lora[l][64:128, 0] = inp["w_aaa_up"][l]
        if l >= 1:
            lora[l][0:32, 1] = inp["w_vres_up"][l - 1]
    cst = np.zeros((128, 8, 128), f)
    i = np.arange(128)
    cst[:, 0] = np.eye(128)
    cst[:, 1] = 1.0
    cst[:, 2] = (i[:, None] // 64 == i[None, :] // 64)
    cst[:, 3] = (i[:, None] < i[None, :])
    cst[:, 4] = (i[:, None] <= i[None, :])
    cst[:, 5] = (i[:, None] > i[None, :])
    rm = np.ones((128, 512), f)
    rm[:, 0::128] = 0.0
    return pv, wbig, lora, cst, rm


class _Stop(Exception):
    pass


def build(nc, NT=SEQ // T, dbg_names=(), stop_after=None):
    P = Prog(nc)
    try:
        _build(nc, P, NT, dbg_names, stop_after)
    except _Stop:
        pass
    P.finish()
    return P


def _build(nc, P, NT, dbg_names, stop_after):
    def chk(tag):
        if tag == stop_after:
            raise _Stop()
    dt = nc.dram_tensor
    xT_d = dt("xT", [D, SEQ], F32, kind="ExternalInput").ap()
    memT_d = dt("memT", [D, NMEM], F32, kind="ExternalInput").ap()
    pv_d = dt("pv", [2, 128, NPV], F32, kind="ExternalInput").ap()
    wbig_d = dt("wbig", [2, 128, KC, TOTC], F32, kind="ExternalInput").ap()
    lora_d = dt("lora", [2, 128, 2, 1024], F32, kind="ExternalInput").ap()
    cst_d = dt("cst", [128, 8, 128], F32, kind="ExternalInput").ap()
    rm_d = dt("rm", [128, 512], F32, kind="ExternalInput").ap()
    outT_d = dt("outT", [D, SEQ], F32, kind="ExternalOutput").ap()
    dbg_d = None
    if dbg_names:
        dbg_d = dt("dbg", [len(dbg_names), 128, 512], F32, kind="ExternalOutput").ap()
    Bdram_in = Buf(None)
    Bout = Buf(None)
    cnt = [0]

    def sb(shape, dtype, name=None):
        cnt[0] += 1
        return nc.alloc_sbuf_tensor(name or f"t{cnt[0]}", list(shape), dtype)

    def sbuf(shape, dtype):
        return Buf(sb(shape, dtype).ap())

    big8 = [sbuf([128, T], F32) for _ in range(8)]
    cst_f = Buf(big8[0].ap.rearrange("p (a b) -> p a b", a=4))
    cst_f2 = Buf(big8[1].ap.rearrange("p (a b) -> p a b", a=4))
    P.dma("sp", cst_f.ap, cst_d[:, 0:4, :], reads=[Bdram_in], writes=[cst_f, big8[0]])
    P.dma("sp", cst_f2.ap, cst_d[:, 4:8, :], reads=[Bdram_in], writes=[cst_f2, big8[1]])
    cst_b = sbuf([128, 8, 128], BF16)
    P.op("dve", lambda e: e.tensor_copy(out=cst_b[:, 0:4, :], in_=cst_f.ap), [cst_f, big8[0]], [cst_b])
    P.op("dve", lambda e: e.tensor_copy(out=cst_b[:, 4:8, :], in_=cst_f2.ap), [cst_f2, big8[1]], [cst_b])
    ident, ones, bones = cst_b[:, 0, :], cst_b[:, 1, :], cst_b[:, 2, :]
    identf = sbuf([128, 128], F32)
    P.op("dve", lambda e: e.tensor_copy(out=identf.ap, in_=cst_f[:, 0, :]), [cst_f, big8[0]], [identf])
    m12 = Buf(cst_b[:, 3:5, :])
    m12.w = None
    mSL2 = sbuf([128, 2, 128], BF16)
    id2 = sbuf([128, 2, 128], BF16)
    for h in range(2):
        P.op("dve", lambda e: e.tensor_copy(out=mSL2[:, h, :], in_=cst_b[:, 5, :]), [cst_b], [mSL2])
        P.op("dve", lambda e: e.tensor_copy(out=id2[:, h, :], in_=cst_b[:, 0, :]), [cst_b], [id2])
    rmf = Buf(big8[2].ap)
    P.dma("sp", rmf.ap, rm_d, reads=[Bdram_in], writes=[big8[2]])
    rmask = sbuf([128, 512], BF16)
    P.op("dve", lambda e: e.tensor_copy(out=rmask.ap, in_=big8[2].ap), [big8[2]], [rmask])
    pvs = sbuf([128, 2, NPV], F32)
    for l in range(2):
        P.dma("sp", pvs[:, l, :], pv_d[l], reads=[Bdram_in], writes=[pvs])
    for l in range(2):
        for (src, dst, w) in [("mu", "omu", 26), ("k_a", "omka", 8)]:
            P.op("dve", lambda e: e.tensor_scalar(out=pvs[:, l, PV[dst]:PV[dst] + w], in0=pvs[:, l, PV[src]:PV[src] + w],
                                                  scalar1=-1.0, scalar2=1.0, op0=ALU.mult, op1=ALU.add), [pvs], [pvs])
    epsc = sbuf([128, 4], F32)
    for i, v in enumerate([1e-6, 64e-5, 1e-5, 0.0]):
        P.op("dve", lambda e: e.memset(epsc[:, i:i + 1], v), [], [epsc])

    def pcol(l, name, c=0):
        o = PV[name] + c
        return pvs[:, l, o:o + 1]

    lor = [sbuf([128, 2, 1024], BF16) for _ in range(2)]
    for l in range(2):
        P.dma("pool", lor[l].ap, lora_d[l], reads=[Bdram_in], writes=[lor[l]])

    banks = [Buf(nc.alloc_psum_tensor(f"ps{i}", [128, 512], F32).ap()) for i in range(8)]
    ring = banks[:6]
    pin = banks[6:]
    rp = [0]

    def ps_next():
        b = ring[rp[0] % len(ring)]
        rp[0] += 1
        return b

    def bfv(b):
        return b.ap.bitcast(BF16)

    class Ring:
        def __init__(self, n, shape, dtype):
            self.b = [sbuf(shape, dtype) for _ in range(n)]
            self.i = 0

        def get(self):
            b = self.b[self.i % len(self.b)]
            self.i += 1
            return b

    slots = [sbuf([128, 512], F32) for _ in range(12)]
    tf = Ring(0, [128, 512], F32)
    tf.b = slots[0:10]
    tb = Ring(6, [128, 512], BF16)
    lob_, vlo_ = sbuf([128, 512], BF16), sbuf([128, 512], BF16)
    wring = Ring(2, [128, KC, 512], BF16)

    xT = [sbuf([128, T], F32) for _ in range(8)]
    vf = [sbuf([128, T], F32) for _ in range(8)]
    hT = [sbuf([128, T], BF16) for _ in range(8)]
    OG = [sbuf([128, T], BF16) for _ in range(8)]
    yacc = [sbuf([128, T], F32) for _ in range(8)]
    carry = [sbuf([128, 26], F32) for _ in range(2)]
    Sf = [[sbuf([128, 64], F32) for _ in range(8)] for _ in range(2)]
    Sb = [[sbuf([128, 2, 64], BF16) for _ in range(8)] for _ in range(2)]
    halo = [sbuf([128, 8, 30], BF16) for _ in range(2)]
    ubr = Ring(2, [128, 30 + T], BF16)
    dg_keep = sbuf([128, 31, 128], BF16)
    prT_keep = [[sbuf([128, T], BF16) for _ in range(2)] for _ in range(2)]
    small_keep = sbuf([128, 8], F32)
    kmT = [[sbuf([128, NMEM], BF16) for _ in range(8)] for _ in range(2)]
    vmt = [[sbuf([128, D], BF16) for _ in range(2)] for _ in range(2)]
    for l in range(2):
        P.op("pool", lambda e: e.memset(carry[l].ap, 0.0), [], [carry[l]])
        P.op("pool", lambda e: e.memset(halo[l].ap, 0.0), [], [halo[l]])
        for c in range(8):
            P.op("pool", lambda e: e.memset(Sf[l][c].ap, 0.0), [], [Sf[l][c]])
            P.op("pool", lambda e: e.memset(Sb[l][c].ap, 0.0), [], [Sb[l][c]])

    dbg_list = list(dbg_names)
    dbg_buf = sbuf([128, 512], F32) if dbg_names else None

    def dump(name, b, ap=None):
        if name in dbg_list:
            i = dbg_list.index(name)
            a = b.ap if ap is None else ap
            t = dbg_buf
            P.op("dve", lambda e: e.tensor_copy(out=t[:, 0:a.shape[-1]], in_=a), [b], [t])
            P.dma("sp", dbg_d[i][0:a.shape[0], 0:a.shape[-1]], t[0:a.shape[0], 0:a.shape[-1]], reads=[t], writes=[Bout])
            dbg_list[i] = None

    def act(out_b, out_ap, in_b, in_ap, func, extra_r=(), **kw):
        P.op("act", lambda e: e.activation(out=out_ap, in_=in_ap, func=func, **kw), [in_b] + list(extra_r), [out_b])

    def tt(eng, out_b, out_ap, a_b, a_ap, b_b, b_ap, op):
        P.op(eng, lambda e: e.tensor_tensor(out=out_ap, in0=a_ap, in1=b_ap, op=op), [a_b, b_b], [out_b])

    def stt(out_b, out_ap, a_b, a_ap, scalar, b_b, b_ap, op0, op1, extra_r=()):
        P.op("dve", lambda e: e.scalar_tensor_tensor(out=out_ap, in0=a_ap, scalar=scalar, in1=b_ap, op0=op0, op1=op1),
             [a_b, b_b] + list(extra_r), [out_b])

    def ts(eng, out_b, out_ap, a_b, a_ap, s1, s2, op0, op1=None, extra_r=()):
        if op1 is None:
            P.op(eng, lambda e: e.tensor_scalar(out=out_ap, in0=a_ap, scalar1=s1, scalar2=None, op0=op0),
                 [a_b] + list(extra_r), [out_b])
        else:
            P.op(eng, lambda e: e.tensor_scalar(out=out_ap, in0=a_ap, scalar1=s1, scalar2=s2, op0=op0, op1=op1),
                 [a_b] + list(extra_r), [out_b])

    def cp(eng, out_b, out_ap, in_b, in_ap):
        if eng == "act":
            P.op("act", lambda e: e.copy(out=out_ap, in_=in_ap), [in_b], [out_b])
        elif eng == "dve":
            P.op(eng, lambda e: e.tensor_scalar(out=out_ap, in0=in_ap, scalar1=1.0, scalar2=None, op0=ALU.mult), [in_b], [out_b])
        else:
            P.op(eng, lambda e: e.tensor_copy(out=out_ap, in_=in_ap), [in_b], [out_b])

    def mm(out_b, out_ap, l_b, l_ap, r_b, r_ap, start=True, stop=True):
        P.op("pe", lambda e: e.matmul(out_ap, lhsT=l_ap, rhs=r_ap, start=start, stop=stop), [l_b, r_b], [out_b])

    def tr(out_b, out_ap, in_b, in_ap):
        P.op("pe", lambda e: e.transpose(out_ap, in_ap, identf.ap), [in_b, identf], [out_b])

    def wload(l, name):
        o, w = WCOL[name]
        wb = wring.get()
        P.dma("pool", wb[:, :, 0:w], wbig_d[l][:, :, o:o + w], reads=[Bdram_in], writes=[wb])
        return wb

    def proj(l, name, rhs_for_chunk, consume):
        o, w = WCOL[name]
        wb = wload(l, name)
        for j in range(w // 128):
            rhs = rhs_for_chunk(j)
            if rhs is None:
                continue
            ps = ps_next()
            for kc in range(KC):
                mm(ps, ps.ap, wb, wb[:, kc, j * 128:(j + 1) * 128], rhs[kc], rhs[kc].ap, start=(kc == 0), stop=(kc == KC - 1))
            consume(j, ps)

    def bcast_stat(src_list, src_aps, scale, epscol):
        raise NotImplementedError

    def rms_to(l, gname, src, dst, n):
        ps = pin[0]
        for c in range(8):
            sq = tb.get()
            act(sq, sq[:, 0:n], src[c], src[c][:, 0:n], AF.Square)
            mm(ps, ps[:, 0:n], cst_b, ones, sq, sq[:, 0:n], start=(c == 0), stop=(c == 7))
        sd = tf.get()
        act(sd, sd[:, 0:n], ps, ps[:, 0:n], AF.Sqrt, extra_r=[epsc], scale=1.0 / D, bias=epsc[:, 0:1])
        rs = tf.get()
        P.op("dve", lambda e: e.reciprocal(out=rs[:, 0:n], in_=sd[:, 0:n]), [sd], [rs])
        for c in range(8):
            stt(dst[c], dst[c][:, 0:n], src[c], src[c][:, 0:n], pcol(l, gname, c), rs, rs[:, 0:n], ALU.mult, ALU.mult, extra_r=[pvs])
        return rs

    mraw = [Buf(big8[c][:, 0:NMEM]) for c in range(8)]
    for c in range(8):
        P.dma("sp", mraw[c].ap, memT_d[c * 128:(c + 1) * 128, :], reads=[Bdram_in], writes=[big8[c]])
        mraw[c] = big8[c]
    for l in range(2):
        mT = OG
        rms_to(l, "g_mem", mraw, mT, NMEM)
        for j in range(2):
            def cons(jj, ps, j=j):
                cp("act", kmT[l][j * 4 + jj], kmT[l][j * 4 + jj].ap, ps, ps[:, 0:NMEM])
            o, w = WCOL[f"kv{j}"]
            wb = wload(l, f"kv{j}")
            for jj in range(4):
                ps = ps_next()
                for kc in range(KC):
                    mm(ps, ps[:, 0:NMEM], wb, wb[:, kc, jj * 128:(jj + 1) * 128], mT[kc], mT[kc][:, 0:NMEM], start=(kc == 0), stop=(kc == KC - 1))
                cons(jj, ps)
        for j in range(2):
            wb = wload(l, f"kv{2 + j}")
            for mb in range(2):
                ps = ps_next()
                for kc in range(KC):
                    mm(ps, ps.ap, mT[kc], mT[kc][:, mb * 128:(mb + 1) * 128], wb, wb[:, kc, :], start=(kc == 0), stop=(kc == KC - 1))
                cp("act", vmt[l][mb], vmt[l][mb][:, j * 512:(j + 1) * 512], ps, ps.ap)

    chk("memkv")
    AR = sbuf([128, 4, 2, 128], BF16)
    Bbd = sbuf([128, 4, 2, 128], BF16)
    Kbd = sbuf([128, 4, 2, 128], BF16)
    P.op("pool", lambda e: e.memset(Bbd.ap, 0.0), [], [Bbd])
    P.op("pool", lambda e: e.memset(Kbd.ap, 0.0), [], [Kbd])
    NXr = Ring(4, [128, 2, 2, 128], BF16)
    Ar = Ring(4, [128, 2, 128], BF16)
    Arb = Ring(4, [128, 2, 128], BF16)
    Aak = Ring(2, [128, 2, 128], BF16)
    Ark = Ring(4, [128, 2, 128], BF16)
    Atok = [sbuf([128, 2, 128], BF16) for _ in range(2)]
    Vtok = [sbuf([128, 2, 128], BF16) for _ in range(4)]
    Utok = [sbuf([128, 2, 128], BF16) for _ in range(2)]
    for b_ in Atok + Vtok + Utok:
        P.op("pool", lambda e: e.memset(b_.ap, 0.0), [], [b_])
    BKtok = Ring(4, [128, 2, 128], BF16)
    ApT = Ring(4, [128, 128], BF16)
    Wp = Ring(2, [128, 128], BF16)
    stage = slots[0]
    Up4 = slots[11]
    cntr = dict(at=0, vt=0, ut=0, up=0)

    def scan_pre(l, c, qs2, ctx, E1, bpT, kpT, vb):
        for q in qs2:
            d = ctx[q] = {}
            d["at"] = Atok[cntr["at"] % 2]; cntr["at"] += 1
            d["vt"] = Vtok[cntr["vt"] % 4]; cntr["vt"] += 1
            d["upc"] = cntr["up"] % 4; cntr["up"] += 1
            d["bk"], d["arb"], d["aak"], d["ark"] = BKtok.get(), Arb.get(), Aak.get(), Ark.get()
            d["apT"], d["wp"] = ApT.get(), Wp.get()
        for q in qs2:
            d = ctx[q]
            qs = slice(q * 128, (q + 1) * 128)
            pst = ps_next()
            pv_ = pst.ap
            cp("dve", stage, stage[:, 0:128], AR, AR[:, q, 0, :])
            cp("dve", stage, stage[:, 128:256], bpT, bpT[:, qs])
            cp("dve", stage, stage[:, 256:384], kpT, kpT[:, qs])
            cp("dve", stage, stage[:, 384:512], vb, vb[:, qs])
            for i4 in range(4):
                tr(pst, pv_[:, i4 * 128:(i4 + 1) * 128], stage, stage[:, i4 * 128:(i4 + 1) * 128])
            at, vt, bk = d["at"], d["vt"], d["bk"]
            for h in range(2):
                cp("act", at, at[:, h, h * 64:(h + 1) * 64], pst, pv_[:, h * 64:(h + 1) * 64])
                cp("act", vt, vt[:, h, h * 64:(h + 1) * 64], pst, pv_[:, 384 + h * 64:384 + (h + 1) * 64])
            cp("act", bk, bk.ap, pst, pv_[:, 128:384].rearrange("p (a b) -> p a b", a=2))
            yield
        for q in qs2:
            d = ctx[q]
            ps1, ps2, ps3 = ps_next(), ps_next(), ps_next()
            arq = AR[:, q, :, :].rearrange("p a t -> p (a t)")
            for h in range(2):
                mm(ps1, ps1[:, h * 256:(h + 1) * 256], Bbd, Bbd[:, q, h, :], AR, arq)
                mm(ps2, ps2[:, h * 256:(h + 1) * 256], Kbd, Kbd[:, q, h, :], AR, arq)
            mm(ps3, ps3[:, 0:256], AR, AR[:, q, 0, :], Bbd, Bbd[:, q, :, :].rearrange("p h j -> p (h j)"))
            nx = NXr.get()
            arb, aak, ark, a0 = d["arb"], d["aak"], d["ark"], Ar.get()
            p1v = ps1.ap.rearrange("p (h a t) -> p h a t", h=2, a=2)
            p2v = ps2.ap.rearrange("p (h a t) -> p h a t", h=2, a=2)
            for h in range(2):
                tt("dve", nx, nx[:, h, 0, :], ps1, p1v[:, h, 0, :], cst_b, cst_b[:, 3, :], ALU.mult)
                tt("dve", arb, arb[:, h, :], ps1, p1v[:, h, 1, :], cst_b, cst_b[:, 4, :], ALU.mult)
                tt("dve", aak, aak[:, h, :], ps2, p2v[:, h, 0, :], cst_b, cst_b[:, 3, :], ALU.mult)
                tt("dve", ark, ark[:, h, :], ps2, p2v[:, h, 1, :], cst_b, cst_b[:, 4, :], ALU.mult)
            tt("dve", a0, a0.ap, ps3, ps3[:, 0:256].rearrange("p (h t) -> p h t", h=2), mSL2, mSL2.ap, ALU.mult)
            tt("dve", nx, nx[:, :, 1, :], nx, nx[:, :, 0, :], id2, id2.ap, ALU.add)
            d["nx"], d["A"] = nx, a0
            yield
        for lev in range(7):
            last = (lev == 6)
            pss = {}
            for q in qs2:
                d = ctx[q]
                nx, A_i = d["nx"], d["A"]
                psn = ps_next()
                for h in range(2):
                    if lev == 0:
                        mm(psn, psn[:, h * 256:h * 256 + 128], A_i, A_i[:, h, :], nx, nx[:, h, 0, :])
                    elif not last:
                        mm(psn, psn[:, h * 256:(h + 1) * 256], A_i, A_i[:, h, :], nx, nx[:, h, :, :].rearrange("p a t -> p (a t)"))
                    else:
                        mm(psn, psn[:, h * 256 + 128:(h + 1) * 256], A_i, A_i[:, h, :], nx, nx[:, h, 1, :])
                psa = None
                if not last:
                    psa = ps_next()
                    for h in range(2):
                        mm(psa, psa[:, h * 128:(h + 1) * 128], nx, nx[:, h, 0, :], A_i, A_i[:, h, :])
                pss[q] = (psn, psa)
            for q in qs2:
                d = ctx[q]
                nx = d["nx"]
                psn, psa = pss[q]
                pnv = psn.ap.rearrange("p (h a t) -> p h a t", h=2, a=2)
                nx2 = NXr.get()
                if lev == 0:
                    cp("act", nx2, nx2[:, :, 0, :], psn, pnv[:, :, 0, :])
                    cp("dve", nx2, nx2[:, :, 1, :], nx, nx[:, :, 1, :])
                else:
                    if not last:
                        cp("act", nx2, nx2[:, :, 0, :], psn, pnv[:, :, 0, :])
                    tt("dve", nx2, nx2[:, :, 1, :], psn, pnv[:, :, 1, :], nx, nx[:, :, 1, :], ALU.add)
                if not last:
                    a2 = Ar.get()
                    cp("act", a2, a2.ap, psa, psa[:, 0:256].rearrange("p (h t) -> p h t", h=2))
                    d["A"] = a2
                d["nx"] = nx2
            yield
        for q in qs2:
            d = ctx[q]
            nx, at, vt, aak, apT, wp = d["nx"], d["at"], d["vt"], d["aak"], d["apT"], d["wp"]
            psw = ps_next()
            for h in range(2):
                mm(psw, psw[:, 0:128], at, at[:, h, :], nx, nx[:, h, 1, :], start=(h == 0), stop=(h == 1))
            for h in range(2):
                mm(psw, psw[:, 128 + h * 64:128 + (h + 1) * 64], aak, aak[:, h, :], vt, vt[:, h, h * 64:(h + 1) * 64])
            cp("act", apT, apT.ap, psw, psw[:, 0:128])
            cp("act", wp, wp.ap, psw, psw[:, 128:256])
        yield
        for q in qs2:
            d = ctx[q]
            nx, wp = d["nx"], d["wp"]
            psu = ps_next()
            for h in range(2):
                mm(psu, psu[:, h * 64:(h + 1) * 64], nx, nx[:, h, 1, :], wp, wp[:, h * 64:(h + 1) * 64])
            uc_ = d["upc"]
            cp("act", Up4, Up4[:, uc_ * 128:(uc_ + 1) * 128], psu, psu[:, 0:128])
        yield

    def scan_seq(l, c, qs2, ctx, E1, yT):
        sbd, sfd = Sb[l][c], Sf[l][c]
        sbd2 = sbd.ap.rearrange("p h v -> p (h v)")
        for q in qs2:
            d = ctx[q]
            qs = slice(q * 128, (q + 1) * 128)
            vt, bk, arb, ark, apT, uc_ = d["vt"], d["bk"], d["arb"], d["ark"], d["apT"], d["upc"]
            ut = Utok[cntr["ut"] % 2]; cntr["ut"] += 1
            ps_u = ps_next()
            mm(ps_u, ps_u[:, 0:128], apT, apT.ap, sbd, sbd2)
            for h in range(2):
                tt("dve", ut, ut[:, h, h * 64:(h + 1) * 64], ps_u, ps_u[:, h * 64:(h + 1) * 64],
                   Up4, Up4[:, uc_ * 128 + h * 64:uc_ * 128 + (h + 1) * 64], ALU.add)
            yield
            ps_y = ps_next()
            mm(ps_y, ps_y[:, 0:128], sbd, sbd2, AR, AR[:, q, 1, :], start=True, stop=False)
            for h in range(2):
                mm(ps_y, ps_y[:, 0:128], ut, ut[:, h, :], arb, arb[:, h, :], start=False, stop=False)
                mm(ps_y, ps_y[:, 0:128], vt, vt[:, h, :], ark, ark[:, h, :], start=False, stop=(h == 1))
            ps_s = ps_next()
            mm(ps_s, ps_s[:, 0:256], bk, bk[:, 0, :], ut, ut.ap.rearrange("p h v -> p (h v)"), start=True, stop=False)
            mm(ps_s, ps_s[:, 0:256], bk, bk[:, 1, :], vt, vt.ap.rearrange("p h v -> p (h v)"), start=False, stop=True)
            pc = E1[:, q * 128 + 127:q * 128 + 128]
            for h in range(2):
                hp = slice(h * 64, (h + 1) * 64)
                stt(sfd, sfd[hp, :], sfd, sfd[hp, :], pc[hp, :], ps_s, ps_s[hp, h * 128 + h * 64:h * 128 + (h + 1) * 64],
                    ALU.mult, ALU.add, extra_r=[E1])
            for h in range(2):
                hp = slice(h * 64, (h + 1) * 64)
                cp("act", sbd, sbd[hp, h, :], sfd, sfd[hp, :])
            cp("act", yT, yT[:, qs], ps_y, ps_y[:, 0:128])
            yield

    def scan_all(l, c, E1, bpT, kpT, vb, yT):
        ctxA, ctxB = {}, {}
        for _ in scan_pre(l, c, (0, 1), ctxA, E1, bpT, kpT, vb):
            pass
        gB = scan_pre(l, c, (2, 3), ctxB, E1, bpT, kpT, vb)
        gA = scan_seq(l, c, (0, 1), ctxA, E1, yT)
        aliveA = aliveB = True
        while aliveA or aliveB:
            if aliveB:
                try:
                    next(gB)
                except StopIteration:
                    aliveB = False
            if aliveA:
                try:
                    next(gA)
                except StopIteration:
                    aliveA = False
        for _ in scan_seq(l, c, (2, 3), ctxB, E1, yT):
            pass

    for it in range(NT):
        tsl = slice(it * T, (it + 1) * T)
        for c in range(8):
            P.dma("sp", xT[c].ap, xT_d[c * 128:(c + 1) * 128, tsl], reads=[Bdram_in], writes=[xT[c]])
        for l in range(2):
            rms_to(l, "g_norm", xT, hT, T)
            chk(f"rms{l}")
            hrhs = lambda j: hT
            car = carry[l]

            def shiftmix(ps, mi, npart=128, A=None):
                A = A or slots[11]
                pp = slice(0, npart)
                mu_c, omu_c = pcol(l, "mu", mi), pcol(l, "omu", mi)
                act(A, A[pp, :], ps, ps[pp, :], AF.Identity, extra_r=[pvs], scale=omu_c[pp, :])
                stt(A, A[pp, 1:T], ps, ps[pp, 0:T - 1], mu_c[pp, :], A, A[pp, 1:T], ALU.mult, ALU.add, extra_r=[pvs])
                stt(A, A[pp, 0:1], car, car[pp, mi:mi + 1], mu_c[pp, :], A, A[pp, 0:1], ALU.mult, ALU.add, extra_r=[pvs])
                cp("act", car, car[pp, mi:mi + 1], ps, ps[pp, T - 1:T])
                return A

            lob, vlo = lob_, vlo_

            def cons_lora(j, ps):
                if j == 0:
                    lo = shiftmix(ps, 24)
                    act(lob, lob[0:64, :], lo, lo[0:64, :], AF.Tanh)
                    cp("dve", lob, lob[64:128, :], lo, lo[64:128, :])
                elif l == 1:
                    v_ = shiftmix(ps, 25, 32)
                    cp("dve", vlo, vlo[0:32, :], v_, v_[0:32, :])
            proj(l, "lora", lambda j: hT if (j == 0 or l == 1) else None, cons_lora)

            chk(f"lora{l}")
            for c in range(8):
                got = {}

                def cons_rkvg(j, ps):
                    if j < 3:
                        got[j] = shiftmix(ps, j * 8 + c, A=slots[j])
                    else:
                        g = slots[3]
                        act(g, g.ap, ps, ps.ap, AF.Silu)
                        got[3] = g
                proj(l, f"rkvg{c}", hrhs, cons_rkvg)
                r_, k_, v_, gs = got[0], got[1], got[2], got[3]
                cs = slice(c * 128, (c + 1) * 128)
                psd, psa = ps_next(), ps_next()
                mm(psd, psd.ap, lor[l], lor[l][0:64, 0, cs], lob, lob[0:64, :])
                mm(psa, psa.ap, lor[l], lor[l][64:128, 0, cs], lob, lob[64:128, :])
                sgd, a_ = slots[4], slots[5]
                act(sgd, sgd.ap, psd, psd.ap, AF.Sigmoid, extra_r=[pvs], bias=pcol(l, "w0", c))
                act(a_, a_.ap, psa, psa.ap, AF.Sigmoid, extra_r=[pvs], bias=pcol(l, "a0", c))
                if l == 1:
                    psv = ps_next()
                    mm(psv, psv.ap, lor[l], lor[l][0:32, 1, cs], vlo, vlo[0:32, :])
                    gv = slots[7]
                    act(gv, gv.ap, psv, psv.ap, AF.Sigmoid, extra_r=[pvs], bias=pcol(l, "v0", c))
                    dd = slots[8]
                    tt("dve", dd, dd.ap, vf[c], vf[c].ap, v_, v_.ap, ALU.subtract)
                    tt("dve", dd, dd.ap, dd, dd.ap, gv, gv.ap, ALU.mult)
                    tt("dve", v_, v_.ap, v_, v_.ap, dd, dd.ap, ALU.add)
                else:
                    cp("act", vf[c], vf[c].ap, v_, v_.ap)
                dump(f"r{l}_{c}", r_)
                dump(f"k{l}_{c}", k_)
                dump(f"v{l}_{c}", v_)
                dump(f"a{l}_{c}", a_)
                kkr = slots[6]
                ts("dve", kkr, kkr.ap, k_, k_.ap, pcol(l, "k_k", c), None, ALU.mult, extra_r=[pvs])
                sq = tb.get()
                act(sq, sq.ap, kkr, kkr.ap, AF.Square)
                psn = ps_next()
                mm(psn, psn.ap, cst_b, bones, sq, sq.ap)
                nrm = slots[7]
                act(nrm, nrm.ap, psn, psn.ap, AF.Sqrt)
                ts("dve", nrm, nrm.ap, nrm, nrm.ap, 1e-12, None, ALU.max)
                P.op("dve", lambda e: e.reciprocal(out=nrm.ap, in_=nrm.ap), [nrm], [nrm])
                tt("dve", kkr, kkr.ap, kkr, kkr.ap, nrm, nrm.ap, ALU.mult)
                kk = kkr
                f_ = slots[7]
                ts("dve", f_, f_.ap, a_, a_.ap, pcol(l, "k_a", c), pcol(l, "omka", c), ALU.mult, ALU.add, extra_r=[pvs])
                tt("dve", k_, k_.ap, k_, k_.ap, f_, f_.ap, ALU.mult)
                k2 = k_
                rk = tb.get()
                stt(rk, rk.ap, r_, r_.ap, pcol(l, "r_k", c), k2, k2.ap, ALU.mult, ALU.mult, extra_r=[pvs])
                psb = ps_next()
                mm(psb, psb.ap, cst_b, bones, rk, rk.ap)
                bon = slots[8]
                tt("dve", bon, bon.ap, psb, psb.ap, v_, v_.ap, ALU.mult)
                cum = slots[7]
                P.op("dve", lambda e: e.tensor_tensor_scan(out=cum.ap, data0=rmask.ap, data1=sgd.ap, initial=0.0,
                                                           op0=ALU.mult, op1=ALU.add), [rmask, sgd], [cum])
                E1, E2, E3 = slots[9], slots[10], slots[11]
                act(E1, E1.ap, cum, cum.ap, AF.Exp, scale=-C0)
                act(E2, E2.ap, cum, cum.ap, AF.Exp, scale=C0)
                tt("dve", sgd, sgd.ap, cum, cum.ap, sgd, sgd.ap, ALU.subtract)
                act(E3, E3.ap, sgd, sgd.ap, AF.Exp, scale=-C0)
                dump(f"E1{l}_{c}", E1)
                tt("dve", AR, AR[:, :, 1, :], r_, r_.ap.rearrange("p (q t) -> p q t", q=4), E1, E1.ap.rearrange("p (q t) -> p q t", q=4), ALU.mult)
                stt(AR, AR[:, :, 0, :], kk, kk.ap.rearrange("p (q t) -> p q t", q=4), -1.0, E3, E3.ap.rearrange("p (q t) -> p q t", q=4), ALU.mult, ALU.mult)
                bt, kt = slots[0], slots[11]
                tt("dve", bt, bt.ap, kk, kk.ap, a_, a_.ap, ALU.mult)
                tt("dve", bt, bt.ap, bt, bt.ap, E2, E2.ap, ALU.mult)
                tt("dve", kt, kt.ap, k2, k2.ap, E2, E2.ap, ALU.mult)
                for h in range(2):
                    hp = slice(h * 64, (h + 1) * 64)
                    cp("act", Bbd, Bbd[hp, :, h, :], bt, bt[hp, :].rearrange("p (q t) -> p q t", q=4))
                    cp("dve", Kbd, Kbd[hp, :, h, :], kt, kt[hp, :].rearrange("p (q t) -> p q t", q=4))
                bpT, kpT, vb = tb.get(), tb.get(), tb.get()
                for q in range(4):
                    qs = slice(q * 128, (q + 1) * 128)
                    pc = E1[:, q * 128 + 127:q * 128 + 128]
                    act(bpT, bpT[:, qs], bt, bt[:, qs], AF.Identity, extra_r=[E1], scale=pc)
                    act(kpT, kpT[:, qs], kt, kt[:, qs], AF.Identity, extra_r=[E1], scale=pc)
                cp("act", vb, vb.ap, v_, v_.ap)
                chk(f"prep{l}_{c}")
                yT = slots[1]
                scan_all(l, c, E1, bpT, kpT, vb, yT)
                dump(f"y{l}_{c}", yT)
                yb_, ysq = tb.get(), tb.get()
                cp("dve", yb_, yb_.ap, yT, yT.ap)
                act(ysq, ysq.ap, yT, yT.ap, AF.Square)
                p1, p2 = ps_next(), ps_next()
                mm(p1, p1.ap, cst_b, bones, yb_, yb_.ap)
                mm(p2, p2.ap, cst_b, bones, ysq, ysq.ap)
                mean, var = slots[5], slots[6]
                act(mean, mean.ap, p1, p1.ap, AF.Identity, scale=1.0 / 64)
                tt("dve", var, var.ap, mean, mean.ap, mean, mean.ap, ALU.mult)
                stt(var, var.ap, p2, p2.ap, 1.0 / 64, var, var.ap, ALU.mult, ALU.subtract)
                act(var, var.ap, var, var.ap, AF.Sqrt, extra_r=[epsc], bias=epsc[:, 1:2])
                P.op("dve", lambda e: e.reciprocal(out=var.ap, in_=var.ap), [var], [var])
                tt("dve", yT, yT.ap, yT, yT.ap, mean, mean.ap, ALU.subtract)
                tt("dve", yT, yT.ap, yT, yT.ap, var, var.ap, ALU.mult)
                act(yT, yT.ap, yT, yT.ap, AF.Identity, extra_r=[pvs], scale=pcol(l, "gn_g", c), bias=pcol(l, "gn_b", c))
                tt("dve", yT, yT.ap, yT, yT.ap, bon, bon.ap, ALU.add)
                tt("dve", OG[c], OG[c].ap, yT, yT.ap, gs, gs.ap, ALU.mult)
                dump(f"og{l}_{c}", OG[c])

            def branch_out(pn, first, bias_name=None):
                for j in range(4):
                    tmpy = {}

                    def cons(jj, ps, j=j):
                        if jj < 2:
                            t_ = tf.get()
                            if bias_name is None:
                                cp("act", t_, t_.ap, ps, ps.ap)
                            else:
                                act(t_, t_.ap, ps, ps.ap, AF.Identity, extra_r=[pvs], bias=pcol(l, bias_name, 2 * j + jj))
                            tmpy[jj] = t_
                        else:
                            cidx = 2 * j + (jj - 2)
                            sg = tf.get()
                            act(sg, sg.ap, ps, ps.ap, AF.Sigmoid)
                            yb = tmpy[jj - 2]
                            if first:
                                tt("dve", yacc[cidx], yacc[cidx].ap, sg, sg.ap, yb, yb.ap, ALU.mult)
                            else:
                                tt("dve", sg, sg.ap, sg, sg.ap, yb, yb.ap, ALU.mult)
                                tt("dve", yacc[cidx], yacc[cidx].ap, yacc[cidx], yacc[cidx].ap, sg, sg.ap, ALU.add)
                    proj(l, f"{pn}{j}", lambda jj: OG if jj < 2 else hT, cons)

            chk(f"rwkv{l}")
            branch_out("pr", True)
            chk(f"pr{l}")
            dump(f"yacc0_{l}", yacc[0])

            uc = big8
            s1, s2 = pin[0], pin[1]
            dg = dg_keep
            for j in range(4):
                def cons_glu(jj, ps, j=j):
                    c = 2 * j + jj // 2
                    if jj % 2 == 0:
                        cons_glu.pa = ps
                        return
                    gb = tf.get()
                    act(gb, gb.ap, ps, ps.ap, AF.Sigmoid, extra_r=[pvs], bias=pcol(l, "b_glu", 8 + c))
                    pa = cons_glu.pa
                    u = ubr.get()
                    cp("act", u, u[:, 0:30], halo[l], halo[l][:, c, :])
                    stt(u, u[:, 30:30 + T], pa, pa.ap, pcol(l, "b_glu", c), gb, gb.ap, ALU.add, ALU.mult, extra_r=[pvs])
                    for tp in range(31):
                        if tp % 3 == 2:
                            act(dg, dg[:, tp, :], cst_b, ident, AF.Identity, extra_r=[pvs], scale=pcol(l, "w_dw", c * 31 + tp))
                        else:
                            ts("dve", dg, dg[:, tp, :], cst_b, ident, pcol(l, "w_dw", c * 31 + tp), None, ALU.mult, extra_r=[pvs])
                    pc_ = ps_next()
                    for tp in range(31):
                        mm(pc_, pc_.ap, dg, dg[:, tp, :], u, u[:, tp:tp + T], start=(tp == 0), stop=(tp == 30))
                    act(uc[c], uc[c].ap, pc_, pc_.ap, AF.Identity, extra_r=[pvs], bias=pcol(l, "b_dw", c))
                    cp("act", halo[l], halo[l][:, c, :], u, u[:, T:T + 30])
                    ucb, ucs = tb.get(), tb.get()
                    cp("dve", ucb, ucb.ap, uc[c], uc[c].ap)
                    act(ucs, ucs.ap, uc[c], uc[c].ap, AF.Square)
                    mm(s1, s1.ap, cst_b, ones, ucb, ucb.ap, start=(c == 0), stop=(c == 7))
                    mm(s2, s2.ap, cst_b, ones, ucs, ucs.ap, start=(c == 0), stop=(c == 7))
                proj(l, f"glu{j}", hrhs, cons_glu)
            mean, var = slots[10], slots[11]
            act(mean, mean.ap, s1, s1.ap, AF.Identity, scale=1.0 / D)
            tt("dve", var, var.ap, mean, mean.ap, mean, mean.ap, ALU.mult)
            stt(var, var.ap, s2, s2.ap, 1.0 / D, var, var.ap, ALU.mult, ALU.subtract)
            act(var, var.ap, var, var.ap, AF.Sqrt, extra_r=[epsc], bias=epsc[:, 2:3])
            P.op("dve", lambda e: e.reciprocal(out=var.ap, in_=var.ap), [var], [var])
            dump(f"uc{l}_0", uc[0])
            for j in range(2):
                def cons_cg(jj, ps, j=j):
                    c = 4 * j + jj
                    cg = tf.get()
                    act(cg, cg.ap, ps, ps.ap, AF.Silu)
                    t_ = uc[c]
                    tt("dve", t_, t_.ap, t_, t_.ap, mean, mean.ap, ALU.subtract)
                    tt("dve", t_, t_.ap, t_, t_.ap, var, var.ap, ALU.mult)
                    act(t_, t_.ap, t_, t_.ap, AF.Identity, extra_r=[pvs], scale=pcol(l, "ln_g", c), bias=pcol(l, "ln_b", c))
                    act(t_, t_.ap, t_, t_.ap, AF.Silu)
                    tt("dve", OG[c], OG[c].ap, t_, t_.ap, cg, cg.ap, ALU.mult)
                proj(l, f"cg{j}", hrhs, cons_cg)
            dump(f"ug{l}_0", OG[0])
            chk(f"conv{l}")
            branch_out("pc", False, "b_pc")
            chk(f"pc{l}")
            dump(f"yacc1_{l}", yacc[0])

            qT = OG
            for j in range(2):
                def cons_q(jj, ps, j=j):
                    c = 4 * j + jj
                    act(qT[c], qT[c].ap, ps, ps.ap, AF.Identity, scale=1.0 / 16.0)
                proj(l, f"q{j}", hrhs, cons_q)
            att = big8
            prT = prT_keep
            small = small_keep
            for hm in range(4):
                pt = prT[hm % 2]
                for sbk in range(4):
                    ss = slice(sbk * 128, (sbk + 1) * 128)
                    psc = ps_next()
                    for dc in range(2):
                        mm(psc, psc[:, 0:NMEM], qT[2 * hm + dc], qT[2 * hm + dc][:, ss], kmT[l][2 * hm + dc], kmT[l][2 * hm + dc].ap,
                           start=(dc == 0), stop=(dc == 1))
                    P.op("dve", lambda e: e.tensor_reduce(out=small[:, 0:1], in_=psc[:, 0:NMEM], axis=AX.X, op=ALU.max), [psc], [small])
                    ts("dve", small, small[:, 1:2], small, small[:, 0:1], -1.0, None, ALU.mult)
                    ex = tf.get()
                    P.op("act", lambda e: e.activation(out=ex[:, 0:NMEM], in_=psc[:, 0:NMEM], func=AF.Exp, bias=small[:, 1:2],
                                                       accum_out=small[:, 2:3]), [psc, small], [ex, small])
                    P.op("dve", lambda e: e.reciprocal(out=small[:, 3:4], in_=small[:, 2:3]), [small], [small])
                    pb = tf.get()
                    ts("dve", pb, pb[:, 0:NMEM], ex, ex[:, 0:NMEM], small[:, 3:4], None, ALU.mult, extra_r=[small])
                    ptp = ps_next()
                    pv_ = ptp.ap
                    for mb in range(2):
                        tr(ptp, pv_[:, mb * 128:(mb + 1) * 128], pb, pb[:, mb * 128:(mb + 1) * 128])
                    for mb in range(2):
                        cp("act", pt[mb], pt[mb][:, ss], ptp, pv_[:, mb * 128:(mb + 1) * 128])
                for dc in range(2):
                    c = 2 * hm + dc
                    pa_ = ps_next()
                    for mb in range(2):
                        mm(pa_, pa_.ap, vmt[l][mb], vmt[l][mb][:, c * 128:(c + 1) * 128], pt[mb], pt[mb].ap, start=(mb == 0), stop=(mb == 1))
                    cp("act", att[c], att[c].ap, pa_, pa_.ap)
            dump(f"att{l}_0", att[0])
            for j in range(2):
                def cons_mg(jj, ps, j=j):
                    c = 4 * j + jj
                    mg = tf.get()
                    act(mg, mg.ap, ps, ps.ap, AF.Silu)
                    tt("dve", OG[c], OG[c].ap, att[c], att[c].ap, mg, mg.ap, ALU.mult)
                proj(l, f"mg{j}", hrhs, cons_mg)
            chk(f"mem{l}")
            branch_out("pm", False)
            chk(f"pm{l}")
            dump(f"yacc2_{l}", yacc[0])

            for c in range(8):
                cp("act", OG[c], OG[c].ap, yacc[c], yacc[c].ap)
            for j in range(2):
                def cons_o(jj, ps, j=j):
                    c = 4 * j + jj
                    tt("dve", xT[c], xT[c].ap, xT[c], xT[c].ap, ps, ps.ap, ALU.add)
                proj(l, f"wo{j}", lambda jj: OG, cons_o)
            dump(f"x{l}_0", xT[0])

        ps = pin[0]
        for c in range(8):
            sq = tb.get()
            act(sq, sq.ap, xT[c], xT[c].ap, AF.Square)
            mm(ps, ps.ap, cst_b, ones, sq, sq.ap, start=(c == 0), stop=(c == 7))
        sd, rs = tf.get(), tf.get()
        act(sd, sd.ap, ps, ps.ap, AF.Sqrt, extra_r=[epsc], scale=1.0 / D, bias=epsc[:, 0:1])
        P.op("dve", lambda e: e.reciprocal(out=rs.ap, in_=sd.ap), [sd], [rs])
        for c in range(8):
            o_ = tf.get()
            stt(o_, o_.ap, xT[c], xT[c].ap, pcol(0, "g_final", c), rs, rs.ap, ALU.mult, ALU.mult, extra_r=[pvs])
            P.dma("sp", outT_d[c * 128:(c + 1) * 128, tsl], o_.ap, reads=[o_], writes=[Bout])


_CACHE = {}


def kernel(**inp):
    inp = {k: np.asarray(v) for k, v in inp.items()}
    pv, wbig, lora, cst, rm = host_prep(inp)
    x, mem = inp["x"], inp["mem"]
    B = x.shape[0]
    nc = bass.Bass("TRN2", target_bir_lowering=False)
    build(nc)
    in_maps = []
    for b in range(B):
        in_maps.append({"xT": np.ascontiguousarray(x[b].T), "memT": np.ascontiguousarray(mem[b].T),
                        "pv": pv, "wbig": wbig, "lora": lora, "cst": cst, "rm": rm})
    res = run_bass_kernel_spmd(nc, in_maps, core_ids=list(range(B)))
    out = np.stack([np.ascontiguousarray(r["outT"].T) for r in res.results], axis=0)
    return out.astype(np.float32)
```

```python
import numpy as np
import concourse.bass as bass
import concourse.mybir as mybir
from concourse.bass_utils import run_bass_kernel_spmd

F32 = mybir.dt.float32
BF16 = mybir.dt.bfloat16
AF = mybir.ActivationFunctionType
ALU = mybir.AluOpType
AX = mybir.AxisListType
NDS = 24

D = 1024
SEQ = 4096
T = 512
NMEM = 256
KC = 8
C0 = float(np.exp(-0.5))


class Buf:
    __slots__ = ("ap", "w", "r")

    def __init__(self, ap):
        self.ap = ap
        self.w = None
        self.r = {}

    def __getitem__(self, k):
        return self.ap[k]


class Prog:
    def __init__(self, nc):
        self.nc = nc
        self.eng = dict(pe=nc.tensor, dve=nc.vector, act=nc.scalar, pool=nc.gpsimd, sp=nc.sync)
        self.esem = {k: nc.alloc_semaphore("es_" + k) for k in self.eng}
        self.ecnt = {k: 0 for k in self.eng}
        self.seen = {k: {} for k in self.eng}
        self.dsem, self.dtgt, self.dnext = {}, {}, {}
        self.ninst = 0

    def _wait(self, e, ev):
        sem, key, val = ev
        if key == ("e", e) and e == "pe":
            return
        if self.seen[e].get(key, 0) >= val:
            return
        self.eng[e].wait_ge(sem, val)
        self.seen[e][key] = val

    def _deps(self, e, reads, writes):
        for b in reads:
            if b.w is not None:
                self._wait(e, b.w)
        for b in writes:
            if b.w is not None:
                self._wait(e, b.w)
            for ev in b.r.values():
                self._wait(e, ev)

    def _record(self, ev, reads, writes):
        for b in reads:
            b.r[ev[1]] = ev
        for b in writes:
            b.w = ev
            b.r = {}

    def op(self, e, fn, reads=(), writes=()):
        self._deps(e, reads, writes)
        inst = fn(self.eng[e])
        self.ecnt[e] += 1
        inst.then_inc(self.esem[e], 1)
        self._record((self.esem[e], ("e", e), self.ecnt[e]), reads, writes)
        self.ninst += 1

    def dma(self, q, out_ap, in_ap, reads=(), writes=(), **kw):
        if q not in self.dsem:
            self.dsem[q] = [self.nc.alloc_semaphore(f"ds_{q}{i}") for i in range(NDS)]
            self.dtgt[q] = [0] * NDS
            self.dnext[q] = 0
        j = self.dnext[q]
        self.dnext[q] = (j + 1) % NDS
        key = ("d", q, j)
        if self.dtgt[q][j] > 0:
            self._wait(q, (self.dsem[q][j], key, self.dtgt[q][j]))
        self._deps(q, reads, writes)
        inst = self.eng[q].dma_start(out=out_ap, in_=in_ap, **kw)
        self.dtgt[q][j] += 16
        inst.then_inc(self.dsem[q][j], 16)
        self._record((self.dsem[q][j], key, self.dtgt[q][j]), reads, writes)
        self.ninst += 1

    def finish(self, e="sp"):
        for q in self.dsem:
            for j in range(NDS):
                if self.dtgt[q][j] > 0:
                    self._wait(e, (self.dsem[q][j], ("d", q, j), self.dtgt[q][j]))
        for k in self.eng:
            if k != e and self.ecnt[k] > 0:
                self._wait(e, (self.esem[k], ("e", k), self.ecnt[k]))


PV = {}
_o = 0
for _n, _w in [("g_norm", 8), ("mu", 26), ("w0", 8), ("a0", 8), ("k_k", 8), ("k_a", 8), ("r_k", 8),
               ("gn_g", 8), ("gn_b", 8), ("v0", 8), ("b_glu", 16), ("w_dw", 248), ("b_dw", 8),
               ("ln_g", 8), ("ln_b", 8), ("b_pc", 8), ("g_mem", 8), ("g_final", 8), ("omu", 26), ("omka", 8)]:
    PV[_n] = _o
    _o += _w
NPV = _o

WCOL = {}
_o = 0
for _n, _w in [("lora", 256)] + [(f"rkvg{c}", 512) for c in range(8)] + \
        [(f"pr{j}", 512) for j in range(4)] + [(f"glu{j}", 512) for j in range(4)] + \
        [(f"cg{j}", 512) for j in range(2)] + [(f"pc{j}", 512) for j in range(4)] + \
        [(f"q{j}", 512) for j in range(2)] + [(f"mg{j}", 512) for j in range(2)] + \
        [(f"pm{j}", 512) for j in range(4)] + [(f"wo{j}", 512) for j in range(2)] + \
        [(f"kv{j}", 512) for j in range(4)]:
    WCOL[_n] = (_o, _w)
    _o += _w
TOTC = _o


def _fm(v):
    return np.ascontiguousarray(v.reshape(8, 128).T)


def host_prep(inp):
    f = np.float32
    L = 2
    pv = np.zeros((L, 128, NPV), f)
    wbig = np.zeros((L, 128, KC, TOTC), f)
    lora = np.zeros((L, 128, 2, 1024), f)
    for l in range(L):
        def put(name, arr):
            pv[l][:, PV[name]:PV[name] + arr.shape[1]] = arr
        put("g_norm", _fm(inp["g_norm"][l]))
        mu = inp["mu_shift"][l]
        mucols = np.zeros((128, 26), f)
        mucols[:, 0:25] = mu.reshape(25, 128).T
        if l >= 1:
            mucols[0:32, 25] = inp["mu_vres"][l - 1]
        put("mu", mucols)
        for n in ["w0", "a0", "k_k", "k_a", "gn_g", "gn_b", "b_dw", "ln_g", "ln_b", "g_mem"]:
            src = {"g_mem": "g_mem_norm"}.get(n, n)
            put(n, _fm(inp[src][l]))
        put("r_k", _fm(inp["r_k"][l].reshape(-1)))
        put("b_pc", _fm(inp["b_proj_conv"][l]))
        if l >= 1:
            put("v0", _fm(inp["v0"][l - 1]))
        put("b_glu", np.ascontiguousarray(inp["b_glu"][l].reshape(16, 128).T))
        wd = inp["w_dw"][l]
        put("w_dw", np.ascontiguousarray(wd.reshape(31, 8, 128).transpose(2, 1, 0).reshape(128, 248)))
        put("g_final", _fm(inp["g_final"]))
        w_in = inp["w_in"][l]
        Wc = np.zeros((D, TOTC), f)

        def setc(name, off, arr):
            o, w = WCOL[name]
            Wc[:, o + off:o + off + arr.shape[1]] = arr
        setc("lora", 0, w_in[:, 3072:3200])
        if l >= 1:
            setc("lora", 128, inp["w_vres_down"][l - 1])
        for c in range(8):
            setc(f"rkvg{c}", 0, w_in[:, c * 128:(c + 1) * 128])
            setc(f"rkvg{c}", 128, w_in[:, 1024 + c * 128:1024 + (c + 1) * 128])
            setc(f"rkvg{c}", 256, w_in[:, 2048 + c * 128:2048 + (c + 1) * 128])
            setc(f"rkvg{c}", 384, w_in[:, 3200 + c * 128:3200 + (c + 1) * 128])
        for br, (pn, wp) in enumerate([("pr", inp["w_proj_rwkv"][l]), ("pc", inp["w_proj_conv"][l]),
                                       ("pm", inp["w_proj_mem"][l])]):
            for j in range(4):
                setc(f"{pn}{j}", 0, wp[:, j * 256:(j + 1) * 256])
                mo = 9344 + br * 1024 + j * 256
                setc(f"{pn}{j}", 256, w_in[:, mo:mo + 256])
        for j in range(4):
            for i in range(2):
                c = 2 * j + i
                setc(f"glu{j}", i * 256, w_in[:, 4224 + c * 128:4224 + (c + 1) * 128])
                setc(f"glu{j}", i * 256 + 128, w_in[:, 5248 + c * 128:5248 + (c + 1) * 128])
        for j in range(2):
            setc(f"cg{j}", 0, w_in[:, 6272 + j * 512:6272 + (j + 1) * 512])
            setc(f"q{j}", 0, w_in[:, 7296 + j * 512:7296 + (j + 1) * 512])
            setc(f"mg{j}", 0, w_in[:, 8320 + j * 512:8320 + (j + 1) * 512])
            setc(f"wo{j}", 0, inp["w_out"][l][:, j * 512:(j + 1) * 512])
        for j in range(4):
            setc(f"kv{j}", 0, inp["w_mem_kv"][l][:, j * 512:(j + 1) * 512])
        wbig[l] = Wc.reshape(KC, 128, TOTC).transpose(1, 0, 2)
        lora[l][0:64, 0] = inp["w_decay_up"][l]
        lora[l][64:128, 0] = inp["w_aaa_up"][l]
        if l >= 1:
            lora[l][0:32, 1] = inp["w_vres_up"][l - 1]
    cst = np.zeros((128, 8, 128), f)
    i = np.arange(128)
    cst[:, 0] = np.eye(128)
    cst[:, 1] = 1.0
    cst[:, 2] = (i[:, None] // 64 == i[None, :] // 64)
    cst[:, 3] = (i[:, None] < i[None, :])
    cst[:, 4] = (i[:, None] <= i[None, :])
    cst[:, 5] = (i[:, None] > i[None, :])
    rm = np.ones((128, 512), f)
    rm[:, 0::128] = 0.0
    return pv, wbig, lora, cst, rm


class _Stop(Exception):
    pass


def build(nc, NT=SEQ // T, dbg_names=(), stop_after=None):
    P = Prog(nc)
    try:
        _build(nc, P, NT, dbg_names, stop_after)
    except _Stop:
        pass
    P.finish()
    return P


def _build(nc, P, NT, dbg_names, stop_after):
    def chk(tag):
        if tag == stop_after:
            raise _Stop()
    dt = nc.dram_tensor
    xT_d = dt("xT", [D, SEQ], F32, kind="ExternalInput").ap()
    memT_d = dt("memT", [D, NMEM], F32, kind="ExternalInput").ap()
    pv_d = dt("pv", [2, 128, NPV], F32, kind="ExternalInput").ap()
    wbig_d = dt("wbig", [2, 128, KC, TOTC], F32, kind="ExternalInput").ap()
    lora_d = dt("lora", [2, 128, 2, 1024], F32, kind="ExternalInput").ap()
    cst_d = dt("cst", [128, 8, 128], F32, kind="ExternalInput").ap()
    rm_d = dt("rm", [128, 512], F32, kind="ExternalInput").ap()
    outT_d = dt("outT", [D, SEQ], F32, kind="ExternalOutput").ap()
    dbg_d = None
    if dbg_names:
        dbg_d = dt("dbg", [len(dbg_names), 128, 512], F32, kind="ExternalOutput").ap()
    Bdram_in = Buf(None)
    Bout = Buf(None)
    cnt = [0]

    def sb(shape, dtype, name=None):
        cnt[0] += 1
        return nc.alloc_sbuf_tensor(name or f"t{cnt[0]}", list(shape), dtype)

    def sbuf(shape, dtype):
        return Buf(sb(shape, dtype).ap())

    big8 = [sbuf([128, T], F32) for _ in range(8)]
    cst_f = Buf(big8[0].ap.rearrange("p (a b) -> p a b", a=4))
    cst_f2 = Buf(big8[1].ap.rearrange("p (a b) -> p a b", a=4))
    P.dma("sp", cst_f.ap, cst_d[:, 0:4, :], reads=[Bdram_in], writes=[cst_f, big8[0]])
    P.dma("sp", cst_f2.ap, cst_d[:, 4:8, :], reads=[Bdram_in], writes=[cst_f2, big8[1]])
    cst_b = sbuf([128, 8, 128], BF16)
    P.op("dve", lambda e: e.tensor_copy(out=cst_b[:, 0:4, :], in_=cst_f.ap), [cst_f, big8[0]], [cst_b])
    P.op("dve", lambda e: e.tensor_copy(out=cst_b[:, 4:8, :], in_=cst_f2.ap), [cst_f2, big8[1]], [cst_b])
    ident, ones, bones = cst_b[:, 0, :], cst_b[:, 1, :], cst_b[:, 2, :]
    identf = sbuf([128, 128], F32)
    P.op("dve", lambda e: e.tensor_copy(out=identf.ap, in_=cst_f[:, 0, :]), [cst_f, big8[0]], [identf])
    m12 = Buf(cst_b[:, 3:5, :])
    m12.w = None
    mSL2 = sbuf([128, 2, 128], BF16)
    id2 = sbuf([128, 2, 128], BF16)
    for h in range(2):
        P.op("dve", lambda e: e.tensor_copy(out=mSL2[:, h, :], in_=cst_b[:, 5, :]), [cst_b], [mSL2])
        P.op("dve", lambda e: e.tensor_copy(out=id2[:, h, :], in_=cst_b[:, 0, :]), [cst_b], [id2])
    rmf = Buf(big8[2].ap)
    P.dma("sp", rmf.ap, rm_d, reads=[Bdram_in], writes=[big8[2]])
    rmask = sbuf([128, 512], BF16)
    P.op("dve", lambda e: e.tensor_copy(out=rmask.ap, in_=big8[2].ap), [big8[2]], [rmask])
    pvs = sbuf([128, 2, NPV], F32)
    for l in range(2):
        P.dma("sp", pvs[:, l, :], pv_d[l], reads=[Bdram_in], writes=[pvs])
    for l in range(2):
        for (src, dst, w) in [("mu", "omu", 26), ("k_a", "omka", 8)]:
            P.op("dve", lambda e: e.tensor_scalar(out=pvs[:, l, PV[dst]:PV[dst] + w], in0=pvs[:, l, PV[src]:PV[src] + w],
                                                  scalar1=-1.0, scalar2=1.0, op0=ALU.mult, op1=ALU.add), [pvs], [pvs])
    epsc = sbuf([128, 4], F32)
    for i, v in enumerate([1e-6, 64e-5, 1e-5, 0.0]):
        P.op("dve", lambda e: e.memset(epsc[:, i:i + 1], v), [], [epsc])

    def pcol(l, name, c=0):
        o = PV[name] + c
        return pvs[:, l, o:o + 1]

    lor = [sbuf([128, 2, 1024], BF16) for _ in range(2)]
    for l in range(2):
        P.dma("pool", lor[l].ap, lora_d[l], reads=[Bdram_in], writes=[lor[l]])

    banks = [Buf(nc.alloc_psum_tensor(f"ps{i}", [128, 512], F32).ap()) for i in range(8)]
    ring = banks[:6]
    pin = banks[6:]
    rp = [0]

    def ps_next():
        b = ring[rp[0] % len(ring)]
        rp[0] += 1
        return b

    def bfv(b):
        return b.ap.bitcast(BF16)

    class Ring:
        def __init__(self, n, shape, dtype):
            self.b = [sbuf(shape, dtype) for _ in range(n)]
            self.i = 0

        def get(self):
            b = self.b[self.i % len(self.b)]
            self.i += 1
            return b

    slots = [sbuf([128, 512], F32) for _ in range(12)]
    tf = Ring(0, [128, 512], F32)
    tf.b = slots[0:10]
    tb = Ring(6, [128, 512], BF16)
    lob_, vlo_ = sbuf([128, 512], BF16), sbuf([128, 512], BF16)
    wring = Ring(2, [128, KC, 512], BF16)

    xT = [sbuf([128, T], F32) for _ in range(8)]
    vf = [sbuf([128, T], F32) for _ in range(8)]
    hT = [sbuf([128, T], BF16) for _ in range(8)]
    OG = [sbuf([128, T], BF16) for _ in range(8)]
    yacc = [sbuf([128, T], F32) for _ in range(8)]
    carry = [sbuf([128, 26], F32) for _ in range(2)]
    Sf = [[sbuf([128, 64], F32) for _ in range(8)] for _ in range(2)]
    Sb = [[sbuf([128, 2, 64], BF16) for _ in range(8)] for _ in range(2)]
    halo = [sbuf([128, 8, 30], BF16) for _ in range(2)]
    ubr = Ring(2, [128, 30 + T], BF16)
    dg_keep = sbuf([128, 31, 128], BF16)
    dgT = [Buf(dg_keep[:, tp_, :]) for tp_ in range(31)]
    prT_keep = [[sbuf([128, T], BF16) for _ in range(2)] for _ in range(2)]
    small_keep = sbuf([128, 8], F32)
    kmT = [[sbuf([128, NMEM], BF16) for _ in range(8)] for _ in range(2)]
    vmt = [[sbuf([128, D], BF16) for _ in range(2)] for _ in range(2)]
    for l in range(2):
        P.op("pool", lambda e: e.memset(carry[l].ap, 0.0), [], [carry[l]])
        P.op("pool", lambda e: e.memset(halo[l].ap, 0.0), [], [halo[l]])
        for c in range(8):
            P.op("pool", lambda e: e.memset(Sf[l][c].ap, 0.0), [], [Sf[l][c]])
            P.op("pool", lambda e: e.memset(Sb[l][c].ap, 0.0), [], [Sb[l][c]])

    dbg_list = list(dbg_names)
    dbg_buf = sbuf([128, 512], F32) if dbg_names else None

    def dump(name, b, ap=None):
        if name in dbg_list:
            i = dbg_list.index(name)
            a = b.ap if ap is None else ap
            t = dbg_buf
            P.op("dve", lambda e: e.tensor_copy(out=t[:, 0:a.shape[-1]], in_=a), [b], [t])
            P.dma("sp", dbg_d[i][0:a.shape[0], 0:a.shape[-1]], t[0:a.shape[0], 0:a.shape[-1]], reads=[t], writes=[Bout])
            dbg_list[i] = None

    def act(out_b, out_ap, in_b, in_ap, func, extra_r=(), **kw):
        P.op("act", lambda e: e.activation(out=out_ap, in_=in_ap, func=func, **kw), [in_b] + list(extra_r), [out_b])

    def tt(eng, out_b, out_ap, a_b, a_ap, b_b, b_ap, op):
        P.op(eng, lambda e: e.tensor_tensor(out=out_ap, in0=a_ap, in1=b_ap, op=op), [a_b, b_b], [out_b])

    def stt(out_b, out_ap, a_b, a_ap, scalar, b_b, b_ap, op0, op1, extra_r=()):
        P.op("dve", lambda e: e.scalar_tensor_tensor(out=out_ap, in0=a_ap, scalar=scalar, in1=b_ap, op0=op0, op1=op1),
             [a_b, b_b] + list(extra_r), [out_b])

    def ts(eng, out_b, out_ap, a_b, a_ap, s1, s2, op0, op1=None, extra_r=()):
        if op1 is None:
            P.op(eng, lambda e: e.tensor_scalar(out=out_ap, in0=a_ap, scalar1=s1, scalar2=None, op0=op0),
                 [a_b] + list(extra_r), [out_b])
        else:
            P.op(eng, lambda e: e.tensor_scalar(out=out_ap, in0=a_ap, scalar1=s1, scalar2=s2, op0=op0, op1=op1),
                 [a_b] + list(extra_r), [out_b])

    def cp(eng, out_b, out_ap, in_b, in_ap):
        if eng == "act":
            P.op("act", lambda e: e.copy(out=out_ap, in_=in_ap), [in_b], [out_b])
        elif eng == "dve":
            P.op(eng, lambda e: e.tensor_scalar(out=out_ap, in0=in_ap, scalar1=1.0, scalar2=None, op0=ALU.mult), [in_b], [out_b])
        else:
            P.op(eng, lambda e: e.tensor_copy(out=out_ap, in_=in_ap), [in_b], [out_b])

    def mm(out_b, out_ap, l_b, l_ap, r_b, r_ap, start=True, stop=True, extra_r=()):
        P.op("pe", lambda e: e.matmul(out_ap, lhsT=l_ap, rhs=r_ap, start=start, stop=stop), [l_b, r_b] + list(extra_r), [out_b])

    def tr(out_b, out_ap, in_b, in_ap):
        P.op("pe", lambda e: e.transpose(out_ap, in_ap, identf.ap), [in_b, identf], [out_b])

    def wload(l, name):
        o, w = WCOL[name]
        wb = wring.get()
        P.dma("pool", wb[:, :, 0:w], wbig_d[l][:, :, o:o + w], reads=[Bdram_in], writes=[wb])
        return wb

    def proj(l, name, rhs_for_chunk, consume):
        o, w = WCOL[name]
        wb = wload(l, name)
        for j in range(w // 128):
            rhs = rhs_for_chunk(j)
            if rhs is None:
                continue
            ps = ps_next()
            for kc in range(KC):
                mm(ps, ps.ap, wb, wb[:, kc, j * 128:(j + 1) * 128], rhs[kc], rhs[kc].ap, start=(kc == 0), stop=(kc == KC - 1))
            consume(j, ps)

    def bcast_stat(src_list, src_aps, scale, epscol):
        raise NotImplementedError

    def rms_to(l, gname, src, dst, n):
        ps = pin[0]
        for c in range(8):
            sq = tb.get()
            act(sq, sq[:, 0:n], src[c], src[c][:, 0:n], AF.Square)
            mm(ps, ps[:, 0:n], cst_b, ones, sq, sq[:, 0:n], start=(c == 0), stop=(c == 7))
        sd = tf.get()
        act(sd, sd[:, 0:n], ps, ps[:, 0:n], AF.Sqrt, extra_r=[epsc], scale=1.0 / D, bias=epsc[:, 0:1])
        rs = tf.get()
        P.op("dve", lambda e: e.reciprocal(out=rs[:, 0:n], in_=sd[:, 0:n]), [sd], [rs])
        for c in range(8):
            stt(dst[c], dst[c][:, 0:n], src[c], src[c][:, 0:n], pcol(l, gname, c), rs, rs[:, 0:n], ALU.mult, ALU.mult, extra_r=[pvs])
        return rs

    mraw = [Buf(big8[c][:, 0:NMEM]) for c in range(8)]
    for c in range(8):
        P.dma("sp", mraw[c].ap, memT_d[c * 128:(c + 1) * 128, :], reads=[Bdram_in], writes=[big8[c]])
        mraw[c] = big8[c]
    for l in range(2):
        mT = OG
        rms_to(l, "g_mem", mraw, mT, NMEM)
        for j in range(2):
            def cons(jj, ps, j=j):
                cp("act", kmT[l][j * 4 + jj], kmT[l][j * 4 + jj].ap, ps, ps[:, 0:NMEM])
            o, w = WCOL[f"kv{j}"]
            wb = wload(l, f"kv{j}")
            for jj in range(4):
                ps = ps_next()
                for kc in range(KC):
                    mm(ps, ps[:, 0:NMEM], wb, wb[:, kc, jj * 128:(jj + 1) * 128], mT[kc], mT[kc][:, 0:NMEM], start=(kc == 0), stop=(kc == KC - 1))
                cons(jj, ps)
        for j in range(2):
            wb = wload(l, f"kv{2 + j}")
            for mb in range(2):
                ps = ps_next()
                for kc in range(KC):
                    mm(ps, ps.ap, mT[kc], mT[kc][:, mb * 128:(mb + 1) * 128], wb, wb[:, kc, :], start=(kc == 0), stop=(kc == KC - 1))
                cp("act", vmt[l][mb], vmt[l][mb][:, j * 512:(j + 1) * 512], ps, ps.ap)

    chk("memkv")
    AR = sbuf([128, 4, 2, 128], BF16)
    Bbd = sbuf([128, 4, 2, 128], BF16)
    Kbd = sbuf([128, 4, 2, 128], BF16)
    P.op("pool", lambda e: e.memset(Bbd.ap, 0.0), [], [Bbd])
    P.op("pool", lambda e: e.memset(Kbd.ap, 0.0), [], [Kbd])
    NXr = Ring(4, [128, 2, 2, 128], BF16)
    NXn = {id(b_): Buf(b_[:, :, 0, :]) for b_ in NXr.b}
    NXx = {id(b_): Buf(b_[:, :, 1, :]) for b_ in NXr.b}
    Ar = Ring(4, [128, 2, 128], BF16)
    Arb = Ring(4, [128, 2, 128], BF16)
    Aak = Ring(2, [128, 2, 128], BF16)
    Ark = Ring(4, [128, 2, 128], BF16)
    Atok = [sbuf([128, 2, 128], BF16) for _ in range(2)]
    Vtok = [sbuf([128, 2, 128], BF16) for _ in range(4)]
    Utok = [sbuf([128, 2, 128], BF16) for _ in range(2)]
    for b_ in Atok + Vtok + Utok:
        P.op("pool", lambda e: e.memset(b_.ap, 0.0), [], [b_])
    BKtok = Ring(4, [128, 2, 128], BF16)
    ApT = Ring(4, [128, 128], BF16)
    Wp = Ring(2, [128, 128], BF16)
    stage = slots[0]
    Up4 = slots[11]
    cntr = dict(at=0, vt=0, ut=0, up=0)

    def scan_pre(l, c, qs2, ctx, E1, bpT, kpT, vb):
        for q in qs2:
            d = ctx[q] = {}
            d["at"] = Atok[cntr["at"] % 2]; cntr["at"] += 1
            d["vt"] = Vtok[cntr["vt"] % 4]; cntr["vt"] += 1
            d["upc"] = cntr["up"] % 4; cntr["up"] += 1
            d["bk"], d["arb"], d["aak"], d["ark"] = BKtok.get(), Arb.get(), Aak.get(), Ark.get()
            d["apT"], d["wp"] = ApT.get(), Wp.get()
        for q in qs2:
            d = ctx[q]
            qs = slice(q * 128, (q + 1) * 128)
            pst = ps_next()
            pv_ = pst.ap
            cp("dve", stage, stage[:, 0:128], AR, AR[:, q, 0, :])
            cp("dve", stage, stage[:, 128:256], bpT, bpT[:, qs])
            cp("dve", stage, stage[:, 256:384], kpT, kpT[:, qs])
            cp("dve", stage, stage[:, 384:512], vb, vb[:, qs])
            for i4 in range(4):
                tr(pst, pv_[:, i4 * 128:(i4 + 1) * 128], stage, stage[:, i4 * 128:(i4 + 1) * 128])
            at, vt, bk = d["at"], d["vt"], d["bk"]
            for h in range(2):
                cp("act", at, at[:, h, h * 64:(h + 1) * 64], pst, pv_[:, h * 64:(h + 1) * 64])
                cp("act", vt, vt[:, h, h * 64:(h + 1) * 64], pst, pv_[:, 384 + h * 64:384 + (h + 1) * 64])
            cp("act", bk, bk.ap, pst, pv_[:, 128:384].rearrange("p (a b) -> p a b", a=2))
            yield
        for q in qs2:
            d = ctx[q]
            ps1, ps2, ps3 = ps_next(), ps_next(), ps_next()
            arq = AR[:, q, :, :].rearrange("p a t -> p (a t)")
            for h in range(2):
                mm(ps1, ps1[:, h * 256:(h + 1) * 256], Bbd, Bbd[:, q, h, :], AR, arq)
                mm(ps2, ps2[:, h * 256:(h + 1) * 256], Kbd, Kbd[:, q, h, :], AR, arq)
            mm(ps3, ps3[:, 0:256], AR, AR[:, q, 0, :], Bbd, Bbd[:, q, :, :].rearrange("p h j -> p (h j)"))
            nx = NXr.get()
            arb, aak, ark, a0 = d["arb"], d["aak"], d["ark"], Ar.get()
            p1v = ps1.ap.rearrange("p (h a t) -> p h a t", h=2, a=2)
            p2v = ps2.ap.rearrange("p (h a t) -> p h a t", h=2, a=2)
            for h in range(2):
                tt("dve", NXn[id(nx)], nx[:, h, 0, :], ps1, p1v[:, h, 0, :], cst_b, cst_b[:, 3, :], ALU.mult)
                tt("dve", arb, arb[:, h, :], ps1, p1v[:, h, 1, :], cst_b, cst_b[:, 4, :], ALU.mult)
                tt("dve", aak, aak[:, h, :], ps2, p2v[:, h, 0, :], cst_b, cst_b[:, 3, :], ALU.mult)
                tt("dve", ark, ark[:, h, :], ps2, p2v[:, h, 1, :], cst_b, cst_b[:, 4, :], ALU.mult)
            tt("dve", a0, a0.ap, ps3, ps3[:, 0:256].rearrange("p (h t) -> p h t", h=2), mSL2, mSL2.ap, ALU.mult)
            tt("dve", NXx[id(nx)], nx[:, :, 1, :], NXn[id(nx)], nx[:, :, 0, :], id2, id2.ap, ALU.add)
            d["nx"], d["A"] = nx, a0
            yield
        for lev in range(7):
            last = (lev == 6)
            pss = {}
            for q in qs2:
                d = ctx[q]
                nx, A_i = d["nx"], d["A"]
                psn = ps_next()
                for h in range(2):
                    if lev == 0:
                        mm(psn, psn[:, h * 256:h * 256 + 128], A_i, A_i[:, h, :], NXn[id(nx)], nx[:, h, 0, :])
                    elif not last:
                        mm(psn, psn[:, h * 256:(h + 1) * 256], A_i, A_i[:, h, :], NXn[id(nx)], nx[:, h, :, :].rearrange("p a t -> p (a t)"), extra_r=[NXx[id(nx)]])
                    else:
                        mm(psn, psn[:, h * 256 + 128:(h + 1) * 256], A_i, A_i[:, h, :], NXx[id(nx)], nx[:, h, 1, :])
                psa = None
                if not last:
                    psa = ps_next()
                    for h in range(2):
                        mm(psa, psa[:, h * 128:(h + 1) * 128], NXn[id(nx)], nx[:, h, 0, :], A_i, A_i[:, h, :])
                pss[q] = (psn, psa)
            for q in qs2:
                d = ctx[q]
                nx = d["nx"]
                psn, psa = pss[q]
                pnv = psn.ap.rearrange("p (h a t) -> p h a t", h=2, a=2)
                nx2 = NXr.get()
                if lev == 0:
                    cp("act", NXn[id(nx2)], nx2[:, :, 0, :], psn, pnv[:, :, 0, :])
                    cp("dve", NXx[id(nx2)], nx2[:, :, 1, :], NXx[id(nx)], nx[:, :, 1, :])
                else:
                    if not last:
                        cp("dve", NXn[id(nx2)], nx2[:, :, 0, :], psn, pnv[:, :, 0, :])
                    tt("dve", NXx[id(nx2)], nx2[:, :, 1, :], psn, pnv[:, :, 1, :], NXx[id(nx)], nx[:, :, 1, :], ALU.add)
                if not last:
                    a2 = Ar.get()
                    cp("act", a2, a2.ap, psa, psa[:, 0:256].rearrange("p (h t) -> p h t", h=2))
                    d["A"] = a2
                d["nx"] = nx2
            yield
        for q in qs2:
            d = ctx[q]
            nx, at, vt, aak, apT, wp = d["nx"], d["at"], d["vt"], d["aak"], d["apT"], d["wp"]
            psw = ps_next()
            for h in range(2):
                mm(psw, psw[:, 0:128], at, at[:, h, :], NXx[id(nx)], nx[:, h, 1, :], start=(h == 0), stop=(h == 1))
            for h in range(2):
                mm(psw, psw[:, 128 + h * 64:128 + (h + 1) * 64], aak, aak[:, h, :], vt, vt[:, h, h * 64:(h + 1) * 64])
            cp("act", apT, apT.ap, psw, psw[:, 0:128])
            cp("act", wp, wp.ap, psw, psw[:, 128:256])
        yield
        for q in qs2:
            d = ctx[q]
            nx, wp = d["nx"], d["wp"]
            psu = ps_next()
            for h in range(2):
                mm(psu, psu[:, h * 64:(h + 1) * 64], NXx[id(nx)], nx[:, h, 1, :], wp, wp[:, h * 64:(h + 1) * 64])
            uc_ = d["upc"]
            cp("act", Up4, Up4[:, uc_ * 128:(uc_ + 1) * 128], psu, psu[:, 0:128])
        yield

    def scan_seq(l, c, qs2, ctx, E1, yT):
        sbd, sfd = Sb[l][c], Sf[l][c]
        sbd2 = sbd.ap.rearrange("p h v -> p (h v)")
        for q in qs2:
            d = ctx[q]
            qs = slice(q * 128, (q + 1) * 128)
            vt, bk, arb, ark, apT, uc_ = d["vt"], d["bk"], d["arb"], d["ark"], d["apT"], d["upc"]
            ut = Utok[cntr["ut"] % 2]; cntr["ut"] += 1
            ps_u = ps_next()
            mm(ps_u, ps_u[:, 0:128], apT, apT.ap, sbd, sbd2)
            for h in range(2):
                tt("dve", ut, ut[:, h, h * 64:(h + 1) * 64], ps_u, ps_u[:, h * 64:(h + 1) * 64],
                   Up4, Up4[:, uc_ * 128 + h * 64:uc_ * 128 + (h + 1) * 64], ALU.add)
            yield
            ps_y = ps_next()
            mm(ps_y, ps_y[:, 0:128], sbd, sbd2, AR, AR[:, q, 1, :], start=True, stop=False)
            for h in range(2):
                mm(ps_y, ps_y[:, 0:128], ut, ut[:, h, :], arb, arb[:, h, :], start=False, stop=False)
                mm(ps_y, ps_y[:, 0:128], vt, vt[:, h, :], ark, ark[:, h, :], start=False, stop=(h == 1))
            ps_s = ps_next()
            mm(ps_s, ps_s[:, 0:256], bk, bk[:, 0, :], ut, ut.ap.rearrange("p h v -> p (h v)"), start=True, stop=False)
            mm(ps_s, ps_s[:, 0:256], bk, bk[:, 1, :], vt, vt.ap.rearrange("p h v -> p (h v)"), start=False, stop=True)
            pc = E1[:, q * 128 + 127:q * 128 + 128]
            for h in range(2):
                hp = slice(h * 64, (h + 1) * 64)
                stt(sfd, sfd[hp, :], sfd, sfd[hp, :], pc[hp, :], ps_s, ps_s[hp, h * 128 + h * 64:h * 128 + (h + 1) * 64],
                    ALU.mult, ALU.add, extra_r=[E1])
            for h in range(2):
                hp = slice(h * 64, (h + 1) * 64)
                cp("act", sbd, sbd[hp, h, :], sfd, sfd[hp, :])
            cp("act", yT, yT[:, qs], ps_y, ps_y[:, 0:128])
            yield

    def scan_all(l, c, E1, bpT, kpT, vb, yT):
        ctxA, ctxB = {}, {}
        for _ in scan_pre(l, c, (0, 1), ctxA, E1, bpT, kpT, vb):
            pass
        gB = scan_pre(l, c, (2, 3), ctxB, E1, bpT, kpT, vb)
        gA = scan_seq(l, c, (0, 1), ctxA, E1, yT)
        aliveA = aliveB = True
        while aliveA or aliveB:
            if aliveB:
                try:
                    next(gB)
                except StopIteration:
                    aliveB = False
            if aliveA:
                try:
                    next(gA)
                except StopIteration:
                    aliveA = False
        for _ in scan_seq(l, c, (2, 3), ctxB, E1, yT):
            pass

    for it in range(NT):
        tsl = slice(it * T, (it + 1) * T)
        for c in range(8):
            P.dma("sp", xT[c].ap, xT_d[c * 128:(c + 1) * 128, tsl], reads=[Bdram_in], writes=[xT[c]])
        for l in range(2):
            rms_to(l, "g_norm", xT, hT, T)
            chk(f"rms{l}")
            hrhs = lambda j: hT
            car = carry[l]

            def shiftmix(ps, mi, npart=128, A=None):
                A = A or slots[11]
                pp = slice(0, npart)
                mu_c, omu_c = pcol(l, "mu", mi), pcol(l, "omu", mi)
                act(A, A[pp, :], ps, ps[pp, :], AF.Identity, extra_r=[pvs], scale=omu_c[pp, :])
                stt(A, A[pp, 1:T], ps, ps[pp, 0:T - 1], mu_c[pp, :], A, A[pp, 1:T], ALU.mult, ALU.add, extra_r=[pvs])
                stt(A, A[pp, 0:1], car, car[pp, mi:mi + 1], mu_c[pp, :], A, A[pp, 0:1], ALU.mult, ALU.add, extra_r=[pvs])
                cp("act", car, car[pp, mi:mi + 1], ps, ps[pp, T - 1:T])
                return A

            lob, vlo = lob_, vlo_

            def cons_lora(j, ps):
                if j == 0:
                    lo = shiftmix(ps, 24)
                    act(lob, lob[0:64, :], lo, lo[0:64, :], AF.Tanh)
                    cp("dve", lob, lob[64:128, :], lo, lo[64:128, :])
                elif l == 1:
                    v_ = shiftmix(ps, 25, 32)
                    cp("dve", vlo, vlo[0:32, :], v_, v_[0:32, :])
            proj(l, "lora", lambda j: hT if (j == 0 or l == 1) else None, cons_lora)

            chk(f"lora{l}")
            for c in range(8):
                got = {}

                def cons_rkvg(j, ps):
                    if j < 3:
                        got[j] = shiftmix(ps, j * 8 + c, A=slots[j])
                    else:
                        g = slots[3]
                        act(g, g.ap, ps, ps.ap, AF.Silu)
                        got[3] = g
                proj(l, f"rkvg{c}", hrhs, cons_rkvg)
                r_, k_, v_, gs = got[0], got[1], got[2], got[3]
                cs = slice(c * 128, (c + 1) * 128)
                psd, psa = ps_next(), ps_next()
                mm(psd, psd.ap, lor[l], lor[l][0:64, 0, cs], lob, lob[0:64, :])
                mm(psa, psa.ap, lor[l], lor[l][64:128, 0, cs], lob, lob[64:128, :])
                sgd, a_ = slots[4], slots[5]
                act(sgd, sgd.ap, psd, psd.ap, AF.Sigmoid, extra_r=[pvs], bias=pcol(l, "w0", c))
                act(a_, a_.ap, psa, psa.ap, AF.Sigmoid, extra_r=[pvs], bias=pcol(l, "a0", c))
                if l == 1:
                    psv = ps_next()
                    mm(psv, psv.ap, lor[l], lor[l][0:32, 1, cs], vlo, vlo[0:32, :])
                    gv = slots[7]
                    act(gv, gv.ap, psv, psv.ap, AF.Sigmoid, extra_r=[pvs], bias=pcol(l, "v0", c))
                    dd = slots[8]
                    tt("dve", dd, dd.ap, vf[c], vf[c].ap, v_, v_.ap, ALU.subtract)
                    tt("dve", dd, dd.ap, dd, dd.ap, gv, gv.ap, ALU.mult)
                    tt("dve", v_, v_.ap, v_, v_.ap, dd, dd.ap, ALU.add)
                else:
                    cp("act", vf[c], vf[c].ap, v_, v_.ap)
                dump(f"r{l}_{c}", r_)
                dump(f"k{l}_{c}", k_)
                dump(f"v{l}_{c}", v_)
                dump(f"a{l}_{c}", a_)
                kkr = slots[6]
                ts("dve", kkr, kkr.ap, k_, k_.ap, pcol(l, "k_k", c), None, ALU.mult, extra_r=[pvs])
                sq = tb.get()
                act(sq, sq.ap, kkr, kkr.ap, AF.Square)
                psn = ps_next()
                mm(psn, psn.ap, cst_b, bones, sq, sq.ap)
                nrm = slots[7]
                act(nrm, nrm.ap, psn, psn.ap, AF.Sqrt)
                ts("dve", nrm, nrm.ap, nrm, nrm.ap, 1e-12, None, ALU.max)
                P.op("dve", lambda e: e.reciprocal(out=nrm.ap, in_=nrm.ap), [nrm], [nrm])
                tt("dve", kkr, kkr.ap, kkr, kkr.ap, nrm, nrm.ap, ALU.mult)
                kk = kkr
                f_ = slots[7]
                ts("dve", f_, f_.ap, a_, a_.ap, pcol(l, "k_a", c), pcol(l, "omka", c), ALU.mult, ALU.add, extra_r=[pvs])
                tt("dve", k_, k_.ap, k_, k_.ap, f_, f_.ap, ALU.mult)
                k2 = k_
                rk = tb.get()
                stt(rk, rk.ap, r_, r_.ap, pcol(l, "r_k", c), k2, k2.ap, ALU.mult, ALU.mult, extra_r=[pvs])
                psb = ps_next()
                mm(psb, psb.ap, cst_b, bones, rk, rk.ap)
                bon = slots[8]
                tt("dve", bon, bon.ap, psb, psb.ap, v_, v_.ap, ALU.mult)
                cum = slots[7]
                P.op("dve", lambda e: e.tensor_tensor_scan(out=cum.ap, data0=rmask.ap, data1=sgd.ap, initial=0.0,
                                                           op0=ALU.mult, op1=ALU.add), [rmask, sgd], [cum])
                E1, E2, E3 = slots[9], slots[10], slots[11]
                act(E1, E1.ap, cum, cum.ap, AF.Exp, scale=-C0)
                act(E2, E2.ap, cum, cum.ap, AF.Exp, scale=C0)
                tt("dve", sgd, sgd.ap, cum, cum.ap, sgd, sgd.ap, ALU.subtract)
                act(E3, E3.ap, sgd, sgd.ap, AF.Exp, scale=-C0)
                dump(f"E1{l}_{c}", E1)
                tt("dve", AR, AR[:, :, 1, :], r_, r_.ap.rearrange("p (q t) -> p q t", q=4), E1, E1.ap.rearrange("p (q t) -> p q t", q=4), ALU.mult)
                stt(AR, AR[:, :, 0, :], kk, kk.ap.rearrange("p (q t) -> p q t", q=4), -1.0, E3, E3.ap.rearrange("p (q t) -> p q t", q=4), ALU.mult, ALU.mult)
                bt, kt = slots[0], slots[11]
                tt("dve", bt, bt.ap, kk, kk.ap, a_, a_.ap, ALU.mult)
                tt("dve", bt, bt.ap, bt, bt.ap, E2, E2.ap, ALU.mult)
                tt("dve", kt, kt.ap, k2, k2.ap, E2, E2.ap, ALU.mult)
                for h in range(2):
                    hp = slice(h * 64, (h + 1) * 64)
                    cp("act", Bbd, Bbd[hp, :, h, :], bt, bt[hp, :].rearrange("p (q t) -> p q t", q=4))
                    cp("dve", Kbd, Kbd[hp, :, h, :], kt, kt[hp, :].rearrange("p (q t) -> p q t", q=4))
                bpT, kpT, vb = tb.get(), tb.get(), tb.get()
                for q in range(4):
                    qs = slice(q * 128, (q + 1) * 128)
                    pc = E1[:, q * 128 + 127:q * 128 + 128]
                    act(bpT, bpT[:, qs], bt, bt[:, qs], AF.Identity, extra_r=[E1], scale=pc)
                    act(kpT, kpT[:, qs], kt, kt[:, qs], AF.Identity, extra_r=[E1], scale=pc)
                cp("act", vb, vb.ap, v_, v_.ap)
                chk(f"prep{l}_{c}")
                yT = slots[1]
                scan_all(l, c, E1, bpT, kpT, vb, yT)
                chk(f"scanend{l}_{c}")
                dump(f"y{l}_{c}", yT)
                yb_, ysq = tb.get(), tb.get()
                cp("dve", yb_, yb_.ap, yT, yT.ap)
                act(ysq, ysq.ap, yT, yT.ap, AF.Square)
                p1, p2 = ps_next(), ps_next()
                mm(p1, p1.ap, cst_b, bones, yb_, yb_.ap)
                mm(p2, p2.ap, cst_b, bones, ysq, ysq.ap)
                mean, var = slots[5], slots[6]
                act(mean, mean.ap, p1, p1.ap, AF.Identity, scale=1.0 / 64)
                tt("dve", var, var.ap, mean, mean.ap, mean, mean.ap, ALU.mult)
                stt(var, var.ap, p2, p2.ap, 1.0 / 64, var, var.ap, ALU.mult, ALU.subtract)
                act(var, var.ap, var, var.ap, AF.Sqrt, extra_r=[epsc], bias=epsc[:, 1:2])
                P.op("dve", lambda e: e.reciprocal(out=var.ap, in_=var.ap), [var], [var])
                tt("dve", yT, yT.ap, yT, yT.ap, mean, mean.ap, ALU.subtract)
                tt("dve", yT, yT.ap, yT, yT.ap, var, var.ap, ALU.mult)
                act(yT, yT.ap, yT, yT.ap, AF.Identity, extra_r=[pvs], scale=pcol(l, "gn_g", c), bias=pcol(l, "gn_b", c))
                tt("dve", yT, yT.ap, yT, yT.ap, bon, bon.ap, ALU.add)
                tt("dve", OG[c], OG[c].ap, yT, yT.ap, gs, gs.ap, ALU.mult)
                dump(f"og{l}_{c}", OG[c])

            def branch_out(pn, first, bias_name=None):
                for j in range(4):
                    tmpy = {}

                    def cons(jj, ps, j=j):
                        if jj < 2:
                            t_ = tf.get()
                            if bias_name is None:
                                cp("act", t_, t_.ap, ps, ps.ap)
                            else:
                                act(t_, t_.ap, ps, ps.ap, AF.Identity, extra_r=[pvs], bias=pcol(l, bias_name, 2 * j + jj))
                            tmpy[jj] = t_
                        else:
                            cidx = 2 * j + (jj - 2)
                            sg = tf.get()
                            act(sg, sg.ap, ps, ps.ap, AF.Sigmoid)
                            yb = tmpy[jj - 2]
                            if first:
                                tt("dve", yacc[cidx], yacc[cidx].ap, sg, sg.ap, yb, yb.ap, ALU.mult)
                            else:
                                tt("dve", sg, sg.ap, sg, sg.ap, yb, yb.ap, ALU.mult)
                                tt("dve", yacc[cidx], yacc[cidx].ap, yacc[cidx], yacc[cidx].ap, sg, sg.ap, ALU.add)
                    proj(l, f"{pn}{j}", lambda jj: OG if jj < 2 else hT, cons)

            chk(f"rwkv{l}")
            branch_out("pr", True)
            chk(f"pr{l}")
            dump(f"yacc0_{l}", yacc[0])

            uc = big8
            s1, s2 = pin[0], pin[1]
            dg = dg_keep
            for j in range(4):
                def cons_glu(jj, ps, j=j):
                    c = 2 * j + jj // 2
                    if jj % 2 == 0:
                        cons_glu.pa = ps
                        return
                    gb = tf.get()
                    act(gb, gb.ap, ps, ps.ap, AF.Sigmoid, extra_r=[pvs], bias=pcol(l, "b_glu", 8 + c))
                    pa = cons_glu.pa
                    u = ubr.get()
                    cp("act", u, u[:, 0:30], halo[l], halo[l][:, c, :])
                    stt(u, u[:, 30:30 + T], pa, pa.ap, pcol(l, "b_glu", c), gb, gb.ap, ALU.add, ALU.mult, extra_r=[pvs])
                    for tp in range(31):
                        if tp % 3 == 2:
                            act(dgT[tp], dgT[tp].ap, cst_b, ident, AF.Identity, extra_r=[pvs], scale=pcol(l, "w_dw", c * 31 + tp))
                        else:
                            ts("dve", dgT[tp], dgT[tp].ap, cst_b, ident, pcol(l, "w_dw", c * 31 + tp), None, ALU.mult, extra_r=[pvs])
                    pc_ = ps_next()
                    for tp in range(31):
                        mm(pc_, pc_.ap, dgT[tp], dgT[tp].ap, u, u[:, tp:tp + T], start=(tp == 0), stop=(tp == 30))
                    act(uc[c], uc[c].ap, pc_, pc_.ap, AF.Identity, extra_r=[pvs], bias=pcol(l, "b_dw", c))
                    cp("act", halo[l], halo[l][:, c, :], u, u[:, T:T + 30])
                    ucb, ucs = tb.get(), tb.get()
                    cp("dve", ucb, ucb.ap, uc[c], uc[c].ap)
                    act(ucs, ucs.ap, uc[c], uc[c].ap, AF.Square)
                    mm(s1, s1.ap, cst_b, ones, ucb, ucb.ap, start=(c == 0), stop=(c == 7))
                    mm(s2, s2.ap, cst_b, ones, ucs, ucs.ap, start=(c == 0), stop=(c == 7))
                proj(l, f"glu{j}", hrhs, cons_glu)
            mean, var = slots[10], slots[11]
            act(mean, mean.ap, s1, s1.ap, AF.Identity, scale=1.0 / D)
            tt("dve", var, var.ap, mean, mean.ap, mean, mean.ap, ALU.mult)
            stt(var, var.ap, s2, s2.ap, 1.0 / D, var, var.ap, ALU.mult, ALU.subtract)
            act(var, var.ap, var, var.ap, AF.Sqrt, extra_r=[epsc], bias=epsc[:, 2:3])
            P.op("dve", lambda e: e.reciprocal(out=var.ap, in_=var.ap), [var], [var])
            dump(f"uc{l}_0", uc[0])
            for j in range(2):
                def cons_cg(jj, ps, j=j):
                    c = 4 * j + jj
                    cg = tf.get()
                    act(cg, cg.ap, ps, ps.ap, AF.Silu)
                    t_ = uc[c]
                    tt("dve", t_, t_.ap, t_, t_.ap, mean, mean.ap, ALU.subtract)
                    tt("dve", t_, t_.ap, t_, t_.ap, var, var.ap, ALU.mult)
                    act(t_, t_.ap, t_, t_.ap, AF.Identity, extra_r=[pvs], scale=pcol(l, "ln_g", c), bias=pcol(l, "ln_b", c))
                    act(t_, t_.ap, t_, t_.ap, AF.Silu)
                    tt("dve", OG[c], OG[c].ap, t_, t_.ap, cg, cg.ap, ALU.mult)
                proj(l, f"cg{j}", hrhs, cons_cg)
            dump(f"ug{l}_0", OG[0])
            chk(f"conv{l}")
            branch_out("pc", False, "b_pc")
            chk(f"pc{l}")
            dump(f"yacc1_{l}", yacc[0])

            qT = OG
            for j in range(2):
                def cons_q(jj, ps, j=j):
                    c = 4 * j + jj
                    act(qT[c], qT[c].ap, ps, ps.ap, AF.Identity, scale=1.0 / 16.0)
                proj(l, f"q{j}", hrhs, cons_q)
            att = big8
            prT = prT_keep
            small = small_keep
            for hm in range(4):
                pt = prT[hm % 2]
                for sbk in range(4):
                    ss = slice(sbk * 128, (sbk + 1) * 128)
                    psc = ps_next()
                    for dc in range(2):
                        mm(psc, psc[:, 0:NMEM], qT[2 * hm + dc], qT[2 * hm + dc][:, ss], kmT[l][2 * hm + dc], kmT[l][2 * hm + dc].ap,
                           start=(dc == 0), stop=(dc == 1))
                    P.op("dve", lambda e: e.tensor_reduce(out=small[:, 0:1], in_=psc[:, 0:NMEM], axis=AX.X, op=ALU.max), [psc], [small])
                    ts("dve", small, small[:, 1:2], small, small[:, 0:1], -1.0, None, ALU.mult)
                    ex = tf.get()
                    P.op("act", lambda e: e.activation(out=ex[:, 0:NMEM], in_=psc[:, 0:NMEM], func=AF.Exp, bias=small[:, 1:2],
                                                       accum_out=small[:, 2:3]), [psc, small], [ex, small])
                    P.op("dve", lambda e: e.reciprocal(out=small[:, 3:4], in_=small[:, 2:3]), [small], [small])
                    pb = tf.get()
                    ts("dve", pb, pb[:, 0:NMEM], ex, ex[:, 0:NMEM], small[:, 3:4], None, ALU.mult, extra_r=[small])
                    ptp = ps_next()
                    pv_ = ptp.ap
                    for mb in range(2):
                        tr(ptp, pv_[:, mb * 128:(mb + 1) * 128], pb, pb[:, mb * 128:(mb + 1) * 128])
                    for mb in range(2):
                        cp("act", pt[mb], pt[mb][:, ss], ptp, pv_[:, mb * 128:(mb + 1) * 128])
                for dc in range(2):
                    c = 2 * hm + dc
                    pa_ = ps_next()
                    for mb in range(2):
                        mm(pa_, pa_.ap, vmt[l][mb], vmt[l][mb][:, c * 128:(c + 1) * 128], pt[mb], pt[mb].ap, start=(mb == 0), stop=(mb == 1))
                    cp("act", att[c], att[c].ap, pa_, pa_.ap)
            dump(f"att{l}_0", att[0])
            for j in range(2):
                def cons_mg(jj, ps, j=j):
                    c = 4 * j + jj
                    mg = tf.get()
                    act(mg, mg.ap, ps, ps.ap, AF.Silu)
                    tt("dve", OG[c], OG[c].ap, att[c], att[c].ap, mg, mg.ap, ALU.mult)
                proj(l, f"mg{j}", hrhs, cons_mg)
            chk(f"mem{l}")
            branch_out("pm", False)
            chk(f"pm{l}")
            dump(f"yacc2_{l}", yacc[0])

            for c in range(8):
                cp("act", OG[c], OG[c].ap, yacc[c], yacc[c].ap)
            for j in range(2):
                def cons_o(jj, ps, j=j):
                    c = 4 * j + jj
                    tt("dve", xT[c], xT[c].ap, xT[c], xT[c].ap, ps, ps.ap, ALU.add)
                proj(l, f"wo{j}", lambda jj: OG, cons_o)
            dump(f"x{l}_0", xT[0])

        ps = pin[0]
        for c in range(8):
            sq = tb.get()
            act(sq, sq.ap, xT[c], xT[c].ap, AF.Square)
            mm(ps, ps.ap, cst_b, ones, sq, sq.ap, start=(c == 0), stop=(c == 7))
        sd, rs = tf.get(), tf.get()
        act(sd, sd.ap, ps, ps.ap, AF.Sqrt, extra_r=[epsc], scale=1.0 / D, bias=epsc[:, 0:1])
        P.op("dve", lambda e: e.reciprocal(out=rs.ap, in_=sd.ap), [sd], [rs])
        for c in range(8):
            o_ = tf.get()
            stt(o_, o_.ap, xT[c], xT[c].ap, pcol(0, "g_final", c), rs, rs.ap, ALU.mult, ALU.mult, extra_r=[pvs])
            P.dma("sp", outT_d[c * 128:(c + 1) * 128, tsl], o_.ap, reads=[o_], writes=[Bout])


_CACHE = {}


def kernel(**inp):
    inp = {k: np.asarray(v) for k, v in inp.items()}
    pv, wbig, lora, cst, rm = host_prep(inp)
    x, mem = inp["x"], inp["mem"]
    B = x.shape[0]
    nc = bass.Bass("TRN2", target_bir_lowering=False)
    build(nc)
    in_maps = []
    for b in range(B):
        in_maps.append({"xT": np.ascontiguousarray(x[b].T), "memT": np.ascontiguousarray(mem[b].T),
                        "pv": pv, "wbig": wbig, "lora": lora, "cst": cst, "rm": rm})
    res = run_bass_kernel_spmd(nc, in_maps, core_ids=list(range(B)))
    out = np.stack([np.ascontiguousarray(r["outT"].T) for r in res.results], axis=0)
    return out.astype(np.float32)
```

```python
import numpy as np
import concourse.bass as bass
import concourse.mybir as mybir
from concourse.bass_utils import run_bass_kernel_spmd

F32 = mybir.dt.float32
BF16 = mybir.dt.bfloat16
AF = mybir.ActivationFunctionType
ALU = mybir.AluOpType
AX = mybir.AxisListType
NDS = 24

D = 1024
SEQ = 4096
T = 512
NMEM = 256
KC = 8
C0 = float(np.exp(-0.5))


class Buf:
    __slots__ = ("ap", "w", "r")

    def __init__(self, ap):
        self.ap = ap
        self.w = None
        self.r = {}

    def __getitem__(self, k):
        return self.ap[k]


class Prog:
    def __init__(self, nc):
        self.nc = nc
        self.eng = dict(pe=nc.tensor, dve=nc.vector, act=nc.scalar, pool=nc.gpsimd, sp=nc.sync)
        self.esem = {k: nc.alloc_semaphore("es_" + k) for k in self.eng}
        self.ecnt = {k: 0 for k in self.eng}
        self.seen = {k: {} for k in self.eng}
        self.dsem, self.dtgt, self.dnext = {}, {}, {}
        self.ninst = 0

    def _wait(self, e, ev):
        sem, key, val = ev
        if key == ("e", e) and e == "pe":
            return
        if self.seen[e].get(key, 0) >= val:
            return
        self.eng[e].wait_ge(sem, val)
        self.seen[e][key] = val

    def _deps(self, e, reads, writes):
        for b in reads:
            if b.w is not None:
                self._wait(e, b.w)
        me = ("e", e)
        for b in writes:
            if b.w is not None and b.w[1] != me:
                self._wait(e, b.w)
            for ev in b.r.values():
                if ev[1] != me:
                    self._wait(e, ev)

    def _record(self, ev, reads, writes):
        for b in reads:
            b.r[ev[1]] = ev
        for b in writes:
            b.w = ev
            b.r = {}

    def op(self, e, fn, reads=(), writes=()):
        self._deps(e, reads, writes)
        inst = fn(self.eng[e])
        self.ecnt[e] += 1
        inst.then_inc(self.esem[e], 1)
        self._record((self.esem[e], ("e", e), self.ecnt[e]), reads, writes)
        self.ninst += 1

    def dma(self, q, out_ap, in_ap, reads=(), writes=(), **kw):
        if q not in self.dsem:
            self.dsem[q] = [self.nc.alloc_semaphore(f"ds_{q}{i}") for i in range(NDS)]
            self.dtgt[q] = [0] * NDS
            self.dnext[q] = 0
        j = self.dnext[q]
        self.dnext[q] = (j + 1) % NDS
        key = ("d", q, j)
        if self.dtgt[q][j] > 0:
            self._wait(q, (self.dsem[q][j], key, self.dtgt[q][j]))
        self._deps(q, reads, writes)
        inst = self.eng[q].dma_start(out=out_ap, in_=in_ap, **kw)
        self.dtgt[q][j] += 16
        inst.then_inc(self.dsem[q][j], 16)
        self._record((self.dsem[q][j], key, self.dtgt[q][j]), reads, writes)
        self.ninst += 1

    def finish(self, e="sp"):
        for q in self.dsem:
            for j in range(NDS):
                if self.dtgt[q][j] > 0:
                    self._wait(e, (self.dsem[q][j], ("d", q, j), self.dtgt[q][j]))
        for k in self.eng:
            if k != e and self.ecnt[k] > 0:
                self._wait(e, (self.esem[k], ("e", k), self.ecnt[k]))


PV = {}
_o = 0
for _n, _w in [("g_norm", 8), ("mu", 26), ("w0", 8), ("a0", 8), ("k_k", 8), ("k_a", 8), ("r_k", 8),
               ("gn_g", 8), ("gn_b", 8), ("v0", 8), ("b_glu", 16), ("w_dw", 248), ("b_dw", 8),
               ("ln_g", 8), ("ln_b", 8), ("b_pc", 8), ("g_mem", 8), ("g_final", 8), ("omu", 26), ("omka", 8)]:
    PV[_n] = _o
    _o += _w
NPV = _o

WCOL = {}
_o = 0
for _n, _w in [("lora", 256)] + [(f"rkvg{c}", 512) for c in range(8)] + \
        [(f"pr{j}", 512) for j in range(4)] + [(f"glu{j}", 512) for j in range(4)] + \
        [(f"cg{j}", 512) for j in range(2)] + [(f"pc{j}", 512) for j in range(4)] + \
        [(f"q{j}", 512) for j in range(2)] + [(f"mg{j}", 512) for j in range(2)] + \
        [(f"pm{j}", 512) for j in range(4)] + [(f"wo{j}", 512) for j in range(2)] + \
        [(f"kv{j}", 512) for j in range(4)]:
    WCOL[_n] = (_o, _w)
    _o += _w
TOTC = _o


def _fm(v):
    return np.ascontiguousarray(v.reshape(8, 128).T)


def host_prep(inp):
    f = np.float32
    L = 2
    pv = np.zeros((L, 128, NPV), f)
    wbig = np.zeros((L, 128, KC, TOTC), f)
    lora = np.zeros((L, 128, 2, 1024), f)
    for l in range(L):
        def put(name, arr):
            pv[l][:, PV[name]:PV[name] + arr.shape[1]] = arr
        put("g_norm", _fm(inp["g_norm"][l]))
        mu = inp["mu_shift"][l]
        mucols = np.zeros((128, 26), f)
        mucols[:, 0:25] = mu.reshape(25, 128).T
        if l >= 1:
            mucols[0:32, 25] = inp["mu_vres"][l - 1]
        put("mu", mucols)
        for n in ["w0", "a0", "k_k", "k_a", "gn_g", "gn_b", "b_dw", "ln_g", "ln_b", "g_mem"]:
            src = {"g_mem": "g_mem_norm"}.get(n, n)
            put(n, _fm(inp[src][l]))
        put("r_k", _fm(inp["r_k"][l].reshape(-1)))
        put("b_pc", _fm(inp["b_proj_conv"][l]))
        if l >= 1:
            put("v0", _fm(inp["v0"][l - 1]))
        put("b_glu", np.ascontiguousarray(inp["b_glu"][l].reshape(16, 128).T))
        wd = inp["w_dw"][l]
        put("w_dw", np.ascontiguousarray(wd.reshape(31, 8, 128).transpose(2, 1, 0).reshape(128, 248)))
        put("g_final", _fm(inp["g_final"]))
        w_in = inp["w_in"][l]
        Wc = np.zeros((D, TOTC), f)

        def setc(name, off, arr):
            o, w = WCOL[name]
            Wc[:, o + off:o + off + arr.shape[1]] = arr
        setc("lora", 0, w_in[:, 3072:3200])
        if l >= 1:
            setc("lora", 128, inp["w_vres_down"][l - 1])
        for c in range(8):
            setc(f"rkvg{c}", 0, w_in[:, c * 128:(c + 1) * 128])
            setc(f"rkvg{c}", 128, w_in[:, 1024 + c * 128:1024 + (c + 1) * 128])
            setc(f"rkvg{c}", 256, w_in[:, 2048 + c * 128:2048 + (c + 1) * 128])
            setc(f"rkvg{c}", 384, w_in[:, 3200 + c * 128:3200 + (c + 1) * 128])
        for br, (pn, wp) in enumerate([("pr", inp["w_proj_rwkv"][l]), ("pc", inp["w_proj_conv"][l]),
                                       ("pm", inp["w_proj_mem"][l])]):
            for j in range(4):
                setc(f"{pn}{j}", 0, wp[:, j * 256:(j + 1) * 256])
                mo = 9344 + br * 1024 + j * 256
                setc(f"{pn}{j}", 256, w_in[:, mo:mo + 256])
        for j in range(4):
            for i in range(2):
                c = 2 * j + i
                setc(f"glu{j}", i * 256, w_in[:, 4224 + c * 128:4224 + (c + 1) * 128])
                setc(f"glu{j}", i * 256 + 128, w_in[:, 5248 + c * 128:5248 + (c + 1) * 128])
        for j in range(2):
            setc(f"cg{j}", 0, w_in[:, 6272 + j * 512:6272 + (j + 1) * 512])
            setc(f"q{j}", 0, w_in[:, 7296 + j * 512:7296 + (j + 1) * 512])
            setc(f"mg{j}", 0, w_in[:, 8320 + j * 512:8320 + (j + 1) * 512])
            setc(f"wo{j}", 0, inp["w_out"][l][:, j * 512:(j + 1) * 512])
        for j in range(4):
            setc(f"kv{j}", 0, inp["w_mem_kv"][l][:, j * 512:(j + 1) * 512])
        wbig[l] = Wc.reshape(KC, 128, TOTC).transpose(1, 0, 2)
        lora[l][0:64, 0] = inp["w_decay_up"][l]
        lora[l][64:128, 0] = inp["w_aaa_up"][l]
        if l >= 1:
            lora[l][0:32, 1] = inp["w_vres_up"][l - 1]
    cst = np.zeros((128, 8, 128), f)
    i = np.arange(128)
    cst[:, 0] = np.eye(128)
    cst[:, 1] = 1.0
    cst[:, 2] = (i[:, None] // 64 == i[None, :] // 64)
    cst[:, 3] = (i[:, None] < i[None, :])
    cst[:, 4] = (i[:, None] <= i[None, :])
    cst[:, 5] = (i[:, None] > i[None, :])
    rm = np.ones((128, 512), f)
    rm[:, 0::128] = 0.0
    return pv, wbig, lora, cst, rm


class _Stop(Exception):
    pass


def build(nc, NT=SEQ // T, dbg_names=(), stop_after=None):
    P = Prog(nc)
    try:
        _build(nc, P, NT, dbg_names, stop_after)
    except _Stop:
        pass
    P.finish()
    return P


def _build(nc, P, NT, dbg_names, stop_after):
    def chk(tag):
        if tag == stop_after:
            raise _Stop()
    dt = nc.dram_tensor
    xT_d = dt("xT", [D, SEQ], F32, kind="ExternalInput").ap()
    memT_d = dt("memT", [D, NMEM], F32, kind="ExternalInput").ap()
    pv_d = dt("pv", [2, 128, NPV], F32, kind="ExternalInput").ap()
    wbig_d = dt("wbig", [2, 128, KC, TOTC], F32, kind="ExternalInput").ap()
    lora_d = dt("lora", [2, 128, 2, 1024], F32, kind="ExternalInput").ap()
    cst_d = dt("cst", [128, 8, 128], F32, kind="ExternalInput").ap()
    rm_d = dt("rm", [128, 512], F32, kind="ExternalInput").ap()
    outT_d = dt("outT", [D, SEQ], F32, kind="ExternalOutput").ap()
    dbg_d = None
    if dbg_names:
        dbg_d = dt("dbg", [len(dbg_names), 128, 512], F32, kind="ExternalOutput").ap()
    Bdram_in = Buf(None)
    Bout = Buf(None)
    cnt = [0]

    def sb(shape, dtype, name=None):
        cnt[0] += 1
        return nc.alloc_sbuf_tensor(name or f"t{cnt[0]}", list(shape), dtype)

    def sbuf(shape, dtype):
        return Buf(sb(shape, dtype).ap())

    big8 = [sbuf([128, T], F32) for _ in range(8)]
    cst_f = Buf(big8[0].ap.rearrange("p (a b) -> p a b", a=4))
    cst_f2 = Buf(big8[1].ap.rearrange("p (a b) -> p a b", a=4))
    P.dma("sp", cst_f.ap, cst_d[:, 0:4, :], reads=[Bdram_in], writes=[cst_f, big8[0]])
    P.dma("sp", cst_f2.ap, cst_d[:, 4:8, :], reads=[Bdram_in], writes=[cst_f2, big8[1]])
    cst_b = sbuf([128, 8, 128], BF16)
    P.op("dve", lambda e: e.tensor_copy(out=cst_b[:, 0:4, :], in_=cst_f.ap), [cst_f, big8[0]], [cst_b])
    P.op("dve", lambda e: e.tensor_copy(out=cst_b[:, 4:8, :], in_=cst_f2.ap), [cst_f2, big8[1]], [cst_b])
    ident, ones, bones = cst_b[:, 0, :], cst_b[:, 1, :], cst_b[:, 2, :]
    identf = sbuf([128, 128], F32)
    P.op("dve", lambda e: e.tensor_copy(out=identf.ap, in_=cst_f[:, 0, :]), [cst_f, big8[0]], [identf])
    m12 = Buf(cst_b[:, 3:5, :])
    m12.w = None
    mSL2 = sbuf([128, 2, 128], BF16)
    id2 = sbuf([128, 2, 128], BF16)
    for h in range(2):
        P.op("dve", lambda e: e.tensor_copy(out=mSL2[:, h, :], in_=cst_b[:, 5, :]), [cst_b], [mSL2])
        P.op("dve", lambda e: e.tensor_copy(out=id2[:, h, :], in_=cst_b[:, 0, :]), [cst_b], [id2])
    rmf = Buf(big8[2].ap)
    P.dma("sp", rmf.ap, rm_d, reads=[Bdram_in], writes=[big8[2]])
    rmask = sbuf([128, 512], BF16)
    P.op("dve", lambda e: e.tensor_copy(out=rmask.ap, in_=big8[2].ap), [big8[2]], [rmask])
    pvs = sbuf([128, 2, NPV], F32)
    for l in range(2):
        P.dma("sp", pvs[:, l, :], pv_d[l], reads=[Bdram_in], writes=[pvs])
    for l in range(2):
        for (src, dst, w) in [("mu", "omu", 26), ("k_a", "omka", 8)]:
            P.op("dve", lambda e: e.tensor_scalar(out=pvs[:, l, PV[dst]:PV[dst] + w], in0=pvs[:, l, PV[src]:PV[src] + w],
                                                  scalar1=-1.0, scalar2=1.0, op0=ALU.mult, op1=ALU.add), [pvs], [pvs])
    epsc = sbuf([128, 4], F32)
    for i, v in enumerate([1e-6, 64e-5, 1e-5, 0.0]):
        P.op("dve", lambda e: e.memset(epsc[:, i:i + 1], v), [], [epsc])

    def pcol(l, name, c=0):
        o = PV[name] + c
        return pvs[:, l, o:o + 1]

    lor = [sbuf([128, 2, 1024], BF16) for _ in range(2)]
    for l in range(2):
        P.dma("pool", lor[l].ap, lora_d[l], reads=[Bdram_in], writes=[lor[l]])

    banks = [Buf(nc.alloc_psum_tensor(f"ps{i}", [128, 512], F32).ap()) for i in range(8)]
    ring = banks[:6]
    pin = banks[6:]
    rp = [0]

    def ps_next():
        b = ring[rp[0] % len(ring)]
        rp[0] += 1
        return b

    def bfv(b):
        return b.ap.bitcast(BF16)

    class Ring:
        def __init__(self, n, shape, dtype):
            self.b = [sbuf(shape, dtype) for _ in range(n)]
            self.i = 0

        def get(self):
            b = self.b[self.i % len(self.b)]
            self.i += 1
            return b

    slots = [sbuf([128, 512], F32) for _ in range(12)]
    tf = Ring(0, [128, 512], F32)
    tf.b = slots[0:10]
    tb = Ring(6, [128, 512], BF16)
    lob_, vlo_ = sbuf([128, 512], BF16), sbuf([128, 512], BF16)
    wring = Ring(2, [128, KC, 512], BF16)

    xT = [sbuf([128, T], F32) for _ in range(8)]
    vf = [sbuf([128, T], F32) for _ in range(8)]
    hT = [sbuf([128, T], BF16) for _ in range(8)]
    OG = [sbuf([128, T], BF16) for _ in range(8)]
    yacc = [sbuf([128, T], F32) for _ in range(8)]
    carry = [sbuf([128, 26], F32) for _ in range(2)]
    Sf = [[sbuf([128, 64], F32) for _ in range(8)] for _ in range(2)]
    Sb = [[sbuf([128, 2, 64], BF16) for _ in range(8)] for _ in range(2)]
    halo = [sbuf([128, 8, 30], BF16) for _ in range(2)]
    ubr = Ring(2, [128, 30 + T], BF16)
    dg_keep = sbuf([128, 31, 128], BF16)
    dgT = [Buf(dg_keep[:, tp_, :]) for tp_ in range(31)]
    prT_keep = [[sbuf([128, T], BF16) for _ in range(2)] for _ in range(2)]
    small_keep = sbuf([128, 8], F32)
    kmT = [[sbuf([128, NMEM], BF16) for _ in range(8)] for _ in range(2)]
    vmt = [[sbuf([128, D], BF16) for _ in range(2)] for _ in range(2)]
    for l in range(2):
        P.op("pool", lambda e: e.memset(carry[l].ap, 0.0), [], [carry[l]])
        P.op("pool", lambda e: e.memset(halo[l].ap, 0.0), [], [halo[l]])
        for c in range(8):
            P.op("pool", lambda e: e.memset(Sf[l][c].ap, 0.0), [], [Sf[l][c]])
            P.op("pool", lambda e: e.memset(Sb[l][c].ap, 0.0), [], [Sb[l][c]])

    dbg_list = list(dbg_names)
    dbg_buf = sbuf([128, 512], F32) if dbg_names else None

    def dump(name, b, ap=None):
        if name in dbg_list:
            i = dbg_list.index(name)
            a = b.ap if ap is None else ap
            t = dbg_buf
            P.op("dve", lambda e: e.tensor_copy(out=t[:, 0:a.shape[-1]], in_=a), [b], [t])
            P.dma("sp", dbg_d[i][0:a.shape[0], 0:a.shape[-1]], t[0:a.shape[0], 0:a.shape[-1]], reads=[t], writes=[Bout])
            dbg_list[i] = None

    def act(out_b, out_ap, in_b, in_ap, func, extra_r=(), **kw):
        P.op("act", lambda e: e.activation(out=out_ap, in_=in_ap, func=func, **kw), [in_b] + list(extra_r), [out_b])

    def tt(eng, out_b, out_ap, a_b, a_ap, b_b, b_ap, op):
        P.op(eng, lambda e: e.tensor_tensor(out=out_ap, in0=a_ap, in1=b_ap, op=op), [a_b, b_b], [out_b])

    def stt(out_b, out_ap, a_b, a_ap, scalar, b_b, b_ap, op0, op1, extra_r=()):
        P.op("dve", lambda e: e.scalar_tensor_tensor(out=out_ap, in0=a_ap, scalar=scalar, in1=b_ap, op0=op0, op1=op1),
             [a_b, b_b] + list(extra_r), [out_b])

    def ts(eng, out_b, out_ap, a_b, a_ap, s1, s2, op0, op1=None, extra_r=()):
        if op1 is None:
            P.op(eng, lambda e: e.tensor_scalar(out=out_ap, in0=a_ap, scalar1=s1, scalar2=None, op0=op0),
                 [a_b] + list(extra_r), [out_b])
        else:
            P.op(eng, lambda e: e.tensor_scalar(out=out_ap, in0=a_ap, scalar1=s1, scalar2=s2, op0=op0, op1=op1),
                 [a_b] + list(extra_r), [out_b])

    def cp(eng, out_b, out_ap, in_b, in_ap):
        if eng == "act":
            P.op("act", lambda e: e.copy(out=out_ap, in_=in_ap), [in_b], [out_b])
        elif eng == "dve":
            P.op(eng, lambda e: e.tensor_scalar(out=out_ap, in0=in_ap, scalar1=1.0, scalar2=None, op0=ALU.mult), [in_b], [out_b])
        else:
            P.op(eng, lambda e: e.tensor_copy(out=out_ap, in_=in_ap), [in_b], [out_b])

    def mm(out_b, out_ap, l_b, l_ap, r_b, r_ap, start=True, stop=True, extra_r=()):
        P.op("pe", lambda e: e.matmul(out_ap, lhsT=l_ap, rhs=r_ap, start=start, stop=stop), [l_b, r_b] + list(extra_r), [out_b])

    def tr(out_b, out_ap, in_b, in_ap):
        P.op("pe", lambda e: e.transpose(out_ap, in_ap, identf.ap), [in_b, identf], [out_b])

    def wload(l, name):
        o, w = WCOL[name]
        wb = wring.get()
        P.dma("pool", wb[:, :, 0:w], wbig_d[l][:, :, o:o + w], reads=[Bdram_in], writes=[wb])
        return wb

    def proj(l, name, rhs_for_chunk, consume):
        o, w = WCOL[name]
        wb = wload(l, name)
        for j in range(w // 128):
            rhs = rhs_for_chunk(j)
            if rhs is None:
                continue
            ps = ps_next()
            for kc in range(KC):
                mm(ps, ps.ap, wb, wb[:, kc, j * 128:(j + 1) * 128], rhs[kc], rhs[kc].ap, start=(kc == 0), stop=(kc == KC - 1))
            consume(j, ps)

    def bcast_stat(src_list, src_aps, scale, epscol):
        raise NotImplementedError

    def rms_to(l, gname, src, dst, n):
        ps = pin[0]
        for c in range(8):
            sq = tb.get()
            act(sq, sq[:, 0:n], src[c], src[c][:, 0:n], AF.Square)
            mm(ps, ps[:, 0:n], cst_b, ones, sq, sq[:, 0:n], start=(c == 0), stop=(c == 7))
        sd = tf.get()
        act(sd, sd[:, 0:n], ps, ps[:, 0:n], AF.Sqrt, extra_r=[epsc], scale=1.0 / D, bias=epsc[:, 0:1])
        rs = tf.get()
        P.op("dve", lambda e: e.reciprocal(out=rs[:, 0:n], in_=sd[:, 0:n]), [sd], [rs])
        for c in range(8):
            stt(dst[c], dst[c][:, 0:n], src[c], src[c][:, 0:n], pcol(l, gname, c), rs, rs[:, 0:n], ALU.mult, ALU.mult, extra_r=[pvs])
        return rs

    mraw = [Buf(big8[c][:, 0:NMEM]) for c in range(8)]
    for c in range(8):
        P.dma("sp", mraw[c].ap, memT_d[c * 128:(c + 1) * 128, :], reads=[Bdram_in], writes=[big8[c]])
        mraw[c] = big8[c]
    for l in range(2):
        mT = OG
        rms_to(l, "g_mem", mraw, mT, NMEM)
        for j in range(2):
            def cons(jj, ps, j=j):
                cp("act", kmT[l][j * 4 + jj], kmT[l][j * 4 + jj].ap, ps, ps[:, 0:NMEM])
            o, w = WCOL[f"kv{j}"]
            wb = wload(l, f"kv{j}")
            for jj in range(4):
                ps = ps_next()
                for kc in range(KC):
                    mm(ps, ps[:, 0:NMEM], wb, wb[:, kc, jj * 128:(jj + 1) * 128], mT[kc], mT[kc][:, 0:NMEM], start=(kc == 0), stop=(kc == KC - 1))
                cons(jj, ps)
        for j in range(2):
            wb = wload(l, f"kv{2 + j}")
            for mb in range(2):
                ps = ps_next()
                for kc in range(KC):
                    mm(ps, ps.ap, mT[kc], mT[kc][:, mb * 128:(mb + 1) * 128], wb, wb[:, kc, :], start=(kc == 0), stop=(kc == KC - 1))
                cp("act", vmt[l][mb], vmt[l][mb][:, j * 512:(j + 1) * 512], ps, ps.ap)

    chk("memkv")
    AR = sbuf([128, 4, 2, 128], BF16)
    Bbd = sbuf([128, 4, 2, 128], BF16)
    Kbd = sbuf([128, 4, 2, 128], BF16)
    P.op("pool", lambda e: e.memset(Bbd.ap, 0.0), [], [Bbd])
    P.op("pool", lambda e: e.memset(Kbd.ap, 0.0), [], [Kbd])
    NXr = Ring(4, [128, 2, 2, 128], BF16)
    NXn = {id(b_): Buf(b_[:, :, 0, :]) for b_ in NXr.b}
    NXx = {id(b_): Buf(b_[:, :, 1, :]) for b_ in NXr.b}
    Ar = Ring(4, [128, 2, 128], BF16)
    Arb = Ring(4, [128, 2, 128], BF16)
    Aak = Ring(2, [128, 2, 128], BF16)
    Ark = Ring(4, [128, 2, 128], BF16)
    Atok = [sbuf([128, 2, 128], BF16) for _ in range(2)]
    Vtok = [sbuf([128, 2, 128], BF16) for _ in range(4)]
    Utok = [sbuf([128, 2, 128], BF16) for _ in range(2)]
    for b_ in Atok + Vtok + Utok:
        P.op("pool", lambda e: e.memset(b_.ap, 0.0), [], [b_])
    BKtok = Ring(4, [128, 2, 128], BF16)
    ApT = Ring(4, [128, 128], BF16)
    Wp = Ring(2, [128, 128], BF16)
    stage = slots[0]
    Up4 = slots[11]
    cntr = dict(at=0, vt=0, ut=0, up=0)

    def scan_pre(l, c, qs2, ctx, E1, bpT, kpT, vb):
        for q in qs2:
            d = ctx[q] = {}
            d["at"] = Atok[cntr["at"] % 2]; cntr["at"] += 1
            d["vt"] = Vtok[cntr["vt"] % 4]; cntr["vt"] += 1
            d["upc"] = cntr["up"] % 4; cntr["up"] += 1
            d["bk"], d["arb"], d["aak"], d["ark"] = BKtok.get(), Arb.get(), Aak.get(), Ark.get()
            d["apT"], d["wp"] = ApT.get(), Wp.get()
        for q in qs2:
            d = ctx[q]
            qs = slice(q * 128, (q + 1) * 128)
            pst = ps_next()
            pv_ = pst.ap
            cp("dve", stage, stage[:, 0:128], AR, AR[:, q, 0, :])
            cp("dve", stage, stage[:, 128:256], bpT, bpT[:, qs])
            cp("dve", stage, stage[:, 256:384], kpT, kpT[:, qs])
            cp("dve", stage, stage[:, 384:512], vb, vb[:, qs])
            for i4 in range(4):
                tr(pst, pv_[:, i4 * 128:(i4 + 1) * 128], stage, stage[:, i4 * 128:(i4 + 1) * 128])
            at, vt, bk = d["at"], d["vt"], d["bk"]
            for h in range(2):
                cp("act", at, at[:, h, h * 64:(h + 1) * 64], pst, pv_[:, h * 64:(h + 1) * 64])
                cp("act", vt, vt[:, h, h * 64:(h + 1) * 64], pst, pv_[:, 384 + h * 64:384 + (h + 1) * 64])
            cp("act", bk, bk.ap, pst, pv_[:, 128:384].rearrange("p (a b) -> p a b", a=2))
            yield
        for q in qs2:
            d = ctx[q]
            ps1, ps2, ps3 = ps_next(), ps_next(), ps_next()
            arq = AR[:, q, :, :].rearrange("p a t -> p (a t)")
            for h in range(2):
                mm(ps1, ps1[:, h * 256:(h + 1) * 256], Bbd, Bbd[:, q, h, :], AR, arq)
                mm(ps2, ps2[:, h * 256:(h + 1) * 256], Kbd, Kbd[:, q, h, :], AR, arq)
            mm(ps3, ps3[:, 0:256], AR, AR[:, q, 0, :], Bbd, Bbd[:, q, :, :].rearrange("p h j -> p (h j)"))
            nx = NXr.get()
            arb, aak, ark, a0 = d["arb"], d["aak"], d["ark"], Ar.get()
            p1v = ps1.ap.rearrange("p (h a t) -> p h a t", h=2, a=2)
            p2v = ps2.ap.rearrange("p (h a t) -> p h a t", h=2, a=2)
            for h in range(2):
                tt("dve", NXn[id(nx)], nx[:, h, 0, :], ps1, p1v[:, h, 0, :], cst_b, cst_b[:, 3, :], ALU.mult)
                tt("dve", arb, arb[:, h, :], ps1, p1v[:, h, 1, :], cst_b, cst_b[:, 4, :], ALU.mult)
                tt("dve", aak, aak[:, h, :], ps2, p2v[:, h, 0, :], cst_b, cst_b[:, 3, :], ALU.mult)
                tt("dve", ark, ark[:, h, :], ps2, p2v[:, h, 1, :], cst_b, cst_b[:, 4, :], ALU.mult)
            tt("dve", a0, a0.ap, ps3, ps3[:, 0:256].rearrange("p (h t) -> p h t", h=2), mSL2, mSL2.ap, ALU.mult)
            tt("dve", NXx[id(nx)], nx[:, :, 1, :], NXn[id(nx)], nx[:, :, 0, :], id2, id2.ap, ALU.add)
            d["nx"], d["A"] = nx, a0
            yield
        for lev in range(7):
            last = (lev == 6)
            pss = {}
            for q in qs2:
                d = ctx[q]
                nx, A_i = d["nx"], d["A"]
                psn = ps_next()
                for h in range(2):
                    if lev == 0:
                        mm(psn, psn[:, h * 256:h * 256 + 128], A_i, A_i[:, h, :], NXn[id(nx)], nx[:, h, 0, :])
                    elif not last:
                        mm(psn, psn[:, h * 256:(h + 1) * 256], A_i, A_i[:, h, :], NXn[id(nx)], nx[:, h, :, :].rearrange("p a t -> p (a t)"), extra_r=[NXx[id(nx)]])
                    else:
                        mm(psn, psn[:, h * 256 + 128:(h + 1) * 256], A_i, A_i[:, h, :], NXx[id(nx)], nx[:, h, 1, :])
                psa = None
                if not last:
                    psa = ps_next()
                    for h in range(2):
                        mm(psa, psa[:, h * 128:(h + 1) * 128], NXn[id(nx)], nx[:, h, 0, :], A_i, A_i[:, h, :])
                pss[q] = (psn, psa)
            for q in qs2:
                d = ctx[q]
                nx = d["nx"]
                psn, psa = pss[q]
                pnv = psn.ap.rearrange("p (h a t) -> p h a t", h=2, a=2)
                nx2 = NXr.get()
                if lev == 0:
                    cp("act", NXn[id(nx2)], nx2[:, :, 0, :], psn, pnv[:, :, 0, :])
                    cp("dve", NXx[id(nx2)], nx2[:, :, 1, :], NXx[id(nx)], nx[:, :, 1, :])
                else:
                    if not last:
                        cp("dve", NXn[id(nx2)], nx2[:, :, 0, :], psn, pnv[:, :, 0, :])
                    tt("dve", NXx[id(nx2)], nx2[:, :, 1, :], psn, pnv[:, :, 1, :], NXx[id(nx)], nx[:, :, 1, :], ALU.add)
                if not last:
                    a2 = Ar.get()
                    cp("act", a2, a2.ap, psa, psa[:, 0:256].rearrange("p (h t) -> p h t", h=2))
                    d["A"] = a2
                d["nx"] = nx2
            yield
        for q in qs2:
            d = ctx[q]
            nx, at, vt, aak, apT, wp = d["nx"], d["at"], d["vt"], d["aak"], d["apT"], d["wp"]
            psw = ps_next()
            for h in range(2):
                mm(psw, psw[:, 0:128], at, at[:, h, :], NXx[id(nx)], nx[:, h, 1, :], start=(h == 0), stop=(h == 1))
            for h in range(2):
                mm(psw, psw[:, 128 + h * 64:128 + (h + 1) * 64], aak, aak[:, h, :], vt, vt[:, h, h * 64:(h + 1) * 64])
            cp("act", apT, apT.ap, psw, psw[:, 0:128])
            cp("act", wp, wp.ap, psw, psw[:, 128:256])
        yield
        for q in qs2:
            d = ctx[q]
            nx, wp = d["nx"], d["wp"]
            psu = ps_next()
            for h in range(2):
                mm(psu, psu[:, h * 64:(h + 1) * 64], NXx[id(nx)], nx[:, h, 1, :], wp, wp[:, h * 64:(h + 1) * 64])
            uc_ = d["upc"]
            cp("act", Up4, Up4[:, uc_ * 128:(uc_ + 1) * 128], psu, psu[:, 0:128])
        yield

    def scan_seq(l, c, qs2, ctx, E1, yT):
        sbd, sfd = Sb[l][c], Sf[l][c]
        sbd2 = sbd.ap.rearrange("p h v -> p (h v)")
        for q in qs2:
            d = ctx[q]
            qs = slice(q * 128, (q + 1) * 128)
            vt, bk, arb, ark, apT, uc_ = d["vt"], d["bk"], d["arb"], d["ark"], d["apT"], d["upc"]
            ut = Utok[cntr["ut"] % 2]; cntr["ut"] += 1
            ps_u = ps_next()
            mm(ps_u, ps_u[:, 0:128], apT, apT.ap, sbd, sbd2)
            for h in range(2):
                tt("dve", ut, ut[:, h, h * 64:(h + 1) * 64], ps_u, ps_u[:, h * 64:(h + 1) * 64],
                   Up4, Up4[:, uc_ * 128 + h * 64:uc_ * 128 + (h + 1) * 64], ALU.add)
            yield
            ps_y = ps_next()
            mm(ps_y, ps_y[:, 0:128], sbd, sbd2, AR, AR[:, q, 1, :], start=True, stop=False)
            for h in range(2):
                mm(ps_y, ps_y[:, 0:128], ut, ut[:, h, :], arb, arb[:, h, :], start=False, stop=False)
                mm(ps_y, ps_y[:, 0:128], vt, vt[:, h, :], ark, ark[:, h, :], start=False, stop=(h == 1))
            ps_s = ps_next()
            mm(ps_s, ps_s[:, 0:256], bk, bk[:, 0, :], ut, ut.ap.rearrange("p h v -> p (h v)"), start=True, stop=False)
            mm(ps_s, ps_s[:, 0:256], bk, bk[:, 1, :], vt, vt.ap.rearrange("p h v -> p (h v)"), start=False, stop=True)
            pc = E1[:, q * 128 + 127:q * 128 + 128]
            for h in range(2):
                hp = slice(h * 64, (h + 1) * 64)
                stt(sfd, sfd[hp, :], sfd, sfd[hp, :], pc[hp, :], ps_s, ps_s[hp, h * 128 + h * 64:h * 128 + (h + 1) * 64],
                    ALU.mult, ALU.add, extra_r=[E1])
            for h in range(2):
                hp = slice(h * 64, (h + 1) * 64)
                cp("act", sbd, sbd[hp, h, :], sfd, sfd[hp, :])
            cp("act", yT, yT[:, qs], ps_y, ps_y[:, 0:128])
            yield

    def scan_all(l, c, E1, bpT, kpT, vb, yT):
        ctxA, ctxB = {}, {}
        for _ in scan_pre(l, c, (0, 1), ctxA, E1, bpT, kpT, vb):
            pass
        gB = scan_pre(l, c, (2, 3), ctxB, E1, bpT, kpT, vb)
        gA = scan_seq(l, c, (0, 1), ctxA, E1, yT)
        aliveA = aliveB = True
        while aliveA or aliveB:
            if aliveB:
                try:
                    next(gB)
                except StopIteration:
                    aliveB = False
            if aliveA:
                try:
                    next(gA)
                except StopIteration:
                    aliveA = False
        for _ in scan_seq(l, c, (2, 3), ctxB, E1, yT):
            pass

    for it in range(NT):
        tsl = slice(it * T, (it + 1) * T)
        for c in range(8):
            P.dma("sp", xT[c].ap, xT_d[c * 128:(c + 1) * 128, tsl], reads=[Bdram_in], writes=[xT[c]])
        for l in range(2):
            rms_to(l, "g_norm", xT, hT, T)
            chk(f"rms{l}")
            hrhs = lambda j: hT
            car = carry[l]

            def shiftmix(ps, mi, npart=128, A=None):
                A = A or slots[11]
                pp = slice(0, npart)
                mu_c, omu_c = pcol(l, "mu", mi), pcol(l, "omu", mi)
                act(A, A[pp, :], ps, ps[pp, :], AF.Identity, extra_r=[pvs], scale=omu_c[pp, :])
                stt(A, A[pp, 1:T], ps, ps[pp, 0:T - 1], mu_c[pp, :], A, A[pp, 1:T], ALU.mult, ALU.add, extra_r=[pvs])
                stt(A, A[pp, 0:1], car, car[pp, mi:mi + 1], mu_c[pp, :], A, A[pp, 0:1], ALU.mult, ALU.add, extra_r=[pvs])
                cp("act", car, car[pp, mi:mi + 1], ps, ps[pp, T - 1:T])
                return A

            lob, vlo = lob_, vlo_

            def cons_lora(j, ps):
                if j == 0:
                    lo = shiftmix(ps, 24)
                    act(lob, lob[0:64, :], lo, lo[0:64, :], AF.Tanh)
                    cp("dve", lob, lob[64:128, :], lo, lo[64:128, :])
                elif l == 1:
                    v_ = shiftmix(ps, 25, 32)
                    cp("dve", vlo, vlo[0:32, :], v_, v_[0:32, :])
            proj(l, "lora", lambda j: hT if (j == 0 or l == 1) else None, cons_lora)

            chk(f"lora{l}")
            for c in range(8):
                got = {}

                def cons_rkvg(j, ps):
                    if j < 3:
                        got[j] = shiftmix(ps, j * 8 + c, A=slots[j])
                    else:
                        g = slots[3]
                        act(g, g.ap, ps, ps.ap, AF.Silu)
                        got[3] = g
                proj(l, f"rkvg{c}", hrhs, cons_rkvg)
                r_, k_, v_, gs = got[0], got[1], got[2], got[3]
                cs = slice(c * 128, (c + 1) * 128)
                psd, psa = ps_next(), ps_next()
                mm(psd, psd.ap, lor[l], lor[l][0:64, 0, cs], lob, lob[0:64, :])
                mm(psa, psa.ap, lor[l], lor[l][64:128, 0, cs], lob, lob[64:128, :])
                sgd, a_ = slots[4], slots[5]
                act(sgd, sgd.ap, psd, psd.ap, AF.Sigmoid, extra_r=[pvs], bias=pcol(l, "w0", c))
                act(a_, a_.ap, psa, psa.ap, AF.Sigmoid, extra_r=[pvs], bias=pcol(l, "a0", c))
                if l == 1:
                    psv = ps_next()
                    mm(psv, psv.ap, lor[l], lor[l][0:32, 1, cs], vlo, vlo[0:32, :])
                    gv = slots[7]
                    act(gv, gv.ap, psv, psv.ap, AF.Sigmoid, extra_r=[pvs], bias=pcol(l, "v0", c))
                    dd = slots[8]
                    tt("dve", dd, dd.ap, vf[c], vf[c].ap, v_, v_.ap, ALU.subtract)
                    tt("dve", dd, dd.ap, dd, dd.ap, gv, gv.ap, ALU.mult)
                    tt("dve", v_, v_.ap, v_, v_.ap, dd, dd.ap, ALU.add)
                else:
                    cp("act", vf[c], vf[c].ap, v_, v_.ap)
                dump(f"r{l}_{c}", r_)
                dump(f"k{l}_{c}", k_)
                dump(f"v{l}_{c}", v_)
                dump(f"a{l}_{c}", a_)
                kkr = slots[6]
                ts("dve", kkr, kkr.ap, k_, k_.ap, pcol(l, "k_k", c), None, ALU.mult, extra_r=[pvs])
                sq = tb.get()
                act(sq, sq.ap, kkr, kkr.ap, AF.Square)
                psn = ps_next()
                mm(psn, psn.ap, cst_b, bones, sq, sq.ap)
                nrm = slots[7]
                act(nrm, nrm.ap, psn, psn.ap, AF.Sqrt)
                ts("dve", nrm, nrm.ap, nrm, nrm.ap, 1e-12, None, ALU.max)
                P.op("dve", lambda e: e.reciprocal(out=nrm.ap, in_=nrm.ap), [nrm], [nrm])
                tt("dve", kkr, kkr.ap, kkr, kkr.ap, nrm, nrm.ap, ALU.mult)
                kk = kkr
                f_ = slots[7]
                ts("dve", f_, f_.ap, a_, a_.ap, pcol(l, "k_a", c), pcol(l, "omka", c), ALU.mult, ALU.add, extra_r=[pvs])
                tt("dve", k_, k_.ap, k_, k_.ap, f_, f_.ap, ALU.mult)
                k2 = k_
                rk = tb.get()
                stt(rk, rk.ap, r_, r_.ap, pcol(l, "r_k", c), k2, k2.ap, ALU.mult, ALU.mult, extra_r=[pvs])
                psb = ps_next()
                mm(psb, psb.ap, cst_b, bones, rk, rk.ap)
                bon = slots[8]
                tt("dve", bon, bon.ap, psb, psb.ap, v_, v_.ap, ALU.mult)
                cum = slots[7]
                P.op("dve", lambda e: e.tensor_tensor_scan(out=cum.ap, data0=rmask.ap, data1=sgd.ap, initial=0.0,
                                                           op0=ALU.mult, op1=ALU.add), [rmask, sgd], [cum])
                E1, E2, E3 = slots[9], slots[10], slots[11]
                act(E1, E1.ap, cum, cum.ap, AF.Exp, scale=-C0)
                act(E2, E2.ap, cum, cum.ap, AF.Exp, scale=C0)
                tt("dve", sgd, sgd.ap, cum, cum.ap, sgd, sgd.ap, ALU.subtract)
                act(E3, E3.ap, sgd, sgd.ap, AF.Exp, scale=-C0)
                dump(f"E1{l}_{c}", E1)
                tt("dve", AR, AR[:, :, 1, :], r_, r_.ap.rearrange("p (q t) -> p q t", q=4), E1, E1.ap.rearrange("p (q t) -> p q t", q=4), ALU.mult)
                stt(AR, AR[:, :, 0, :], kk, kk.ap.rearrange("p (q t) -> p q t", q=4), -1.0, E3, E3.ap.rearrange("p (q t) -> p q t", q=4), ALU.mult, ALU.mult)
                bt, kt = slots[0], slots[11]
                tt("dve", bt, bt.ap, kk, kk.ap, a_, a_.ap, ALU.mult)
                tt("dve", bt, bt.ap, bt, bt.ap, E2, E2.ap, ALU.mult)
                tt("dve", kt, kt.ap, k2, k2.ap, E2, E2.ap, ALU.mult)
                for h in range(2):
                    hp = slice(h * 64, (h + 1) * 64)
                    cp("act", Bbd, Bbd[hp, :, h, :], bt, bt[hp, :].rearrange("p (q t) -> p q t", q=4))
                    cp("dve", Kbd, Kbd[hp, :, h, :], kt, kt[hp, :].rearrange("p (q t) -> p q t", q=4))
                bpT, kpT, vb = tb.get(), tb.get(), tb.get()
                for q in range(4):
                    qs = slice(q * 128, (q + 1) * 128)
                    pc = E1[:, q * 128 + 127:q * 128 + 128]
                    act(bpT, bpT[:, qs], bt, bt[:, qs], AF.Identity, extra_r=[E1], scale=pc)
                    act(kpT, kpT[:, qs], kt, kt[:, qs], AF.Identity, extra_r=[E1], scale=pc)
                cp("act", vb, vb.ap, v_, v_.ap)
                chk(f"prep{l}_{c}")
                yT = slots[1]
                scan_all(l, c, E1, bpT, kpT, vb, yT)
                chk(f"scanend{l}_{c}")
                dump(f"y{l}_{c}", yT)
                yb_, ysq = tb.get(), tb.get()
                cp("dve", yb_, yb_.ap, yT, yT.ap)
                act(ysq, ysq.ap, yT, yT.ap, AF.Square)
                p1, p2 = ps_next(), ps_next()
                mm(p1, p1.ap, cst_b, bones, yb_, yb_.ap)
                mm(p2, p2.ap, cst_b, bones, ysq, ysq.ap)
                mean, var = slots[5], slots[6]
                act(mean, mean.ap, p1, p1.ap, AF.Identity, scale=1.0 / 64)
                tt("dve", var, var.ap, mean, mean.ap, mean, mean.ap, ALU.mult)
                stt(var, var.ap, p2, p2.ap, 1.0 / 64, var, var.ap, ALU.mult, ALU.subtract)
                act(var, var.ap, var, var.ap, AF.Sqrt, extra_r=[epsc], bias=epsc[:, 1:2])
                P.op("dve", lambda e: e.reciprocal(out=var.ap, in_=var.ap), [var], [var])
                tt("dve", yT, yT.ap, yT, yT.ap, mean, mean.ap, ALU.subtract)
                tt("dve", yT, yT.ap, yT, yT.ap, var, var.ap, ALU.mult)
                act(yT, yT.ap, yT, yT.ap, AF.Identity, extra_r=[pvs], scale=pcol(l, "gn_g", c), bias=pcol(l, "gn_b", c))
                tt("dve", yT, yT.ap, yT, yT.ap, bon, bon.ap, ALU.add)
                tt("dve", OG[c], OG[c].ap, yT, yT.ap, gs, gs.ap, ALU.mult)
                dump(f"og{l}_{c}", OG[c])

            def branch_out(pn, first, bias_name=None):
                for j in range(4):
                    tmpy = {}

                    def cons(jj, ps, j=j):
                        if jj < 2:
                            t_ = tf.get()
                            if bias_name is None:
                                cp("act", t_, t_.ap, ps, ps.ap)
                            else:
                                act(t_, t_.ap, ps, ps.ap, AF.Identity, extra_r=[pvs], bias=pcol(l, bias_name, 2 * j + jj))
                            tmpy[jj] = t_
                        else:
                            cidx = 2 * j + (jj - 2)
                            sg = tf.get()
                            act(sg, sg.ap, ps, ps.ap, AF.Sigmoid)
                            yb = tmpy[jj - 2]
                            if first:
                                tt("dve", yacc[cidx], yacc[cidx].ap, sg, sg.ap, yb, yb.ap, ALU.mult)
                            else:
                                tt("dve", sg, sg.ap, sg, sg.ap, yb, yb.ap, ALU.mult)
                                tt("dve", yacc[cidx], yacc[cidx].ap, yacc[cidx], yacc[cidx].ap, sg, sg.ap, ALU.add)
                    proj(l, f"{pn}{j}", lambda jj: OG if jj < 2 else hT, cons)

            chk(f"rwkv{l}")
            branch_out("pr", True)
            chk(f"pr{l}")
            dump(f"yacc0_{l}", yacc[0])

            uc = big8
            s1, s2 = pin[0], pin[1]
            dg = dg_keep
            for j in range(4):
                def cons_glu(jj, ps, j=j):
                    c = 2 * j + jj // 2
                    if jj % 2 == 0:
                        cons_glu.pa = ps
                        return
                    gb = tf.get()
                    act(gb, gb.ap, ps, ps.ap, AF.Sigmoid, extra_r=[pvs], bias=pcol(l, "b_glu", 8 + c))
                    pa = cons_glu.pa
                    u = ubr.get()
                    cp("act", u, u[:, 0:30], halo[l], halo[l][:, c, :])
                    stt(u, u[:, 30:30 + T], pa, pa.ap, pcol(l, "b_glu", c), gb, gb.ap, ALU.add, ALU.mult, extra_r=[pvs])
                    for tp in range(31):
                        if tp % 3 == 2:
                            act(dgT[tp], dgT[tp].ap, cst_b, ident, AF.Identity, extra_r=[pvs], scale=pcol(l, "w_dw", c * 31 + tp))
                        else:
                            ts("dve", dgT[tp], dgT[tp].ap, cst_b, ident, pcol(l, "w_dw", c * 31 + tp), None, ALU.mult, extra_r=[pvs])
                    pc_ = ps_next()
                    for tp in range(31):
                        mm(pc_, pc_.ap, dgT[tp], dgT[tp].ap, u, u[:, tp:tp + T], start=(tp == 0), stop=(tp == 30))
                    act(uc[c], uc[c].ap, pc_, pc_.ap, AF.Identity, extra_r=[pvs], bias=pcol(l, "b_dw", c))
                    cp("act", halo[l], halo[l][:, c, :], u, u[:, T:T + 30])
                    ucb, ucs = tb.get(), tb.get()
                    cp("dve", ucb, ucb.ap, uc[c], uc[c].ap)
                    act(ucs, ucs.ap, uc[c], uc[c].ap, AF.Square)
                    mm(s1, s1.ap, cst_b, ones, ucb, ucb.ap, start=(c == 0), stop=(c == 7))
                    mm(s2, s2.ap, cst_b, ones, ucs, ucs.ap, start=(c == 0), stop=(c == 7))
                proj(l, f"glu{j}", hrhs, cons_glu)
            mean, var = slots[10], slots[11]
            act(mean, mean.ap, s1, s1.ap, AF.Identity, scale=1.0 / D)
            tt("dve", var, var.ap, mean, mean.ap, mean, mean.ap, ALU.mult)
            stt(var, var.ap, s2, s2.ap, 1.0 / D, var, var.ap, ALU.mult, ALU.subtract)
            act(var, var.ap, var, var.ap, AF.Sqrt, extra_r=[epsc], bias=epsc[:, 2:3])
            P.op("dve", lambda e: e.reciprocal(out=var.ap, in_=var.ap), [var], [var])
            dump(f"uc{l}_0", uc[0])
            for j in range(2):
                def cons_cg(jj, ps, j=j):
                    c = 4 * j + jj
                    cg = tf.get()
                    act(cg, cg.ap, ps, ps.ap, AF.Silu)
                    t_ = uc[c]
                    tt("dve", t_, t_.ap, t_, t_.ap, mean, mean.ap, ALU.subtract)
                    tt("dve", t_, t_.ap, t_, t_.ap, var, var.ap, ALU.mult)
                    act(t_, t_.ap, t_, t_.ap, AF.Identity, extra_r=[pvs], scale=pcol(l, "ln_g", c), bias=pcol(l, "ln_b", c))
                    act(t_, t_.ap, t_, t_.ap, AF.Silu)
                    tt("dve", OG[c], OG[c].ap, t_, t_.ap, cg, cg.ap, ALU.mult)
                proj(l, f"cg{j}", hrhs, cons_cg)
            dump(f"ug{l}_0", OG[0])
            chk(f"conv{l}")
            branch_out("pc", False, "b_pc")
            chk(f"pc{l}")
            dump(f"yacc1_{l}", yacc[0])

            qT = OG
            for j in range(2):
                def cons_q(jj, ps, j=j):
                    c = 4 * j + jj
                    act(qT[c], qT[c].ap, ps, ps.ap, AF.Identity, scale=1.0 / 16.0)
                proj(l, f"q{j}", hrhs, cons_q)
            att = big8
            prT = prT_keep
            small = small_keep
            for hm in range(4):
                pt = prT[hm % 2]
                for sbk in range(4):
                    ss = slice(sbk * 128, (sbk + 1) * 128)
                    psc = ps_next()
                    for dc in range(2):
                        mm(psc, psc[:, 0:NMEM], qT[2 * hm + dc], qT[2 * hm + dc][:, ss], kmT[l][2 * hm + dc], kmT[l][2 * hm + dc].ap,
                           start=(dc == 0), stop=(dc == 1))
                    P.op("dve", lambda e: e.tensor_reduce(out=small[:, 0:1], in_=psc[:, 0:NMEM], axis=AX.X, op=ALU.max), [psc], [small])
                    ts("dve", small, small[:, 1:2], small, small[:, 0:1], -1.0, None, ALU.mult)
                    ex = tf.get()
                    P.op("act", lambda e: e.activation(out=ex[:, 0:NMEM], in_=psc[:, 0:NMEM], func=AF.Exp, bias=small[:, 1:2],
                                                       accum_out=small[:, 2:3]), [psc, small], [ex, small])
                    P.op("dve", lambda e: e.reciprocal(out=small[:, 3:4], in_=small[:, 2:3]), [small], [small])
                    pb = tf.get()
                    ts("dve", pb, pb[:, 0:NMEM], ex, ex[:, 0:NMEM], small[:, 3:4], None, ALU.mult, extra_r=[small])
                    ptp = ps_next()
                    pv_ = ptp.ap
                    for mb in range(2):
                        tr(ptp, pv_[:, mb * 128:(mb + 1) * 128], pb, pb[:, mb * 128:(mb + 1) * 128])
                    for mb in range(2):
                        cp("act", pt[mb], pt[mb][:, ss], ptp, pv_[:, mb * 128:(mb + 1) * 128])
                for dc in range(2):
                    c = 2 * hm + dc
                    pa_ = ps_next()
                    for mb in range(2):
                        mm(pa_, pa_.ap, vmt[l][mb], vmt[l][mb][:, c * 128:(c + 1) * 128], pt[mb], pt[mb].ap, start=(mb == 0), stop=(mb == 1))
                    cp("act", att[c], att[c].ap, pa_, pa_.ap)
            dump(f"att{l}_0", att[0])
            for j in range(2):
                def cons_mg(jj, ps, j=j):
                    c = 4 * j + jj
                    mg = tf.get()
                    act(mg, mg.ap, ps, ps.ap, AF.Silu)
                    tt("dve", OG[c], OG[c].ap, att[c], att[c].ap, mg, mg.ap, ALU.mult)
                proj(l, f"mg{j}", hrhs, cons_mg)
            chk(f"mem{l}")
            branch_out("pm", False)
            chk(f"pm{l}")
            dump(f"yacc2_{l}", yacc[0])

            for c in range(8):
                cp("act", OG[c], OG[c].ap, yacc[c], yacc[c].ap)
            for j in range(2):
                def cons_o(jj, ps, j=j):
                    c = 4 * j + jj
                    tt("dve", xT[c], xT[c].ap, xT[c], xT[c].ap, ps, ps.ap, ALU.add)
                proj(l, f"wo{j}", lambda jj: OG, cons_o)
            dump(f"x{l}_0", xT[0])

        ps = pin[0]
        for c in range(8):
            sq = tb.get()
            act(sq, sq.ap, xT[c], xT[c].ap, AF.Square)
            mm(ps, ps.ap, cst_b, ones, sq, sq.ap, start=(c == 0), stop=(c == 7))
        sd, rs = tf.get(), tf.get()
        act(sd, sd.ap, ps, ps.ap, AF.Sqrt, extra_r=[epsc], scale=1.0 / D, bias=epsc[:, 0:1])
        P.op("dve", lambda e: e.reciprocal(out=rs.ap, in_=sd.ap), [sd], [rs])
        for c in range(8):
            o_ = tf.get()
            stt(o_, o_.ap, xT[c], xT[c].ap, pcol(0, "g_final", c), rs, rs.ap, ALU.mult, ALU.mult, extra_r=[pvs])
            P.dma("sp", outT_d[c * 128:(c + 1) * 128, tsl], o_.ap, reads=[o_], writes=[Bout])


_CACHE = {}


def kernel(**inp):
    inp = {k: np.asarray(v) for k, v in inp.items()}
    pv, wbig, lora, cst, rm = host_prep(inp)
    x, mem = inp["x"], inp["mem"]
    B = x.shape[0]
    nc = bass.Bass("TRN2", target_bir_lowering=False)
    build(nc)
    in_maps = []
    for b in range(B):
        in_maps.append({"xT": np.ascontiguousarray(x[b].T), "memT": np.ascontiguousarray(mem[b].T),
                        "pv": pv, "wbig": wbig, "lora": lora, "cst": cst, "rm": rm})
    res = run_bass_kernel_spmd(nc, in_maps, core_ids=list(range(B)))
    out = np.stack([np.ascontiguousarray(r["outT"].T) for r in res.results], axis=0)
    return out.astype(np.float32)
```

```python
import numpy as np
import concourse.bass as bass
import concourse.mybir as mybir
from concourse.bass_utils import run_bass_kernel_spmd

F32 = mybir.dt.float32
BF16 = mybir.dt.bfloat16
AF = mybir.ActivationFunctionType
ALU = mybir.AluOpType
AX = mybir.AxisListType
NDS = 24

D = 1024
SEQ = 4096
T = 512
NMEM = 256
KC = 8
C0 = float(np.exp(-0.5))


class Buf:
    __slots__ = ("ap", "w", "r")

    def __init__(self, ap):
        self.ap = ap
        self.w = None
        self.r = {}

    def __getitem__(self, k):
        return self.ap[k]


class Prog:
    def __init__(self, nc):
        self.nc = nc
        self.eng = dict(pe=nc.tensor, dve=nc.vector, act=nc.scalar, pool=nc.gpsimd, sp=nc.sync)
        self.esem = {k: nc.alloc_semaphore("es_" + k) for k in self.eng}
        self.ecnt = {k: 0 for k in self.eng}
        self.seen = {k: {} for k in self.eng}
        self.dsem, self.dtgt, self.dnext = {}, {}, {}
        self.ninst = 0

    def _wait(self, e, ev):
        sem, key, val = ev
        if key == ("e", e) and e == "pe":
            return
        if self.seen[e].get(key, 0) >= val:
            return
        self.eng[e].wait_ge(sem, val)
        self.seen[e][key] = val

    def _deps(self, e, reads, writes):
        for b in reads:
            if b.w is not None:
                self._wait(e, b.w)
        me = ("e", e)
        for b in writes:
            if b.w is not None and b.w[1] != me:
                self._wait(e, b.w)
            for ev in b.r.values():
                if ev[1] != me:
                    self._wait(e, ev)

    def _record(self, ev, reads, writes):
        for b in reads:
            b.r[ev[1]] = ev
        for b in writes:
            b.w = ev
            b.r = {}

    def op(self, e, fn, reads=(), writes=()):
        self._deps(e, reads, writes)
        inst = fn(self.eng[e])
        self.ecnt[e] += 1
        inst.then_inc(self.esem[e], 1)
        self._record((self.esem[e], ("e", e), self.ecnt[e]), reads, writes)
        self.ninst += 1

    def dma(self, q, out_ap, in_ap, reads=(), writes=(), **kw):
        if q not in self.dsem:
            self.dsem[q] = [self.nc.alloc_semaphore(f"ds_{q}{i}") for i in range(NDS)]
            self.dtgt[q] = [0] * NDS
            self.dnext[q] = 0
        j = self.dnext[q]
        self.dnext[q] = (j + 1) % NDS
        key = ("d", q, j)
        if self.dtgt[q][j] > 0:
            self._wait(q, (self.dsem[q][j], key, self.dtgt[q][j]))
        self._deps(q, reads, writes)
        inst = self.eng[q].dma_start(out=out_ap, in_=in_ap, **kw)
        self.dtgt[q][j] += 16
        inst.then_inc(self.dsem[q][j], 16)
        self._record((self.dsem[q][j], key, self.dtgt[q][j]), reads, writes)
        self.ninst += 1

    def finish(self, e="sp"):
        for q in self.dsem:
            for j in range(NDS):
                if self.dtgt[q][j] > 0:
                    self._wait(e, (self.dsem[q][j], ("d", q, j), self.dtgt[q][j]))
        for k in self.eng:
            if k != e and self.ecnt[k] > 0:
                self._wait(e, (self.esem[k], ("e", k), self.ecnt[k]))


PV = {}
_o = 0
for _n, _w in [("g_norm", 8), ("mu", 26), ("w0", 8), ("a0", 8), ("k_k", 8), ("k_a", 8), ("r_k", 8),
               ("gn_g", 8), ("gn_b", 8), ("v0", 8), ("b_glu", 16), ("w_dw", 248), ("b_dw", 8),
               ("ln_g", 8), ("ln_b", 8), ("b_pc", 8), ("g_mem", 8), ("g_final", 8), ("omu", 26), ("omka", 8)]:
    PV[_n] = _o
    _o += _w
NPV = _o

WCOL = {}
_o = 0
for _n, _w in [("lora", 256)] + [(f"rkvg{c}", 512) for c in range(8)] + \
        [(f"pr{j}", 512) for j in range(4)] + [(f"glu{j}", 512) for j in range(4)] + \
        [(f"cg{j}", 512) for j in range(2)] + [(f"pc{j}", 512) for j in range(4)] + \
        [(f"q{j}", 512) for j in range(2)] + [(f"mg{j}", 512) for j in range(2)] + \
        [(f"pm{j}", 512) for j in range(4)] + [(f"wo{j}", 512) for j in range(2)] + \
        [(f"kv{j}", 512) for j in range(4)]:
    WCOL[_n] = (_o, _w)
    _o += _w
TOTC = _o


def _fm(v):
    return np.ascontiguousarray(v.reshape(8, 128).T)


def host_prep(inp):
    f = np.float32
    L = 2
    pv = np.zeros((L, 128, NPV), f)
    wbig = np.zeros((L, 128, KC, TOTC), f)
    lora = np.zeros((L, 128, 2, 1024), f)
    for l in range(L):
        def put(name, arr):
            pv[l][:, PV[name]:PV[name] + arr.shape[1]] = arr
        put("g_norm", _fm(inp["g_norm"][l]))
        mu = inp["mu_shift"][l]
        mucols = np.zeros((128, 26), f)
        mucols[:, 0:25] = mu.reshape(25, 128).T
        if l >= 1:
            mucols[0:32, 25] = inp["mu_vres"][l - 1]
        put("mu", mucols)
        for n in ["w0", "a0", "k_k", "k_a", "gn_g", "gn_b", "b_dw", "ln_g", "ln_b", "g_mem"]:
            src = {"g_mem": "g_mem_norm"}.get(n, n)
            put(n, _fm(inp[src][l]))
        put("r_k", _fm(inp["r_k"][l].reshape(-1)))
        put("b_pc", _fm(inp["b_proj_conv"][l]))
        if l >= 1:
            put("v0", _fm(inp["v0"][l - 1]))
        put("b_glu", np.ascontiguousarray(inp["b_glu"][l].reshape(16, 128).T))
        wd = inp["w_dw"][l]
        put("w_dw", np.ascontiguousarray(wd.reshape(31, 8, 128).transpose(2, 1, 0).reshape(128, 248)))
        put("g_final", _fm(inp["g_final"]))
        w_in = inp["w_in"][l]
        Wc = np.zeros((D, TOTC), f)

        def setc(name, off, arr):
            o, w = WCOL[name]
            Wc[:, o + off:o + off + arr.shape[1]] = arr
        setc("lora", 0, w_in[:, 3072:3200])
        if l >= 1:
            setc("lora", 128, inp["w_vres_down"][l - 1])
        for c in range(8):
            setc(f"rkvg{c}", 0, w_in[:, c * 128:(c + 1) * 128])
            setc(f"rkvg{c}", 128, w_in[:, 1024 + c * 128:1024 + (c + 1) * 128])
            setc(f"rkvg{c}", 256, w_in[:, 2048 + c * 128:2048 + (c + 1) * 128])
            setc(f"rkvg{c}", 384, w_in[:, 3200 + c * 128:3200 + (c + 1) * 128])
        for br, (pn, wp) in enumerate([("pr", inp["w_proj_rwkv"][l]), ("pc", inp["w_proj_conv"][l]),
                                       ("pm", inp["w_proj_mem"][l])]):
            for j in range(4):
                setc(f"{pn}{j}", 0, wp[:, j * 256:(j + 1) * 256])
                mo = 9344 + br * 1024 + j * 256
                setc(f"{pn}{j}", 256, w_in[:, mo:mo + 256])
        for j in range(4):
            for i in range(2):
                c = 2 * j + i
                setc(f"glu{j}", i * 256, w_in[:, 4224 + c * 128:4224 + (c + 1) * 128])
                setc(f"glu{j}", i * 256 + 128, w_in[:, 5248 + c * 128:5248 + (c + 1) * 128])
        for j in range(2):
            setc(f"cg{j}", 0, w_in[:, 6272 + j * 512:6272 + (j + 1) * 512])
            setc(f"q{j}", 0, w_in[:, 7296 + j * 512:7296 + (j + 1) * 512])
            setc(f"mg{j}", 0, w_in[:, 8320 + j * 512:8320 + (j + 1) * 512])
            setc(f"wo{j}", 0, inp["w_out"][l][:, j * 512:(j + 1) * 512])
        for j in range(4):
            setc(f"kv{j}", 0, inp["w_mem_kv"][l][:, j * 512:(j + 1) * 512])
        wbig[l] = Wc.reshape(KC, 128, TOTC).transpose(1, 0, 2)
        lora[l][0:64, 0] = inp["w_decay_up"][l]
        lora[l][64:128, 0] = inp["w_aaa_up"][l]
        if l >= 1:
            lora[l][0:32, 1] = inp["w_vres_up"][l - 1]
    cst = np.zeros((128, 8, 128), f)
    i = np.arange(128)
    cst[:, 0] = np.eye(128)
    cst[:, 1] = 1.0
    cst[:, 2] = (i[:, None] // 64 == i[None, :] // 64)
    cst[:, 3] = (i[:, None] < i[None, :])
    cst[:, 4] = (i[:, None] <= i[None, :])
    cst[:, 5] = (i[:, None] > i[None, :])
    rm = np.ones((128, 512), f)
    rm[:, 0::128] = 0.0
    return pv, wbig, lora, cst, rm


class _Stop(Exception):
    pass


def build(nc, NT=SEQ // T, dbg_names=(), stop_after=None):
    P = Prog(nc)
    try:
        _build(nc, P, NT, dbg_names, stop_after)
    except _Stop:
        pass
    P.finish()
    return P


def _build(nc, P, NT, dbg_names, stop_after):
    def chk(tag):
        if tag == stop_after:
            raise _Stop()
    dt = nc.dram_tensor
    xT_d = dt("xT", [D, SEQ], F32, kind="ExternalInput").ap()
    memT_d = dt("memT", [D, NMEM], F32, kind="ExternalInput").ap()
    pv_d = dt("pv", [2, 128, NPV], F32, kind="ExternalInput").ap()
    wbig_d = dt("wbig", [2, 128, KC, TOTC], F32, kind="ExternalInput").ap()
    lora_d = dt("lora", [2, 128, 2, 1024], F32, kind="ExternalInput").ap()
    cst_d = dt("cst", [128, 8, 128], F32, kind="ExternalInput").ap()
    rm_d = dt("rm", [128, 512], F32, kind="ExternalInput").ap()
    outT_d = dt("outT", [D, SEQ], F32, kind="ExternalOutput").ap()
    dbg_d = None
    if dbg_names:
        dbg_d = dt("dbg", [len(dbg_names), 128, 512], F32, kind="ExternalOutput").ap()
    Bdram_in = Buf(None)
    Bout = Buf(None)
    cnt = [0]

    def sb(shape, dtype, name=None):
        cnt[0] += 1
        return nc.alloc_sbuf_tensor(name or f"t{cnt[0]}", list(shape), dtype)

    def sbuf(shape, dtype):
        return Buf(sb(shape, dtype).ap())

    big8 = [sbuf([128, T], F32) for _ in range(8)]
    cst_f = Buf(big8[0].ap.rearrange("p (a b) -> p a b", a=4))
    cst_f2 = Buf(big8[1].ap.rearrange("p (a b) -> p a b", a=4))
    P.dma("sp", cst_f.ap, cst_d[:, 0:4, :], reads=[Bdram_in], writes=[cst_f, big8[0]])
    P.dma("sp", cst_f2.ap, cst_d[:, 4:8, :], reads=[Bdram_in], writes=[cst_f2, big8[1]])
    cst_b = sbuf([128, 8, 128], BF16)
    P.op("dve", lambda e: e.tensor_copy(out=cst_b[:, 0:4, :], in_=cst_f.ap), [cst_f, big8[0]], [cst_b])
    P.op("dve", lambda e: e.tensor_copy(out=cst_b[:, 4:8, :], in_=cst_f2.ap), [cst_f2, big8[1]], [cst_b])
    ident, ones, bones = cst_b[:, 0, :], cst_b[:, 1, :], cst_b[:, 2, :]
    identf = sbuf([128, 128], F32)
    P.op("dve", lambda e: e.tensor_copy(out=identf.ap, in_=cst_f[:, 0, :]), [cst_f, big8[0]], [identf])
    m12 = Buf(cst_b[:, 3:5, :])
    m12.w = None
    mSL2 = sbuf([128, 2, 128], BF16)
    id2 = sbuf([128, 2, 128], BF16)
    for h in range(2):
        P.op("dve", lambda e: e.tensor_copy(out=mSL2[:, h, :], in_=cst_b[:, 5, :]), [cst_b], [mSL2])
        P.op("dve", lambda e: e.tensor_copy(out=id2[:, h, :], in_=cst_b[:, 0, :]), [cst_b], [id2])
    rmf = Buf(big8[2].ap)
    P.dma("sp", rmf.ap, rm_d, reads=[Bdram_in], writes=[big8[2]])
    rmask = sbuf([128, 512], BF16)
    P.op("dve", lambda e: e.tensor_copy(out=rmask.ap, in_=big8[2].ap), [big8[2]], [rmask])
    pvs = sbuf([128, 2, NPV], F32)
    for l in range(2):
        P.dma("sp", pvs[:, l, :], pv_d[l], reads=[Bdram_in], writes=[pvs])
    for l in range(2):
        for (src, dst, w) in [("mu", "omu", 26), ("k_a", "omka", 8)]:
            P.op("dve", lambda e: e.tensor_scalar(out=pvs[:, l, PV[dst]:PV[dst] + w], in0=pvs[:, l, PV[src]:PV[src] + w],
                                                  scalar1=-1.0, scalar2=1.0, op0=ALU.mult, op1=ALU.add), [pvs], [pvs])
    epsc = sbuf([128, 4], F32)
    for i, v in enumerate([1e-6, 64e-5, 1e-5, 0.0]):
        P.op("dve", lambda e: e.memset(epsc[:, i:i + 1], v), [], [epsc])

    def pcol(l, name, c=0):
        o = PV[name] + c
        return pvs[:, l, o:o + 1]

    lor = [sbuf([128, 2, 1024], BF16) for _ in range(2)]
    for l in range(2):
        P.dma("pool", lor[l].ap, lora_d[l], reads=[Bdram_in], writes=[lor[l]])

    banks = [Buf(nc.alloc_psum_tensor(f"ps{i}", [128, 512], F32).ap()) for i in range(8)]
    ring = banks[:6]
    pin = banks[6:]
    rp = [0]

    def ps_next():
        b = ring[rp[0] % len(ring)]
        rp[0] += 1
        return b

    def bfv(b):
        return b.ap.bitcast(BF16)

    class Ring:
        def __init__(self, n, shape, dtype):
            self.b = [sbuf(shape, dtype) for _ in range(n)]
            self.i = 0

        def get(self):
            b = self.b[self.i % len(self.b)]
            self.i += 1
            return b

    slots = [sbuf([128, 512], F32) for _ in range(12)]
    tf = Ring(0, [128, 512], F32)
    tf.b = slots[0:10]
    tb = Ring(6, [128, 512], BF16)
    lob_, vlo_ = sbuf([128, 512], BF16), sbuf([128, 512], BF16)
    wring = Ring(2, [128, KC, 512], BF16)

    xT = [sbuf([128, T], F32) for _ in range(8)]
    vf = [sbuf([128, T], F32) for _ in range(8)]
    hT = [sbuf([128, T], BF16) for _ in range(8)]
    OG = [sbuf([128, T], BF16) for _ in range(8)]
    yacc = [sbuf([128, T], F32) for _ in range(8)]
    carry = [sbuf([128, 26], F32) for _ in range(2)]
    Sf = [[sbuf([128, 64], F32) for _ in range(8)] for _ in range(2)]
    Sb = [[sbuf([128, 2, 64], BF16) for _ in range(8)] for _ in range(2)]
    halo = [sbuf([128, 8, 30], BF16) for _ in range(2)]
    ubr = Ring(2, [128, 30 + T], BF16)
    dg_keep = sbuf([128, 31, 128], BF16)
    dgT = [Buf(dg_keep[:, tp_, :]) for tp_ in range(31)]
    prT_keep = [[sbuf([128, T], BF16) for _ in range(2)] for _ in range(2)]
    small_keep = sbuf([128, 8], F32)
    kmT = [[sbuf([128, NMEM], BF16) for _ in range(8)] for _ in range(2)]
    vmt = [[sbuf([128, D], BF16) for _ in range(2)] for _ in range(2)]
    for l in range(2):
        P.op("pool", lambda e: e.memset(carry[l].ap, 0.0), [], [carry[l]])
        P.op("pool", lambda e: e.memset(halo[l].ap, 0.0), [], [halo[l]])
        for c in range(8):
            P.op("pool", lambda e: e.memset(Sf[l][c].ap, 0.0), [], [Sf[l][c]])
            P.op("pool", lambda e: e.memset(Sb[l][c].ap, 0.0), [], [Sb[l][c]])

    dbg_list = list(dbg_names)
    dbg_buf = sbuf([128, 512], F32) if dbg_names else None

    def dump(name, b, ap=None):
        if name in dbg_list:
            i = dbg_list.index(name)
            a = b.ap if ap is None else ap
            t = dbg_buf
            P.op("dve", lambda e: e.tensor_copy(out=t[:, 0:a.shape[-1]], in_=a), [b], [t])
            P.dma("sp", dbg_d[i][0:a.shape[0], 0:a.shape[-1]], t[0:a.shape[0], 0:a.shape[-1]], reads=[t], writes=[Bout])
            dbg_list[i] = None

    def act(out_b, out_ap, in_b, in_ap, func, extra_r=(), **kw):
        P.op("act", lambda e: e.activation(out=out_ap, in_=in_ap, func=func, **kw), [in_b] + list(extra_r), [out_b])

    def tt(eng, out_b, out_ap, a_b, a_ap, b_b, b_ap, op):
        P.op(eng, lambda e: e.tensor_tensor(out=out_ap, in0=a_ap, in1=b_ap, op=op), [a_b, b_b], [out_b])

    def stt(out_b, out_ap, a_b, a_ap, scalar, b_b, b_ap, op0, op1, extra_r=()):
        P.op("dve", lambda e: e.scalar_tensor_tensor(out=out_ap, in0=a_ap, scalar=scalar, in1=b_ap, op0=op0, op1=op1),
             [a_b, b_b] + list(extra_r), [out_b])

    def ts(eng, out_b, out_ap, a_b, a_ap, s1, s2, op0, op1=None, extra_r=()):
        if op1 is None:
            P.op(eng, lambda e: e.tensor_scalar(out=out_ap, in0=a_ap, scalar1=s1, scalar2=None, op0=op0),
                 [a_b] + list(extra_r), [out_b])
        else:
            P.op(eng, lambda e: e.tensor_scalar(out=out_ap, in0=a_ap, scalar1=s1, scalar2=s2, op0=op0, op1=op1),
                 [a_b] + list(extra_r), [out_b])

    def cp(eng, out_b, out_ap, in_b, in_ap):
        if eng == "act":
            P.op("act", lambda e: e.copy(out=out_ap, in_=in_ap), [in_b], [out_b])
        elif eng == "dve":
            P.op(eng, lambda e: e.tensor_scalar(out=out_ap, in0=in_ap, scalar1=1.0, scalar2=None, op0=ALU.mult), [in_b], [out_b])
        else:
            P.op(eng, lambda e: e.tensor_copy(out=out_ap, in_=in_ap), [in_b], [out_b])

    def mm(out_b, out_ap, l_b, l_ap, r_b, r_ap, start=True, stop=True, extra_r=()):
        P.op("pe", lambda e: e.matmul(out_ap, lhsT=l_ap, rhs=r_ap, start=start, stop=stop), [l_b, r_b] + list(extra_r), [out_b])

    def tr(out_b, out_ap, in_b, in_ap):
        P.op("pe", lambda e: e.transpose(out_ap, in_ap, identf.ap), [in_b, identf], [out_b])

    def wload(l, name):
        o, w = WCOL[name]
        wb = wring.get()
        P.dma("pool", wb[:, :, 0:w], wbig_d[l][:, :, o:o + w], reads=[Bdram_in], writes=[wb])
        return wb

    def proj(l, name, rhs_for_chunk, consume):
        o, w = WCOL[name]
        wb = wload(l, name)
        for j in range(w // 128):
            rhs = rhs_for_chunk(j)
            if rhs is None:
                continue
            ps = ps_next()
            for kc in range(KC):
                mm(ps, ps.ap, wb, wb[:, kc, j * 128:(j + 1) * 128], rhs[kc], rhs[kc].ap, start=(kc == 0), stop=(kc == KC - 1))
            consume(j, ps)

    def bcast_stat(src_list, src_aps, scale, epscol):
        raise NotImplementedError

    def rms_to(l, gname, src, dst, n):
        ps = pin[0]
        for c in range(8):
            sq = tb.get()
            act(sq, sq[:, 0:n], src[c], src[c][:, 0:n], AF.Square)
            mm(ps, ps[:, 0:n], cst_b, ones, sq, sq[:, 0:n], start=(c == 0), stop=(c == 7))
        sd = tf.get()
        act(sd, sd[:, 0:n], ps, ps[:, 0:n], AF.Sqrt, extra_r=[epsc], scale=1.0 / D, bias=epsc[:, 0:1])
        rs = tf.get()
        P.op("dve", lambda e: e.reciprocal(out=rs[:, 0:n], in_=sd[:, 0:n]), [sd], [rs])
        for c in range(8):
            stt(dst[c], dst[c][:, 0:n], src[c], src[c][:, 0:n], pcol(l, gname, c), rs, rs[:, 0:n], ALU.mult, ALU.mult, extra_r=[pvs])
        return rs

    mraw = [Buf(big8[c][:, 0:NMEM]) for c in range(8)]
    for c in range(8):
        P.dma("sp", mraw[c].ap, memT_d[c * 128:(c + 1) * 128, :], reads=[Bdram_in], writes=[big8[c]])
        mraw[c] = big8[c]
    for l in range(2):
        mT = OG
        rms_to(l, "g_mem", mraw, mT, NMEM)
        for j in range(2):
            def cons(jj, ps, j=j):
                cp("act", kmT[l][j * 4 + jj], kmT[l][j * 4 + jj].ap, ps, ps[:, 0:NMEM])
            o, w = WCOL[f"kv{j}"]
            wb = wload(l, f"kv{j}")
            for jj in range(4):
                ps = ps_next()
                for kc in range(KC):
                    mm(ps, ps[:, 0:NMEM], wb, wb[:, kc, jj * 128:(jj + 1) * 128], mT[kc], mT[kc][:, 0:NMEM], start=(kc == 0), stop=(kc == KC - 1))
                cons(jj, ps)
        for j in range(2):
            wb = wload(l, f"kv{2 + j}")
            for mb in range(2):
                ps = ps_next()
                for kc in range(KC):
                    mm(ps, ps.ap, mT[kc], mT[kc][:, mb * 128:(mb + 1) * 128], wb, wb[:, kc, :], start=(kc == 0), stop=(kc == KC - 1))
                cp("act", vmt[l][mb], vmt[l][mb][:, j * 512:(j + 1) * 512], ps, ps.ap)

    chk("memkv")
    AR = sbuf([128, 4, 2, 128], BF16)
    Bbd = sbuf([128, 4, 2, 128], BF16)
    Kbd = sbuf([128, 4, 2, 128], BF16)
    P.op("pool", lambda e: e.memset(Bbd.ap, 0.0), [], [Bbd])
    P.op("pool", lambda e: e.memset(Kbd.ap, 0.0), [], [Kbd])
    NXr = Ring(4, [128, 2, 2, 128], BF16)
    NXn = {id(b_): Buf(b_[:, :, 0, :]) for b_ in NXr.b}
    NXx = {id(b_): Buf(b_[:, :, 1, :]) for b_ in NXr.b}
    Ar = Ring(4, [128, 2, 128], BF16)
    Arb = Ring(4, [128, 2, 128], BF16)
    Aak = Ring(2, [128, 2, 128], BF16)
    Ark = Ring(4, [128, 2, 128], BF16)
    Atok = [sbuf([128, 2, 128], BF16) for _ in range(2)]
    Vtok = [sbuf([128, 2, 128], BF16) for _ in range(4)]
    Utok = [sbuf([128, 2, 128], BF16) for _ in range(2)]
    for b_ in Atok + Vtok + Utok:
        P.op("pool", lambda e: e.memset(b_.ap, 0.0), [], [b_])
    BKtok = Ring(4, [128, 2, 128], BF16)
    ApT = Ring(4, [128, 128], BF16)
    Wp = Ring(2, [128, 128], BF16)
    stage = slots[0]
    Up4 = slots[11]
    cntr = dict(at=0, vt=0, ut=0, up=0)

    def scan_pre(l, c, qs2, ctx, E1, bpT, kpT, vb):
        for q in qs2:
            d = ctx[q] = {}
            d["at"] = Atok[cntr["at"] % 2]; cntr["at"] += 1
            d["vt"] = Vtok[cntr["vt"] % 4]; cntr["vt"] += 1
            d["upc"] = cntr["up"] % 4; cntr["up"] += 1
            d["bk"], d["arb"], d["aak"], d["ark"] = BKtok.get(), Arb.get(), Aak.get(), Ark.get()
            d["apT"], d["wp"] = ApT.get(), Wp.get()
        for q in qs2:
            d = ctx[q]
            qs = slice(q * 128, (q + 1) * 128)
            pst = ps_next()
            pv_ = pst.ap
            cp("dve", stage, stage[:, 0:128], AR, AR[:, q, 0, :])
            cp("dve", stage, stage[:, 128:256], bpT, bpT[:, qs])
            cp("dve", stage, stage[:, 256:384], kpT, kpT[:, qs])
            cp("dve", stage, stage[:, 384:512], vb, vb[:, qs])
            for i4 in range(4):
                tr(pst, pv_[:, i4 * 128:(i4 + 1) * 128], stage, stage[:, i4 * 128:(i4 + 1) * 128])
            at, vt, bk = d["at"], d["vt"], d["bk"]
            for h in range(2):
                cp("act", at, at[:, h, h * 64:(h + 1) * 64], pst, pv_[:, h * 64:(h + 1) * 64])
                cp("act", vt, vt[:, h, h * 64:(h + 1) * 64], pst, pv_[:, 384 + h * 64:384 + (h + 1) * 64])
            cp("act", bk, bk.ap, pst, pv_[:, 128:384].rearrange("p (a b) -> p a b", a=2))
            yield
        for q in qs2:
            d = ctx[q]
            ps1, ps2, ps3 = ps_next(), ps_next(), ps_next()
            arq = AR[:, q, :, :].rearrange("p a t -> p (a t)")
            for h in range(2):
                mm(ps1, ps1[:, h * 256:(h + 1) * 256], Bbd, Bbd[:, q, h, :], AR, arq)
                mm(ps2, ps2[:, h * 256:(h + 1) * 256], Kbd, Kbd[:, q, h, :], AR, arq)
            mm(ps3, ps3[:, 0:256], AR, AR[:, q, 0, :], Bbd, Bbd[:, q, :, :].rearrange("p h j -> p (h j)"))
            nx = NXr.get()
            arb, aak, ark, a0 = d["arb"], d["aak"], d["ark"], Ar.get()
            p1v = ps1.ap.rearrange("p (h a t) -> p h a t", h=2, a=2)
            p2v = ps2.ap.rearrange("p (h a t) -> p h a t", h=2, a=2)
            for h in range(2):
                tt("dve", NXn[id(nx)], nx[:, h, 0, :], ps1, p1v[:, h, 0, :], cst_b, cst_b[:, 3, :], ALU.mult)
                tt("dve", arb, arb[:, h, :], ps1, p1v[:, h, 1, :], cst_b, cst_b[:, 4, :], ALU.mult)
                tt("dve", aak, aak[:, h, :], ps2, p2v[:, h, 0, :], cst_b, cst_b[:, 3, :], ALU.mult)
                tt("dve", ark, ark[:, h, :], ps2, p2v[:, h, 1, :], cst_b, cst_b[:, 4, :], ALU.mult)
            tt("dve", a0, a0.ap, ps3, ps3[:, 0:256].rearrange("p (h t) -> p h t", h=2), mSL2, mSL2.ap, ALU.mult)
            tt("dve", NXx[id(nx)], nx[:, :, 1, :], NXn[id(nx)], nx[:, :, 0, :], id2, id2.ap, ALU.add)
            d["nx"], d["A"] = nx, a0
            yield
        for lev in range(7):
            last = (lev == 6)
            pss = {}
            for q in qs2:
                d = ctx[q]
                nx, A_i = d["nx"], d["A"]
                psn = ps_next()
                for h in range(2):
                    if lev == 0:
                        mm(psn, psn[:, h * 256:h * 256 + 128], A_i, A_i[:, h, :], NXn[id(nx)], nx[:, h, 0, :])
                    elif not last:
                        mm(psn, psn[:, h * 256:(h + 1) * 256], A_i, A_i[:, h, :], NXn[id(nx)], nx[:, h, :, :].rearrange("p a t -> p (a t)"), extra_r=[NXx[id(nx)]])
                    else:
                        mm(psn, psn[:, h * 256 + 128:(h + 1) * 256], A_i, A_i[:, h, :], NXx[id(nx)], nx[:, h, 1, :])
                psa = None
                if not last:
                    psa = ps_next()
                    for h in range(2):
                        mm(psa, psa[:, h * 128:(h + 1) * 128], NXn[id(nx)], nx[:, h, 0, :], A_i, A_i[:, h, :])
                pss[q] = (psn, psa)
            for q in qs2:
                d = ctx[q]
                nx = d["nx"]
                psn, psa = pss[q]
                pnv = psn.ap.rearrange("p (h a t) -> p h a t", h=2, a=2)
                nx2 = NXr.get()
                if lev == 0:
                    cp("act", NXn[id(nx2)], nx2[:, :, 0, :], psn, pnv[:, :, 0, :])
                    cp("dve", NXx[id(nx2)], nx2[:, :, 1, :], NXx[id(nx)], nx[:, :, 1, :])
                else:
                    if not last:
                        cp("dve", NXn[id(nx2)], nx2[:, :, 0, :], psn, pnv[:, :, 0, :])
                    tt("dve", NXx[id(nx2)], nx2[:, :, 1, :], psn, pnv[:, :, 1, :], NXx[id(nx)], nx[:, :, 1, :], ALU.add)
                if not last:
                    a2 = Ar.get()
                    cp("act", a2, a2.ap, psa, psa[:, 0:256].rearrange("p (h t) -> p h t", h=2))
                    d["A"] = a2
                d["nx"] = nx2
            yield
        for q in qs2:
            d = ctx[q]
            nx, at, vt, aak, apT, wp = d["nx"], d["at"], d["vt"], d["aak"], d["apT"], d["wp"]
            psw = ps_next()
            for h in range(2):
                mm(psw, psw[:, 0:128], at, at[:, h, :], NXx[id(nx)], nx[:, h, 1, :], start=(h == 0), stop=(h == 1))
            for h in range(2):
                mm(psw, psw[:, 128 + h * 64:128 + (h + 1) * 64], aak, aak[:, h, :], vt, vt[:, h, h * 64:(h + 1) * 64])
            cp("act", apT, apT.ap, psw, psw[:, 0:128])
            cp("act", wp, wp.ap, psw, psw[:, 128:256])
        yield
        for q in qs2:
            d = ctx[q]
            nx, wp = d["nx"], d["wp"]
            psu = ps_next()
            for h in range(2):
                mm(psu, psu[:, h * 64:(h + 1) * 64], NXx[id(nx)], nx[:, h, 1, :], wp, wp[:, h * 64:(h + 1) * 64])
            uc_ = d["upc"]
            cp("act", Up4, Up4[:, uc_ * 128:(uc_ + 1) * 128], psu, psu[:, 0:128])
        yield

    def scan_seq(l, c, qs2, ctx, E1, yT):
        sbd, sfd = Sb[l][c], Sf[l][c]
        sbd2 = sbd.ap.rearrange("p h v -> p (h v)")
        for q in qs2:
            d = ctx[q]
            qs = slice(q * 128, (q + 1) * 128)
            vt, bk, arb, ark, apT, uc_ = d["vt"], d["bk"], d["arb"], d["ark"], d["apT"], d["upc"]
            ut = Utok[cntr["ut"] % 2]; cntr["ut"] += 1
            ps_u = ps_next()
            mm(ps_u, ps_u[:, 0:128], apT, apT.ap, sbd, sbd2)
            for h in range(2):
                tt("dve", ut, ut[:, h, h * 64:(h + 1) * 64], ps_u, ps_u[:, h * 64:(h + 1) * 64],
                   Up4, Up4[:, uc_ * 128 + h * 64:uc_ * 128 + (h + 1) * 64], ALU.add)
            yield
            ps_y = ps_next()
            mm(ps_y, ps_y[:, 0:128], sbd, sbd2, AR, AR[:, q, 1, :], start=True, stop=False)
            for h in range(2):
                mm(ps_y, ps_y[:, 0:128], ut, ut[:, h, :], arb, arb[:, h, :], start=False, stop=False)
                mm(ps_y, ps_y[:, 0:128], vt, vt[:, h, :], ark, ark[:, h, :], start=False, stop=(h == 1))
            ps_s = ps_next()
            mm(ps_s, ps_s[:, 0:256], bk, bk[:, 0, :], ut, ut.ap.rearrange("p h v -> p (h v)"), start=True, stop=False)
            mm(ps_s, ps_s[:, 0:256], bk, bk[:, 1, :], vt, vt.ap.rearrange("p h v -> p (h v)"), start=False, stop=True)
            pc = E1[:, q * 128 + 127:q * 128 + 128]
            for h in range(2):
                hp = slice(h * 64, (h + 1) * 64)
                stt(sfd, sfd[hp, :], sfd, sfd[hp, :], pc[hp, :], ps_s, ps_s[hp, h * 128 + h * 64:h * 128 + (h + 1) * 64],
                    ALU.mult, ALU.add, extra_r=[E1])
            for h in range(2):
                hp = slice(h * 64, (h + 1) * 64)
                cp("act", sbd, sbd[hp, h, :], sfd, sfd[hp, :])
            cp("act", yT, yT[:, qs], ps_y, ps_y[:, 0:128])
            yield

    def scan_all(l, c, E1, bpT, kpT, vb, yT):
        ctxA, ctxB = {}, {}
        for _ in scan_pre(l, c, (0, 1), ctxA, E1, bpT, kpT, vb):
            pass
        gB = scan_pre(l, c, (2, 3), ctxB, E1, bpT, kpT, vb)
        gA = scan_seq(l, c, (0, 1), ctxA, E1, yT)
        aliveA = aliveB = True
        while aliveA or aliveB:
            if aliveB:
                try:
                    next(gB)
                except StopIteration:
                    aliveB = False
            if aliveA:
                try:
                    next(gA)
                except StopIteration:
                    aliveA = False
        for _ in scan_seq(l, c, (2, 3), ctxB, E1, yT):
            pass

    for it in range(NT):
        tsl = slice(it * T, (it + 1) * T)
        for c in range(8):
            P.dma("sp", xT[c].ap, xT_d[c * 128:(c + 1) * 128, tsl], reads=[Bdram_in], writes=[xT[c]])
        for l in range(2):
            rms_to(l, "g_norm", xT, hT, T)
            chk(f"rms{l}")
            hrhs = lambda j: hT
            car = carry[l]

            def shiftmix(ps, mi, npart=128, A=None):
                A = A or slots[11]
                pp = slice(0, npart)
                mu_c, omu_c = pcol(l, "mu", mi), pcol(l, "omu", mi)
                act(A, A[pp, :], ps, ps[pp, :], AF.Identity, extra_r=[pvs], scale=omu_c[pp, :])
                stt(A, A[pp, 1:T], ps, ps[pp, 0:T - 1], mu_c[pp, :], A, A[pp, 1:T], ALU.mult, ALU.add, extra_r=[pvs])
                stt(A, A[pp, 0:1], car, car[pp, mi:mi + 1], mu_c[pp, :], A, A[pp, 0:1], ALU.mult, ALU.add, extra_r=[pvs])
                cp("act", car, car[pp, mi:mi + 1], ps, ps[pp, T - 1:T])
                return A

            lob, vlo = lob_, vlo_

            def cons_lora(j, ps):
                if j == 0:
                    lo = shiftmix(ps, 24)
                    act(lob, lob[0:64, :], lo, lo[0:64, :], AF.Tanh)
                    cp("dve", lob, lob[64:128, :], lo, lo[64:128, :])
                elif l == 1:
                    v_ = shiftmix(ps, 25, 32)
                    cp("dve", vlo, vlo[0:32, :], v_, v_[0:32, :])
            proj(l, "lora", lambda j: hT if (j == 0 or l == 1) else None, cons_lora)

            chk(f"lora{l}")
            for c in range(8):
                got = {}

                def cons_rkvg(j, ps):
                    if j < 3:
                        got[j] = shiftmix(ps, j * 8 + c, A=slots[j])
                    else:
                        g = slots[3]
                        act(g, g.ap, ps, ps.ap, AF.Silu)
                        got[3] = g
                proj(l, f"rkvg{c}", hrhs, cons_rkvg)
                r_, k_, v_, gs = got[0], got[1], got[2], got[3]
                cs = slice(c * 128, (c + 1) * 128)
                psd, psa = ps_next(), ps_next()
                mm(psd, psd.ap, lor[l], lor[l][0:64, 0, cs], lob, lob[0:64, :])
                mm(psa, psa.ap, lor[l], lor[l][64:128, 0, cs], lob, lob[64:128, :])
                sgd, a_ = slots[4], slots[5]
                act(sgd, sgd.ap, psd, psd.ap, AF.Sigmoid, extra_r=[pvs], bias=pcol(l, "w0", c))
                act(a_, a_.ap, psa, psa.ap, AF.Sigmoid, extra_r=[pvs], bias=pcol(l, "a0", c))
                if l == 1:
                    psv = ps_next()
                    mm(psv, psv.ap, lor[l], lor[l][0:32, 1, cs], vlo, vlo[0:32, :])
                    gv = slots[7]
                    act(gv, gv.ap, psv, psv.ap, AF.Sigmoid, extra_r=[pvs], bias=pcol(l, "v0", c))
                    dd = slots[8]
                    tt("dve", dd, dd.ap, vf[c], vf[c].ap, v_, v_.ap, ALU.subtract)
                    tt("dve", dd, dd.ap, dd, dd.ap, gv, gv.ap, ALU.mult)
                    tt("dve", v_, v_.ap, v_, v_.ap, dd, dd.ap, ALU.add)
                else:
                    cp("act", vf[c], vf[c].ap, v_, v_.ap)
                dump(f"r{l}_{c}", r_)
                dump(f"k{l}_{c}", k_)
                dump(f"v{l}_{c}", v_)
                dump(f"a{l}_{c}", a_)
                kkr = slots[6]
                ts("dve", kkr, kkr.ap, k_, k_.ap, pcol(l, "k_k", c), None, ALU.mult, extra_r=[pvs])
                sq = tb.get()
                act(sq, sq.ap, kkr, kkr.ap, AF.Square)
                psn = ps_next()
                mm(psn, psn.ap, cst_b, bones, sq, sq.ap)
                nrm = slots[7]
                act(nrm, nrm.ap, psn, psn.ap, AF.Sqrt)
                ts("dve", nrm, nrm.ap, nrm, nrm.ap, 1e-12, None, ALU.max)
                P.op("dve", lambda e: e.reciprocal(out=nrm.ap, in_=nrm.ap), [nrm], [nrm])
                tt("dve", kkr, kkr.ap, kkr, kkr.ap, nrm, nrm.ap, ALU.mult)
                kk = kkr
                f_ = slots[7]
                ts("dve", f_, f_.ap, a_, a_.ap, pcol(l, "k_a", c), pcol(l, "omka", c), ALU.mult, ALU.add, extra_r=[pvs])
                tt("dve", k_, k_.ap, k_, k_.ap, f_, f_.ap, ALU.mult)
                k2 = k_
                rk = tb.get()
                stt(rk, rk.ap, r_, r_.ap, pcol(l, "r_k", c), k2, k2.ap, ALU.mult, ALU.mult, extra_r=[pvs])
                psb = ps_next()
                mm(psb, psb.ap, cst_b, bones, rk, rk.ap)
                bon = slots[8]
                tt("dve", bon, bon.ap, psb, psb.ap, v_, v_.ap, ALU.mult)
                cum = slots[7]
                P.op("dve", lambda e: e.tensor_tensor_scan(out=cum.ap, data0=rmask.ap, data1=sgd.ap, initial=0.0,
                                                           op0=ALU.mult, op1=ALU.add), [rmask, sgd], [cum])
                E1, E2, E3 = slots[9], slots[10], slots[11]
                act(E1, E1.ap, cum, cum.ap, AF.Exp, scale=-C0)
                act(E2, E2.ap, cum, cum.ap, AF.Exp, scale=C0)
                tt("dve", sgd, sgd.ap, cum, cum.ap, sgd, sgd.ap, ALU.subtract)
                act(E3, E3.ap, sgd, sgd.ap, AF.Exp, scale=-C0)
                dump(f"E1{l}_{c}", E1)
                tt("dve", AR, AR[:, :, 1, :], r_, r_.ap.rearrange("p (q t) -> p q t", q=4), E1, E1.ap.rearrange("p (q t) -> p q t", q=4), ALU.mult)
                stt(AR, AR[:, :, 0, :], kk, kk.ap.rearrange("p (q t) -> p q t", q=4), -1.0, E3, E3.ap.rearrange("p (q t) -> p q t", q=4), ALU.mult, ALU.mult)
                bt, kt = slots[0], slots[11]
                tt("dve", bt, bt.ap, kk, kk.ap, a_, a_.ap, ALU.mult)
                tt("dve", bt, bt.ap, bt, bt.ap, E2, E2.ap, ALU.mult)
                tt("dve", kt, kt.ap, k2, k2.ap, E2, E2.ap, ALU.mult)
                for h in range(2):
                    hp = slice(h * 64, (h + 1) * 64)
                    cp("act", Bbd, Bbd[hp, :, h, :], bt, bt[hp, :].rearrange("p (q t) -> p q t", q=4))
                    cp("dve", Kbd, Kbd[hp, :, h, :], kt, kt[hp, :].rearrange("p (q t) -> p q t", q=4))
                bpT, kpT, vb = tb.get(), tb.get(), tb.get()
                for q in range(4):
                    qs = slice(q * 128, (q + 1) * 128)
                    pc = E1[:, q * 128 + 127:q * 128 + 128]
                    act(bpT, bpT[:, qs], bt, bt[:, qs], AF.Identity, extra_r=[E1], scale=pc)
                    act(kpT, kpT[:, qs], kt, kt[:, qs], AF.Identity, extra_r=[E1], scale=pc)
                cp("act", vb, vb.ap, v_, v_.ap)
                chk(f"prep{l}_{c}")
                yT = slots[1]
                scan_all(l, c, E1, bpT, kpT, vb, yT)
                chk(f"scanend{l}_{c}")
                dump(f"y{l}_{c}", yT)
                yb_, ysq = tb.get(), tb.get()
                cp("dve", yb_, yb_.ap, yT, yT.ap)
                act(ysq, ysq.ap, yT, yT.ap, AF.Square)
                p1, p2 = ps_next(), ps_next()
                mm(p1, p1.ap, cst_b, bones, yb_, yb_.ap)
                mm(p2, p2.ap, cst_b, bones, ysq, ysq.ap)
                mean, var = slots[5], slots[6]
                ts("dve", mean, mean.ap, p1, p1.ap, 1.0 / 64, None, ALU.mult)
                tt("dve", var, var.ap, mean, mean.ap, mean, mean.ap, ALU.mult)
                stt(var, var.ap, p2, p2.ap, 1.0 / 64, var, var.ap, ALU.mult, ALU.subtract)
                act(var, var.ap, var, var.ap, AF.Sqrt, extra_r=[epsc], bias=epsc[:, 1:2])
                P.op("dve", lambda e: e.reciprocal(out=var.ap, in_=var.ap), [var], [var])
                tt("dve", yT, yT.ap, yT, yT.ap, mean, mean.ap, ALU.subtract)
                tt("dve", yT, yT.ap, yT, yT.ap, var, var.ap, ALU.mult)
                ts("dve", yT, yT.ap, yT, yT.ap, pcol(l, "gn_g", c), pcol(l, "gn_b", c), ALU.mult, ALU.add, extra_r=[pvs])
                tt("dve", yT, yT.ap, yT, yT.ap, bon, bon.ap, ALU.add)
                tt("dve", OG[c], OG[c].ap, yT, yT.ap, gs, gs.ap, ALU.mult)
                dump(f"og{l}_{c}", OG[c])

            def branch_out(pn, first, bias_name=None):
                for j in range(4):
                    tmpy = {}

                    def cons(jj, ps, j=j):
                        if jj < 2:
                            t_ = tf.get()
                            if bias_name is None:
                                cp("act", t_, t_.ap, ps, ps.ap)
                            else:
                                act(t_, t_.ap, ps, ps.ap, AF.Identity, extra_r=[pvs], bias=pcol(l, bias_name, 2 * j + jj))
                            tmpy[jj] = t_
                        else:
                            cidx = 2 * j + (jj - 2)
                            sg = tf.get()
                            act(sg, sg.ap, ps, ps.ap, AF.Sigmoid)
                            yb = tmpy[jj - 2]
                            if first:
                                tt("dve", yacc[cidx], yacc[cidx].ap, sg, sg.ap, yb, yb.ap, ALU.mult)
                            else:
                                tt("dve", sg, sg.ap, sg, sg.ap, yb, yb.ap, ALU.mult)
                                tt("dve", yacc[cidx], yacc[cidx].ap, yacc[cidx], yacc[cidx].ap, sg, sg.ap, ALU.add)
                    proj(l, f"{pn}{j}", lambda jj: OG if jj < 2 else hT, cons)

            chk(f"rwkv{l}")
            branch_out("pr", True)
            chk(f"pr{l}")
            dump(f"yacc0_{l}", yacc[0])

            uc = big8
            s1, s2 = pin[0], pin[1]
            dg = dg_keep
            for j in range(4):
                def cons_glu(jj, ps, j=j):
                    c = 2 * j + jj // 2
                    if jj % 2 == 0:
                        cons_glu.pa = ps
                        return
                    gb = tf.get()
                    act(gb, gb.ap, ps, ps.ap, AF.Sigmoid, extra_r=[pvs], bias=pcol(l, "b_glu", 8 + c))
                    pa = cons_glu.pa
                    u = ubr.get()
                    cp("act", u, u[:, 0:30], halo[l], halo[l][:, c, :])
                    stt(u, u[:, 30:30 + T], pa, pa.ap, pcol(l, "b_glu", c), gb, gb.ap, ALU.add, ALU.mult, extra_r=[pvs])
                    for tp in range(31):
                        if tp % 3 == 2:
                            act(dgT[tp], dgT[tp].ap, cst_b, ident, AF.Identity, extra_r=[pvs], scale=pcol(l, "w_dw", c * 31 + tp))
                        else:
                            ts("dve", dgT[tp], dgT[tp].ap, cst_b, ident, pcol(l, "w_dw", c * 31 + tp), None, ALU.mult, extra_r=[pvs])
                    pc_ = ps_next()
                    for tp in range(31):
                        mm(pc_, pc_.ap, dgT[tp], dgT[tp].ap, u, u[:, tp:tp + T], start=(tp == 0), stop=(tp == 30))
                    act(uc[c], uc[c].ap, pc_, pc_.ap, AF.Identity, extra_r=[pvs], bias=pcol(l, "b_dw", c))
                    cp("act", halo[l], halo[l][:, c, :], u, u[:, T:T + 30])
                    ucb, ucs = tb.get(), tb.get()
                    cp("dve", ucb, ucb.ap, uc[c], uc[c].ap)
                    act(ucs, ucs.ap, uc[c], uc[c].ap, AF.Square)
                    mm(s1, s1.ap, cst_b, ones, ucb, ucb.ap, start=(c == 0), stop=(c == 7))
                    mm(s2, s2.ap, cst_b, ones, ucs, ucs.ap, start=(c == 0), stop=(c == 7))
                proj(l, f"glu{j}", hrhs, cons_glu)
            mean, var = slots[10], slots[11]
            act(mean, mean.ap, s1, s1.ap, AF.Identity, scale=1.0 / D)
            tt("dve", var, var.ap, mean, mean.ap, mean, mean.ap, ALU.mult)
            stt(var, var.ap, s2, s2.ap, 1.0 / D, var, var.ap, ALU.mult, ALU.subtract)
            act(var, var.ap, var, var.ap, AF.Sqrt, extra_r=[epsc], bias=epsc[:, 2:3])
            P.op("dve", lambda e: e.reciprocal(out=var.ap, in_=var.ap), [var], [var])
            dump(f"uc{l}_0", uc[0])
            for j in range(2):
                def cons_cg(jj, ps, j=j):
                    c = 4 * j + jj
                    cg = tf.get()
                    act(cg, cg.ap, ps, ps.ap, AF.Silu)
                    t_ = uc[c]
                    tt("dve", t_, t_.ap, t_, t_.ap, mean, mean.ap, ALU.subtract)
                    tt("dve", t_, t_.ap, t_, t_.ap, var, var.ap, ALU.mult)
                    act(t_, t_.ap, t_, t_.ap, AF.Silu, extra_r=[pvs], scale=pcol(l, "ln_g", c), bias=pcol(l, "ln_b", c))
                    tt("dve", OG[c], OG[c].ap, t_, t_.ap, cg, cg.ap, ALU.mult)
                proj(l, f"cg{j}", hrhs, cons_cg)
            dump(f"ug{l}_0", OG[0])
            chk(f"conv{l}")
            branch_out("pc", False, "b_pc")
            chk(f"pc{l}")
            dump(f"yacc1_{l}", yacc[0])

            qT = OG
            for j in range(2):
                def cons_q(jj, ps, j=j):
                    c = 4 * j + jj
                    act(qT[c], qT[c].ap, ps, ps.ap, AF.Identity, scale=1.0 / 16.0)
                proj(l, f"q{j}", hrhs, cons_q)
            att = big8
            prT = prT_keep
            small = small_keep
            for hm in range(4):
                pt = prT[hm % 2]
                for sbk in range(4):
                    ss = slice(sbk * 128, (sbk + 1) * 128)
                    psc = ps_next()
                    for dc in range(2):
                        mm(psc, psc[:, 0:NMEM], qT[2 * hm + dc], qT[2 * hm + dc][:, ss], kmT[l][2 * hm + dc], kmT[l][2 * hm + dc].ap,
                           start=(dc == 0), stop=(dc == 1))
                    P.op("dve", lambda e: e.tensor_reduce(out=small[:, 0:1], in_=psc[:, 0:NMEM], axis=AX.X, op=ALU.max), [psc], [small])
                    ts("dve", small, small[:, 1:2], small, small[:, 0:1], -1.0, None, ALU.mult)
                    ex = tf.get()
                    P.op("act", lambda e: e.activation(out=ex[:, 0:NMEM], in_=psc[:, 0:NMEM], func=AF.Exp, bias=small[:, 1:2],
                                                       accum_out=small[:, 2:3]), [psc, small], [ex, small])
                    P.op("dve", lambda e: e.reciprocal(out=small[:, 3:4], in_=small[:, 2:3]), [small], [small])
                    pb = tf.get()
                    ts("dve", pb, pb[:, 0:NMEM], ex, ex[:, 0:NMEM], small[:, 3:4], None, ALU.mult, extra_r=[small])
                    ptp = ps_next()
                    pv_ = ptp.ap
                    for mb in range(2):
                        tr(ptp, pv_[:, mb * 128:(mb + 1) * 128], pb, pb[:, mb * 128:(mb + 1) * 128])
                    for mb in range(2):
                        cp("act", pt[mb], pt[mb][:, ss], ptp, pv_[:, mb * 128:(mb + 1) * 128])
                for dc in range(2):
                    c = 2 * hm + dc
                    pa_ = ps_next()
                    for mb in range(2):
                        mm(pa_, pa_.ap, vmt[l][mb], vmt[l][mb][:, c * 128:(c + 1) * 128], pt[mb], pt[mb].ap, start=(mb == 0), stop=(mb == 1))
                    cp("act", att[c], att[c].ap, pa_, pa_.ap)
            dump(f"att{l}_0", att[0])
            for j in range(2):
                def cons_mg(jj, ps, j=j):
                    c = 4 * j + jj
                    mg = tf.get()
                    act(mg, mg.ap, ps, ps.ap, AF.Silu)
                    tt("dve", OG[c], OG[c].ap, att[c], att[c].ap, mg, mg.ap, ALU.mult)
                proj(l, f"mg{j}", hrhs, cons_mg)
            chk(f"mem{l}")
            branch_out("pm", False)
            chk(f"pm{l}")
            dump(f"yacc2_{l}", yacc[0])

            for c in range(8):
                cp("act", OG[c], OG[c].ap, yacc[c], yacc[c].ap)
            for j in range(2):
                def cons_o(jj, ps, j=j):
                    c = 4 * j + jj
                    tt("dve", xT[c], xT[c].ap, xT[c], xT[c].ap, ps, ps.ap, ALU.add)
                proj(l, f"wo{j}", lambda jj: OG, cons_o)
            dump(f"x{l}_0", xT[0])

        ps = pin[0]
        for c in range(8):
            sq = tb.get()
            act(sq, sq.ap, xT[c], xT[c].ap, AF.Square)
            mm(ps, ps.ap, cst_b, ones, sq, sq.ap, start=(c == 0), stop=(c == 7))
        sd, rs = tf.get(), tf.get()
        act(sd, sd.ap, ps, ps.ap, AF.Sqrt, extra_r=[epsc], scale=1.0 / D, bias=epsc[:, 0:1])
        P.op("dve", lambda e: e.reciprocal(out=rs.ap, in_=sd.ap), [sd], [rs])
        for c in range(8):
            o_ = tf.get()
            stt(o_, o_.ap, xT[c], xT[c].ap, pcol(0, "g_final", c), rs, rs.ap, ALU.mult, ALU.mult, extra_r=[pvs])
            P.dma("sp", outT_d[c * 128:(c + 1) * 128, tsl], o_.ap, reads=[o_], writes=[Bout])


_CACHE = {}


def kernel(**inp):
    inp = {k: np.asarray(v) for k, v in inp.items()}
    pv, wbig, lora, cst, rm = host_prep(inp)
    x, mem = inp["x"], inp["mem"]
    B = x.shape[0]
    nc = bass.Bass("TRN2", target_bir_lowering=False)
    build(nc)
    in_maps = []
    for b in range(B):
        in_maps.append({"xT": np.ascontiguousarray(x[b].T), "memT": np.ascontiguousarray(mem[b].T),
                        "pv": pv, "wbig": wbig, "lora": lora, "cst": cst, "rm": rm})
    res = run_bass_kernel_spmd(nc, in_maps, core_ids=list(range(B)))
    out = np.stack([np.ascontiguousarray(r["outT"].T) for r in res.results], axis=0)
    return out.astype(np.float32)
```

```python
import numpy as np
import concourse.bass as bass
import concourse.mybir as mybir
from concourse.bass_utils import run_bass_kernel_spmd

F32 = mybir.dt.float32
BF16 = mybir.dt.bfloat16
AF = mybir.ActivationFunctionType
ALU = mybir.AluOpType
AX = mybir.AxisListType
NDS = 24

D = 1024
SEQ = 4096
T = 512
NMEM = 256
KC = 8
C0 = float(np.exp(-0.5))


class Buf:
    __slots__ = ("ap", "w", "r")

    def __init__(self, ap):
        self.ap = ap
        self.w = None
        self.r = {}

    def __getitem__(self, k):
        return self.ap[k]


class Prog:
    def __init__(self, nc):
        self.nc = nc
        self.eng = dict(pe=nc.tensor, dve=nc.vector, act=nc.scalar, pool=nc.gpsimd, sp=nc.sync)
        self.esem = {k: nc.alloc_semaphore("es_" + k) for k in self.eng}
        self.ecnt = {k: 0 for k in self.eng}
        self.seen = {k: {} for k in self.eng}
        self.dsem, self.dtgt, self.dnext = {}, {}, {}
        self.ninst = 0

    def _wait(self, e, ev):
        sem, key, val = ev
        if key == ("e", e) and e == "pe":
            return
        if self.seen[e].get(key, 0) >= val:
            return
        self.eng[e].wait_ge(sem, val)
        self.seen[e][key] = val

    def _deps(self, e, reads, writes):
        for b in reads:
            if b.w is not None:
                self._wait(e, b.w)
        me = ("e", e)
        for b in writes:
            if b.w is not None and b.w[1] != me:
                self._wait(e, b.w)
            for ev in b.r.values():
                if ev[1] != me:
                    self._wait(e, ev)

    def _record(self, ev, reads, writes):
        for b in reads:
            b.r[ev[1]] = ev
        for b in writes:
            b.w = ev
            b.r = {}

    def op(self, e, fn, reads=(), writes=()):
        self._deps(e, reads, writes)
        inst = fn(self.eng[e])
        self.ecnt[e] += 1
        inst.then_inc(self.esem[e], 1)
        self._record((self.esem[e], ("e", e), self.ecnt[e]), reads, writes)
        self.ninst += 1

    def dma(self, q, out_ap, in_ap, reads=(), writes=(), **kw):
        if q not in self.dsem:
            self.dsem[q] = [self.nc.alloc_semaphore(f"ds_{q}{i}") for i in range(NDS)]
            self.dtgt[q] = [0] * NDS
            self.dnext[q] = 0
        j = self.dnext[q]
        self.dnext[q] = (j + 1) % NDS
        key = ("d", q, j)
        if self.dtgt[q][j] > 0:
            self._wait(q, (self.dsem[q][j], key, self.dtgt[q][j]))
        self._deps(q, reads, writes)
        inst = self.eng[q].dma_start(out=out_ap, in_=in_ap, **kw)
        self.dtgt[q][j] += 16
        inst.then_inc(self.dsem[q][j], 16)
        self._record((self.dsem[q][j], key, self.dtgt[q][j]), reads, writes)
        self.ninst += 1

    def finish(self, e="sp"):
        for q in self.dsem:
            for j in range(NDS):
                if self.dtgt[q][j] > 0:
                    self._wait(e, (self.dsem[q][j], ("d", q, j), self.dtgt[q][j]))
        for k in self.eng:
            if k != e and self.ecnt[k] > 0:
                self._wait(e, (self.esem[k], ("e", k), self.ecnt[k]))


PV = {}
_o = 0
for _n, _w in [("g_norm", 8), ("mu", 26), ("w0", 8), ("a0", 8), ("k_k", 8), ("k_a", 8), ("r_k", 8),
               ("gn_g", 8), ("gn_b", 8), ("v0", 8), ("b_glu", 16), ("w_dw", 248), ("b_dw", 8),
               ("ln_g", 8), ("ln_b", 8), ("b_pc", 8), ("g_mem", 8), ("g_final", 8), ("omu", 26), ("omka", 8)]:
    PV[_n] = _o
    _o += _w
NPV = _o

WCOL = {}
_o = 0
for _n, _w in [("lora", 256)] + [(f"rkvg{c}", 512) for c in range(8)] + \
        [(f"pr{j}", 512) for j in range(4)] + [(f"glu{j}", 512) for j in range(4)] + \
        [(f"cg{j}", 512) for j in range(2)] + [(f"pc{j}", 512) for j in range(4)] + \
        [(f"q{j}", 512) for j in range(2)] + [(f"mg{j}", 512) for j in range(2)] + \
        [(f"pm{j}", 512) for j in range(4)] + [(f"wo{j}", 512) for j in range(2)] + \
        [(f"kv{j}", 512) for j in range(4)]:
    WCOL[_n] = (_o, _w)
    _o += _w
TOTC = _o


def _fm(v):
    return np.ascontiguousarray(v.reshape(8, 128).T)


def host_prep(inp):
    f = np.float32
    L = 2
    pv = np.zeros((L, 128, NPV), f)
    wbig = np.zeros((L, 128, KC * TOTC), f)
    lora = np.zeros((L, 128, 2, 1024), f)
    for l in range(L):
        def put(name, arr):
            pv[l][:, PV[name]:PV[name] + arr.shape[1]] = arr
        put("g_norm", _fm(inp["g_norm"][l]))
        mu = inp["mu_shift"][l]
        mucols = np.zeros((128, 26), f)
        mucols[:, 0:25] = mu.reshape(25, 128).T
        if l >= 1:
            mucols[0:32, 25] = inp["mu_vres"][l - 1]
        put("mu", mucols)
        for n in ["w0", "a0", "k_k", "k_a", "gn_g", "gn_b", "b_dw", "ln_g", "ln_b", "g_mem"]:
            src = {"g_mem": "g_mem_norm"}.get(n, n)
            put(n, _fm(inp[src][l]))
        put("r_k", _fm(inp["r_k"][l].reshape(-1)))
        put("b_pc", _fm(inp["b_proj_conv"][l]))
        if l >= 1:
            put("v0", _fm(inp["v0"][l - 1]))
        put("b_glu", np.ascontiguousarray(inp["b_glu"][l].reshape(16, 128).T))
        wd = inp["w_dw"][l]
        put("w_dw", np.ascontiguousarray(wd.reshape(31, 8, 128).transpose(2, 1, 0).reshape(128, 248)))
        put("g_final", _fm(inp["g_final"]))
        w_in = inp["w_in"][l]
        Wc = np.zeros((D, TOTC), f)

        def setc(name, off, arr):
            o, w = WCOL[name]
            Wc[:, o + off:o + off + arr.shape[1]] = arr
        setc("lora", 0, w_in[:, 3072:3200])
        if l >= 1:
            setc("lora", 128, inp["w_vres_down"][l - 1])
        for c in range(8):
            setc(f"rkvg{c}", 0, w_in[:, c * 128:(c + 1) * 128])
            setc(f"rkvg{c}", 128, w_in[:, 1024 + c * 128:1024 + (c + 1) * 128])
            setc(f"rkvg{c}", 256, w_in[:, 2048 + c * 128:2048 + (c + 1) * 128])
            setc(f"rkvg{c}", 384, w_in[:, 3200 + c * 128:3200 + (c + 1) * 128])
        for br, (pn, wp) in enumerate([("pr", inp["w_proj_rwkv"][l]), ("pc", inp["w_proj_conv"][l]),
                                       ("pm", inp["w_proj_mem"][l])]):
            for j in range(4):
                setc(f"{pn}{j}", 0, wp[:, j * 256:(j + 1) * 256])
                mo = 9344 + br * 1024 + j * 256
                setc(f"{pn}{j}", 256, w_in[:, mo:mo + 256])
        for j in range(4):
            for i in range(2):
                c = 2 * j + i
                setc(f"glu{j}", i * 256, w_in[:, 4224 + c * 128:4224 + (c + 1) * 128])
                setc(f"glu{j}", i * 256 + 128, w_in[:, 5248 + c * 128:5248 + (c + 1) * 128])
        for j in range(2):
            setc(f"cg{j}", 0, w_in[:, 6272 + j * 512:6272 + (j + 1) * 512])
            setc(f"q{j}", 0, w_in[:, 7296 + j * 512:7296 + (j + 1) * 512])
            setc(f"mg{j}", 0, w_in[:, 8320 + j * 512:8320 + (j + 1) * 512])
            setc(f"wo{j}", 0, inp["w_out"][l][:, j * 512:(j + 1) * 512])
        for j in range(4):
            setc(f"kv{j}", 0, inp["w_mem_kv"][l][:, j * 512:(j + 1) * 512])
        for (o_, w_) in WCOL.values():
            wbig[l][:, KC * o_:KC * (o_ + w_)] = Wc[:, o_:o_ + w_].reshape(KC, 128, w_).transpose(1, 0, 2).reshape(128, KC * w_)
        lora[l][0:64, 0] = inp["w_decay_up"][l]
        lora[l][64:128, 0] = inp["w_aaa_up"][l]
        if l >= 1:
            lora[l][0:32, 1] = inp["w_vres_up"][l - 1]
    cst = np.zeros((128, 8, 128), f)
    i = np.arange(128)
    cst[:, 0] = np.eye(128)
    cst[:, 1] = 1.0
    cst[:, 2] = (i[:, None] // 64 == i[None, :] // 64)
    cst[:, 3] = (i[:, None] < i[None, :])
    cst[:, 4] = (i[:, None] <= i[None, :])
    cst[:, 5] = (i[:, None] > i[None, :])
    rm = np.ones((128, 512), f)
    rm[:, 0::128] = 0.0
    return pv, wbig, lora, cst, rm


class _Stop(Exception):
    pass


def build(nc, NT=SEQ // T, dbg_names=(), stop_after=None):
    P = Prog(nc)
    try:
        _build(nc, P, NT, dbg_names, stop_after)
    except _Stop:
        pass
    P.finish()
    return P


def _build(nc, P, NT, dbg_names, stop_after):
    def chk(tag):
        if tag == stop_after:
            raise _Stop()
    dt = nc.dram_tensor
    xT_d = dt("xT", [D, SEQ], F32, kind="ExternalInput").ap()
    memT_d = dt("memT", [D, NMEM], F32, kind="ExternalInput").ap()
    pv_d = dt("pv", [2, 128, NPV], F32, kind="ExternalInput").ap()
    wbig_d = dt("wbig", [2, 128, KC * TOTC], F32, kind="ExternalInput").ap()
    lora_d = dt("lora", [2, 128, 2, 1024], F32, kind="ExternalInput").ap()
    cst_d = dt("cst", [128, 8, 128], F32, kind="ExternalInput").ap()
    rm_d = dt("rm", [128, 512], F32, kind="ExternalInput").ap()
    outT_d = dt("outT", [D, SEQ], F32, kind="ExternalOutput").ap()
    dbg_d = None
    if dbg_names:
        dbg_d = dt("dbg", [len(dbg_names), 128, 512], F32, kind="ExternalOutput").ap()
    Bdram_in = Buf(None)
    Bout = Buf(None)
    cnt = [0]

    def sb(shape, dtype, name=None):
        cnt[0] += 1
        return nc.alloc_sbuf_tensor(name or f"t{cnt[0]}", list(shape), dtype)

    def sbuf(shape, dtype):
        return Buf(sb(shape, dtype).ap())

    big8 = [sbuf([128, T], F32) for _ in range(8)]
    cst_f = Buf(big8[0].ap.rearrange("p (a b) -> p a b", a=4))
    cst_f2 = Buf(big8[1].ap.rearrange("p (a b) -> p a b", a=4))
    P.dma("sp", cst_f.ap, cst_d[:, 0:4, :], reads=[Bdram_in], writes=[cst_f, big8[0]])
    P.dma("sp", cst_f2.ap, cst_d[:, 4:8, :], reads=[Bdram_in], writes=[cst_f2, big8[1]])
    cst_b = sbuf([128, 8, 128], BF16)
    P.op("dve", lambda e: e.tensor_copy(out=cst_b[:, 0:4, :], in_=cst_f.ap), [cst_f, big8[0]], [cst_b])
    P.op("dve", lambda e: e.tensor_copy(out=cst_b[:, 4:8, :], in_=cst_f2.ap), [cst_f2, big8[1]], [cst_b])
    ident, ones, bones = cst_b[:, 0, :], cst_b[:, 1, :], cst_b[:, 2, :]
    identf = sbuf([128, 128], F32)
    P.op("dve", lambda e: e.tensor_copy(out=identf.ap, in_=cst_f[:, 0, :]), [cst_f, big8[0]], [identf])
    m12 = Buf(cst_b[:, 3:5, :])
    m12.w = None
    mSL2 = sbuf([128, 2, 128], BF16)
    id2 = sbuf([128, 2, 128], BF16)
    for h in range(2):
        P.op("dve", lambda e: e.tensor_copy(out=mSL2[:, h, :], in_=cst_b[:, 5, :]), [cst_b], [mSL2])
        P.op("dve", lambda e: e.tensor_copy(out=id2[:, h, :], in_=cst_b[:, 0, :]), [cst_b], [id2])
    rmf = Buf(big8[2].ap)
    P.dma("sp", rmf.ap, rm_d, reads=[Bdram_in], writes=[big8[2]])
    rmask = sbuf([128, 512], BF16)
    P.op("dve", lambda e: e.tensor_copy(out=rmask.ap, in_=big8[2].ap), [big8[2]], [rmask])
    pvs = sbuf([128, 2, NPV], F32)
    for l in range(2):
        P.dma("sp", pvs[:, l, :], pv_d[l], reads=[Bdram_in], writes=[pvs])
    for l in range(2):
        for (src, dst, w) in [("mu", "omu", 26), ("k_a", "omka", 8)]:
            P.op("dve", lambda e: e.tensor_scalar(out=pvs[:, l, PV[dst]:PV[dst] + w], in0=pvs[:, l, PV[src]:PV[src] + w],
                                                  scalar1=-1.0, scalar2=1.0, op0=ALU.mult, op1=ALU.add), [pvs], [pvs])
    epsc = sbuf([128, 4], F32)
    for i, v in enumerate([1e-6, 64e-5, 1e-5, 0.0]):
        P.op("dve", lambda e: e.memset(epsc[:, i:i + 1], v), [], [epsc])

    def pcol(l, name, c=0):
        o = PV[name] + c
        return pvs[:, l, o:o + 1]

    lor = [sbuf([128, 2, 1024], BF16) for _ in range(2)]
    for l in range(2):
        P.dma("pool", lor[l].ap, lora_d[l], reads=[Bdram_in], writes=[lor[l]])

    banks = [Buf(nc.alloc_psum_tensor(f"ps{i}", [128, 512], F32).ap()) for i in range(8)]
    ring = banks[:6]
    pin = banks[6:]
    rp = [0]

    def ps_next():
        b = ring[rp[0] % len(ring)]
        rp[0] += 1
        return b

    def bfv(b):
        return b.ap.bitcast(BF16)

    class Ring:
        def __init__(self, n, shape, dtype):
            self.b = [sbuf(shape, dtype) for _ in range(n)]
            self.i = 0

        def get(self):
            b = self.b[self.i % len(self.b)]
            self.i += 1
            return b

    slots = [sbuf([128, 512], F32) for _ in range(12)]
    tf = Ring(0, [128, 512], F32)
    tf.b = slots[0:10]
    tb = Ring(6, [128, 512], BF16)
    lob_, vlo_ = sbuf([128, 512], BF16), sbuf([128, 512], BF16)
    wring = Ring(2, [128, KC, 512], BF16)

    xT = [sbuf([128, T], F32) for _ in range(8)]
    vf = [sbuf([128, T], F32) for _ in range(8)]
    hT = [sbuf([128, T], BF16) for _ in range(8)]
    OG = [sbuf([128, T], BF16) for _ in range(8)]
    yacc = [sbuf([128, T], F32) for _ in range(8)]
    carry = [sbuf([128, 26], F32) for _ in range(2)]
    Sf = [[sbuf([128, 64], F32) for _ in range(8)] for _ in range(2)]
    Sb = [[sbuf([128, 2, 64], BF16) for _ in range(8)] for _ in range(2)]
    halo = [sbuf([128, 8, 30], BF16) for _ in range(2)]
    ubr = Ring(2, [128, 30 + T], BF16)
    dg_keep = sbuf([128, 31, 128], BF16)
    dgT = [Buf(dg_keep[:, tp_, :]) for tp_ in range(31)]
    prT_keep = [[sbuf([128, T], BF16) for _ in range(2)] for _ in range(2)]
    small_keep = sbuf([128, 8], F32)
    kmT = [[sbuf([128, NMEM], BF16) for _ in range(8)] for _ in range(2)]
    vmt = [[sbuf([128, D], BF16) for _ in range(2)] for _ in range(2)]
    for l in range(2):
        P.op("pool", lambda e: e.memset(carry[l].ap, 0.0), [], [carry[l]])
        P.op("pool", lambda e: e.memset(halo[l].ap, 0.0), [], [halo[l]])
        for c in range(8):
            P.op("pool", lambda e: e.memset(Sf[l][c].ap, 0.0), [], [Sf[l][c]])
            P.op("pool", lambda e: e.memset(Sb[l][c].ap, 0.0), [], [Sb[l][c]])

    dbg_list = list(dbg_names)
    dbg_buf = sbuf([128, 512], F32) if dbg_names else None

    def dump(name, b, ap=None):
        if name in dbg_list:
            i = dbg_list.index(name)
            a = b.ap if ap is None else ap
            t = dbg_buf
            P.op("dve", lambda e: e.tensor_copy(out=t[:, 0:a.shape[-1]], in_=a), [b], [t])
            P.dma("sp", dbg_d[i][0:a.shape[0], 0:a.shape[-1]], t[0:a.shape[0], 0:a.shape[-1]], reads=[t], writes=[Bout])
            dbg_list[i] = None

    def act(out_b, out_ap, in_b, in_ap, func, extra_r=(), **kw):
        P.op("act", lambda e: e.activation(out=out_ap, in_=in_ap, func=func, **kw), [in_b] + list(extra_r), [out_b])

    def tt(eng, out_b, out_ap, a_b, a_ap, b_b, b_ap, op):
        P.op(eng, lambda e: e.tensor_tensor(out=out_ap, in0=a_ap, in1=b_ap, op=op), [a_b, b_b], [out_b])

    def stt(out_b, out_ap, a_b, a_ap, scalar, b_b, b_ap, op0, op1, extra_r=()):
        P.op("dve", lambda e: e.scalar_tensor_tensor(out=out_ap, in0=a_ap, scalar=scalar, in1=b_ap, op0=op0, op1=op1),
             [a_b, b_b] + list(extra_r), [out_b])

    def ts(eng, out_b, out_ap, a_b, a_ap, s1, s2, op0, op1=None, extra_r=()):
        if op1 is None:
            P.op(eng, lambda e: e.tensor_scalar(out=out_ap, in0=a_ap, scalar1=s1, scalar2=None, op0=op0),
                 [a_b] + list(extra_r), [out_b])
        else:
            P.op(eng, lambda e: e.tensor_scalar(out=out_ap, in0=a_ap, scalar1=s1, scalar2=s2, op0=op0, op1=op1),
                 [a_b] + list(extra_r), [out_b])

    def cp(eng, out_b, out_ap, in_b, in_ap):
        if eng == "act":
            P.op("act", lambda e: e.copy(out=out_ap, in_=in_ap), [in_b], [out_b])
        elif eng == "dve":
            P.op(eng, lambda e: e.tensor_scalar(out=out_ap, in0=in_ap, scalar1=1.0, scalar2=None, op0=ALU.mult), [in_b], [out_b])
        else:
            P.op(eng, lambda e: e.tensor_copy(out=out_ap, in_=in_ap), [in_b], [out_b])

    def mm(out_b, out_ap, l_b, l_ap, r_b, r_ap, start=True, stop=True, extra_r=()):
        P.op("pe", lambda e: e.matmul(out_ap, lhsT=l_ap, rhs=r_ap, start=start, stop=stop), [l_b, r_b] + list(extra_r), [out_b])

    def tr(out_b, out_ap, in_b, in_ap):
        P.op("pe", lambda e: e.transpose(out_ap, in_ap, identf.ap), [in_b, identf], [out_b])

    def wload(l, name):
        o, w = WCOL[name]
        wb = wring.get()
        src = wbig_d[l][:, KC * o:KC * (o + w)].rearrange("p (k n) -> p k n", k=KC)
        P.dma("pool", wb[:, :, 0:w], src, reads=[Bdram_in], writes=[wb], max_dma_last_dim=8192)
        return wb

    def proj(l, name, rhs_for_chunk, consume):
        o, w = WCOL[name]
        wb = wload(l, name)
        for j in range(w // 128):
            rhs = rhs_for_chunk(j)
            if rhs is None:
                continue
            ps = ps_next()
            for kc in range(KC):
                mm(ps, ps.ap, wb, wb[:, kc, j * 128:(j + 1) * 128], rhs[kc], rhs[kc].ap, start=(kc == 0), stop=(kc == KC - 1))
            consume(j, ps)

    def bcast_stat(src_list, src_aps, scale, epscol):
        raise NotImplementedError

    def rms_to(l, gname, src, dst, n):
        ps = pin[0]
        for c in range(8):
            sq = tb.get()
            act(sq, sq[:, 0:n], src[c], src[c][:, 0:n], AF.Square)
            mm(ps, ps[:, 0:n], cst_b, ones, sq, sq[:, 0:n], start=(c == 0), stop=(c == 7))
        sd = tf.get()
        act(sd, sd[:, 0:n], ps, ps[:, 0:n], AF.Sqrt, extra_r=[epsc], scale=1.0 / D, bias=epsc[:, 0:1])
        rs = tf.get()
        P.op("dve", lambda e: e.reciprocal(out=rs[:, 0:n], in_=sd[:, 0:n]), [sd], [rs])
        for c in range(8):
            stt(dst[c], dst[c][:, 0:n], src[c], src[c][:, 0:n], pcol(l, gname, c), rs, rs[:, 0:n], ALU.mult, ALU.mult, extra_r=[pvs])
        return rs

    mraw = [Buf(big8[c][:, 0:NMEM]) for c in range(8)]
    for c in range(8):
        P.dma("sp", mraw[c].ap, memT_d[c * 128:(c + 1) * 128, :], reads=[Bdram_in], writes=[big8[c]])
        mraw[c] = big8[c]
    for l in range(2):
        mT = OG
        rms_to(l, "g_mem", mraw, mT, NMEM)
        for j in range(2):
            def cons(jj, ps, j=j):
                cp("act", kmT[l][j * 4 + jj], kmT[l][j * 4 + jj].ap, ps, ps[:, 0:NMEM])
            o, w = WCOL[f"kv{j}"]
            wb = wload(l, f"kv{j}")
            for jj in range(4):
                ps = ps_next()
                for kc in range(KC):
                    mm(ps, ps[:, 0:NMEM], wb, wb[:, kc, jj * 128:(jj + 1) * 128], mT[kc], mT[kc][:, 0:NMEM], start=(kc == 0), stop=(kc == KC - 1))
                cons(jj, ps)
        for j in range(2):
            wb = wload(l, f"kv{2 + j}")
            for mb in range(2):
                ps = ps_next()
                for kc in range(KC):
                    mm(ps, ps.ap, mT[kc], mT[kc][:, mb * 128:(mb + 1) * 128], wb, wb[:, kc, :], start=(kc == 0), stop=(kc == KC - 1))
                cp("act", vmt[l][mb], vmt[l][mb][:, j * 512:(j + 1) * 512], ps, ps.ap)

    chk("memkv")
    AR = sbuf([128, 4, 2, 128], BF16)
    Bbd = sbuf([128, 4, 2, 128], BF16)
    Kbd = sbuf([128, 4, 2, 128], BF16)
    P.op("pool", lambda e: e.memset(Bbd.ap, 0.0), [], [Bbd])
    P.op("pool", lambda e: e.memset(Kbd.ap, 0.0), [], [Kbd])
    NXr = Ring(4, [128, 2, 2, 128], BF16)
    NXn = {id(b_): Buf(b_[:, :, 0, :]) for b_ in NXr.b}
    NXx = {id(b_): Buf(b_[:, :, 1, :]) for b_ in NXr.b}
    Ar = Ring(4, [128, 2, 128], BF16)
    Arb = Ring(4, [128, 2, 128], BF16)
    Aak = Ring(2, [128, 2, 128], BF16)
    Ark = Ring(4, [128, 2, 128], BF16)
    Atok = [sbuf([128, 2, 128], BF16) for _ in range(2)]
    Vtok = [sbuf([128, 2, 128], BF16) for _ in range(4)]
    Utok = [sbuf([128, 2, 128], BF16) for _ in range(2)]
    for b_ in Atok + Vtok + Utok:
        P.op("pool", lambda e: e.memset(b_.ap, 0.0), [], [b_])
    BKtok = Ring(4, [128, 2, 128], BF16)
    ApT = Ring(4, [128, 128], BF16)
    Wp = Ring(2, [128, 128], BF16)
    stage = slots[0]
    Up4 = slots[11]
    cntr = dict(at=0, vt=0, ut=0, up=0)

    def scan_pre(l, c, qs2, ctx, E1, bpT, kpT, vb):
        for q in qs2:
            d = ctx[q] = {}
            d["at"] = Atok[cntr["at"] % 2]; cntr["at"] += 1
            d["vt"] = Vtok[cntr["vt"] % 4]; cntr["vt"] += 1
            d["upc"] = cntr["up"] % 4; cntr["up"] += 1
            d["bk"], d["arb"], d["aak"], d["ark"] = BKtok.get(), Arb.get(), Aak.get(), Ark.get()
            d["apT"], d["wp"] = ApT.get(), Wp.get()
        for q in qs2:
            d = ctx[q]
            qs = slice(q * 128, (q + 1) * 128)
            pst = ps_next()
            pv_ = pst.ap
            cp("dve", stage, stage[:, 0:128], AR, AR[:, q, 0, :])
            cp("dve", stage, stage[:, 128:256], bpT, bpT[:, qs])
            cp("dve", stage, stage[:, 256:384], kpT, kpT[:, qs])
            cp("dve", stage, stage[:, 384:512], vb, vb[:, qs])
            for i4 in range(4):
                tr(pst, pv_[:, i4 * 128:(i4 + 1) * 128], stage, stage[:, i4 * 128:(i4 + 1) * 128])
            at, vt, bk = d["at"], d["vt"], d["bk"]
            for h in range(2):
                cp("act", at, at[:, h, h * 64:(h + 1) * 64], pst, pv_[:, h * 64:(h + 1) * 64])
                cp("act", vt, vt[:, h, h * 64:(h + 1) * 64], pst, pv_[:, 384 + h * 64:384 + (h + 1) * 64])
            cp("act", bk, bk.ap, pst, pv_[:, 128:384].rearrange("p (a b) -> p a b", a=2))
            yield
        for q in qs2:
            d = ctx[q]
            ps1, ps2, ps3 = ps_next(), ps_next(), ps_next()
            arq = AR[:, q, :, :].rearrange("p a t -> p (a t)")
            for h in range(2):
                mm(ps1, ps1[:, h * 256:(h + 1) * 256], Bbd, Bbd[:, q, h, :], AR, arq)
                mm(ps2, ps2[:, h * 256:(h + 1) * 256], Kbd, Kbd[:, q, h, :], AR, arq)
            mm(ps3, ps3[:, 0:256], AR, AR[:, q, 0, :], Bbd, Bbd[:, q, :, :].rearrange("p h j -> p (h j)"))
            nx = NXr.get()
            arb, aak, ark, a0 = d["arb"], d["aak"], d["ark"], Ar.get()
            p1v = ps1.ap.rearrange("p (h a t) -> p h a t", h=2, a=2)
            p2v = ps2.ap.rearrange("p (h a t) -> p h a t", h=2, a=2)
            for h in range(2):
                tt("dve", NXn[id(nx)], nx[:, h, 0, :], ps1, p1v[:, h, 0, :], cst_b, cst_b[:, 3, :], ALU.mult)
                tt("dve", arb, arb[:, h, :], ps1, p1v[:, h, 1, :], cst_b, cst_b[:, 4, :], ALU.mult)
                tt("dve", aak, aak[:, h, :], ps2, p2v[:, h, 0, :], cst_b, cst_b[:, 3, :], ALU.mult)
                tt("dve", ark, ark[:, h, :], ps2, p2v[:, h, 1, :], cst_b, cst_b[:, 4, :], ALU.mult)
            tt("dve", a0, a0.ap, ps3, ps3[:, 0:256].rearrange("p (h t) -> p h t", h=2), mSL2, mSL2.ap, ALU.mult)
            tt("dve", NXx[id(nx)], nx[:, :, 1, :], NXn[id(nx)], nx[:, :, 0, :], id2, id2.ap, ALU.add)
            d["nx"], d["A"] = nx, a0
            yield
        for lev in range(7):
            last = (lev == 6)
            pss = {}
            for q in qs2:
                d = ctx[q]
                nx, A_i = d["nx"], d["A"]
                psn = ps_next()
                for h in range(2):
                    if lev == 0:
                        mm(psn, psn[:, h * 256:h * 256 + 128], A_i, A_i[:, h, :], NXn[id(nx)], nx[:, h, 0, :])
                    elif not last:
                        mm(psn, psn[:, h * 256:(h + 1) * 256], A_i, A_i[:, h, :], NXn[id(nx)], nx[:, h, :, :].rearrange("p a t -> p (a t)"), extra_r=[NXx[id(nx)]])
                    else:
                        mm(psn, psn[:, h * 256 + 128:(h + 1) * 256], A_i, A_i[:, h, :], NXx[id(nx)], nx[:, h, 1, :])
                psa = None
                if not last:
                    psa = ps_next()
                    for h in range(2):
                        mm(psa, psa[:, h * 128:(h + 1) * 128], NXn[id(nx)], nx[:, h, 0, :], A_i, A_i[:, h, :])
                pss[q] = (psn, psa)
            for q in qs2:
                d = ctx[q]
                nx = d["nx"]
                psn, psa = pss[q]
                pnv = psn.ap.rearrange("p (h a t) -> p h a t", h=2, a=2)
                nx2 = NXr.get()
                if lev == 0:
                    cp("act", NXn[id(nx2)], nx2[:, :, 0, :], psn, pnv[:, :, 0, :])
                    cp("dve", NXx[id(nx2)], nx2[:, :, 1, :], NXx[id(nx)], nx[:, :, 1, :])
                else:
                    if not last:
                        cp("dve", NXn[id(nx2)], nx2[:, :, 0, :], psn, pnv[:, :, 0, :])
                    tt("dve", NXx[id(nx2)], nx2[:, :, 1, :], psn, pnv[:, :, 1, :], NXx[id(nx)], nx[:, :, 1, :], ALU.add)
                if not last:
                    a2 = Ar.get()
                    cp("act", a2, a2.ap, psa, psa[:, 0:256].rearrange("p (h t) -> p h t", h=2))
                    d["A"] = a2
                d["nx"] = nx2
            yield
        for q in qs2:
            d = ctx[q]
            nx, at, vt, aak, apT, wp = d["nx"], d["at"], d["vt"], d["aak"], d["apT"], d["wp"]
            psw = ps_next()
            for h in range(2):
                mm(psw, psw[:, 0:128], at, at[:, h, :], NXx[id(nx)], nx[:, h, 1, :], start=(h == 0), stop=(h == 1))
            for h in range(2):
                mm(psw, psw[:, 128 + h * 64:128 + (h + 1) * 64], aak, aak[:, h, :], vt, vt[:, h, h * 64:(h + 1) * 64])
            cp("act", apT, apT.ap, psw, psw[:, 0:128])
            cp("act", wp, wp.ap, psw, psw[:, 128:256])
        yield
        for q in qs2:
            d = ctx[q]
            nx, wp = d["nx"], d["wp"]
            psu = ps_next()
            for h in range(2):
                mm(psu, psu[:, h * 64:(h + 1) * 64], NXx[id(nx)], nx[:, h, 1, :], wp, wp[:, h * 64:(h + 1) * 64])
            uc_ = d["upc"]
            cp("act", Up4, Up4[:, uc_ * 128:(uc_ + 1) * 128], psu, psu[:, 0:128])
        yield

    def scan_seq(l, c, qs2, ctx, E1, yT):
        sbd, sfd = Sb[l][c], Sf[l][c]
        sbd2 = sbd.ap.rearrange("p h v -> p (h v)")
        for q in qs2:
            d = ctx[q]
            qs = slice(q * 128, (q + 1) * 128)
            vt, bk, arb, ark, apT, uc_ = d["vt"], d["bk"], d["arb"], d["ark"], d["apT"], d["upc"]
            ut = Utok[cntr["ut"] % 2]; cntr["ut"] += 1
            ps_u = ps_next()
            mm(ps_u, ps_u[:, 0:128], apT, apT.ap, sbd, sbd2)
            for h in range(2):
                tt("dve", ut, ut[:, h, h * 64:(h + 1) * 64], ps_u, ps_u[:, h * 64:(h + 1) * 64],
                   Up4, Up4[:, uc_ * 128 + h * 64:uc_ * 128 + (h + 1) * 64], ALU.add)
            yield
            ps_y = ps_next()
            mm(ps_y, ps_y[:, 0:128], sbd, sbd2, AR, AR[:, q, 1, :], start=True, stop=False)
            for h in range(2):
                mm(ps_y, ps_y[:, 0:128], ut, ut[:, h, :], arb, arb[:, h, :], start=False, stop=False)
                mm(ps_y, ps_y[:, 0:128], vt, vt[:, h, :], ark, ark[:, h, :], start=False, stop=(h == 1))
            ps_s = ps_next()
            mm(ps_s, ps_s[:, 0:256], bk, bk[:, 0, :], ut, ut.ap.rearrange("p h v -> p (h v)"), start=True, stop=False)
            mm(ps_s, ps_s[:, 0:256], bk, bk[:, 1, :], vt, vt.ap.rearrange("p h v -> p (h v)"), start=False, stop=True)
            pc = E1[:, q * 128 + 127:q * 128 + 128]
            for h in range(2):
                hp = slice(h * 64, (h + 1) * 64)
                stt(sfd, sfd[hp, :], sfd, sfd[hp, :], pc[hp, :], ps_s, ps_s[hp, h * 128 + h * 64:h * 128 + (h + 1) * 64],
                    ALU.mult, ALU.add, extra_r=[E1])
            for h in range(2):
                hp = slice(h * 64, (h + 1) * 64)
                cp("act", sbd, sbd[hp, h, :], sfd, sfd[hp, :])
            cp("act", yT, yT[:, qs], ps_y, ps_y[:, 0:128])
            yield

    def scan_all(l, c, E1, bpT, kpT, vb, yT):
        ctxA, ctxB = {}, {}
        for _ in scan_pre(l, c, (0, 1), ctxA, E1, bpT, kpT, vb):
            pass
        gB = scan_pre(l, c, (2, 3), ctxB, E1, bpT, kpT, vb)
        gA = scan_seq(l, c, (0, 1), ctxA, E1, yT)
        aliveA = aliveB = True
        while aliveA or aliveB:
            if aliveB:
                try:
                    next(gB)
                except StopIteration:
                    aliveB = False
            if aliveA:
                try:
                    next(gA)
                except StopIteration:
                    aliveA = False
        for _ in scan_seq(l, c, (2, 3), ctxB, E1, yT):
            pass

    for it in range(NT):
        tsl = slice(it * T, (it + 1) * T)
        for c in range(8):
            P.dma("sp", xT[c].ap, xT_d[c * 128:(c + 1) * 128, tsl], reads=[Bdram_in], writes=[xT[c]])
        for l in range(2):
            rms_to(l, "g_norm", xT, hT, T)
            chk(f"rms{l}")
            hrhs = lambda j: hT
            car = carry[l]

            def shiftmix(ps, mi, npart=128, A=None):
                A = A or slots[11]
                pp = slice(0, npart)
                mu_c, omu_c = pcol(l, "mu", mi), pcol(l, "omu", mi)
                act(A, A[pp, :], ps, ps[pp, :], AF.Identity, extra_r=[pvs], scale=omu_c[pp, :])
                stt(A, A[pp, 1:T], ps, ps[pp, 0:T - 1], mu_c[pp, :], A, A[pp, 1:T], ALU.mult, ALU.add, extra_r=[pvs])
                stt(A, A[pp, 0:1], car, car[pp, mi:mi + 1], mu_c[pp, :], A, A[pp, 0:1], ALU.mult, ALU.add, extra_r=[pvs])
                cp("act", car, car[pp, mi:mi + 1], ps, ps[pp, T - 1:T])
                return A

            lob, vlo = lob_, vlo_

            def cons_lora(j, ps):
                if j == 0:
                    lo = shiftmix(ps, 24)
                    act(lob, lob[0:64, :], lo, lo[0:64, :], AF.Tanh)
                    cp("dve", lob, lob[64:128, :], lo, lo[64:128, :])
                elif l == 1:
                    v_ = shiftmix(ps, 25, 32)
                    cp("dve", vlo, vlo[0:32, :], v_, v_[0:32, :])
            proj(l, "lora", lambda j: hT if (j == 0 or l == 1) else None, cons_lora)

            chk(f"lora{l}")
            for c in range(8):
                got = {}

                def cons_rkvg(j, ps):
                    if j < 3:
                        got[j] = shiftmix(ps, j * 8 + c, A=slots[j])
                    else:
                        g = slots[3]
                        act(g, g.ap, ps, ps.ap, AF.Silu)
                        got[3] = g
                proj(l, f"rkvg{c}", hrhs, cons_rkvg)
                r_, k_, v_, gs = got[0], got[1], got[2], got[3]
                cs = slice(c * 128, (c + 1) * 128)
                psd, psa = ps_next(), ps_next()
                mm(psd, psd.ap, lor[l], lor[l][0:64, 0, cs], lob, lob[0:64, :])
                mm(psa, psa.ap, lor[l], lor[l][64:128, 0, cs], lob, lob[64:128, :])
                sgd, a_ = slots[4], slots[5]
                act(sgd, sgd.ap, psd, psd.ap, AF.Sigmoid, extra_r=[pvs], bias=pcol(l, "w0", c))
                act(a_, a_.ap, psa, psa.ap, AF.Sigmoid, extra_r=[pvs], bias=pcol(l, "a0", c))
                if l == 1:
                    psv = ps_next()
                    mm(psv, psv.ap, lor[l], lor[l][0:32, 1, cs], vlo, vlo[0:32, :])
                    gv = slots[7]
                    act(gv, gv.ap, psv, psv.ap, AF.Sigmoid, extra_r=[pvs], bias=pcol(l, "v0", c))
                    dd = slots[8]
                    tt("dve", dd, dd.ap, vf[c], vf[c].ap, v_, v_.ap, ALU.subtract)
                    tt("dve", dd, dd.ap, dd, dd.ap, gv, gv.ap, ALU.mult)
                    tt("dve", v_, v_.ap, v_, v_.ap, dd, dd.ap, ALU.add)
                else:
                    cp("act", vf[c], vf[c].ap, v_, v_.ap)
                dump(f"r{l}_{c}", r_)
                dump(f"k{l}_{c}", k_)
                dump(f"v{l}_{c}", v_)
                dump(f"a{l}_{c}", a_)
                kkr = slots[6]
                ts("dve", kkr, kkr.ap, k_, k_.ap, pcol(l, "k_k", c), None, ALU.mult, extra_r=[pvs])
                sq = tb.get()
                act(sq, sq.ap, kkr, kkr.ap, AF.Square)
                psn = ps_next()
                mm(psn, psn.ap, cst_b, bones, sq, sq.ap)
                nrm = slots[7]
                act(nrm, nrm.ap, psn, psn.ap, AF.Sqrt)
                ts("dve", nrm, nrm.ap, nrm, nrm.ap, 1e-12, None, ALU.max)
                P.op("dve", lambda e: e.reciprocal(out=nrm.ap, in_=nrm.ap), [nrm], [nrm])
                tt("dve", kkr, kkr.ap, kkr, kkr.ap, nrm, nrm.ap, ALU.mult)
                kk = kkr
                f_ = slots[7]
                ts("dve", f_, f_.ap, a_, a_.ap, pcol(l, "k_a", c), pcol(l, "omka", c), ALU.mult, ALU.add, extra_r=[pvs])
                tt("dve", k_, k_.ap, k_, k_.ap, f_, f_.ap, ALU.mult)
                k2 = k_
                rk = tb.get()
                stt(rk, rk.ap, r_, r_.ap, pcol(l, "r_k", c), k2, k2.ap, ALU.mult, ALU.mult, extra_r=[pvs])
                psb = ps_next()
                mm(psb, psb.ap, cst_b, bones, rk, rk.ap)
                bon = slots[8]
                tt("dve", bon, bon.ap, psb, psb.ap, v_, v_.ap, ALU.mult)
                cum = slots[7]
                P.op("dve", lambda e: e.tensor_tensor_scan(out=cum.ap, data0=rmask.ap, data1=sgd.ap, initial=0.0,
                                                           op0=ALU.mult, op1=ALU.add), [rmask, sgd], [cum])
                E1, E2, E3 = slots[9], slots[10], slots[11]
                act(E1, E1.ap, cum, cum.ap, AF.Exp, scale=-C0)
                act(E2, E2.ap, cum, cum.ap, AF.Exp, scale=C0)
                tt("dve", sgd, sgd.ap, cum, cum.ap, sgd, sgd.ap, ALU.subtract)
                act(E3, E3.ap, sgd, sgd.ap, AF.Exp, scale=-C0)
                dump(f"E1{l}_{c}", E1)
                tt("dve", AR, AR[:, :, 1, :], r_, r_.ap.rearrange("p (q t) -> p q t", q=4), E1, E1.ap.rearrange("p (q t) -> p q t", q=4), ALU.mult)
                stt(AR, AR[:, :, 0, :], kk, kk.ap.rearrange("p (q t) -> p q t", q=4), -1.0, E3, E3.ap.rearrange("p (q t) -> p q t", q=4), ALU.mult, ALU.mult)
                bt, kt = slots[0], slots[11]
                tt("dve", bt, bt.ap, kk, kk.ap, a_, a_.ap, ALU.mult)
                tt("dve", bt, bt.ap, bt, bt.ap, E2, E2.ap, ALU.mult)
                tt("dve", kt, kt.ap, k2, k2.ap, E2, E2.ap, ALU.mult)
                for h in range(2):
                    hp = slice(h * 64, (h + 1) * 64)
                    cp("act", Bbd, Bbd[hp, :, h, :], bt, bt[hp, :].rearrange("p (q t) -> p q t", q=4))
                    cp("dve", Kbd, Kbd[hp, :, h, :], kt, kt[hp, :].rearrange("p (q t) -> p q t", q=4))
                bpT, kpT, vb = tb.get(), tb.get(), tb.get()
                for q in range(4):
                    qs = slice(q * 128, (q + 1) * 128)
                    pc = E1[:, q * 128 + 127:q * 128 + 128]
                    act(bpT, bpT[:, qs], bt, bt[:, qs], AF.Identity, extra_r=[E1], scale=pc)
                    act(kpT, kpT[:, qs], kt, kt[:, qs], AF.Identity, extra_r=[E1], scale=pc)
                cp("act", vb, vb.ap, v_, v_.ap)
                chk(f"prep{l}_{c}")
                yT = slots[1]
                scan_all(l, c, E1, bpT, kpT, vb, yT)
                chk(f"scanend{l}_{c}")
                dump(f"y{l}_{c}", yT)
                yb_, ysq = tb.get(), tb.get()
                cp("dve", yb_, yb_.ap, yT, yT.ap)
                act(ysq, ysq.ap, yT, yT.ap, AF.Square)
                p1, p2 = ps_next(), ps_next()
                mm(p1, p1.ap, cst_b, bones, yb_, yb_.ap)
                mm(p2, p2.ap, cst_b, bones, ysq, ysq.ap)
                mean, var = slots[5], slots[6]
                ts("dve", mean, mean.ap, p1, p1.ap, 1.0 / 64, None, ALU.mult)
                tt("dve", var, var.ap, mean, mean.ap, mean, mean.ap, ALU.mult)
                stt(var, var.ap, p2, p2.ap, 1.0 / 64, var, var.ap, ALU.mult, ALU.subtract)
                act(var, var.ap, var, var.ap, AF.Sqrt, extra_r=[epsc], bias=epsc[:, 1:2])
                P.op("dve", lambda e: e.reciprocal(out=var.ap, in_=var.ap), [var], [var])
                tt("dve", yT, yT.ap, yT, yT.ap, mean, mean.ap, ALU.subtract)
                tt("dve", yT, yT.ap, yT, yT.ap, var, var.ap, ALU.mult)
                ts("dve", yT, yT.ap, yT, yT.ap, pcol(l, "gn_g", c), pcol(l, "gn_b", c), ALU.mult, ALU.add, extra_r=[pvs])
                tt("dve", yT, yT.ap, yT, yT.ap, bon, bon.ap, ALU.add)
                tt("dve", OG[c], OG[c].ap, yT, yT.ap, gs, gs.ap, ALU.mult)
                dump(f"og{l}_{c}", OG[c])

            def branch_out(pn, first, bias_name=None):
                for j in range(4):
                    tmpy = {}

                    def cons(jj, ps, j=j):
                        if jj < 2:
                            t_ = tf.get()
                            if bias_name is None:
                                cp("act", t_, t_.ap, ps, ps.ap)
                            else:
                                act(t_, t_.ap, ps, ps.ap, AF.Identity, extra_r=[pvs], bias=pcol(l, bias_name, 2 * j + jj))
                            tmpy[jj] = t_
                        else:
                            cidx = 2 * j + (jj - 2)
                            sg = tf.get()
                            act(sg, sg.ap, ps, ps.ap, AF.Sigmoid)
                            yb = tmpy[jj - 2]
                            if first:
                                tt("dve", yacc[cidx], yacc[cidx].ap, sg, sg.ap, yb, yb.ap, ALU.mult)
                            else:
                                tt("dve", sg, sg.ap, sg, sg.ap, yb, yb.ap, ALU.mult)
                                tt("dve", yacc[cidx], yacc[cidx].ap, yacc[cidx], yacc[cidx].ap, sg, sg.ap, ALU.add)
                    proj(l, f"{pn}{j}", lambda jj: OG if jj < 2 else hT, cons)

            chk(f"rwkv{l}")
            branch_out("pr", True)
            chk(f"pr{l}")
            dump(f"yacc0_{l}", yacc[0])

            uc = big8
            s1, s2 = pin[0], pin[1]
            dg = dg_keep
            for j in range(4):
                def cons_glu(jj, ps, j=j):
                    c = 2 * j + jj // 2
                    if jj % 2 == 0:
                        cons_glu.pa = ps
                        return
                    gb = tf.get()
                    act(gb, gb.ap, ps, ps.ap, AF.Sigmoid, extra_r=[pvs], bias=pcol(l, "b_glu", 8 + c))
                    pa = cons_glu.pa
                    u = ubr.get()
                    cp("act", u, u[:, 0:30], halo[l], halo[l][:, c, :])
                    stt(u, u[:, 30:30 + T], pa, pa.ap, pcol(l, "b_glu", c), gb, gb.ap, ALU.add, ALU.mult, extra_r=[pvs])
                    for tp in range(31):
                        if tp % 3 == 2:
                            act(dgT[tp], dgT[tp].ap, cst_b, ident, AF.Identity, extra_r=[pvs], scale=pcol(l, "w_dw", c * 31 + tp))
                        else:
                            ts("dve", dgT[tp], dgT[tp].ap, cst_b, ident, pcol(l, "w_dw", c * 31 + tp), None, ALU.mult, extra_r=[pvs])
                    pc_ = ps_next()
                    for tp in range(31):
                        mm(pc_, pc_.ap, dgT[tp], dgT[tp].ap, u, u[:, tp:tp + T], start=(tp == 0), stop=(tp == 30))
                    act(uc[c], uc[c].ap, pc_, pc_.ap, AF.Identity, extra_r=[pvs], bias=pcol(l, "b_dw", c))
                    cp("act", halo[l], halo[l][:, c, :], u, u[:, T:T + 30])
                    ucb, ucs = tb.get(), tb.get()
                    cp("dve", ucb, ucb.ap, uc[c], uc[c].ap)
                    act(ucs, ucs.ap, uc[c], uc[c].ap, AF.Square)
                    mm(s1, s1.ap, cst_b, ones, ucb, ucb.ap, start=(c == 0), stop=(c == 7))
                    mm(s2, s2.ap, cst_b, ones, ucs, ucs.ap, start=(c == 0), stop=(c == 7))
                proj(l, f"glu{j}", hrhs, cons_glu)
            mean, var = slots[10], slots[11]
            act(mean, mean.ap, s1, s1.ap, AF.Identity, scale=1.0 / D)
            tt("dve", var, var.ap, mean, mean.ap, mean, mean.ap, ALU.mult)
            stt(var, var.ap, s2, s2.ap, 1.0 / D, var, var.ap, ALU.mult, ALU.subtract)
            act(var, var.ap, var, var.ap, AF.Sqrt, extra_r=[epsc], bias=epsc[:, 2:3])
            P.op("dve", lambda e: e.reciprocal(out=var.ap, in_=var.ap), [var], [var])
            dump(f"uc{l}_0", uc[0])
            for j in range(2):
                def cons_cg(jj, ps, j=j):
                    c = 4 * j + jj
                    cg = tf.get()
                    act(cg, cg.ap, ps, ps.ap, AF.Silu)
                    t_ = uc[c]
                    tt("dve", t_, t_.ap, t_, t_.ap, mean, mean.ap, ALU.subtract)
                    tt("dve", t_, t_.ap, t_, t_.ap, var, var.ap, ALU.mult)
                    act(t_, t_.ap, t_, t_.ap, AF.Silu, extra_r=[pvs], scale=pcol(l, "ln_g", c), bias=pcol(l, "ln_b", c))
                    tt("dve", OG[c], OG[c].ap, t_, t_.ap, cg, cg.ap, ALU.mult)
                proj(l, f"cg{j}", hrhs, cons_cg)
            dump(f"ug{l}_0", OG[0])
            chk(f"conv{l}")
            branch_out("pc", False, "b_pc")
            chk(f"pc{l}")
            dump(f"yacc1_{l}", yacc[0])

            qT = OG
            for j in range(2):
                def cons_q(jj, ps, j=j):
                    c = 4 * j + jj
                    act(qT[c], qT[c].ap, ps, ps.ap, AF.Identity, scale=1.0 / 16.0)
                proj(l, f"q{j}", hrhs, cons_q)
            att = big8
            prT = prT_keep
            small = small_keep
            for hm in range(4):
                pt = prT[hm % 2]
                for sbk in range(4):
                    ss = slice(sbk * 128, (sbk + 1) * 128)
                    psc = ps_next()
                    for dc in range(2):
                        mm(psc, psc[:, 0:NMEM], qT[2 * hm + dc], qT[2 * hm + dc][:, ss], kmT[l][2 * hm + dc], kmT[l][2 * hm + dc].ap,
                           start=(dc == 0), stop=(dc == 1))
                    P.op("dve", lambda e: e.tensor_reduce(out=small[:, 0:1], in_=psc[:, 0:NMEM], axis=AX.X, op=ALU.max), [psc], [small])
                    ts("dve", small, small[:, 1:2], small, small[:, 0:1], -1.0, None, ALU.mult)
                    ex = tf.get()
                    P.op("act", lambda e: e.activation(out=ex[:, 0:NMEM], in_=psc[:, 0:NMEM], func=AF.Exp, bias=small[:, 1:2],
                                                       accum_out=small[:, 2:3]), [psc, small], [ex, small])
                    P.op("dve", lambda e: e.reciprocal(out=small[:, 3:4], in_=small[:, 2:3]), [small], [small])
                    pb = tf.get()
                    ts("dve", pb, pb[:, 0:NMEM], ex, ex[:, 0:NMEM], small[:, 3:4], None, ALU.mult, extra_r=[small])
                    ptp = ps_next()
                    pv_ = ptp.ap
                    for mb in range(2):
                        tr(ptp, pv_[:, mb * 128:(mb + 1) * 128], pb, pb[:, mb * 128:(mb + 1) * 128])
                    for mb in range(2):
                        cp("act", pt[mb], pt[mb][:, ss], ptp, pv_[:, mb * 128:(mb + 1) * 128])
                for dc in range(2):
                    c = 2 * hm + dc
                    pa_ = ps_next()
                    for mb in range(2):
                        mm(pa_, pa_.ap, vmt[l][mb], vmt[l][mb][:, c * 128:(c + 1) * 128], pt[mb], pt[mb].ap, start=(mb == 0), stop=(mb == 1))
                    cp("act", att[c], att[c].ap, pa_, pa_.ap)
            dump(f"att{l}_0", att[0])
            for j in range(2):
                def cons_mg(jj, ps, j=j):
                    c = 4 * j + jj
                    mg = tf.get()
                    act(mg, mg.ap, ps, ps.ap, AF.Silu)
                    tt("dve", OG[c], OG[c].ap, att[c], att[c].ap, mg, mg.ap, ALU.mult)
                proj(l, f"mg{j}", hrhs, cons_mg)
            chk(f"mem{l}")
            branch_out("pm", False)
            chk(f"pm{l}")
            dump(f"yacc2_{l}", yacc[0])

            for c in range(8):
                cp("act", OG[c], OG[c].ap, yacc[c], yacc[c].ap)
            for j in range(2):
                def cons_o(jj, ps, j=j):
                    c = 4 * j + jj
                    tt("dve", xT[c], xT[c].ap, xT[c], xT[c].ap, ps, ps.ap, ALU.add)
                proj(l, f"wo{j}", lambda jj: OG, cons_o)
            dump(f"x{l}_0", xT[0])

        ps = pin[0]
        for c in range(8):
            sq = tb.get()
            act(sq, sq.ap, xT[c], xT[c].ap, AF.Square)
            mm(ps, ps.ap, cst_b, ones, sq, sq.ap, start=(c == 0), stop=(c == 7))
        sd, rs = tf.get(), tf.get()
        act(sd, sd.ap, ps, ps.ap, AF.Sqrt, extra_r=[epsc], scale=1.0 / D, bias=epsc[:, 0:1])
        P.op("dve", lambda e: e.reciprocal(out=rs.ap, in_=sd.ap), [sd], [rs])
        for c in range(8):
            o_ = tf.get()
            stt(o_, o_.ap, xT[c], xT[c].ap, pcol(0, "g_final", c), rs, rs.ap, ALU.mult, ALU.mult, extra_r=[pvs])
            P.dma("sp", outT_d[c * 128:(c + 1) * 128, tsl], o_.ap, reads=[o_], writes=[Bout])


_CACHE = {}


def kernel(**inp):
    inp = {k: np.asarray(v) for k, v in inp.items()}
    pv, wbig, lora, cst, rm = host_prep(inp)
    x, mem = inp["x"], inp["mem"]
    B = x.shape[0]
    nc = bass.Bass("TRN2", target_bir_lowering=False)
    build(nc)
    in_maps = []
    for b in range(B):
        in_maps.append({"xT": np.ascontiguousarray(x[b].T), "memT": np.ascontiguousarray(mem[b].T),
                        "pv": pv, "wbig": wbig, "lora": lora, "cst": cst, "rm": rm})
    res = run_bass_kernel_spmd(nc, in_maps, core_ids=list(range(B)))
    out = np.stack([np.ascontiguousarray(r["outT"].T) for r in res.results], axis=0)
    return out.astype(np.float32)
```

```python
import numpy as np
import concourse.bass as bass
import concourse.mybir as mybir
from concourse.bass_utils import run_bass_kernel_spmd

F32 = mybir.dt.float32
BF16 = mybir.dt.bfloat16
AF = mybir.ActivationFunctionType
ALU = mybir.AluOpType
AX = mybir.AxisListType
NDS = 24

D = 1024
SEQ = 4096
T = 512
NMEM = 256
KC = 8
C0 = float(np.exp(-0.5))


class Buf:
    __slots__ = ("ap", "w", "r")

    def __init__(self, ap):
        self.ap = ap
        self.w = None
        self.r = {}

    def __getitem__(self, k):
        return self.ap[k]


class Prog:
    def __init__(self, nc):
        self.nc = nc
        self.eng = dict(pe=nc.tensor, dve=nc.vector, act=nc.scalar, pool=nc.gpsimd, sp=nc.sync)
        self.esem = {k: nc.alloc_semaphore("es_" + k) for k in self.eng}
        self.ecnt = {k: 0 for k in self.eng}
        self.seen = {k: {} for k in self.eng}
        self.dsem, self.dtgt, self.dnext = {}, {}, {}
        self.ninst = 0

    def _wait(self, e, ev):
        sem, key, val = ev
        if key == ("e", e) and e == "pe":
            return
        if self.seen[e].get(key, 0) >= val:
            return
        self.eng[e].wait_ge(sem, val)
        self.seen[e][key] = val

    def _deps(self, e, reads, writes):
        for b in reads:
            if b.w is not None:
                self._wait(e, b.w)
        me = ("e", e)
        for b in writes:
            if b.w is not None and b.w[1] != me:
                self._wait(e, b.w)
            for ev in b.r.values():
                if ev[1] != me:
                    self._wait(e, ev)

    def _record(self, ev, reads, writes):
        for b in reads:
            b.r[ev[1]] = ev
        for b in writes:
            b.w = ev
            b.r = {}

    def op(self, e, fn, reads=(), writes=()):
        self._deps(e, reads, writes)
        inst = fn(self.eng[e])
        self.ecnt[e] += 1
        inst.then_inc(self.esem[e], 1)
        self._record((self.esem[e], ("e", e), self.ecnt[e]), reads, writes)
        self.ninst += 1

    def dma(self, q, out_ap, in_ap, reads=(), writes=(), **kw):
        if q not in self.dsem:
            self.dsem[q] = [self.nc.alloc_semaphore(f"ds_{q}{i}") for i in range(NDS)]
            self.dtgt[q] = [0] * NDS
            self.dnext[q] = 0
        j = self.dnext[q]
        self.dnext[q] = (j + 1) % NDS
        key = ("d", q, j)
        if self.dtgt[q][j] > 0:
            self._wait(q, (self.dsem[q][j], key, self.dtgt[q][j]))
        self._deps(q, reads, writes)
        inst = self.eng[q].dma_start(out=out_ap, in_=in_ap, **kw)
        self.dtgt[q][j] += 16
        inst.then_inc(self.dsem[q][j], 16)
        self._record((self.dsem[q][j], key, self.dtgt[q][j]), reads, writes)
        self.ninst += 1

    def finish(self, e="sp"):
        for q in self.dsem:
            for j in range(NDS):
                if self.dtgt[q][j] > 0:
                    self._wait(e, (self.dsem[q][j], ("d", q, j), self.dtgt[q][j]))
        for k in self.eng:
            if k != e and self.ecnt[k] > 0:
                self._wait(e, (self.esem[k], ("e", k), self.ecnt[k]))


PV = {}
_o = 0
for _n, _w in [("g_norm", 8), ("mu", 26), ("w0", 8), ("a0", 8), ("k_k", 8), ("k_a", 8), ("r_k", 8),
               ("gn_g", 8), ("gn_b", 8), ("v0", 8), ("b_glu", 16), ("w_dw", 248), ("b_dw", 8),
               ("ln_g", 8), ("ln_b", 8), ("b_pc", 8), ("g_mem", 8), ("g_final", 8), ("omu", 26), ("omka", 8)]:
    PV[_n] = _o
    _o += _w
NPV = _o

WCOL = {}
_o = 0
for _n, _w in [("lora", 256)] + [(f"rkvg{c}", 512) for c in range(8)] + \
        [(f"pr{j}", 512) for j in range(4)] + [(f"glu{j}", 512) for j in range(4)] + \
        [(f"cg{j}", 512) for j in range(2)] + [(f"pc{j}", 512) for j in range(4)] + \
        [(f"q{j}", 512) for j in range(2)] + [(f"mg{j}", 512) for j in range(2)] + \
        [(f"pm{j}", 512) for j in range(4)] + [(f"wo{j}", 512) for j in range(2)] + \
        [(f"kv{j}", 512) for j in range(4)]:
    WCOL[_n] = (_o, _w)
    _o += _w
TOTC = _o


def _fm(v):
    return np.ascontiguousarray(v.reshape(8, 128).T)


def host_prep(inp):
    f = np.float32
    L = 2
    pv = np.zeros((L, 128, NPV), f)
    wbig = np.zeros((L, 128, KC * TOTC), f)
    lora = np.zeros((L, 128, 2, 1024), f)
    for l in range(L):
        def put(name, arr):
            pv[l][:, PV[name]:PV[name] + arr.shape[1]] = arr
        put("g_norm", _fm(inp["g_norm"][l]))
        mu = inp["mu_shift"][l]
        mucols = np.zeros((128, 26), f)
        mucols[:, 0:25] = mu.reshape(25, 128).T
        if l >= 1:
            mucols[0:32, 25] = inp["mu_vres"][l - 1]
        put("mu", mucols)
        for n in ["w0", "a0", "k_k", "k_a", "gn_g", "gn_b", "b_dw", "ln_g", "ln_b", "g_mem"]:
            src = {"g_mem": "g_mem_norm"}.get(n, n)
            put(n, _fm(inp[src][l]))
        put("r_k", _fm(inp["r_k"][l].reshape(-1)))
        put("b_pc", _fm(inp["b_proj_conv"][l]))
        if l >= 1:
            put("v0", _fm(inp["v0"][l - 1]))
        put("b_glu", np.ascontiguousarray(inp["b_glu"][l].reshape(16, 128).T))
        wd = inp["w_dw"][l]
        put("w_dw", np.ascontiguousarray(wd.reshape(31, 8, 128).transpose(2, 1, 0).reshape(128, 248)))
        put("g_final", _fm(inp["g_final"]))
        w_in = inp["w_in"][l]
        Wc = np.zeros((D, TOTC), f)

        def setc(name, off, arr):
            o, w = WCOL[name]
            Wc[:, o + off:o + off + arr.shape[1]] = arr
        setc("lora", 0, w_in[:, 3072:3200])
        if l >= 1:
            setc("lora", 128, inp["w_vres_down"][l - 1])
        for c in range(8):
            setc(f"rkvg{c}", 0, w_in[:, c * 128:(c + 1) * 128])
            setc(f"rkvg{c}", 128, w_in[:, 1024 + c * 128:1024 + (c + 1) * 128])
            setc(f"rkvg{c}", 256, w_in[:, 2048 + c * 128:2048 + (c + 1) * 128])
            setc(f"rkvg{c}", 384, w_in[:, 3200 + c * 128:3200 + (c + 1) * 128])
        for br, (pn, wp) in enumerate([("pr", inp["w_proj_rwkv"][l]), ("pc", inp["w_proj_conv"][l]),
                                       ("pm", inp["w_proj_mem"][l])]):
            for j in range(4):
                setc(f"{pn}{j}", 0, wp[:, j * 256:(j + 1) * 256])
                mo = 9344 + br * 1024 + j * 256
                setc(f"{pn}{j}", 256, w_in[:, mo:mo + 256])
        for j in range(4):
            for i in range(2):
                c = 2 * j + i
                setc(f"glu{j}", i * 256, w_in[:, 4224 + c * 128:4224 + (c + 1) * 128])
                setc(f"glu{j}", i * 256 + 128, w_in[:, 5248 + c * 128:5248 + (c + 1) * 128])
        for j in range(2):
            setc(f"cg{j}", 0, w_in[:, 6272 + j * 512:6272 + (j + 1) * 512])
            setc(f"q{j}", 0, w_in[:, 7296 + j * 512:7296 + (j + 1) * 512])
            setc(f"mg{j}", 0, w_in[:, 8320 + j * 512:8320 + (j + 1) * 512])
            setc(f"wo{j}", 0, inp["w_out"][l][:, j * 512:(j + 1) * 512])
        for j in range(4):
            setc(f"kv{j}", 0, inp["w_mem_kv"][l][:, j * 512:(j + 1) * 512])
        for (o_, w_) in WCOL.values():
            wbig[l][:, KC * o_:KC * (o_ + w_)] = Wc[:, o_:o_ + w_].reshape(KC, 128, w_).transpose(1, 0, 2).reshape(128, KC * w_)
        lora[l][0:64, 0] = inp["w_decay_up"][l]
        lora[l][64:128, 0] = inp["w_aaa_up"][l]
        if l >= 1:
            lora[l][0:32, 1] = inp["w_vres_up"][l - 1]
    cst = np.zeros((128, 8, 128), f)
    i = np.arange(128)
    cst[:, 0] = np.eye(128)
    cst[:, 1] = 1.0
    cst[:, 2] = (i[:, None] // 64 == i[None, :] // 64)
    cst[:, 3] = (i[:, None] < i[None, :])
    cst[:, 4] = (i[:, None] <= i[None, :])
    cst[:, 5] = (i[:, None] > i[None, :])
    rm = np.ones((128, 512), f)
    rm[:, 0::128] = 0.0
    return pv, wbig, lora, cst, rm


class _Stop(Exception):
    pass


def build(nc, NT=SEQ // T, dbg_names=(), stop_after=None):
    P = Prog(nc)
    try:
        _build(nc, P, NT, dbg_names, stop_after)
    except _Stop:
        pass
    P.finish()
    return P


def _build(nc, P, NT, dbg_names, stop_after):
    def chk(tag):
        if tag == stop_after:
            raise _Stop()
    dt = nc.dram_tensor
    xT_d = dt("xT", [D, SEQ], F32, kind="ExternalInput").ap()
    memT_d = dt("memT", [D, NMEM], F32, kind="ExternalInput").ap()
    pv_d = dt("pv", [2, 128, NPV], F32, kind="ExternalInput").ap()
    wbig_d = dt("wbig", [2, 128, KC * TOTC], F32, kind="ExternalInput").ap()
    lora_d = dt("lora", [2, 128, 2, 1024], F32, kind="ExternalInput").ap()
    cst_d = dt("cst", [128, 8, 128], F32, kind="ExternalInput").ap()
    rm_d = dt("rm", [128, 512], F32, kind="ExternalInput").ap()
    outT_d = dt("outT", [D, SEQ], F32, kind="ExternalOutput").ap()
    dbg_d = None
    if dbg_names:
        dbg_d = dt("dbg", [len(dbg_names), 128, 512], F32, kind="ExternalOutput").ap()
    Bdram_in = Buf(None)
    Bout = Buf(None)
    cnt = [0]

    def sb(shape, dtype, name=None):
        cnt[0] += 1
        return nc.alloc_sbuf_tensor(name or f"t{cnt[0]}", list(shape), dtype)

    def sbuf(shape, dtype):
        return Buf(sb(shape, dtype).ap())

    big8 = [sbuf([128, T], F32) for _ in range(8)]
    cst_f = Buf(big8[0].ap.rearrange("p (a b) -> p a b", a=4))
    cst_f2 = Buf(big8[1].ap.rearrange("p (a b) -> p a b", a=4))
    P.dma("sp", cst_f.ap, cst_d[:, 0:4, :], reads=[Bdram_in], writes=[cst_f, big8[0]])
    P.dma("sp", cst_f2.ap, cst_d[:, 4:8, :], reads=[Bdram_in], writes=[cst_f2, big8[1]])
    cst_b = sbuf([128, 8, 128], BF16)
    P.op("dve", lambda e: e.tensor_copy(out=cst_b[:, 0:4, :], in_=cst_f.ap), [cst_f, big8[0]], [cst_b])
    P.op("dve", lambda e: e.tensor_copy(out=cst_b[:, 4:8, :], in_=cst_f2.ap), [cst_f2, big8[1]], [cst_b])
    ident, ones, bones = cst_b[:, 0, :], cst_b[:, 1, :], cst_b[:, 2, :]
    identf = sbuf([128, 128], F32)
    P.op("dve", lambda e: e.tensor_copy(out=identf.ap, in_=cst_f[:, 0, :]), [cst_f, big8[0]], [identf])
    m12 = Buf(cst_b[:, 3:5, :])
    m12.w = None
    mSL2 = sbuf([128, 2, 128], BF16)
    id2 = sbuf([128, 2, 128], BF16)
    for h in range(2):
        P.op("dve", lambda e: e.tensor_copy(out=mSL2[:, h, :], in_=cst_b[:, 5, :]), [cst_b], [mSL2])
        P.op("dve", lambda e: e.tensor_copy(out=id2[:, h, :], in_=cst_b[:, 0, :]), [cst_b], [id2])
    rmf = Buf(big8[2].ap)
    P.dma("sp", rmf.ap, rm_d, reads=[Bdram_in], writes=[big8[2]])
    rmask = sbuf([128, 512], BF16)
    P.op("dve", lambda e: e.tensor_copy(out=rmask.ap, in_=big8[2].ap), [big8[2]], [rmask])
    pvs = sbuf([128, 2, NPV], F32)
    for l in range(2):
        P.dma("sp", pvs[:, l, :], pv_d[l], reads=[Bdram_in], writes=[pvs])
    for l in range(2):
        for (src, dst, w) in [("mu", "omu", 26), ("k_a", "omka", 8)]:
            P.op("dve", lambda e: e.tensor_scalar(out=pvs[:, l, PV[dst]:PV[dst] + w], in0=pvs[:, l, PV[src]:PV[src] + w],
                                                  scalar1=-1.0, scalar2=1.0, op0=ALU.mult, op1=ALU.add), [pvs], [pvs])
    epsc = sbuf([128, 4], F32)
    for i, v in enumerate([1e-6, 64e-5, 1e-5, 0.0]):
        P.op("dve", lambda e: e.memset(epsc[:, i:i + 1], v), [], [epsc])

    def pcol(l, name, c=0):
        o = PV[name] + c
        return pvs[:, l, o:o + 1]

    lor = [sbuf([128, 2, 1024], BF16) for _ in range(2)]
    for l in range(2):
        P.dma("pool", lor[l].ap, lora_d[l], reads=[Bdram_in], writes=[lor[l]])

    banks = [Buf(nc.alloc_psum_tensor(f"ps{i}", [128, 512], F32).ap()) for i in range(8)]
    ring = banks[:6]
    pin = banks[6:]
    rp = [0]

    def ps_next():
        b = ring[rp[0] % len(ring)]
        rp[0] += 1
        return b

    def bfv(b):
        return b.ap.bitcast(BF16)

    class Ring:
        def __init__(self, n, shape, dtype):
            self.b = [sbuf(shape, dtype) for _ in range(n)]
            self.i = 0

        def get(self):
            b = self.b[self.i % len(self.b)]
            self.i += 1
            return b

    slots = [sbuf([128, 512], F32) for _ in range(12)]
    tf = Ring(0, [128, 512], F32)
    tf.b = slots[0:10]
    tb = Ring(6, [128, 512], BF16)
    lob_, vlo_ = sbuf([128, 512], BF16), sbuf([128, 512], BF16)
    wring = Ring(2, [128, KC, 512], BF16)

    xT = [sbuf([128, T], F32) for _ in range(8)]
    vf = [sbuf([128, T], F32) for _ in range(8)]
    hT = [sbuf([128, T], BF16) for _ in range(8)]
    OG = [sbuf([128, T], BF16) for _ in range(8)]
    yacc = [sbuf([128, T], F32) for _ in range(8)]
    carry = [sbuf([128, 26], F32) for _ in range(2)]
    Sf = [[sbuf([128, 64], F32) for _ in range(8)] for _ in range(2)]
    Sb = [[sbuf([128, 2, 64], BF16) for _ in range(8)] for _ in range(2)]
    halo = [sbuf([128, 8, 30], BF16) for _ in range(2)]
    ubr = Ring(2, [128, 30 + T], BF16)
    dg_keep = sbuf([128, 31, 128], BF16)
    dgT = [Buf(dg_keep[:, tp_, :]) for tp_ in range(31)]
    prT_keep = [[sbuf([128, T], BF16) for _ in range(2)] for _ in range(2)]
    small_keep = sbuf([128, 8], F32)
    kmT = [[sbuf([128, NMEM], BF16) for _ in range(8)] for _ in range(2)]
    vmt = [[sbuf([128, D], BF16) for _ in range(2)] for _ in range(2)]
    for l in range(2):
        P.op("pool", lambda e: e.memset(carry[l].ap, 0.0), [], [carry[l]])
        P.op("pool", lambda e: e.memset(halo[l].ap, 0.0), [], [halo[l]])
        for c in range(8):
            P.op("pool", lambda e: e.memset(Sf[l][c].ap, 0.0), [], [Sf[l][c]])
            P.op("pool", lambda e: e.memset(Sb[l][c].ap, 0.0), [], [Sb[l][c]])

    dbg_list = list(dbg_names)
    dbg_buf = sbuf([128, 512], F32) if dbg_names else None

    def dump(name, b, ap=None):
        if name in dbg_list:
            i = dbg_list.index(name)
            a = b.ap if ap is None else ap
            t = dbg_buf
            P.op("dve", lambda e: e.tensor_copy(out=t[:, 0:a.shape[-1]], in_=a), [b], [t])
            P.dma("sp", dbg_d[i][0:a.shape[0], 0:a.shape[-1]], t[0:a.shape[0], 0:a.shape[-1]], reads=[t], writes=[Bout])
            dbg_list[i] = None

    def act(out_b, out_ap, in_b, in_ap, func, extra_r=(), **kw):
        P.op("act", lambda e: e.activation(out=out_ap, in_=in_ap, func=func, **kw), [in_b] + list(extra_r), [out_b])

    def tt(eng, out_b, out_ap, a_b, a_ap, b_b, b_ap, op):
        P.op(eng, lambda e: e.tensor_tensor(out=out_ap, in0=a_ap, in1=b_ap, op=op), [a_b, b_b], [out_b])

    def stt(out_b, out_ap, a_b, a_ap, scalar, b_b, b_ap, op0, op1, extra_r=()):
        P.op("dve", lambda e: e.scalar_tensor_tensor(out=out_ap, in0=a_ap, scalar=scalar, in1=b_ap, op0=op0, op1=op1),
             [a_b, b_b] + list(extra_r), [out_b])

    def ts(eng, out_b, out_ap, a_b, a_ap, s1, s2, op0, op1=None, extra_r=()):
        if op1 is None:
            P.op(eng, lambda e: e.tensor_scalar(out=out_ap, in0=a_ap, scalar1=s1, scalar2=None, op0=op0),
                 [a_b] + list(extra_r), [out_b])
        else:
            P.op(eng, lambda e: e.tensor_scalar(out=out_ap, in0=a_ap, scalar1=s1, scalar2=s2, op0=op0, op1=op1),
                 [a_b] + list(extra_r), [out_b])

    def cp(eng, out_b, out_ap, in_b, in_ap):
        if eng == "act":
            P.op("act", lambda e: e.copy(out=out_ap, in_=in_ap), [in_b], [out_b])
        elif eng == "dve":
            P.op(eng, lambda e: e.tensor_scalar(out=out_ap, in0=in_ap, scalar1=1.0, scalar2=None, op0=ALU.mult), [in_b], [out_b])
        else:
            P.op(eng, lambda e: e.tensor_copy(out=out_ap, in_=in_ap), [in_b], [out_b])

    def mm(out_b, out_ap, l_b, l_ap, r_b, r_ap, start=True, stop=True, extra_r=()):
        P.op("pe", lambda e: e.matmul(out_ap, lhsT=l_ap, rhs=r_ap, start=start, stop=stop), [l_b, r_b] + list(extra_r), [out_b])

    def tr(out_b, out_ap, in_b, in_ap):
        P.op("pe", lambda e: e.transpose(out_ap, in_ap, identf.ap), [in_b, identf], [out_b])

    def wload(l, name):
        o, w = WCOL[name]
        wb = wring.get()
        src = wbig_d[l][:, KC * o:KC * (o + w)].rearrange("p (k n) -> p k n", k=KC)
        P.dma("pool", wb[:, :, 0:w], src, reads=[Bdram_in], writes=[wb], max_dma_last_dim=8192)
        return wb

    def proj(l, name, rhs_for_chunk, consume):
        o, w = WCOL[name]
        wb = wload(l, name)
        for j in range(w // 128):
            rhs = rhs_for_chunk(j)
            if rhs is None:
                continue
            ps = ps_next()
            for kc in range(KC):
                mm(ps, ps.ap, wb, wb[:, kc, j * 128:(j + 1) * 128], rhs[kc], rhs[kc].ap, start=(kc == 0), stop=(kc == KC - 1))
            consume(j, ps)

    def bcast_stat(src_list, src_aps, scale, epscol):
        raise NotImplementedError

    def rms_to(l, gname, src, dst, n):
        ps = pin[0]
        for c in range(8):
            sq = tb.get()
            act(sq, sq[:, 0:n], src[c], src[c][:, 0:n], AF.Square)
            mm(ps, ps[:, 0:n], cst_b, ones, sq, sq[:, 0:n], start=(c == 0), stop=(c == 7))
        sd = tf.get()
        act(sd, sd[:, 0:n], ps, ps[:, 0:n], AF.Sqrt, extra_r=[epsc], scale=1.0 / D, bias=epsc[:, 0:1])
        rs = tf.get()
        P.op("dve", lambda e: e.reciprocal(out=rs[:, 0:n], in_=sd[:, 0:n]), [sd], [rs])
        for c in range(8):
            stt(dst[c], dst[c][:, 0:n], src[c], src[c][:, 0:n], pcol(l, gname, c), rs, rs[:, 0:n], ALU.mult, ALU.mult, extra_r=[pvs])
        return rs

    mraw = [Buf(big8[c][:, 0:NMEM]) for c in range(8)]
    for c in range(8):
        P.dma("sp", mraw[c].ap, memT_d[c * 128:(c + 1) * 128, :], reads=[Bdram_in], writes=[big8[c]])
        mraw[c] = big8[c]
    for l in range(2):
        mT = OG
        rms_to(l, "g_mem", mraw, mT, NMEM)
        for j in range(2):
            def cons(jj, ps, j=j):
                cp("act", kmT[l][j * 4 + jj], kmT[l][j * 4 + jj].ap, ps, ps[:, 0:NMEM])
            o, w = WCOL[f"kv{j}"]
            wb = wload(l, f"kv{j}")
            for jj in range(4):
                ps = ps_next()
                for kc in range(KC):
                    mm(ps, ps[:, 0:NMEM], wb, wb[:, kc, jj * 128:(jj + 1) * 128], mT[kc], mT[kc][:, 0:NMEM], start=(kc == 0), stop=(kc == KC - 1))
                cons(jj, ps)
        for j in range(2):
            wb = wload(l, f"kv{2 + j}")
            for mb in range(2):
                ps = ps_next()
                for kc in range(KC):
                    mm(ps, ps.ap, mT[kc], mT[kc][:, mb * 128:(mb + 1) * 128], wb, wb[:, kc, :], start=(kc == 0), stop=(kc == KC - 1))
                cp("act", vmt[l][mb], vmt[l][mb][:, j * 512:(j + 1) * 512], ps, ps.ap)

    chk("memkv")
    AR = sbuf([128, 4, 2, 128], BF16)
    Bbd = sbuf([128, 4, 2, 128], BF16)
    Kbd = sbuf([128, 4, 2, 128], BF16)
    P.op("pool", lambda e: e.memset(Bbd.ap, 0.0), [], [Bbd])
    P.op("pool", lambda e: e.memset(Kbd.ap, 0.0), [], [Kbd])
    NXr = Ring(4, [128, 2, 2, 128], BF16)
    NXn = {id(b_): Buf(b_[:, :, 0, :]) for b_ in NXr.b}
    NXx = {id(b_): Buf(b_[:, :, 1, :]) for b_ in NXr.b}
    Ar = Ring(4, [128, 2, 128], BF16)
    Arb = Ring(4, [128, 2, 128], BF16)
    Aak = Ring(2, [128, 2, 128], BF16)
    Ark = Ring(4, [128, 2, 128], BF16)
    Atok = [sbuf([128, 2, 128], BF16) for _ in range(2)]
    Vtok = [sbuf([128, 2, 128], BF16) for _ in range(4)]
    Utok = [sbuf([128, 2, 128], BF16) for _ in range(2)]
    for b_ in Atok + Vtok + Utok:
        P.op("pool", lambda e: e.memset(b_.ap, 0.0), [], [b_])
    BKtok = Ring(4, [128, 2, 128], BF16)
    ApT = Ring(4, [128, 128], BF16)
    Wp = Ring(2, [128, 128], BF16)
    stage = slots[0]
    Up4 = slots[11]
    cntr = dict(at=0, vt=0, ut=0, up=0)

    def scan_pre(l, c, qs2, ctx, E1, bpT, kpT, vb):
        for q in qs2:
            d = ctx[q] = {}
            d["at"] = Atok[cntr["at"] % 2]; cntr["at"] += 1
            d["vt"] = Vtok[cntr["vt"] % 4]; cntr["vt"] += 1
            d["upc"] = cntr["up"] % 4; cntr["up"] += 1
            d["bk"], d["arb"], d["aak"], d["ark"] = BKtok.get(), Arb.get(), Aak.get(), Ark.get()
            d["apT"], d["wp"] = ApT.get(), Wp.get()
        for q in qs2:
            d = ctx[q]
            qs = slice(q * 128, (q + 1) * 128)
            pst = ps_next()
            pv_ = pst.ap
            cp("dve", stage, stage[:, 0:128], AR, AR[:, q, 0, :])
            cp("dve", stage, stage[:, 128:256], bpT, bpT[:, qs])
            cp("dve", stage, stage[:, 256:384], kpT, kpT[:, qs])
            cp("dve", stage, stage[:, 384:512], vb, vb[:, qs])
            for i4 in range(4):
                tr(pst, pv_[:, i4 * 128:(i4 + 1) * 128], stage, stage[:, i4 * 128:(i4 + 1) * 128])
            at, vt, bk = d["at"], d["vt"], d["bk"]
            for h in range(2):
                cp("act", at, at[:, h, h * 64:(h + 1) * 64], pst, pv_[:, h * 64:(h + 1) * 64])
                cp("act", vt, vt[:, h, h * 64:(h + 1) * 64], pst, pv_[:, 384 + h * 64:384 + (h + 1) * 64])
            cp("act", bk, bk.ap, pst, pv_[:, 128:384].rearrange("p (a b) -> p a b", a=2))
            yield
        for q in qs2:
            d = ctx[q]
            ps1, ps2, ps3 = ps_next(), ps_next(), ps_next()
            arq = AR[:, q, :, :].rearrange("p a t -> p (a t)")
            for h in range(2):
                mm(ps1, ps1[:, h * 256:(h + 1) * 256], Bbd, Bbd[:, q, h, :], AR, arq)
                mm(ps2, ps2[:, h * 256:(h + 1) * 256], Kbd, Kbd[:, q, h, :], AR, arq)
            mm(ps3, ps3[:, 0:256], AR, AR[:, q, 0, :], Bbd, Bbd[:, q, :, :].rearrange("p h j -> p (h j)"))
            nx = NXr.get()
            arb, aak, ark, a0 = d["arb"], d["aak"], d["ark"], Ar.get()
            p1v = ps1.ap.rearrange("p (h a t) -> p h a t", h=2, a=2)
            p2v = ps2.ap.rearrange("p (h a t) -> p h a t", h=2, a=2)
            for h in range(2):
                tt("dve", NXn[id(nx)], nx[:, h, 0, :], ps1, p1v[:, h, 0, :], cst_b, cst_b[:, 3, :], ALU.mult)
                tt("dve", arb, arb[:, h, :], ps1, p1v[:, h, 1, :], cst_b, cst_b[:, 4, :], ALU.mult)
                tt("dve", aak, aak[:, h, :], ps2, p2v[:, h, 0, :], cst_b, cst_b[:, 3, :], ALU.mult)
                tt("dve", ark, ark[:, h, :], ps2, p2v[:, h, 1, :], cst_b, cst_b[:, 4, :], ALU.mult)
            tt("dve", a0, a0.ap, ps3, ps3[:, 0:256].rearrange("p (h t) -> p h t", h=2), mSL2, mSL2.ap, ALU.mult)
            tt("dve", NXx[id(nx)], nx[:, :, 1, :], NXn[id(nx)], nx[:, :, 0, :], id2, id2.ap, ALU.add)
            d["nx"], d["A"] = nx, a0
            yield
        for lev in range(7):
            last = (lev == 6)
            pss = {}
            for q in qs2:
                d = ctx[q]
                nx, A_i = d["nx"], d["A"]
                psn = ps_next()
                for h in range(2):
                    if lev == 0:
                        mm(psn, psn[:, h * 256:h * 256 + 128], A_i, A_i[:, h, :], NXn[id(nx)], nx[:, h, 0, :])
                    elif not last:
                        mm(psn, psn[:, h * 256:(h + 1) * 256], A_i, A_i[:, h, :], NXn[id(nx)], nx[:, h, :, :].rearrange("p a t -> p (a t)"), extra_r=[NXx[id(nx)]])
                    else:
                        mm(psn, psn[:, h * 256 + 128:(h + 1) * 256], A_i, A_i[:, h, :], NXx[id(nx)], nx[:, h, 1, :])
                psa = None
                if not last:
                    psa = ps_next()
                    for h in range(2):
                        mm(psa, psa[:, h * 128:(h + 1) * 128], NXn[id(nx)], nx[:, h, 0, :], A_i, A_i[:, h, :])
                pss[q] = (psn, psa)
            for q in qs2:
                d = ctx[q]
                nx = d["nx"]
                psn, psa = pss[q]
                pnv = psn.ap.rearrange("p (h a t) -> p h a t", h=2, a=2)
                nx2 = NXr.get()
                if lev == 0:
                    cp("act", NXn[id(nx2)], nx2[:, :, 0, :], psn, pnv[:, :, 0, :])
                    cp("dve", NXx[id(nx2)], nx2[:, :, 1, :], NXx[id(nx)], nx[:, :, 1, :])
                else:
                    if not last:
                        cp("dve", NXn[id(nx2)], nx2[:, :, 0, :], psn, pnv[:, :, 0, :])
                    tt("dve", NXx[id(nx2)], nx2[:, :, 1, :], psn, pnv[:, :, 1, :], NXx[id(nx)], nx[:, :, 1, :], ALU.add)
                if not last:
                    a2 = Ar.get()
                    cp("act", a2, a2.ap, psa, psa[:, 0:256].rearrange("p (h t) -> p h t", h=2))
                    d["A"] = a2
                d["nx"] = nx2
            yield
        for q in qs2:
            d = ctx[q]
            nx, at, vt, aak, apT, wp = d["nx"], d["at"], d["vt"], d["aak"], d["apT"], d["wp"]
            psw = ps_next()
            for h in range(2):
                mm(psw, psw[:, 0:128], at, at[:, h, :], NXx[id(nx)], nx[:, h, 1, :], start=(h == 0), stop=(h == 1))
            for h in range(2):
                mm(psw, psw[:, 128 + h * 64:128 + (h + 1) * 64], aak, aak[:, h, :], vt, vt[:, h, h * 64:(h + 1) * 64])
            cp("act", apT, apT.ap, psw, psw[:, 0:128])
            cp("act", wp, wp.ap, psw, psw[:, 128:256])
        yield
        for q in qs2:
            d = ctx[q]
            nx, wp = d["nx"], d["wp"]
            psu = ps_next()
            for h in range(2):
                mm(psu, psu[:, h * 64:(h + 1) * 64], NXx[id(nx)], nx[:, h, 1, :], wp, wp[:, h * 64:(h + 1) * 64])
            uc_ = d["upc"]
            cp("act", Up4, Up4[:, uc_ * 128:(uc_ + 1) * 128], psu, psu[:, 0:128])
        yield

    def scan_seq(l, c, qs2, ctx, E1, yT):
        sbd, sfd = Sb[l][c], Sf[l][c]
        sbd2 = sbd.ap.rearrange("p h v -> p (h v)")
        for q in qs2:
            d = ctx[q]
            qs = slice(q * 128, (q + 1) * 128)
            vt, bk, arb, ark, apT, uc_ = d["vt"], d["bk"], d["arb"], d["ark"], d["apT"], d["upc"]
            ut = Utok[cntr["ut"] % 2]; cntr["ut"] += 1
            ps_u = ps_next()
            mm(ps_u, ps_u[:, 0:128], apT, apT.ap, sbd, sbd2)
            for h in range(2):
                tt("dve", ut, ut[:, h, h * 64:(h + 1) * 64], ps_u, ps_u[:, h * 64:(h + 1) * 64],
                   Up4, Up4[:, uc_ * 128 + h * 64:uc_ * 128 + (h + 1) * 64], ALU.add)
            yield
            ps_y = ps_next()
            mm(ps_y, ps_y[:, 0:128], sbd, sbd2, AR, AR[:, q, 1, :], start=True, stop=False)
            for h in range(2):
                mm(ps_y, ps_y[:, 0:128], ut, ut[:, h, :], arb, arb[:, h, :], start=False, stop=False)
                mm(ps_y, ps_y[:, 0:128], vt, vt[:, h, :], ark, ark[:, h, :], start=False, stop=(h == 1))
            ps_s = ps_next()
            mm(ps_s, ps_s[:, 0:256], bk, bk[:, 0, :], ut, ut.ap.rearrange("p h v -> p (h v)"), start=True, stop=False)
            mm(ps_s, ps_s[:, 0:256], bk, bk[:, 1, :], vt, vt.ap.rearrange("p h v -> p (h v)"), start=False, stop=True)
            pc = E1[:, q * 128 + 127:q * 128 + 128]
            for h in range(2):
                hp = slice(h * 64, (h + 1) * 64)
                stt(sfd, sfd[hp, :], sfd, sfd[hp, :], pc[hp, :], ps_s, ps_s[hp, h * 128 + h * 64:h * 128 + (h + 1) * 64],
                    ALU.mult, ALU.add, extra_r=[E1])
            for h in range(2):
                hp = slice(h * 64, (h + 1) * 64)
                cp("dve", sbd, sbd[hp, h, :], sfd, sfd[hp, :])
            cp("act", yT, yT[:, qs], ps_y, ps_y[:, 0:128])
            yield

    def scan_all(l, c, E1, bpT, kpT, vb, yT):
        ctxA, ctxB = {}, {}
        for _ in scan_pre(l, c, (0, 1), ctxA, E1, bpT, kpT, vb):
            pass
        gB = scan_pre(l, c, (2, 3), ctxB, E1, bpT, kpT, vb)
        gA = scan_seq(l, c, (0, 1), ctxA, E1, yT)
        aliveA = aliveB = True
        while aliveA or aliveB:
            if aliveB:
                try:
                    next(gB)
                except StopIteration:
                    aliveB = False
            if aliveA:
                try:
                    next(gA)
                except StopIteration:
                    aliveA = False
        for _ in scan_seq(l, c, (2, 3), ctxB, E1, yT):
            pass

    for it in range(NT):
        tsl = slice(it * T, (it + 1) * T)
        for c in range(8):
            P.dma("sp", xT[c].ap, xT_d[c * 128:(c + 1) * 128, tsl], reads=[Bdram_in], writes=[xT[c]])
        for l in range(2):
            rms_to(l, "g_norm", xT, hT, T)
            chk(f"rms{l}")
            hrhs = lambda j: hT
            car = carry[l]

            def shiftmix(ps, mi, npart=128, A=None):
                A = A or slots[11]
                pp = slice(0, npart)
                mu_c, omu_c = pcol(l, "mu", mi), pcol(l, "omu", mi)
                act(A, A[pp, :], ps, ps[pp, :], AF.Identity, extra_r=[pvs], scale=omu_c[pp, :])
                stt(A, A[pp, 1:T], ps, ps[pp, 0:T - 1], mu_c[pp, :], A, A[pp, 1:T], ALU.mult, ALU.add, extra_r=[pvs])
                stt(A, A[pp, 0:1], car, car[pp, mi:mi + 1], mu_c[pp, :], A, A[pp, 0:1], ALU.mult, ALU.add, extra_r=[pvs])
                cp("act", car, car[pp, mi:mi + 1], ps, ps[pp, T - 1:T])
                return A

            lob, vlo = lob_, vlo_

            def cons_lora(j, ps):
                if j == 0:
                    lo = shiftmix(ps, 24)
                    act(lob, lob[0:64, :], lo, lo[0:64, :], AF.Tanh)
                    cp("dve", lob, lob[64:128, :], lo, lo[64:128, :])
                elif l == 1:
                    v_ = shiftmix(ps, 25, 32)
                    cp("dve", vlo, vlo[0:32, :], v_, v_[0:32, :])
            proj(l, "lora", lambda j: hT if (j == 0 or l == 1) else None, cons_lora)

            chk(f"lora{l}")
            for c in range(8):
                got = {}

                def cons_rkvg(j, ps):
                    if j < 3:
                        got[j] = shiftmix(ps, j * 8 + c, A=slots[j])
                    else:
                        g = slots[3]
                        act(g, g.ap, ps, ps.ap, AF.Silu)
                        got[3] = g
                proj(l, f"rkvg{c}", hrhs, cons_rkvg)
                r_, k_, v_, gs = got[0], got[1], got[2], got[3]
                cs = slice(c * 128, (c + 1) * 128)
                psd, psa = ps_next(), ps_next()
                mm(psd, psd.ap, lor[l], lor[l][0:64, 0, cs], lob, lob[0:64, :])
                mm(psa, psa.ap, lor[l], lor[l][64:128, 0, cs], lob, lob[64:128, :])
                sgd, a_ = slots[4], slots[5]
                act(sgd, sgd.ap, psd, psd.ap, AF.Sigmoid, extra_r=[pvs], bias=pcol(l, "w0", c))
                act(a_, a_.ap, psa, psa.ap, AF.Sigmoid, extra_r=[pvs], bias=pcol(l, "a0", c))
                if l == 1:
                    psv = ps_next()
                    mm(psv, psv.ap, lor[l], lor[l][0:32, 1, cs], vlo, vlo[0:32, :])
                    gv = slots[7]
                    act(gv, gv.ap, psv, psv.ap, AF.Sigmoid, extra_r=[pvs], bias=pcol(l, "v0", c))
                    dd = slots[8]
                    tt("dve", dd, dd.ap, vf[c], vf[c].ap, v_, v_.ap, ALU.subtract)
                    tt("dve", dd, dd.ap, dd, dd.ap, gv, gv.ap, ALU.mult)
                    tt("dve", v_, v_.ap, v_, v_.ap, dd, dd.ap, ALU.add)
                else:
                    cp("act", vf[c], vf[c].ap, v_, v_.ap)
                dump(f"r{l}_{c}", r_)
                dump(f"k{l}_{c}", k_)
                dump(f"v{l}_{c}", v_)
                dump(f"a{l}_{c}", a_)
                kkr = slots[6]
                ts("dve", kkr, kkr.ap, k_, k_.ap, pcol(l, "k_k", c), None, ALU.mult, extra_r=[pvs])
                sq = tb.get()
                act(sq, sq.ap, kkr, kkr.ap, AF.Square)
                psn = ps_next()
                mm(psn, psn.ap, cst_b, bones, sq, sq.ap)
                nrm = slots[7]
                act(nrm, nrm.ap, psn, psn.ap, AF.Sqrt)
                ts("dve", nrm, nrm.ap, nrm, nrm.ap, 1e-12, None, ALU.max)
                P.op("dve", lambda e: e.reciprocal(out=nrm.ap, in_=nrm.ap), [nrm], [nrm])
                tt("dve", kkr, kkr.ap, kkr, kkr.ap, nrm, nrm.ap, ALU.mult)
                kk = kkr
                f_ = slots[7]
                ts("dve", f_, f_.ap, a_, a_.ap, pcol(l, "k_a", c), pcol(l, "omka", c), ALU.mult, ALU.add, extra_r=[pvs])
                tt("dve", k_, k_.ap, k_, k_.ap, f_, f_.ap, ALU.mult)
                k2 = k_
                rk = tb.get()
                stt(rk, rk.ap, r_, r_.ap, pcol(l, "r_k", c), k2, k2.ap, ALU.mult, ALU.mult, extra_r=[pvs])
                psb = ps_next()
                mm(psb, psb.ap, cst_b, bones, rk, rk.ap)
                bon = slots[8]
                tt("dve", bon, bon.ap, psb, psb.ap, v_, v_.ap, ALU.mult)
                cum = slots[7]
                P.op("dve", lambda e: e.tensor_tensor_scan(out=cum.ap, data0=rmask.ap, data1=sgd.ap, initial=0.0,
                                                           op0=ALU.mult, op1=ALU.add), [rmask, sgd], [cum])
                E1, E2, E3 = slots[9], slots[10], slots[11]
                act(E1, E1.ap, cum, cum.ap, AF.Exp, scale=-C0)
                act(E2, E2.ap, cum, cum.ap, AF.Exp, scale=C0)
                tt("dve", sgd, sgd.ap, cum, cum.ap, sgd, sgd.ap, ALU.subtract)
                act(E3, E3.ap, sgd, sgd.ap, AF.Exp, scale=-C0)
                dump(f"E1{l}_{c}", E1)
                tt("dve", AR, AR[:, :, 1, :], r_, r_.ap.rearrange("p (q t) -> p q t", q=4), E1, E1.ap.rearrange("p (q t) -> p q t", q=4), ALU.mult)
                stt(AR, AR[:, :, 0, :], kk, kk.ap.rearrange("p (q t) -> p q t", q=4), -1.0, E3, E3.ap.rearrange("p (q t) -> p q t", q=4), ALU.mult, ALU.mult)
                bt, kt = slots[0], slots[11]
                tt("dve", bt, bt.ap, kk, kk.ap, a_, a_.ap, ALU.mult)
                tt("dve", bt, bt.ap, bt, bt.ap, E2, E2.ap, ALU.mult)
                tt("dve", kt, kt.ap, k2, k2.ap, E2, E2.ap, ALU.mult)
                for h in range(2):
                    hp = slice(h * 64, (h + 1) * 64)
                    cp("act", Bbd, Bbd[hp, :, h, :], bt, bt[hp, :].rearrange("p (q t) -> p q t", q=4))
                    cp("dve", Kbd, Kbd[hp, :, h, :], kt, kt[hp, :].rearrange("p (q t) -> p q t", q=4))
                bpT, kpT, vb = tb.get(), tb.get(), tb.get()
                for q in range(4):
                    qs = slice(q * 128, (q + 1) * 128)
                    pc = E1[:, q * 128 + 127:q * 128 + 128]
                    act(bpT, bpT[:, qs], bt, bt[:, qs], AF.Identity, extra_r=[E1], scale=pc)
                    act(kpT, kpT[:, qs], kt, kt[:, qs], AF.Identity, extra_r=[E1], scale=pc)
                cp("act", vb, vb.ap, v_, v_.ap)
                chk(f"prep{l}_{c}")
                yT = slots[1]
                scan_all(l, c, E1, bpT, kpT, vb, yT)
                chk(f"scanend{l}_{c}")
                dump(f"y{l}_{c}", yT)
                yb_, ysq = tb.get(), tb.get()
                cp("dve", yb_, yb_.ap, yT, yT.ap)
                act(ysq, ysq.ap, yT, yT.ap, AF.Square)
                p1, p2 = ps_next(), ps_next()
                mm(p1, p1.ap, cst_b, bones, yb_, yb_.ap)
                mm(p2, p2.ap, cst_b, bones, ysq, ysq.ap)
                mean, var = slots[5], slots[6]
                ts("dve", mean, mean.ap, p1, p1.ap, 1.0 / 64, None, ALU.mult)
                tt("dve", var, var.ap, mean, mean.ap, mean, mean.ap, ALU.mult)
                stt(var, var.ap, p2, p2.ap, 1.0 / 64, var, var.ap, ALU.mult, ALU.subtract)
                act(var, var.ap, var, var.ap, AF.Sqrt, extra_r=[epsc], bias=epsc[:, 1:2])
                P.op("dve", lambda e: e.reciprocal(out=var.ap, in_=var.ap), [var], [var])
                tt("dve", yT, yT.ap, yT, yT.ap, mean, mean.ap, ALU.subtract)
                tt("dve", yT, yT.ap, yT, yT.ap, var, var.ap, ALU.mult)
                ts("dve", yT, yT.ap, yT, yT.ap, pcol(l, "gn_g", c), pcol(l, "gn_b", c), ALU.mult, ALU.add, extra_r=[pvs])
                tt("dve", yT, yT.ap, yT, yT.ap, bon, bon.ap, ALU.add)
                tt("dve", OG[c], OG[c].ap, yT, yT.ap, gs, gs.ap, ALU.mult)
                dump(f"og{l}_{c}", OG[c])

            def branch_out(pn, first, bias_name=None):
                for j in range(4):
                    tmpy = {}

                    def cons(jj, ps, j=j):
                        if jj < 2:
                            t_ = tf.get()
                            if bias_name is None:
                                cp("act", t_, t_.ap, ps, ps.ap)
                            else:
                                act(t_, t_.ap, ps, ps.ap, AF.Identity, extra_r=[pvs], bias=pcol(l, bias_name, 2 * j + jj))
                            tmpy[jj] = t_
                        else:
                            cidx = 2 * j + (jj - 2)
                            sg = tf.get()
                            act(sg, sg.ap, ps, ps.ap, AF.Sigmoid)
                            yb = tmpy[jj - 2]
                            if first:
                                tt("dve", yacc[cidx], yacc[cidx].ap, sg, sg.ap, yb, yb.ap, ALU.mult)
                            else:
                                tt("dve", sg, sg.ap, sg, sg.ap, yb, yb.ap, ALU.mult)
                                tt("dve", yacc[cidx], yacc[cidx].ap, yacc[cidx], yacc[cidx].ap, sg, sg.ap, ALU.add)
                    proj(l, f"{pn}{j}", lambda jj: OG if jj < 2 else hT, cons)

            chk(f"rwkv{l}")
            branch_out("pr", True)
            chk(f"pr{l}")
            dump(f"yacc0_{l}", yacc[0])

            uc = big8
            s1, s2 = pin[0], pin[1]
            dg = dg_keep
            for j in range(4):
                def cons_glu(jj, ps, j=j):
                    c = 2 * j + jj // 2
                    if jj % 2 == 0:
                        cons_glu.pa = ps
                        return
                    gb = tf.get()
                    act(gb, gb.ap, ps, ps.ap, AF.Sigmoid, extra_r=[pvs], bias=pcol(l, "b_glu", 8 + c))
                    pa = cons_glu.pa
                    u = ubr.get()
                    cp("act", u, u[:, 0:30], halo[l], halo[l][:, c, :])
                    stt(u, u[:, 30:30 + T], pa, pa.ap, pcol(l, "b_glu", c), gb, gb.ap, ALU.add, ALU.mult, extra_r=[pvs])
                    for tp in range(31):
                        if tp % 3 == 2:
                            act(dgT[tp], dgT[tp].ap, cst_b, ident, AF.Identity, extra_r=[pvs], scale=pcol(l, "w_dw", c * 31 + tp))
                        else:
                            ts("dve", dgT[tp], dgT[tp].ap, cst_b, ident, pcol(l, "w_dw", c * 31 + tp), None, ALU.mult, extra_r=[pvs])
                    pc_ = ps_next()
                    for tp in range(31):
                        mm(pc_, pc_.ap, dgT[tp], dgT[tp].ap, u, u[:, tp:tp + T], start=(tp == 0), stop=(tp == 30))
                    act(uc[c], uc[c].ap, pc_, pc_.ap, AF.Identity, extra_r=[pvs], bias=pcol(l, "b_dw", c))
                    cp("act", halo[l], halo[l][:, c, :], u, u[:, T:T + 30])
                    ucb, ucs = tb.get(), tb.get()
                    cp("dve", ucb, ucb.ap, uc[c], uc[c].ap)
                    act(ucs, ucs.ap, uc[c], uc[c].ap, AF.Square)
                    mm(s1, s1.ap, cst_b, ones, ucb, ucb.ap, start=(c == 0), stop=(c == 7))
                    mm(s2, s2.ap, cst_b, ones, ucs, ucs.ap, start=(c == 0), stop=(c == 7))
                proj(l, f"glu{j}", hrhs, cons_glu)
            mean, var = slots[10], slots[11]
            act(mean, mean.ap, s1, s1.ap, AF.Identity, scale=1.0 / D)
            tt("dve", var, var.ap, mean, mean.ap, mean, mean.ap, ALU.mult)
            stt(var, var.ap, s2, s2.ap, 1.0 / D, var, var.ap, ALU.mult, ALU.subtract)
            act(var, var.ap, var, var.ap, AF.Sqrt, extra_r=[epsc], bias=epsc[:, 2:3])
            P.op("dve", lambda e: e.reciprocal(out=var.ap, in_=var.ap), [var], [var])
            dump(f"uc{l}_0", uc[0])
            for j in range(2):
                def cons_cg(jj, ps, j=j):
                    c = 4 * j + jj
                    cg = tf.get()
                    act(cg, cg.ap, ps, ps.ap, AF.Silu)
                    t_ = uc[c]
                    tt("dve", t_, t_.ap, t_, t_.ap, mean, mean.ap, ALU.subtract)
                    tt("dve", t_, t_.ap, t_, t_.ap, var, var.ap, ALU.mult)
                    act(t_, t_.ap, t_, t_.ap, AF.Silu, extra_r=[pvs], scale=pcol(l, "ln_g", c), bias=pcol(l, "ln_b", c))
                    tt("dve", OG[c], OG[c].ap, t_, t_.ap, cg, cg.ap, ALU.mult)
                proj(l, f"cg{j}", hrhs, cons_cg)
            dump(f"ug{l}_0", OG[0])
            chk(f"conv{l}")
            branch_out("pc", False, "b_pc")
            chk(f"pc{l}")
            dump(f"yacc1_{l}", yacc[0])

            qT = OG
            for j in range(2):
                def cons_q(jj, ps, j=j):
                    c = 4 * j + jj
                    act(qT[c], qT[c].ap, ps, ps.ap, AF.Identity, scale=1.0 / 16.0)
                proj(l, f"q{j}", hrhs, cons_q)
            att = big8
            prT = prT_keep
            small = small_keep
            for hm in range(4):
                pt = prT[hm % 2]
                for sbk in range(4):
                    ss = slice(sbk * 128, (sbk + 1) * 128)
                    psc = ps_next()
                    for dc in range(2):
                        mm(psc, psc[:, 0:NMEM], qT[2 * hm + dc], qT[2 * hm + dc][:, ss], kmT[l][2 * hm + dc], kmT[l][2 * hm + dc].ap,
                           start=(dc == 0), stop=(dc == 1))
                    P.op("dve", lambda e: e.tensor_reduce(out=small[:, 0:1], in_=psc[:, 0:NMEM], axis=AX.X, op=ALU.max), [psc], [small])
                    ts("dve", small, small[:, 1:2], small, small[:, 0:1], -1.0, None, ALU.mult)
                    ex = tf.get()
                    P.op("act", lambda e: e.activation(out=ex[:, 0:NMEM], in_=psc[:, 0:NMEM], func=AF.Exp, bias=small[:, 1:2],
                                                       accum_out=small[:, 2:3]), [psc, small], [ex, small])
                    P.op("dve", lambda e: e.reciprocal(out=small[:, 3:4], in_=small[:, 2:3]), [small], [small])
                    pb = tf.get()
                    ts("dve", pb, pb[:, 0:NMEM], ex, ex[:, 0:NMEM], small[:, 3:4], None, ALU.mult, extra_r=[small])
                    ptp = ps_next()
                    pv_ = ptp.ap
                    for mb in range(2):
                        tr(ptp, pv_[:, mb * 128:(mb + 1) * 128], pb, pb[:, mb * 128:(mb + 1) * 128])
                    for mb in range(2):
                        cp("act", pt[mb], pt[mb][:, ss], ptp, pv_[:, mb * 128:(mb + 1) * 128])
                for dc in range(2):
                    c = 2 * hm + dc
                    pa_ = ps_next()
                    for mb in range(2):
                        mm(pa_, pa_.ap, vmt[l][mb], vmt[l][mb][:, c * 128:(c + 1) * 128], pt[mb], pt[mb].ap, start=(mb == 0), stop=(mb == 1))
                    cp("act", att[c], att[c].ap, pa_, pa_.ap)
            dump(f"att{l}_0", att[0])
            for j in range(2):
                def cons_mg(jj, ps, j=j):
                    c = 4 * j + jj
                    mg = tf.get()
                    act(mg, mg.ap, ps, ps.ap, AF.Silu)
                    tt("dve", OG[c], OG[c].ap, att[c], att[c].ap, mg, mg.ap, ALU.mult)
                proj(l, f"mg{j}", hrhs, cons_mg)
            chk(f"mem{l}")
            branch_out("pm", False)
            chk(f"pm{l}")
            dump(f"yacc2_{l}", yacc[0])

            for c in range(8):
                cp("act", OG[c], OG[c].ap, yacc[c], yacc[c].ap)
            for j in range(2):
                def cons_o(jj, ps, j=j):
                    c = 4 * j + jj
                    tt("dve", xT[c], xT[c].ap, xT[c], xT[c].ap, ps, ps.ap, ALU.add)
                proj(l, f"wo{j}", lambda jj: OG, cons_o)
            dump(f"x{l}_0", xT[0])

        ps = pin[0]
        for c in range(8):
            sq = tb.get()
            act(sq, sq.ap, xT[c], xT[c].ap, AF.Square)
            mm(ps, ps.ap, cst_b, ones, sq, sq.ap, start=(c == 0), stop=(c == 7))
        sd, rs = tf.get(), tf.get()
        act(sd, sd.ap, ps, ps.ap, AF.Sqrt, extra_r=[epsc], scale=1.0 / D, bias=epsc[:, 0:1])
        P.op("dve", lambda e: e.reciprocal(out=rs.ap, in_=sd.ap), [sd], [rs])
        for c in range(8):
            o_ = tf.get()
            stt(o_, o_.ap, xT[c], xT[c].ap, pcol(0, "g_final", c), rs, rs.ap, ALU.mult, ALU.mult, extra_r=[pvs])
            P.dma("sp", outT_d[c * 128:(c + 1) * 128, tsl], o_.ap, reads=[o_], writes=[Bout])


_CACHE = {}


def kernel(**inp):
    inp = {k: np.asarray(v) for k, v in inp.items()}
    pv, wbig, lora, cst, rm = host_prep(inp)
    x, mem = inp["x"], inp["mem"]
    B = x.shape[0]
    nc = bass.Bass("TRN2", target_bir_lowering=False)
    build(nc)
    in_maps = []
    for b in range(B):
        in_maps.append({"xT": np.ascontiguousarray(x[b].T), "memT": np.ascontiguousarray(mem[b].T),
                        "pv": pv, "wbig": wbig, "lora": lora, "cst": cst, "rm": rm})
    res = run_bass_kernel_spmd(nc, in_maps, core_ids=list(range(B)))
    out = np.stack([np.ascontiguousarray(r["outT"].T) for r in res.results], axis=0)
    return out.astype(np.float32)
```
